# Optimizing a Trainium2 kernel written in Bass

```python
import math
import jax
import jax.numpy as jnp
from jax import lax
import numpy as np

D_MODEL = 1024
BATCH = 32
SEQ = 2048
DEPTH = 2
DEC_BATCH = 32
DEC_SEQ = 16
PAST_LEN = 4096

CHUNK = 64
Q_BLOCK = 128
N_EVEN = (DEPTH + 1) // 2
N_ODD = DEPTH // 2
EPS = 1e-6

SB_HEADS = 8
SB_HEAD_DIM = 64
SB_WIDTH = SB_HEADS * SB_HEAD_DIM
SB_SCALE = 1.0 / math.sqrt(SB_HEAD_DIM)

CONV_DIM = D_MODEL // 2
CONV_W = 3

SGU_CHUNK = 128
SGU_GROUPS = 4
SGU_DIM = D_MODEL // 2
SGU_GROUP_DIM = SGU_DIM // SGU_GROUPS

MLA_HEADS = 8
Q_LORA = 384
KV_LORA = 256
NOPE_DIM = 64
ROPE_DIM = 32
V_DIM = 64
ROPE_THETA = 10000.0
MLA_SCALE = 1.0 / math.sqrt(NOPE_DIM + ROPE_DIM)

D_FF = 4 * D_MODEL

EVEN_SPLITS = (SB_WIDTH, 2 * SB_WIDTH, 3 * SB_WIDTH, 3 * SB_WIDTH + CONV_DIM, 3 * SB_WIDTH + 2 * CONV_DIM)
EVEN_IN = 3 * SB_WIDTH + 3 * CONV_DIM
EVEN_MIX = SB_WIDTH + CONV_DIM
ODD_SPLITS = (SGU_DIM, 2 * SGU_DIM, 2 * SGU_DIM + Q_LORA, 2 * SGU_DIM + Q_LORA + KV_LORA)
ODD_IN = 2 * SGU_DIM + Q_LORA + KV_LORA + ROPE_DIM
ODD_MIX = SGU_DIM + MLA_HEADS * V_DIM

kernel_name = "hybrid_streaming_encoder_step"


def rms_norm(x, g):
    xf = x.astype(jnp.float32)
    y = xf * lax.rsqrt(jnp.mean(xf * xf, axis=-1, keepdims=True) + EPS)
    return (y * g.astype(jnp.float32)).astype(x.dtype)


def layer_norm(x, g, b):
    xf = x.astype(jnp.float32)
    mu = jnp.mean(xf, axis=-1, keepdims=True)
    xc = xf - mu
    var = jnp.mean(xc * xc, axis=-1, keepdims=True)
    return (xc * lax.rsqrt(var + EPS) * g.astype(jnp.float32) + b.astype(jnp.float32)).astype(x.dtype)


def rope(x, pos):
    half = ROPE_DIM // 2
    inv = ROPE_THETA ** (-jnp.arange(half, dtype=jnp.float32) / half)
    ang = pos.astype(jnp.float32)[:, None] * inv[None, :]
    shape = (1, ang.shape[0]) + (1,) * (x.ndim - 3) + (half,)
    cos = jnp.cos(ang).reshape(shape)
    sin = jnp.sin(ang).reshape(shape)
    xf = x.astype(jnp.float32)
    x1, x2 = xf[..., :half], xf[..., half:]
    return jnp.concatenate([x1 * cos - x2 * sin, x1 * sin + x2 * cos], axis=-1).astype(x.dtype)


def attend_in_query_blocks(fn, q_arrays, q_pos):
    T = q_pos.shape[0]
    if T <= Q_BLOCK:
        return fn(*q_arrays, q_pos)
    nb = T // Q_BLOCK
    blocks = tuple(jnp.moveaxis(a.reshape((a.shape[0], nb, Q_BLOCK) + a.shape[2:]), 1, 0) for a in q_arrays)
    out = lax.map(lambda args: fn(*args), blocks + (q_pos.reshape(nb, Q_BLOCK),))
    out = jnp.moveaxis(out, 0, 1)
    return out.reshape((out.shape[0], T) + out.shape[3:])


def sb_attend(q, k, v, q_pos, k_pos):
    f32 = jnp.float32
    z = jnp.einsum("bqhd,bkhd->bhqk", q.astype(f32), k.astype(f32)) * SB_SCALE
    visible = k_pos[None, :] < q_pos[:, None]
    log_beta = jax.nn.log_sigmoid(z)
    log_1m_beta = jnp.where(visible, jax.nn.log_sigmoid(-z), 0.0)
    later = lax.cumsum(log_1m_beta, axis=3, reverse=True) - log_1m_beta
    w = jnp.where(visible, jnp.exp(log_beta + later), 0.0)
    return jnp.einsum("bhqk,bkhd->bqhd", w, v.astype(f32)).astype(q.dtype)


def short_conv(u, prev, w):
    ext = jnp.concatenate([prev, u], axis=1)
    y = lax.conv_general_dilated(ext, w[:, None, :].astype(ext.dtype), window_strides=(1,), padding="VALID",
                                 dimension_numbers=("NWC", "WIO", "NWC"), feature_group_count=u.shape[-1])
    return y, ext[:, -(CONV_W - 1):]


def spatial_gate(vn, w_s, b_s):
    B, T, _ = vn.shape
    L = min(T, SGU_CHUNK)
    nc = T // L
    vg = vn.reshape(B, nc, L, SGU_GROUPS, SGU_GROUP_DIM)
    tri = jnp.tril(jnp.ones((L, L), dtype=bool))
    ws = jnp.where(tri, w_s[:, :L, :L], 0.0)
    s = jnp.einsum("gts,bcsgd->bctgd", ws, vg) + jnp.transpose(b_s[:, :L])[None, None, :, :, None]
    return s.reshape(B, T, SGU_DIM)


def mla_attend(q_lat, q_pe, ckv, kpe, q_pos, k_pos):
    f32 = jnp.float32
    ckv32 = ckv.astype(f32)
    s = (jnp.einsum("bqhc,bkc->bhqk", q_lat.astype(f32), ckv32)
         + jnp.einsum("bqhr,bkr->bhqk", q_pe.astype(f32), kpe.astype(f32)))
    visible = (k_pos[None, :] // CHUNK) <= (q_pos[:, None] // CHUNK)
    p = jax.nn.softmax(jnp.where(visible, s * MLA_SCALE, -1e30), axis=-1)
    return jnp.einsum("bhqk,bkc->bqhc", p, ckv32).astype(q_lat.dtype)


def even_mixer(h, pos, past, w_in, w_conv, w_out):
    B, T, _ = h.shape
    q, k, v, g_post, g_pre, u = jnp.split(h @ w_in, EVEN_SPLITS, axis=-1)
    q = q.reshape(B, T, SB_HEADS, SB_HEAD_DIM)
    k = k.reshape(B, T, SB_HEADS, SB_HEAD_DIM)
    v = v.reshape(B, T, SB_HEADS, SB_HEAD_DIM)
    conv_in = g_pre * u
    if past is None:
        k_all, v_all, k_pos = k, v, pos
        conv_prev = jnp.zeros((B, CONV_W - 1, CONV_DIM), conv_in.dtype)
    else:
        k_past, v_past, conv_prev = past
        k_all = jnp.concatenate([k_past, k], axis=1)
        v_all = jnp.concatenate([v_past, v], axis=1)
        k_pos = jnp.concatenate([jnp.arange(k_past.shape[1], dtype=jnp.int32), pos])
    attn = attend_in_query_blocks(lambda qb, pb: sb_attend(qb, k_all, v_all, pb, k_pos), (q,), pos)
    conv_out, conv_state = short_conv(conv_in, conv_prev, w_conv)
    mixed = jnp.concatenate([attn.reshape(B, T, SB_WIDTH), g_post * conv_out], axis=-1)
    return mixed @ w_out, (k, v, conv_state)


def odd_mixer(h, pos, past, w_in, ln_g, ln_b, w_s, b_s, q_norm_g, kv_norm_g, w_uq, w_uk, w_uv, w_out):
    B, T, _ = h.shape
    u, v, cq, ckv, kpe = jnp.split(h @ w_in, ODD_SPLITS, axis=-1)
    vn = layer_norm(v, ln_g, ln_b)
    sgu = u * spatial_gate(vn, w_s, b_s)
    cq = rms_norm(cq, q_norm_g)
    qf = (cq @ w_uq).reshape(B, T, MLA_HEADS, NOPE_DIM + ROPE_DIM)
    q_nope = qf[..., :NOPE_DIM]
    q_pe = rope(qf[..., NOPE_DIM:], pos)
    q_lat = jnp.einsum("bthn,hnc->bthc", q_nope, w_uk)
    ckv = rms_norm(ckv, kv_norm_g)
    kpe = rope(kpe, pos)
    if past is None:
        ckv_all, kpe_all, k_pos = ckv, kpe, pos
    else:
        ckv_past, kpe_past = past
        ckv_all = jnp.concatenate([ckv_past, ckv], axis=1)
        kpe_all = jnp.concatenate([kpe_past, kpe], axis=1)
        k_pos = jnp.concatenate([jnp.arange(ckv_past.shape[1], dtype=jnp.int32), pos])
    o_lat = attend_in_query_blocks(lambda ql, qp, pb: mla_attend(ql, qp, ckv_all, kpe_all, pb, k_pos),
                                   (q_lat, q_pe), pos)
    attn = jnp.einsum("bthc,hcv->bthv", o_lat, w_uv).reshape(B, T, MLA_HEADS * V_DIM)
    out = jnp.concatenate([sgu, attn], axis=-1) @ w_out
    return out, (ckv, kpe, vn)


def sq_relu_mlp(h, w_up, w_down):
    return jnp.square(jax.nn.relu(h @ w_up)) @ w_down


def run_trunk(x, pos, past, p):
    sb_k, sb_v, conv, ckv, kpe, sgu_v = [], [], [], [], [], []
    for layer in range(DEPTH):
        j = layer // 2
        h = rms_norm(x, p["mix_pre_g"][layer])
        if layer % 2 == 0:
            lp = None if past is None else (past["sb_k"][j], past["sb_v"][j], past["conv"][j])
            out, (k_new, v_new, c_new) = even_mixer(h, pos, lp, p["even_w_in"][j], p["even_w_conv"][j],
                                                    p["even_w_out"][j])
            sb_k.append(k_new)
            sb_v.append(v_new)
            conv.append(c_new)
        else:
            lp = None if past is None else (past["ckv"][j], past["kpe"][j])
            out, (ckv_new, kpe_new, vn) = odd_mixer(
                h, pos, lp, p["odd_w_in"][j], p["sgu_ln_g"][j], p["sgu_ln_b"][j], p["sgu_w_s"][j],
                p["sgu_b_s"][j], p["mla_q_norm_g"][j], p["mla_kv_norm_g"][j], p["mla_w_uq"][j],
                p["mla_w_uk"][j], p["mla_w_uv"][j], p["odd_w_out"][j])
            ckv.append(ckv_new)
            kpe.append(kpe_new)
            if past is not None:
                sgu_v.append(vn)
        x = x + rms_norm(out, p["mix_post_g"][layer])
        h = rms_norm(x, p["ffn_pre_g"][layer])
        x = x + rms_norm(sq_relu_mlp(h, p["ffn_w_up"][layer], p["ffn_w_down"][layer]), p["ffn_post_g"][layer])
    states = {"sb_k": jnp.stack(sb_k), "sb_v": jnp.stack(sb_v), "conv": jnp.stack(conv),
              "ckv": jnp.stack(ckv), "kpe": jnp.stack(kpe)}
    if past is not None:
        states["sgu_v"] = jnp.stack(sgu_v)
    return x, states


def _normal(k, shape, scale):
    return scale * jax.random.normal(k, shape, jnp.float32)


def _gain(k, shape):
    return 1.0 + 0.05 * jax.random.normal(k, shape, jnp.float32)


def setup_inputs(seed: int = 0) -> dict:
    key = jax.random.key(seed)
    ks = jax.random.split(key, 27)
    return {
        "x_prompt": _normal(ks[0], (BATCH, SEQ, D_MODEL), 1.0),
        "x_sample": _normal(ks[1], (DEC_BATCH, DEC_SEQ, D_MODEL), 1.0),
        "cache_sb_k": _normal(ks[2], (N_EVEN, DEC_BATCH, PAST_LEN, SB_HEADS, SB_HEAD_DIM), 1.0),
        "cache_sb_v": _normal(ks[3], (N_EVEN, DEC_BATCH, PAST_LEN, SB_HEADS, SB_HEAD_DIM), 1.0),
        "state_conv": _normal(ks[4], (N_EVEN, DEC_BATCH, CONV_W - 1, CONV_DIM), 1.0),
        "cache_mla_ckv": _normal(ks[5], (N_ODD, DEC_BATCH, PAST_LEN, KV_LORA), 1.0),
        "cache_mla_kpe": _normal(ks[6], (N_ODD, DEC_BATCH, PAST_LEN, ROPE_DIM), 1.0),
        "mix_pre_g": _gain(ks[7], (DEPTH, D_MODEL)),
        "mix_post_g": _gain(ks[8], (DEPTH, D_MODEL)),
        "ffn_pre_g": _gain(ks[9], (DEPTH, D_MODEL)),
        "ffn_post_g": _gain(ks[10], (DEPTH, D_MODEL)),
        "even_w_in": _normal(ks[11], (N_EVEN, D_MODEL, EVEN_IN), D_MODEL ** -0.5),
        "even_w_conv": _normal(ks[12], (N_EVEN, CONV_W, CONV_DIM), CONV_W ** -0.5),
        "even_w_out": _normal(ks[13], (N_EVEN, EVEN_MIX, D_MODEL), EVEN_MIX ** -0.5),
        "odd_w_in": _normal(ks[14], (N_ODD, D_MODEL, ODD_IN), D_MODEL ** -0.5),
        "sgu_ln_g": _gain(ks[15], (N_ODD, SGU_DIM)),
        "sgu_ln_b": _normal(ks[16], (N_ODD, SGU_DIM), 0.02),
        "sgu_w_s": _normal(ks[17], (N_ODD, SGU_GROUPS, SGU_CHUNK, SGU_CHUNK), SGU_CHUNK ** -0.5),
        "sgu_b_s": _gain(ks[18], (N_ODD, SGU_GROUPS, SGU_CHUNK)),
        "mla_q_norm_g": _gain(ks[19], (N_ODD, Q_LORA)),
        "mla_kv_norm_g": _gain(ks[20], (N_ODD, KV_LORA)),
        "mla_w_uq": _normal(ks[21], (N_ODD, Q_LORA, MLA_HEADS * (NOPE_DIM + ROPE_DIM)), Q_LORA ** -0.5),
        "mla_w_uk": _normal(ks[22], (N_ODD, MLA_HEADS, NOPE_DIM, KV_LORA), KV_LORA ** -0.5),
        "mla_w_uv": _normal(ks[23], (N_ODD, MLA_HEADS, KV_LORA, V_DIM), KV_LORA ** -0.5),
        "odd_w_out": _normal(ks[24], (N_ODD, ODD_MIX, D_MODEL), ODD_MIX ** -0.5),
        "ffn_w_up": _normal(ks[25], (DEPTH, D_MODEL, D_FF), D_MODEL ** -0.5),
        "ffn_w_down": _normal(ks[26], (DEPTH, D_FF, D_MODEL), D_FF ** -0.5),
    }


def reference(x_prompt, x_sample, cache_sb_k, cache_sb_v, state_conv, cache_mla_ckv, cache_mla_kpe,
              mix_pre_g, mix_post_g, ffn_pre_g, ffn_post_g, even_w_in, even_w_conv, even_w_out,
              odd_w_in, sgu_ln_g, sgu_ln_b, sgu_w_s, sgu_b_s, mla_q_norm_g, mla_kv_norm_g,
              mla_w_uq, mla_w_uk, mla_w_uv, odd_w_out, ffn_w_up, ffn_w_down):
    params = {
        "mix_pre_g": mix_pre_g, "mix_post_g": mix_post_g, "ffn_pre_g": ffn_pre_g, "ffn_post_g": ffn_post_g,
        "even_w_in": even_w_in, "even_w_conv": even_w_conv, "even_w_out": even_w_out,
        "odd_w_in": odd_w_in, "sgu_ln_g": sgu_ln_g, "sgu_ln_b": sgu_ln_b, "sgu_w_s": sgu_w_s,
        "sgu_b_s": sgu_b_s, "mla_q_norm_g": mla_q_norm_g, "mla_kv_norm_g": mla_kv_norm_g,
        "mla_w_uq": mla_w_uq, "mla_w_uk": mla_w_uk, "mla_w_uv": mla_w_uv, "odd_w_out": odd_w_out,
        "ffn_w_up": ffn_w_up, "ffn_w_down": ffn_w_down,
    }
    pos_p = jnp.arange(x_prompt.shape[1], dtype=jnp.int32)
    y_prompt, st_p = run_trunk(x_prompt, pos_p, None, params)
    past_len = cache_sb_k.shape[2]
    pos_s = past_len + jnp.arange(x_sample.shape[1], dtype=jnp.int32)
    past = {"sb_k": cache_sb_k, "sb_v": cache_sb_v, "conv": state_conv,
            "ckv": cache_mla_ckv, "kpe": cache_mla_kpe}
    y_sample, st_s = run_trunk(x_sample, pos_s, past, params)
    return (y_prompt, y_sample,
            st_p["sb_k"], st_p["sb_v"], st_p["conv"], st_p["ckv"], st_p["kpe"],
            st_s["sb_k"], st_s["sb_v"], st_s["conv"], st_s["ckv"], st_s["kpe"], st_s["sgu_v"])
```

```python
import os
import math
import numpy as np
from contextlib import ExitStack
import concourse.bass as bass
import concourse.mybir as mybir
from concourse.bass_utils import run_bass_kernel_spmd

F32 = mybir.dt.float32
BF16 = mybir.dt.bfloat16
AF = mybir.ActivationFunctionType
ALU = mybir.AluOpType

NCORES = 8
SPB = 4
T = 2048
D = 1024
PAST = 4096
DS = 16
EPS = 1e-6
SB_SCALE = 0.125
MLA_SCALE = 1.0 / math.sqrt(96.0)

COMPUTE = ("pe", "act", "dve", "pool")
ALLENG = ("pe", "act", "dve", "pool", "sp")
SAME_ENGINE_SYNC = True


class Prog:
    def __init__(self, nc):
        self.nc = nc
        self.ops = []

    def op(self, eng, fn, reads=(), writes=()):
        self.ops.append(dict(eng=eng, fn=fn, reads=tuple(reads), writes=tuple(writes), dma=False, bar=False))

    def dma(self, fn, reads=(), writes=(), semkey=None, queue="sp"):
        self.ops.append(dict(eng=queue, fn=fn, reads=tuple(reads), writes=tuple(writes), dma=True, bar=False,
                             semkey=semkey))

    def barrier(self, fn):
        self.ops.append(dict(eng="dve", fn=fn, reads=(), writes=(), dma=False, bar=True))

    def resolve(self):
        ops = self.ops
        last_w = {}
        readers = {}
        last_dma_by_sem = {}
        eng_pos = {}
        last_on_eng = {}
        pending = {}
        for i, o in enumerate(ops):
            e = o["eng"]
            o["pos"] = eng_pos.get(e, 0)
            eng_pos[e] = o["pos"] + 1
            deps = set()
            raw = set()
            if o["bar"]:
                for e2, j in last_on_eng.items():
                    deps.add(j)
                for sk, j in last_dma_by_sem.items():
                    deps.add(j)
                raw = set(deps)
                for e2 in ALLENG:
                    pending[e2] = i
                pending.pop("dve", None)
            else:
                if e in pending:
                    deps.add(pending.pop(e))
                for k in o["reads"]:
                    if k in last_w:
                        deps.add(last_w[k]); raw.add(last_w[k])
                    if k.startswith("ps") or k.startswith("acc"):
                        for e2, j in readers.get(k, {}).items():
                            if e2 != e and not isinstance(j, list):
                                deps.add(j)
                for k in o["writes"]:
                    if k in last_w:
                        deps.add(last_w[k])
                    for j in readers.get(k, {}).values():
                        if isinstance(j, list):
                            deps.update(j)
                        else:
                            deps.add(j)
                if o["dma"]:
                    sk = o["semkey"]
                    if sk in last_dma_by_sem:
                        deps.add(last_dma_by_sem[sk])
                    last_dma_by_sem[sk] = i
            deps.discard(i)
            o["deps"] = deps
            o["raw"] = raw
            if not o["dma"]:
                last_on_eng[e] = i
            for k in o["reads"]:
                r = readers.setdefault(k, {})
                if o["dma"]:
                    r.setdefault("dma", []).append(i)
                else:
                    r[e] = i
            for k in o["writes"]:
                last_w[k] = i
                readers[k] = {}
        waited = {}
        waited_dma = {}
        dma_count = {}
        for i, o in enumerate(ops):
            if o["dma"]:
                sk = o["semkey"]
                dma_count[sk] = dma_count.get(sk, 0) + 1
                o["dma_n"] = dma_count[sk]
            o["needs_inc"] = False
        for i, o in enumerate(ops):
            e = o["eng"]
            w = waited.setdefault(e, {})
            wd = waited_dma.setdefault(e, {})
            waits = []
            for j in sorted(o["deps"]):
                p = ops[j]
                if p["dma"]:
                    sk = p["semkey"]
                    if wd.get(sk, 0) >= p["dma_n"]:
                        continue
                    wd[sk] = p["dma_n"]
                    waits.append(("d", sk, p["dma_n"]))
                else:
                    pe_ = p["eng"]
                    if pe_ == e:
                        if e == "pe" or e == "sp":
                            continue
                        if (not SAME_ENGINE_SYNC) or (j not in o["raw"]):
                            continue
                    if w.get(pe_, -1) >= p["pos"]:
                        continue
                    w[pe_] = p["pos"]
                    p["needs_inc"] = True
                    waits.append(("c", pe_, j))
            o["waits"] = waits
        last_idx = {}
        for i, o in enumerate(ops):
            if not o["dma"]:
                last_idx[o["eng"]] = i
        for e_, i in last_idx.items():
            if e_ in COMPUTE:
                ops[i]["needs_inc"] = True
        cnt = {}
        for o in ops:
            if o["dma"]:
                continue
            e = o["eng"]
            if o["needs_inc"]:
                cnt[e] = cnt.get(e, 0) + 1
            o["count"] = cnt.get(e, 0)
        self.sem_keys = sorted({o["semkey"] for o in ops if o["dma"]}, key=str)
        self.final_dma = dict(dma_count)
        return cnt

    def emit(self, stack):
        nc = self.nc
        ops = self.ops
        cnt = self.resolve()
        esem = {e: stack.enter_context(nc.semaphore("s_" + e)) for e in COMPUTE}
        dsem = {k: stack.enter_context(nc.semaphore("d_%d" % i)) for i, k in enumerate(self.sem_keys)}
        block = stack.enter_context(nc.Block())
        by_eng = {}
        for o in ops:
            by_eng.setdefault(o["eng"], []).append(o)

        def run(engname, eng):
            for o in by_eng.get(engname, []):
                for wt in o["waits"]:
                    if wt[0] == "d":
                        eng.wait_ge(dsem[wt[1]], 16 * wt[2])
                    else:
                        eng.wait_ge(esem[wt[1]], ops[wt[2]]["count"])
                ins = o["fn"](eng)
                if o["dma"]:
                    ins.then_inc(dsem[o["semkey"]], 16)
                elif o["needs_inc"]:
                    ins.then_inc(esem[engname], 1)
            if engname == "sp":
                for k, n in self.final_dma.items():
                    eng.wait_ge(dsem[k], 16 * n)
                for e2 in COMPUTE:
                    if cnt.get(e2, 0) > 0:
                        eng.wait_ge(esem[e2], cnt[e2])

        @block.sync
        def _(sync):
            run("sp", sync)

        @block.tensor
        def _(tensor):
            run("pe", tensor)

        @block.scalar
        def _(scalar):
            run("act", scalar)

        @block.vector
        def _(vector):
            run("dve", vector)

        @block.gpsimd
        def _(gpsimd):
            run("pool", gpsimd)


SLAB_EIN = 0
SLAB_EOUT = 7
SLAB_FUP0 = 9
SLAB_FDN0 = 17
SLAB_OIN = 25
SLAB_OOUT = 29
SLAB_FUP1 = 31
SLAB_FDN1 = 39
NSLAB = 47


def build_program(units):
    nc = bass.Bass("TRN2", target_bir_lowering=False)

    def din(name, shape):
        return nc.dram_tensor(name, list(shape), F32, kind="ExternalInput").ap()

    def dout(name, shape):
        return nc.dram_tensor(name, list(shape), F32, kind="ExternalOutput").ap()

    xp = din("xp", [SPB, T, D]); xs = din("xs", [SPB * DS, D])
    csk = din("csk", [SPB, PAST, 512]); csv = din("csv", [SPB, PAST, 512])
    sconv = din("sconv", [SPB, 2, 512])
    cckv = din("cckv", [SPB, PAST, 256]); ckpe = din("ckpe", [SPB, PAST, 32])
    ein = din("ein", [D, 3072]); eout = din("eout", [D, D]); oin = din("oin", [D, 1696]); oout = din("oout", [D, D])
    fup = din("fup", [2, D, 4096]); fdn = din("fdn", [2, 4096, D])
    grow = din("grow", [79, 128])
    lng = din("lng", [1, 512]); lnb = din("lnb", [1, 512]); kvg = din("kvg", [1, 256]); bsd = din("bsd", [1, 512])
    bss = din("bss", [1, 4, 16])
    wsd = din("wsd", [4, 128, 128])
    wuq = din("wuq", [384, 768]); wuk = din("wuk", [8, 64, 256]); wuv = din("wuv", [8, 256, 64])
    cst = din("cst", [128, 1408])
    rope_t = din("rope_t", [128, 512]); rope_ts = din("rope_ts", [64, 32])
    rope_f = din("rope_f", [2, 32, T]); rope_fs = din("rope_fs", [2, 32, 64])

    y_p = dout("y_p", [SPB, T, D]); y_s = dout("y_s", [SPB * DS, D])
    sbk_p = dout("sbk_p", [SPB, T, 512]); sbv_p = dout("sbv_p", [SPB, T, 512])
    conv_p = dout("conv_p", [SPB, 2, 512])
    ckv_p = dout("ckv_p", [SPB, T, 256]); kpe_p = dout("kpe_p", [SPB, T, 32])
    sbk_s = dout("sbk_s", [SPB * DS, 512]); sbv_s = dout("sbv_s", [SPB * DS, 512])
    conv_s = dout("conv_s", [SPB, 2, 512])
    ckv_s = dout("ckv_s", [SPB * DS, 256]); kpe_s = dout("kpe_s", [SPB * DS, 32])
    sguv_s = dout("sguv_s", [SPB * DS, 512])

    wsc = nc.dram_tensor("wsc", [NSLAB, 128, 4096], BF16, kind="Internal").ap()

    P = Prog(nc)
    with ExitStack() as st:
        def sb(name, shape, dt):
            return st.enter_context(nc.sbuf_tensor(name, list(shape), dt))

        xT = sb("xT", [128, 8, 512], F32)
        identf = sb("identf", [128, 128], F32)
        identb = sb("identb", [128, 128], BF16)
        ntri = sb("ntri", [128, 128], BF16)
        nones = sb("nones", [128, 128], BF16)
        onesb = sb("onesb", [128, 128], BF16)
        sbm = sb("sbm", [128, 512], BF16)
        mlm = sb("mlm", [128, 512], BF16)
        gcol = sb("gcol", [128, 80], F32)
        lnG = sb("lnG", [128, 512], F32); lnB = sb("lnB", [128, 512], F32)
        kvG = sb("kvG", [128, 256], F32); bsB = sb("bsB", [128, 512], F32); bsBs = sb("bsBs", [128, 4, 64], F32)
        wsT = sb("wsT", [128, 4, 128], BF16); wsbd = sb("wsbd", [128, 4, 64], BF16)
        wuqn = sb("wuqn", [128, 3, 512], BF16); wuqp = sb("wuqp", [128, 3, 256], BF16)
        wuqs = sb("wuqs", [128, 3, 256], BF16)
        wukp = sb("wukp", [128, 4, 256], BF16); wuvb = sb("wuvb", [128, 8, 2, 64], BF16)
        ropeT = sb("ropeT", [128, 512], F32); ropeTs = sb("ropeTs", [128, 32], F32)
        h = sb("h", [128, 8, 512], BF16)
        mix = sb("mix", [128, 8, 512], BF16)
        o = sb("o", [128, 8, 512], F32)
        rstd = sb("rstd", [128, 512], F32)
        sq = sb("sq", [128, 2, 512], BF16)
        xin = sb("xin", [128, 1024], F32)
        yout = sb("yout", [128, 1024], F32)
        tmpf = sb("tmpf", [128, 2, 512], F32)
        cstate = sb("cstate", [128, 4, 2], F32)
        small4 = sb("small4", [128, 4, 32], F32)
        bart = sb("bart", [128, 4], F32)
        slab = sb("slab", [128, 2, 4096], BF16)
        AA = sb("arenaA", [128, 21248], BF16)
        AC = sb("arenaC", [128, 26624], BF16)
        ps = [st.enter_context(nc.psum_tensor("ps%d" % i, [128, 512], F32)) for i in range(8)]
        psb = [p_[:, :].bitcast(BF16) for p_ in ps]

        def carve(arena, off_bytes, nelem, dt):
            a0 = off_bytes // 2
            if dt == BF16:
                return arena[:, a0:a0 + nelem]
            return arena[:, a0:a0 + 2 * nelem].bitcast(F32)

        qT = carve(AA, 0, 2048, BF16).rearrange("p (m t) -> p m t", m=4)
        e_t = carve(AA, 4096, 512, F32)
        sp_t = carve(AA, 6144, 1024, BF16).rearrange("p (a t) -> p a t", a=2)
        spm_t = carve(AA, 8192, 1024, BF16).rearrange("p (a t) -> p a t", a=2)
        w_t = carve(AA, 10240, 1024, BF16).rearrange("p (a t) -> p a t", a=2)
        S_b = [carve(AA, 12288, 512, BF16), carve(AA, 22528, 512, BF16)]
        cin = carve(AA, 13312, 520, F32)
        kvout = carve(AA, 15872, 1024, F32).rearrange("p (a t) -> p a t", a=2)
        ksn = carve(AA, 19968, 256, BF16).rearrange("p (m t) -> p m t", m=4)
        vnew = carve(AA, 20480, 512, BF16)
        qs_b = [carve(AA, 21504, 16, BF16), carve(AA, 21568, 16, BF16)]
        cqT = carve(AA, 0, 1536, BF16).rearrange("p (m t) -> p m t", m=3)
        qnT = carve(AA, 3072, 2048, BF16).rearrange("p (m t) -> p m t", m=4)
        uT = carve(AA, 7168, 2048, F32).rearrange("p (m t) -> p m t", m=4)
        vnb = carve(AA, 15360, 2048, BF16).rearrange("p (a t) -> p a t", a=4)
        ckst_b = [carve(AA, 19456, 288, F32), carve(AA, 40960, 288, F32)]
        qlat_b = [carve(AA, 20992, 1024, BF16).rearrange("p (a t) -> p a t", a=2), carve(AA, 37888, 1024, BF16).rearrange("p (a t) -> p a t", a=2)]
        qpe_b = [carve(AA, 23040, 512, BF16), carve(AA, 39936, 512, BF16)]
        OLs = carve(AA, 24064, 1024, BF16).rearrange("p (a t) -> p a t", a=2)
        rden = carve(AA, 26112, 512, F32)
        cosF = carve(AA, 28160, 512, F32)
        sinF = carve(AA, 30208, 512, F32)
        p_t = carve(AA, 32256, 1024, BF16).rearrange("p (a t) -> p a t", a=2)
        vnf = carve(AA, 34304, 512, F32)
        kpb_b = [carve(AA, 36352, 32, BF16), carve(AA, 42112, 32, BF16)]
        ckvn = carve(AA, 36416, 256, BF16)
        h1 = carve(AA, 0, 16384, BF16).rearrange("p (j t) -> p j t", j=32)
        cstS = carve(AA, 0, 1408, F32)
        growS = carve(AA, 5632, 128, F32)
        wsS = carve(AA, 6144, 512, F32).rearrange("p (g s) -> p g s", g=4)
        wuqS = carve(AA, 8192, 2304, F32).rearrange("p (k n) -> p k n", k=3)
        wukS = carve(AA, 17408, 1024, F32).rearrange("p (j c) -> p j c", j=4)
        wuvS = carve(AA, 21504, 1024, F32).rearrange("p (a b) -> p a b", a=16)

        ksT = carve(AC, 0, 8192, BF16).rearrange("p (m t) -> p m t", m=4)
        vtok = carve(AC, 16384, 8192, BF16).rearrange("p (a t) -> p a t", a=16)
        ckvT = carve(AC, 32768, 4096, BF16).rearrange("p (a t) -> p a t", a=2)
        ckvtok = carve(AC, 40960, 4096, BF16).rearrange("p (a t) -> p a t", a=16)
        kpeT = carve(AC, 49152, 2048, BF16)
        kld = carve(AC, 0, 2048, F32).rearrange("p (a d) -> p a d", a=32)
        vld = carve(AC, 8192, 2048, F32).rearrange("p (a d) -> p a d", a=32)
        kTs_b = [carve(AC, 16384, 4112, BF16), carve(AC, 24640, 4112, BF16)]
        vh_b = [carve(AC, 32896, 33 * 64, BF16).rearrange("p (a d) -> p a d", a=33), carve(AC, 37120, 33 * 64, BF16).rearrange("p (a d) -> p a d", a=33)]
        kbf_b = [carve(AC, 41344, 2048, BF16).rearrange("p (a d) -> p a d", a=32), carve(AC, 45440, 2048, BF16).rearrange("p (a d) -> p a d", a=32)]
        cstg = carve(AC, 0, 2048, F32)
        ckvtok_s = carve(AC, 8192, 33 * 256, BF16).rearrange("p (a d) -> p a d", a=33)
        ckvT_s = carve(AC, 25088, 2 * 4112, BF16).rearrange("p (a t) -> p a t", a=2)
        kpeT_s = carve(AC, 41536, 4112, BF16)
        kpbs = carve(AC, 49792, 1024, BF16).rearrange("p (a d) -> p a d", a=32)
        stg = carve(AC, 0, 8192, F32).rearrange("p (a n) -> p a n", a=2)
        sbf = carve(AC, 32768, 8192, BF16).rearrange("p (a n) -> p a n", a=2)

        state = dict(dbank=0, zb=0, bb=0, slabk=0, slab_issued=0, tmp=0)

        def dbank():
            b = state["dbank"] % 3
            state["dbank"] += 1
            return b

        def bkey(b):
            return "ps%d" % b

        def barrier():
            P.barrier(lambda e: e.memset(bart[:, 0:4], 0.0))

        def setup():
            P.dma(lambda e: e.dma_start(out=cstS[:, :], in_=cst[:, :]), writes=["cstS"], semkey="su0")
            P.op("act", lambda e: e.activation(out=identf[:, :], in_=cstS[:, 0:128], func=AF.Copy), reads=["cstS"], writes=["identf"])
            P.op("dve", lambda e: e.tensor_copy(out=identb[:, :], in_=cstS[:, 0:128]), reads=["cstS"], writes=["identb"])
            P.op("dve", lambda e: e.tensor_copy(out=ntri[:, :], in_=cstS[:, 128:256]), reads=["cstS"], writes=["ntri"])
            P.op("dve", lambda e: e.tensor_copy(out=sbm[:, :], in_=cstS[:, 256:768]), reads=["cstS"], writes=["sbm"])
            P.op("dve", lambda e: e.tensor_copy(out=mlm[:, :], in_=cstS[:, 768:1280]), reads=["cstS"], writes=["mlm"])
            P.op("pool", lambda e: e.memset(nones[:, :], -1.0), writes=["nones"])
            P.op("pool", lambda e: e.memset(onesb[:, :], 1.0), writes=["onesb"])
            P.op("pool", lambda e: e.memset(wsbd[:, :, :], 0.0), writes=["wsbd"])
            P.op("pool", lambda e: e.memset(cstate[:, :, :], 0.0), writes=["cstate"])
            P.dma(lambda e: e.dma_start(out=growS[0:79, :], in_=grow[:, :]), writes=["growS"], semkey="su1")
            P.op("pe", lambda e: e.transpose(ps[0][:, 0:79], growS[0:79, :], identf[0:79, 0:79]), reads=["growS", "identf"], writes=[bkey(0)])
            P.op("act", lambda e: e.activation(out=gcol[:, 0:79], in_=ps[0][:, 0:79], func=AF.Copy), reads=[bkey(0)], writes=["gcol"])
            P.dma(lambda e: e.dma_start(out=lnG[:, :], in_=lng[0:1, :].broadcast_to([128, 512])), writes=["lnG"], semkey="su2")
            P.dma(lambda e: e.dma_start(out=lnB[:, :], in_=lnb[0:1, :].broadcast_to([128, 512])), writes=["lnB"], semkey="su3")
            P.dma(lambda e: e.dma_start(out=kvG[:, :], in_=kvg[0:1, :].broadcast_to([128, 256])), writes=["kvG"], semkey="su4")
            P.dma(lambda e: e.dma_start(out=bsB[:, :], in_=bsd[0:1, :].broadcast_to([128, 512])), writes=["bsB"], semkey="su5")
            for j in range(4):
                P.dma(lambda e, j=j: e.dma_start(out=bsBs[:, :, j * 16:(j + 1) * 16], in_=bss[0:1, :, :].broadcast_to([128, 4, 16])),
                      writes=["bsBs"], semkey="su6")
            P.dma(lambda e: e.dma_start(out=ropeT[:, :], in_=rope_t[:, :]), writes=["ropeT"], semkey="su7")
            P.dma(lambda e: e.dma_start(out=ropeTs[0:64, :], in_=rope_ts[:, :]), writes=["ropeTs"], semkey="su8")
            P.dma(lambda e: e.dma_start(out=wsS[:, :, :], in_=wsd.rearrange("g t s -> t g s")), writes=["wsS"], semkey="su9")
            for g in range(4):
                P.op("dve", lambda e, g=g: e.tensor_tensor(out=wsS[:, g, :], in0=wsS[:, g, :], in1=cstS[:, 1280:1408], op=ALU.mult),
                     reads=["wsS", "cstS"], writes=["wsS"])
            P.op("pe", lambda e: [e.transpose(ps[1][:, g * 128:(g + 1) * 128], wsS[:, g, :], identf[:, :]) for g in range(4)][-1],
                 reads=["wsS", "identf"], writes=[bkey(1)])
            P.op("act", lambda e: e.activation(out=wsT[:, :, :], in_=ps[1][:, :].rearrange("p (g t) -> p g t", g=4), func=AF.Copy),
                 reads=[bkey(1)], writes=["wsT"])
            for j in range(4):
                P.dma(lambda e, j=j: e.dma_start(out=wsbd[16 * j:16 * j + 16, :, 16 * j:16 * j + 16], in_=wsT[0:16, :, 0:16]),
                      reads=["wsT", "wsbd"], writes=["wsbd"], semkey="su10")
            P.dma(lambda e: e.dma_start(out=wuqS[:, :, :], in_=wuq.rearrange("(k p) n -> p k n", p=128)), writes=["wuqS"], semkey="su11")
            for kc in range(3):
                src = wuqS[:, kc, :].rearrange("p (h d) -> p h d", d=96)
                P.op("dve", lambda e, kc=kc, src=src: e.tensor_copy(out=wuqn[:, kc, :].rearrange("p (h d) -> p h d", d=64), in_=src[:, :, 0:64]),
                     reads=["wuqS"], writes=["wuqn"])
                P.op("dve", lambda e, kc=kc, src=src: e.tensor_copy(out=wuqp[:, kc, :].rearrange("p (h d) -> p h d", d=32), in_=src[:, :, 64:96]),
                     reads=["wuqS"], writes=["wuqp"])
                P.op("pool", lambda e, kc=kc, src=src: e.tensor_copy(out=wuqs[:, kc, :].rearrange("p (h d) -> p h d", d=32)[:, :, 0:16], in_=src[:, :, 80:96]),
                     reads=["wuqS"], writes=["wuqs"])
                P.op("pool", lambda e, kc=kc, src=src: e.tensor_copy(out=wuqs[:, kc, :].rearrange("p (h d) -> p h d", d=32)[:, :, 16:32], in_=src[:, :, 64:80]),
                     reads=["wuqS"], writes=["wuqs"])
            for two in range(2):
                P.dma(lambda e, two=two: e.dma_start(out=wukS[two * 64:(two + 1) * 64, :, :], in_=wuk.rearrange("(j two) n c -> two n j c", two=2)[two]),
                      writes=["wukS"], semkey="su12")
            P.op("dve", lambda e: e.tensor_copy(out=wukp[:, :, :], in_=wukS[:, :, :]), reads=["wukS"], writes=["wukp"])
            P.dma(lambda e: e.dma_start(out=wuvS[:, :, :], in_=wuv.rearrange("h (cc p) v -> p (h cc) v", p=128)), writes=["wuvS"], semkey="su13")
            P.op("dve", lambda e: e.tensor_copy(out=wuvb[:, :, :, :].rearrange("p h c v -> p (h c) v"), in_=wuvS[:, :, :]), reads=["wuvS"], writes=["wuvb"])

            def Wv(W, c0, c1):
                return W.rearrange("(kc p) n -> p kc n", p=128)[:, :, c0:c1]

            slabs = []
            for c0 in (0, 512, 1024):
                slabs.append(([(Wv(ein, c0, c0 + 512), 0, 512)], 8, 512))
            for c in range(4):
                slabs.append(([(Wv(ein, 1536 + 128 * c, 1536 + 128 * c + 128), 0, 128),
                               (Wv(ein, 2048 + 128 * c, 2048 + 128 * c + 128), 128, 128),
                               (Wv(ein, 2560 + 128 * c, 2560 + 128 * c + 128), 256, 128)], 8, 384))
            for c0 in (0, 512):
                slabs.append(([(Wv(eout, c0, c0 + 512), 0, 512)], 8, 512))
            for j in range(8):
                slabs.append(([(Wv(fup[0], 512 * j, 512 * j + 512), 0, 512)], 8, 512))
            for n in range(8):
                slabs.append(([(Wv(fdn[0], 128 * n, 128 * n + 128), 0, 128)], 32, 128))
            slabs.append(([(Wv(oin, 0, 512), 0, 512)], 8, 512))
            slabs.append(([(Wv(oin, 1024, 1408), 0, 384)], 8, 384))
            slabs.append(([(Wv(oin, 512, 1024), 0, 512)], 8, 512))
            slabs.append(([(Wv(oin, 1408, 1696), 0, 288)], 8, 288))
            for c0 in (0, 512):
                slabs.append(([(Wv(oout, c0, c0 + 512), 0, 512)], 8, 512))
            for j in range(8):
                slabs.append(([(Wv(fup[1], 512 * j, 512 * j + 512), 0, 512)], 8, 512))
            for n in range(8):
                slabs.append(([(Wv(fdn[1], 128 * n, 128 * n + 128), 0, 128)], 32, 128))
            assert len(slabs) == NSLAB
            engs = ["dve", "act", "pool"]
            for i, (pieces, kc, nct) in enumerate(slabs):
                b = i % 2
                n = kc * nct
                sv = stg[:, b, 0:n].rearrange("p (k n) -> p k n", k=kc)
                for pi, (src, coff, ncol) in enumerate(pieces):
                    P.dma(lambda e, sv=sv, src=src, coff=coff, ncol=ncol: e.dma_start(out=sv[:, :, coff:coff + ncol], in_=src),
                          writes=["stg%d" % b], semkey="stg%d_%d" % (b, pi))
                en = engs[i % 3]
                if en == "act":
                    P.op("act", lambda e, b=b, n=n: e.activation(out=sbf[:, b, 0:n], in_=stg[:, b, 0:n], func=AF.Copy),
                         reads=["stg%d" % b], writes=["sbf%d" % b])
                else:
                    P.op(en, lambda e, b=b, n=n: e.tensor_copy(out=sbf[:, b, 0:n], in_=stg[:, b, 0:n]),
                         reads=["stg%d" % b], writes=["sbf%d" % b])
                P.dma(lambda e, i=i, b=b, n=n: e.dma_start(out=wsc[i, :, 0:n], in_=sbf[:, b, 0:n]),
                      reads=["sbf%d" % b], writes=["wsc%d" % i], semkey="sbfo%d" % b)

        unit_slab_seq = list(range(NSLAB))
        total_slabs = len(units) * NSLAB

        slab_n = {3: 3072, 4: 3072, 5: 3072, 6: 3072, SLAB_OIN + 1: 3072, SLAB_OIN + 3: 2304}

        def issue_slab(k):
            idx = unit_slab_seq[k % NSLAB]
            b = k % 2
            n = slab_n.get(idx, 4096)
            P.dma(lambda e, idx=idx, b=b, n=n: e.dma_start(out=slab[:, b, 0:n], in_=wsc[idx, :, 0:n]),
                  reads=["wsc%d" % idx], writes=["slab%d" % b], semkey="slab%d" % b)

        def get_slab(expect):
            k = state["slabk"]
            assert unit_slab_seq[k % NSLAB] == expect, (k, expect)
            while state["slab_issued"] <= min(k + 1, total_slabs - 1):
                issue_slab(state["slab_issued"])
                state["slab_issued"] += 1
            state["slabk"] += 1
            b = k % 2
            return slab[:, b, :], "slab%d" % b

        def rms_stats(srcs, keys, Dn, U):
            bank = dbank()
            n = len(srcs)
            for c in range(n):
                P.op("act", lambda e, c=c: e.activation(out=sq[:, c % 2, 0:U], in_=srcs[c], func=AF.Square),
                     reads=[keys[c]], writes=["sq%d" % (c % 2)])
                P.op("pe", lambda e, c=c: e.matmul(ps[bank][:, 0:U], onesb[:, :], sq[:, c % 2, 0:U], start=(c == 0), stop=(c == n - 1)),
                     reads=["sq%d" % (c % 2), "onesb"], writes=[bkey(bank)])
            P.op("act", lambda e: e.activation(out=rstd[:, 0:U], in_=ps[bank][:, 0:U], func=AF.Ln, scale=1.0 / Dn, bias=EPS),
                 reads=[bkey(bank)], writes=["rstd"])
            P.op("act", lambda e: e.activation(out=rstd[:, 0:U], in_=rstd[:, 0:U], func=AF.Exp, scale=-0.5),
                 reads=["rstd"], writes=["rstd"])

        def pre_norm(gi, U):
            rms_stats([xT[:, c, 0:U] for c in range(8)], ["xT"] * 8, 1024, U)
            for c in range(8):
                P.op("dve", lambda e, c=c: e.scalar_tensor_tensor(out=h[:, c, 0:U], in0=xT[:, c, 0:U], scalar=gcol[:, gi + c:gi + c + 1],
                                                                 in1=rstd[:, 0:U], op0=ALU.mult, op1=ALU.mult),
                     reads=["xT", "rstd", "gcol"], writes=["h"])

        def post_norm_residual(gi, U):
            rms_stats([o[:, c, 0:U] for c in range(8)], ["o%d" % c for c in range(8)], 1024, U)
            for c in range(8):
                P.op("dve", lambda e, c=c: e.scalar_tensor_tensor(out=o[:, c, 0:U], in0=o[:, c, 0:U], scalar=gcol[:, gi + c:gi + c + 1],
                                                                 in1=rstd[:, 0:U], op0=ALU.mult, op1=ALU.mult),
                     reads=["o%d" % c, "rstd", "gcol"], writes=["o%d" % c])
                P.op("pool", lambda e, c=c: e.tensor_tensor(out=xT[:, c, 0:U], in0=xT[:, c, 0:U], in1=o[:, c, 0:U], op=ALU.add),
                     reads=["xT", "o%d" % c], writes=["xT"])

        def proj_fm(sl, slk, col0, ncol, rhs_fn, rkeys, nkc, U, stride):
            bank = dbank()
            sv = sl[:, 0:nkc * stride].rearrange("p (k n) -> p k n", k=nkc)

            def fn(e):
                ins = None
                for kc in range(nkc):
                    ins = e.matmul(ps[bank][0:ncol, 0:U], sv[:, kc, col0:col0 + ncol], rhs_fn(kc), start=(kc == 0), stop=(kc == nkc - 1))
                return ins
            P.op("pe", fn, reads=[slk] + list(rkeys), writes=[bkey(bank)])
            return bank

        def proj_tm(sl, slk, ncol, tt, TT, stride):
            bank = dbank()
            sv = sl[:, 0:8 * stride].rearrange("p (k n) -> p k n", k=8)

            def fn(e):
                ins = None
                for kc in range(8):
                    ins = e.matmul(ps[bank][0:TT, 0:ncol], h[:, kc, tt * TT:(tt + 1) * TT], sv[:, kc, 0:ncol], start=(kc == 0), stop=(kc == 7))
                return ins
            P.op("pe", fn, reads=[slk, "h"], writes=[bkey(bank)])
            return bank

        def wout_and_post(slab0, gi, U):
            for sl_i in range(2):
                sl, slk = get_slab(slab0 + sl_i)
                for mm in range(4):
                    n = sl_i * 4 + mm
                    bank = proj_fm(sl, slk, mm * 128, 128, lambda kc: mix[:, kc, 0:U], ["mix%d" % c for c in range(8)], 8, U, 512)
                    P.op("dve", lambda e, n=n, bank=bank: e.tensor_copy(out=o[:, n, 0:U], in_=ps[bank][:, 0:U]),
                         reads=[bkey(bank)], writes=["o%d" % n])
            post_norm_residual(gi, U)

        def ffn(layer, U):
            pre_norm(32 + layer * 8, U)
            barrier()
            s_up = SLAB_FUP0 if layer == 0 else SLAB_FUP1
            s_dn = SLAB_FDN0 if layer == 0 else SLAB_FDN1
            for j in range(8):
                sl, slk = get_slab(s_up + j)
                for mm in range(4):
                    bank = proj_fm(sl, slk, mm * 128, 128, lambda kc: h[:, kc, 0:U], ["h"], 8, U, 512)
                    tb = state["tmp"] % 2
                    state["tmp"] += 1
                    P.op("act", lambda e, bank=bank, tb=tb: e.activation(out=tmpf[:, tb, 0:U], in_=ps[bank][:, 0:U], func=AF.Relu),
                         reads=[bkey(bank)], writes=["tmpf%d" % tb])
                    P.op("dve", lambda e, bank=bank, tb=tb, jj=4 * j + mm: e.tensor_tensor(out=h1[:, jj, 0:U], in0=tmpf[:, tb, 0:U], in1=ps[bank][:, 0:U], op=ALU.mult),
                         reads=[bkey(bank), "tmpf%d" % tb], writes=["h1_%d" % (4 * j + mm)])
            for n in range(8):
                sl, slk = get_slab(s_dn + n)
                bank = proj_fm(sl, slk, 0, 128, lambda kc: h1[:, kc, 0:U], ["h1_%d" % c for c in range(32)], 32, U, 128)
                P.op("dve", lambda e, n=n, bank=bank: e.tensor_copy(out=o[:, n, 0:U], in_=ps[bank][:, 0:U]),
                     reads=[bkey(bank)], writes=["o%d" % n])
            post_norm_residual(48 + layer * 8, U)
            barrier()

        def sb_tile(kAP, qAP, vAP, nk, ncols, S_ap, acc_ap, acc_key, mask, first, last, rk, tu, Sk="S0", vk="vtok"):
            zb = 3 + (tu % 2)
            bb = 5 + (tu % 2)
            a = tu % 2
            P.op("pe", lambda e: e.matmul(ps[zb][0:nk, 0:ncols], kAP, qAP, start=True, stop=True), reads=rk, writes=[bkey(zb)])
            P.op("act", lambda e: e.activation(out=e_t[0:nk, 0:ncols], in_=ps[zb][0:nk, 0:ncols], func=AF.Exp), reads=[bkey(zb)], writes=["e_t"])
            P.op("act", lambda e: e.activation(out=sp_t[0:nk, a, 0:ncols], in_=e_t[0:nk, 0:ncols], func=AF.Ln, bias=1.0), reads=["e_t"], writes=["sp%d" % a])
            if mask is not None:
                P.op("dve", lambda e: e.tensor_tensor(out=spm_t[0:nk, a, 0:ncols], in0=sp_t[0:nk, a, 0:ncols], in1=mask, op=ALU.mult),
                     reads=["sp%d" % a, "sbm"], writes=["spm%d" % a])
                spm = spm_t[0:nk, a, 0:ncols]
                spk = "spm%d" % a
            else:
                spm = sp_t[0:nk, a, 0:ncols]
                spk = "sp%d" % a
            yield

            def fnB(e):
                e.matmul(ps[bb][0:nk, 0:ncols], ntri[0:nk, 0:nk], spm, start=True, stop=False)
                if not first:
                    e.matmul(ps[bb][0:nk, 0:ncols], nones[0:128, 0:nk], S_ap, start=False, stop=False)
                return e.matmul(ps[bb][0:nk, 0:ncols], kAP, qAP, start=False, stop=True)
            P.op("pe", fnB, reads=[spk, Sk, "ntri", "nones"] + list(rk), writes=[bkey(bb)])
            if not last:
                P.op("pool", lambda e: e.tensor_tensor(out=S_ap[0:nk, :], in0=S_ap[0:nk, :], in1=spm, op=ALU.add), reads=[spk, Sk], writes=[Sk])
            P.op("act", lambda e: e.activation(out=w_t[0:nk, a, 0:ncols], in_=ps[bb][0:nk, 0:ncols], func=AF.Exp), reads=[bkey(bb)], writes=["w%d" % a])
            if mask is not None:
                P.op("dve", lambda e: e.tensor_tensor(out=w_t[0:nk, a, 0:ncols], in0=w_t[0:nk, a, 0:ncols], in1=mask, op=ALU.mult),
                     reads=["w%d" % a, "sbm"], writes=["w%d" % a])
            yield
            P.op("pe", lambda e: e.matmul(acc_ap, vAP, w_t[0:nk, a, 0:ncols], start=first, stop=last), reads=["w%d" % a, vk], writes=[acc_key])

        def wrap(g, pre=None, post=None):
            if pre is not None:
                pre()
            try:
                while True:
                    next(g)
                    yield
            except StopIteration:
                pass
            if post is not None:
                post()

        def run_pipelined(gens, hook_step=None, hook=None):
            pending = list(gens)
            active = []
            step = 0
            while pending or active:
                while pending and not hasattr(pending[0], "__next__"):
                    pending.pop(0)()
                if pending:
                    active.append(pending.pop(0))
                nxt = []
                for g in reversed(active):
                    try:
                        next(g)
                        nxt.append(g)
                    except StopIteration:
                        pass
                active = list(reversed(nxt))
                if hook is not None and step == hook_step:
                    hook()
                step += 1
            if hook is not None and step <= hook_step:
                hook()

        def mla_tile(cT0, cT1, kpT, ctok, nk, q0, q1, mask, first, last, hp, tu, qb_=0):
            zb = 3 + (tu % 2)
            a = tu % 2
            qlat = qlat_b[qb_]
            qpe = qpe_b[qb_]

            def fnZ(e):
                e.matmul(ps[zb][0:nk, q0:q1], cT0, qlat[:, 0, q0:q1], start=True, stop=False)
                e.matmul(ps[zb][0:nk, q0:q1], cT1, qlat[:, 1, q0:q1], start=False, stop=False)
                return e.matmul(ps[zb][0:nk, q0:q1], kpT, qpe[0:32, q0:q1], start=False, stop=True)
            P.op("pe", fnZ, reads=["ckvT", "kpeT", "qlat%d" % qb_, "qpe%d" % qb_], writes=[bkey(zb)])
            P.op("act", lambda e: e.activation(out=p_t[0:nk, a, q0:q1], in_=ps[zb][0:nk, q0:q1], func=AF.Exp, scale=MLA_SCALE),
                 reads=[bkey(zb)], writes=["p%d" % a])
            if mask is not None:
                P.op("dve", lambda e: e.tensor_tensor(out=p_t[0:nk, a, q0:q1], in0=p_t[0:nk, a, q0:q1], in1=mask, op=ALU.mult),
                     reads=["p%d" % a, "mlm"], writes=["p%d" % a])
            yield

            def fnO(e):
                e.matmul(ps[5][:, q0:q1], ctok[:, 0:128], p_t[0:nk, a, q0:q1], start=first, stop=last)
                e.matmul(ps[6][:, q0:q1], ctok[:, 128:256], p_t[0:nk, a, q0:q1], start=first, stop=last)
                return e.matmul(ps[7][:, q0:q1], onesb[0:nk, :], p_t[0:nk, a, q0:q1], start=first, stop=last)
            P.op("pe", fnO, reads=["p%d" % a, "ckvtok", "onesb"], writes=[bkey(5), bkey(6), bkey(7)])

        def unit(kind, s, qb):
            isp = (kind == "p")
            U = 512 if isp else 64
            TT = 128 if isp else 64
            NTT = U // TT
            row0 = qb * 512

            for tt in range(NTT):
                src = xp[s, row0 + tt * 128: row0 + tt * 128 + 128, :] if isp else xs[0:64, :]
                P.dma(lambda e, src=src: e.dma_start(out=xin[0:TT, :], in_=src), writes=["xin"], semkey="xin")
                for half in range(2):
                    bank = dbank()
                    P.op("pe", lambda e, half=half, bank=bank: [e.transpose(ps[bank][:, c4 * TT:(c4 + 1) * TT], xin[0:TT, (half * 4 + c4) * 128:(half * 4 + c4 + 1) * 128], identf[0:TT, 0:TT]) for c4 in range(4)][-1],
                         reads=["xin", "identf"], writes=[bkey(bank)])
                    P.op("act", lambda e, half=half, bank=bank, tt=tt: e.activation(out=xT[:, half * 4:half * 4 + 4, tt * TT:(tt + 1) * TT],
                                                                                in_=ps[bank][:, 0:4 * TT].rearrange("p (a t) -> p a t", a=4), func=AF.Copy),
                         reads=[bkey(bank)], writes=["xT"])

            if float(os.environ.get("KSTOP", 99)) <= 1:
                return
            pre_norm(0, U)
            if float(os.environ.get("KSTOP", 99)) <= 1.2:
                return
            sl, slk = get_slab(SLAB_EIN + 0)
            for m in range(4):
                bank = proj_fm(sl, slk, m * 128, 128, lambda kc: h[:, kc, 0:U], ["h"], 8, U, 512)
                P.op("act", lambda e, m=m, bank=bank: e.activation(out=qT[:, m, 0:U], in_=ps[bank][:, 0:U], func=AF.Copy),
                     reads=[bkey(bank)], writes=["qT"])
            if float(os.environ.get("KSTOP", 99)) <= 1.4:
                return
            sl, slk = get_slab(SLAB_EIN + 1)
            for m in range(4):
                bank = proj_fm(sl, slk, m * 128, 128, lambda kc: h[:, kc, 0:U], ["h"], 8, U, 512)
                if isp:
                    P.op("act", lambda e, m=m, bank=bank: e.activation(out=ksT[:, m, row0:row0 + 512], in_=ps[bank][:, 0:U], func=AF.Copy, scale=SB_SCALE),
                         reads=[bkey(bank)], writes=["ksT"])
                else:
                    P.op("act", lambda e, m=m, bank=bank: e.activation(out=ksn[:, m, 0:U], in_=ps[bank][:, 0:U], func=AF.Copy, scale=SB_SCALE),
                         reads=[bkey(bank)], writes=["ksn"])
            for tt in range(NTT):
                bank = proj_tm(sl, slk, 512, tt, TT, 512)
                P.op("dve", lambda e, bank=bank: e.tensor_copy(out=kvout[0:TT, 0, :], in_=ps[bank][0:TT, :]), reads=[bkey(bank)], writes=["kvout0"])
                dst = sbk_p[s, row0 + tt * 128: row0 + tt * 128 + 128, :] if isp else sbk_s[0:64, :]
                P.dma(lambda e, dst=dst: e.dma_start(out=dst, in_=kvout[0:TT, 0, :]), reads=["kvout0"], writes=["out_sbk"], semkey="kvout0")
            if float(os.environ.get("KSTOP", 99)) <= 1.6:
                return
            sl, slk = get_slab(SLAB_EIN + 2)
            for tt in range(NTT):
                bank = proj_tm(sl, slk, 512, tt, TT, 512)
                P.op("dve", lambda e, bank=bank: e.tensor_copy(out=kvout[0:TT, 1, :], in_=ps[bank][0:TT, :]), reads=[bkey(bank)], writes=["kvout1"])
                dst = sbv_p[s, row0 + tt * 128: row0 + tt * 128 + 128, :] if isp else sbv_s[0:64, :]
                P.dma(lambda e, dst=dst: e.dma_start(out=dst, in_=kvout[0:TT, 1, :]), reads=["kvout1"], writes=["out_sbv"], semkey="kvout1")
                if isp:
                    P.op("act", lambda e, bank=bank, tt=tt: e.activation(out=vtok[:, qb * 4 + tt, :], in_=ps[bank][:, :], func=AF.Copy),
                         reads=[bkey(bank)], writes=["vtok"])
                elif not os.environ.get("KNOVNEW"):
                    P.op("act", lambda e, bank=bank: e.activation(out=vnew[0:64, :], in_=ps[bank][0:64, :], func=AF.Copy),
                         reads=[bkey(bank)], writes=["vnew"])
            if float(os.environ.get("KSTOP", 99)) <= 1.8:
                return
            if isp and qb == 0:
                P.op("pool", lambda e: e.memset(cstate[:, :, :], 0.0), writes=["cstate"])
            for c in range(4):
                if os.environ.get("KNOCONV") and not isp:
                    state["slabk"] += 1
                    continue
                sl, slk = get_slab(SLAB_EIN + 3 + c)
                b0 = proj_fm(sl, slk, 0, 128, lambda kc: h[:, kc, 0:U], ["h"], 8, U, 384)
                b1 = proj_fm(sl, slk, 128, 128, lambda kc: h[:, kc, 0:U], ["h"], 8, U, 384)
                b2 = proj_fm(sl, slk, 256, 128, lambda kc: h[:, kc, 0:U], ["h"], 8, U, 384)
                if isp:
                    cv = cin[:, 0:514]
                    cur = cv[:, 2:514]
                    t0, t1_, t2 = cv[:, 0:512], cv[:, 1:513], cv[:, 2:514]
                    tv = tmpf[:, 1, 0:512]
                    g1 = tmpf[:, 0, 0:512]
                    pv = lambda b: ps[b][:, 0:512]
                    P.op("dve", lambda e, c=c: e.tensor_copy(out=cin[:, 0:2], in_=cstate[:, c, :]), reads=["cstate"], writes=["cin"])
                else:
                    cv = cin[:, 0:72].rearrange("p (s t) -> p s t", s=4)
                    cur = cv[:, :, 2:18]
                    t0, t1_, t2 = cv[:, :, 0:16], cv[:, :, 1:17], cv[:, :, 2:18]
                    tv = tmpf[:, 1, 0:64].rearrange("p (s t) -> p s t", s=4)
                    g1 = tmpf[:, 0, 0:64].rearrange("p (s t) -> p s t", s=4)
                    pv = lambda b: ps[b][:, 0:64].rearrange("p (s t) -> p s t", s=4)
                    for s4 in range(4):
                        for j2 in range(2):
                            P.dma(lambda e, c=c, cv=cv, s4=s4, j2=j2: e.dma_start(out=cv[:, s4, j2:j2 + 1], in_=sconv[s4, j2:j2 + 1, c * 128:(c + 1) * 128].rearrange("j p -> p j")),
                                  reads=[], writes=["cinp%d" % (s4 * 2 + j2)], semkey="cinld%d" % (s4 * 2 + j2))
                P.op("act", lambda e, g1=g1, b1=b1, pv=pv: e.activation(out=g1, in_=pv(b1), func=AF.Copy), reads=[bkey(b1)], writes=["tmpf0"])
                P.op("dve", lambda e, cur=cur, g1=g1, b2=b2, pv=pv: e.tensor_tensor(out=cur, in0=pv(b2), in1=g1, op=ALU.mult),
                     reads=[bkey(b2), "tmpf0"], writes=["cin"])
                wi = 64 + c
                cpk = [] if isp else ["cinp%d" % i for i in range(8)]
                P.op("dve", lambda e, tv=tv, t0=t0, wi=wi: e.tensor_scalar(out=tv, in0=t0, scalar1=gcol[:, wi:wi + 1], scalar2=None, op0=ALU.mult), reads=["cin", "gcol"] + cpk, writes=["tmpf1"])
                P.op("dve", lambda e, tv=tv, t1_=t1_, wi=wi: e.scalar_tensor_tensor(out=tv, in0=t1_, scalar=gcol[:, wi + 4:wi + 5], in1=tv, op0=ALU.mult, op1=ALU.add), reads=["cin", "tmpf1", "gcol"] + cpk, writes=["tmpf1"])
                P.op("dve", lambda e, tv=tv, t2=t2, wi=wi: e.scalar_tensor_tensor(out=tv, in0=t2, scalar=gcol[:, wi + 8:wi + 9], in1=tv, op0=ALU.mult, op1=ALU.add), reads=["cin", "tmpf1", "gcol"] + cpk, writes=["tmpf1"])
                if isp:
                    P.op("dve", lambda e, c=c, tv=tv, b0=b0: e.tensor_tensor(out=mix[:, 4 + c, 0:512], in0=tv, in1=ps[b0][:, 0:512], op=ALU.mult), reads=["tmpf1", bkey(b0)], writes=["mix%d" % (4 + c)])
                    P.op("pool", lambda e, c=c: e.tensor_copy(out=cstate[:, c, :], in_=cin[:, 512:514]), reads=["cin"], writes=["cstate"])
                    if qb == 3:
                        for j2 in range(2):
                            P.dma(lambda e, c=c, j2=j2: e.dma_start(out=conv_p[s, j2:j2 + 1, c * 128:(c + 1) * 128].rearrange("j p -> p j"), in_=cin[:, 512 + j2:513 + j2]),
                                  reads=["cin"], writes=["out_conv"], semkey="convo%d" % j2)
                else:
                    P.op("dve", lambda e, c=c, tv=tv, b0=b0, pv=pv: e.tensor_tensor(out=mix[:, 4 + c, 0:64].rearrange("p (s t) -> p s t", s=4), in0=tv, in1=pv(b0), op=ALU.mult),
                         reads=["tmpf1", bkey(b0)], writes=["mix%d" % (4 + c)])
                    for s4 in range(4):
                        for j2 in range(2):
                            P.dma(lambda e, c=c, cv=cv, s4=s4, j2=j2: e.dma_start(out=conv_s[s4, j2:j2 + 1, c * 128:(c + 1) * 128].rearrange("j p -> p j"), in_=cv[:, s4, 16 + j2:17 + j2]),
                                  reads=["cin"], writes=["out_conv"], semkey="convo%d" % (s4 * 2 + j2))

            if float(os.environ.get("KSTOP", 99)) <= 2:
                return
            tu = 0
            if isp:
                nt = 4 * qb + 4
                gens = []
                for hh in range(8):
                    hp, m = hh % 2, hh // 2
                    S_t = S_b[hp]
                    Sk = "S%d" % hp
                    pre = lambda S_t=S_t, Sk=Sk: P.op("pool", lambda e: e.memset(S_t[:, 0:512], 0.0), writes=[Sk])
                    post = lambda hp=hp, m=m: P.op("act", lambda e: e.activation(out=mix[hp * 64:(hp + 1) * 64, m, 0:512], in_=ps[7][hp * 64:(hp + 1) * 64, 0:512], func=AF.Copy),
                                                   reads=["acc%d" % hp], writes=["mix%d" % m])
                    for idx, kt in enumerate(range(nt - 1, -1, -1)):
                        j = kt - 4 * qb
                        q0 = 128 * j if j > 0 else 0
                        mask = sbm[:, 0:512 - q0] if j >= 0 else None
                        g = sb_tile(ksT[hp * 64:(hp + 1) * 64, m, kt * 128:(kt + 1) * 128], qT[hp * 64:(hp + 1) * 64, m, q0:512],
                                    vtok[:, kt, hh * 64:(hh + 1) * 64], 128, 512 - q0, S_t[:, q0:512],
                                    ps[7][hp * 64:(hp + 1) * 64, q0:512], "acc%d" % hp, mask, idx == 0, idx == nt - 1, ["ksT", "qT"], tu, Sk)
                        gens.append(wrap(g, pre if idx == 0 else None, post if idx == nt - 1 else None))
                        tu += 1
                run_pipelined(gens)
            else:
                items = []
                heads = [(ss, hh) for ss in range(SPB) for hh in range(8)]

                def prologue(ss, hh, b):
                    hp, m = hh % 2, hh // 2
                    kTs, vh, kbf, qs_t = kTs_b[b], vh_b[b], kbf_b[b], qs_b[b]
                    P.dma(lambda e: e.dma_start(out=kld[:, :, :], in_=csk[ss, :, hh * 64:(hh + 1) * 64].rearrange("(a p) d -> p a d", p=128)),
                          writes=["kld"], semkey="kld")
                    P.dma(lambda e: e.dma_start(out=vld[:, :, :], in_=csv[ss, :, hh * 64:(hh + 1) * 64].rearrange("(a p) d -> p a d", p=128)),
                          writes=["vld"], semkey="vld")
                    P.op("dve", lambda e: e.tensor_copy(out=kbf[:, :, :], in_=kld[:, :, :]), reads=["kld"], writes=["kbf%d" % b])
                    P.op("pool", lambda e: e.tensor_copy(out=vh[:, 0:32, :], in_=vld[:, :, :]), reads=["vld"], writes=["vh%d" % b])
                    for g in range(8):
                        bank = dbank()
                        P.op("pe", lambda e, g=g, bank=bank: [e.transpose(psb[bank][0:64, i * 128:(i + 1) * 128], kbf[:, 4 * g + i, :], identb[:, :]) for i in range(4)][-1],
                             reads=["kbf%d" % b, "identb"], writes=[bkey(bank)])
                        P.op("dve", lambda e, g=g, bank=bank: e.tensor_scalar(out=kTs[0:64, g * 512:(g + 1) * 512], in0=psb[bank][0:64, 0:512], scalar1=SB_SCALE, scalar2=None, op0=ALU.mult),
                             reads=[bkey(bank)], writes=["kTs%d" % b])
                    P.op("dve", lambda e: e.tensor_copy(out=kTs[0:64, 4096:4112], in_=ksn[hp * 64:(hp + 1) * 64, m, ss * 16:(ss + 1) * 16]),
                         reads=["ksn"], writes=["kTs%d" % b])
                    P.op("dve", lambda e: e.tensor_copy(out=qs_t[0:64, 0:16], in_=qT[hp * 64:(hp + 1) * 64, m, ss * 16:(ss + 1) * 16]),
                         reads=["qT"], writes=["qs%d" % b])
                    P.dma(lambda e: e.dma_start(out=vh[0:16, 32, :], in_=vnew[ss * 16:(ss + 1) * 16, hh * 64:(hh + 1) * 64]),
                          reads=["vnew", "vh%d" % b], writes=["vh%d" % b], semkey="vhn")

                for hi, (ss, hh) in enumerate(heads):
                    b = hi % 2
                    hp, m = hh % 2, hh // 2
                    kTs, vh, qs_t = kTs_b[b], vh_b[b], qs_b[b]
                    S_t = S_b[b]
                    Sk = "S%d" % b
                    if hi == 0:
                        items.append(lambda: prologue(heads[0][0], heads[0][1], 0))
                    pre = lambda S_t=S_t, Sk=Sk: P.op("pool", lambda e: e.memset(S_t[:, 0:16], 0.0), writes=[Sk])
                    post = lambda hp=hp, m=m, ss=ss: P.op("act", lambda e: e.activation(out=mix[hp * 64:(hp + 1) * 64, m, ss * 16:(ss + 1) * 16], in_=ps[7][hp * 64:(hp + 1) * 64, 0:16], func=AF.Copy),
                                                          reads=["acc%d" % hp], writes=["mix%d" % m])
                    for idx, kt in enumerate(range(32, -1, -1)):
                        nk = 16 if kt == 32 else 128
                        mask = sbm[0:16, 0:16] if kt == 32 else None
                        k0 = kt * 128
                        g = sb_tile(kTs[0:64, k0:k0 + nk], qs_t[0:64, 0:16], vh[0:nk, kt, :], nk, 16, S_t[:, 0:16],
                                    ps[7][hp * 64:(hp + 1) * 64, 0:16], "acc%d" % hp, mask, idx == 0, idx == 32, ["kTs%d" % b, "qs%d" % b], tu, Sk, "vh%d" % b)
                        items.append(wrap(g, pre if idx == 0 else None, post if idx == 32 else None))
                        tu += 1
                        if idx == 3 and hi + 1 < len(heads):
                            items.append(lambda hi=hi: prologue(heads[hi + 1][0], heads[hi + 1][1], (hi + 1) % 2))
                run_pipelined(items)
            if float(os.environ.get("KSTOP", 99)) <= 3:
                return
            wout_and_post(SLAB_EOUT, 16 + 0, U)
            ffn(0, U)

            if float(os.environ.get("KSTOP", 99)) <= 4:
                return
            pre_norm(8, U)
            sl, slk = get_slab(SLAB_OIN + 0)
            for m in range(4):
                bank = proj_fm(sl, slk, m * 128, 128, lambda kc: h[:, kc, 0:U], ["h"], 8, U, 512)
                P.op("act", lambda e, m=m, bank=bank: e.activation(out=uT[:, m, 0:U], in_=ps[bank][:, 0:U], func=AF.Copy), reads=[bkey(bank)], writes=["uT"])
            sl, slk = get_slab(SLAB_OIN + 1)
            for m in range(3):
                bank = proj_fm(sl, slk, m * 128, 128, lambda kc: h[:, kc, 0:U], ["h"], 8, U, 384)
                P.op("dve", lambda e, m=m, bank=bank: e.tensor_copy(out=o[:, m, 0:U], in_=ps[bank][:, 0:U]), reads=[bkey(bank)], writes=["o%d" % m])
            rms_stats([o[:, m, 0:U] for m in range(3)], ["o0", "o1", "o2"], 384, U)
            for m in range(3):
                P.op("dve", lambda e, m=m: e.scalar_tensor_tensor(out=cqT[:, m, 0:U], in0=o[:, m, 0:U], scalar=gcol[:, 76 + m:77 + m], in1=rstd[:, 0:U], op0=ALU.mult, op1=ALU.mult),
                     reads=["o%d" % m, "rstd", "gcol"], writes=["cqT"])
            sl, slk = get_slab(SLAB_OIN + 2)
            for tt in range(NTT):
                bank = proj_tm(sl, slk, 512, tt, TT, 512)
                sm = small4[:, tt, :]
                k_ = "sm%d_" % tt
                ts = tt % 2
                tk = "tmpf%d" % ts
                P.op("dve", lambda e, bank=bank, sm=sm: e.bn_stats(out=sm[0:TT, 0:6], in_=ps[bank][0:TT, 0:512]), reads=[bkey(bank)], writes=[k_ + "bn"])
                P.op("dve", lambda e, sm=sm: e.bn_aggr(out=sm[0:TT, 6:8], in_=sm[0:TT, 0:6]), reads=[k_ + "bn"], writes=[k_ + "mv"])
                P.op("act", lambda e, sm=sm: e.activation(out=sm[0:TT, 8:9], in_=sm[0:TT, 7:8], func=AF.Ln, bias=EPS), reads=[k_ + "mv"], writes=[k_ + "rv"])
                P.op("act", lambda e, sm=sm: e.activation(out=sm[0:TT, 8:9], in_=sm[0:TT, 8:9], func=AF.Exp, scale=-0.5), reads=[k_ + "rv"], writes=[k_ + "rv"])
                P.op("dve", lambda e, bank=bank, sm=sm, ts=ts: e.tensor_scalar(out=tmpf[0:TT, ts, :], in0=ps[bank][0:TT, 0:512], scalar1=sm[0:TT, 6:7], scalar2=sm[0:TT, 8:9], op0=ALU.subtract, op1=ALU.mult),
                     reads=[bkey(bank), k_ + "mv", k_ + "rv"], writes=[tk])
                P.op("pool", lambda e, ts=ts: e.tensor_tensor(out=tmpf[0:TT, ts, :], in0=tmpf[0:TT, ts, :], in1=lnG[0:TT, :], op=ALU.mult), reads=[tk, "lnG"], writes=[tk])
                if isp:
                    P.op("pool", lambda e, tt=tt, ts=ts: e.tensor_tensor(out=vnb[0:TT, tt, :], in0=tmpf[0:TT, ts, :], in1=lnB[0:TT, :], op=ALU.add), reads=[tk, "lnB"], writes=["vnb"])
                else:
                    P.op("pool", lambda e, ts=ts: e.tensor_tensor(out=vnf[0:TT, :], in0=tmpf[0:TT, ts, :], in1=lnB[0:TT, :], op=ALU.add), reads=[tk, "lnB"], writes=["vnf"])
                    P.op("pool", lambda e, tt=tt: e.tensor_copy(out=vnb[0:TT, tt, :], in_=vnf[0:TT, :]), reads=["vnf"], writes=["vnb"])
                    P.dma(lambda e: e.dma_start(out=sguv_s[0:64, :], in_=vnf[0:64, :]), reads=["vnf"], writes=["out_sguv"], semkey="vnfo")
            for g in range(4):
                bank = dbank()

                def fng(e, g=g, bank=bank):
                    ins = None
                    for tt in range(NTT):
                        rhs = wsT[:, g, :] if isp else wsbd[0:64, g, :]
                        ins = e.matmul(ps[bank][:, tt * TT:(tt + 1) * TT], vnb[0:TT, tt, g * 128:(g + 1) * 128], rhs, start=True, stop=True)
                    return ins
                P.op("pe", fng, reads=["vnb", "wsT", "wsbd"], writes=[bkey(bank)])
                if isp:
                    for tt in range(NTT):
                        P.op("dve", lambda e, g=g, tt=tt, bank=bank: e.tensor_tensor(out=tmpf[:, 1, tt * 128:(tt + 1) * 128], in0=ps[bank][:, tt * 128:(tt + 1) * 128], in1=bsB[:, g * 128:(g + 1) * 128], op=ALU.add),
                             reads=[bkey(bank), "bsB"], writes=["tmpf1"])
                else:
                    P.op("dve", lambda e, g=g, bank=bank: e.tensor_tensor(out=tmpf[:, 1, 0:64], in0=ps[bank][:, 0:64], in1=bsBs[:, g, :], op=ALU.add),
                         reads=[bkey(bank), "bsBs"], writes=["tmpf1"])
                P.op("pool", lambda e, g=g: e.tensor_tensor(out=mix[:, g, 0:U], in0=tmpf[:, 1, 0:U], in1=uT[:, g, 0:U], op=ALU.mult), reads=["tmpf1", "uT"], writes=["mix%d" % g])
            sl, slk = get_slab(SLAB_OIN + 3)
            for tt in range(NTT):
                bank = proj_tm(sl, slk, 288, tt, TT, 288)
                tile_i = qb * 4 + tt
                sm = small4[:, tt, :]
                k_ = "sm%d_" % tt
                ts = tt % 2
                tk = "tmpf%d" % ts
                ckst = ckst_b[ts]
                ck = "ckst%d" % ts
                kpb = kpb_b[ts]
                kk = "kpb%d" % ts
                P.op("act", lambda e, bank=bank, sm=sm, ts=ts: e.activation(out=tmpf[0:TT, ts, 0:256], in_=ps[bank][0:TT, 0:256], func=AF.Square, accum_out=sm[0:TT, 10:11]),
                     reads=[bkey(bank)], writes=[tk, k_ + "ss"])
                P.op("act", lambda e, sm=sm: e.activation(out=sm[0:TT, 11:12], in_=sm[0:TT, 10:11], func=AF.Ln, scale=1.0 / 256, bias=EPS), reads=[k_ + "ss"], writes=[k_ + "rk"])
                P.op("act", lambda e, sm=sm: e.activation(out=sm[0:TT, 11:12], in_=sm[0:TT, 11:12], func=AF.Exp, scale=-0.5), reads=[k_ + "rk"], writes=[k_ + "rk"])
                P.op("dve", lambda e, bank=bank, sm=sm, ckst=ckst: e.scalar_tensor_tensor(out=ckst[0:TT, 0:256], in0=ps[bank][0:TT, 0:256], scalar=sm[0:TT, 11:12], in1=kvG[0:TT, :], op0=ALU.mult, op1=ALU.mult),
                     reads=[bkey(bank), k_ + "rk", "kvG"], writes=[ck])
                if isp:
                    cosv = ropeT[0:TT, tile_i * 16:(tile_i + 1) * 16]
                    sinv = ropeT[0:TT, 256 + tile_i * 16:256 + (tile_i + 1) * 16]
                else:
                    cosv = ropeTs[0:TT, 0:16]
                    sinv = ropeTs[0:TT, 16:32]
                x1 = ps[bank][0:TT, 256:272]
                x2 = ps[bank][0:TT, 272:288]
                t16 = sm[0:TT, 16:32]
                P.op("dve", lambda e, x1=x1, cosv=cosv, ckst=ckst: e.tensor_tensor(out=ckst[0:TT, 256:272], in0=x1, in1=cosv, op=ALU.mult), reads=[bkey(bank), "ropeT", ck], writes=[ck])
                P.op("dve", lambda e, x2=x2, sinv=sinv, t16=t16: e.tensor_tensor(out=t16, in0=x2, in1=sinv, op=ALU.mult), reads=[bkey(bank), "ropeT"], writes=[k_ + "t16"])
                P.op("dve", lambda e, ckst=ckst, t16=t16: e.tensor_tensor(out=ckst[0:TT, 256:272], in0=ckst[0:TT, 256:272], in1=t16, op=ALU.subtract), reads=[ck, k_ + "t16"], writes=[ck])
                P.op("dve", lambda e, x1=x1, sinv=sinv, ckst=ckst: e.tensor_tensor(out=ckst[0:TT, 272:288], in0=x1, in1=sinv, op=ALU.mult), reads=[bkey(bank), "ropeT", ck], writes=[ck])
                P.op("dve", lambda e, x2=x2, cosv=cosv, t16=t16: e.tensor_tensor(out=t16, in0=x2, in1=cosv, op=ALU.mult), reads=[bkey(bank), "ropeT", k_ + "t16"], writes=[k_ + "t16"])
                P.op("dve", lambda e, ckst=ckst, t16=t16: e.tensor_tensor(out=ckst[0:TT, 272:288], in0=ckst[0:TT, 272:288], in1=t16, op=ALU.add), reads=[ck, k_ + "t16"], writes=[ck])
                if isp:
                    r0 = row0 + tt * 128
                    P.dma(lambda e, r0=r0, ckst=ckst: e.dma_start(out=ckv_p[s, r0:r0 + 128, :], in_=ckst[0:128, 0:256]), reads=[ck], writes=["out_ckv"], semkey="ckvo%d" % ts)
                    P.dma(lambda e, r0=r0, ckst=ckst: e.dma_start(out=kpe_p[s, r0:r0 + 128, :], in_=ckst[0:128, 256:288]), reads=[ck], writes=["out_kpe"], semkey="kpeo%d" % ts)
                    ctk = ckvtok[:, tile_i, :]
                else:
                    P.dma(lambda e, ckst=ckst: e.dma_start(out=ckv_s[0:64, :], in_=ckst[0:64, 0:256]), reads=[ck], writes=["out_ckv"], semkey="ckvo%d" % ts)
                    P.dma(lambda e, ckst=ckst: e.dma_start(out=kpe_s[0:64, :], in_=ckst[0:64, 256:288]), reads=[ck], writes=["out_kpe"], semkey="kpeo%d" % ts)
                    ctk = ckvn[:, :]
                P.op("pool", lambda e, ctk=ctk, ckst=ckst: e.tensor_copy(out=ctk[0:TT, :], in_=ckst[0:TT, 0:256]), reads=[ck], writes=["ckvtok"])
                P.op("pool", lambda e, ckst=ckst, kpb=kpb: e.tensor_copy(out=kpb[0:TT, :], in_=ckst[0:TT, 256:288]), reads=[ck], writes=[kk])
                bank2 = dbank()

                def fnt(e, ctk=ctk, bank2=bank2, kpb=kpb):
                    e.transpose(psb[bank2][:, 0:TT], ctk[0:TT, 0:128], identb[0:TT, 0:TT])
                    e.transpose(psb[bank2][:, 128:128 + TT], ctk[0:TT, 128:256], identb[0:TT, 0:TT])
                    return e.transpose(psb[bank2][0:32, 256:256 + TT], kpb[0:TT, 0:32], identb[0:TT, 0:TT])
                P.op("pe", fnt, reads=["ckvtok", kk, "identb"], writes=[bkey(bank2)])
                if isp:
                    c0 = tile_i * 128
                    P.op("act", lambda e, c0=c0, bank2=bank2: e.activation(out=ckvT[:, :, c0:c0 + 128], in_=psb[bank2][:, 0:256].rearrange("p (a t) -> p a t", a=2), func=AF.Copy),
                         reads=[bkey(bank2)], writes=["ckvT"])
                    P.op("dve", lambda e, c0=c0, bank2=bank2: e.tensor_copy(out=kpeT[0:32, c0:c0 + 128], in_=psb[bank2][0:32, 256:384]), reads=[bkey(bank2)], writes=["kpeT"])
                else:
                    P.op("act", lambda e, bank2=bank2: e.activation(out=sq[:, 0, 0:64], in_=psb[bank2][:, 0:64], func=AF.Copy), reads=[bkey(bank2)], writes=["sq0"])
                    P.op("act", lambda e, bank2=bank2: e.activation(out=sq[:, 0, 64:128], in_=psb[bank2][:, 128:192], func=AF.Copy), reads=[bkey(bank2)], writes=["sq0"])
                    P.op("act", lambda e, bank2=bank2: e.activation(out=sq[0:32, 0, 128:192], in_=psb[bank2][0:32, 256:320], func=AF.Copy), reads=[bkey(bank2)], writes=["sq0"])
            if float(os.environ.get("KSTOP", 99)) <= 5:
                return
            for m in range(4):
                bank = proj_fm(wuqn[:, :, :].rearrange("p k n -> p (k n)"), "wuqn", m * 128, 128, lambda kc: cqT[:, kc, 0:U], ["cqT"], 3, U, 512)
                P.op("act", lambda e, m=m, bank=bank: e.activation(out=qnT[:, m, 0:U], in_=ps[bank][:, 0:U], func=AF.Copy), reads=[bkey(bank)], writes=["qnT"])
            if isp:
                P.dma(lambda e: e.dma_start(out=cosF[0:32, 0:512], in_=rope_f[0, :, row0:row0 + 512]), writes=["cosF"], semkey="cosF")
                P.dma(lambda e: e.dma_start(out=sinF[0:32, 0:512], in_=rope_f[1, :, row0:row0 + 512]), writes=["sinF"], semkey="sinF")
            else:
                P.dma(lambda e: e.dma_start(out=cosF[0:32, 0:64], in_=rope_fs[0, :, :]), writes=["cosF"], semkey="cosF")
                P.dma(lambda e: e.dma_start(out=sinF[0:32, 0:64], in_=rope_fs[1, :, :]), writes=["sinF"], semkey="sinF")

            def head_q(hh, q0, q1, qb_=0):
                hp, m = hh % 2, hh // 2
                qlat = qlat_b[qb_]
                qpe = qpe_b[qb_]
                for cc in range(2):
                    bank = dbank()
                    P.op("pe", lambda e, cc=cc, bank=bank: e.matmul(ps[bank][:, q0:q1], wukp[hp * 64:(hp + 1) * 64, m, cc * 128:(cc + 1) * 128], qnT[hp * 64:(hp + 1) * 64, m, q0:q1], start=True, stop=True),
                         reads=["wukp", "qnT"], writes=[bkey(bank)])
                    if cc == 0:
                        P.op("act", lambda e, bank=bank: e.activation(out=qlat[:, 0, q0:q1], in_=ps[bank][:, q0:q1], func=AF.Copy), reads=[bkey(bank)], writes=["qlat%d" % qb_])
                    else:
                        P.op("dve", lambda e, bank=bank: e.tensor_copy(out=qlat[:, 1, q0:q1], in_=ps[bank][:, q0:q1]), reads=[bkey(bank)], writes=["qlat%d" % qb_])
                b1 = dbank()
                b2 = dbank()

                def fq(e, b1=b1, b2=b2):
                    ins = None
                    for kc in range(3):
                        e.matmul(ps[b1][0:32, q0:q1], wuqp[:, kc, hh * 32:(hh + 1) * 32], cqT[:, kc, q0:q1], start=(kc == 0), stop=(kc == 2))
                    for kc in range(3):
                        ins = e.matmul(ps[b2][0:32, q0:q1], wuqs[:, kc, hh * 32:(hh + 1) * 32], cqT[:, kc, q0:q1], start=(kc == 0), stop=(kc == 2))
                    return ins
                P.op("pe", fq, reads=["wuqp", "wuqs", "cqT"], writes=[bkey(b1), bkey(b2)])
                P.op("dve", lambda e, b1=b1: e.tensor_tensor(out=tmpf[0:32, 0, q0:q1], in0=ps[b1][0:32, q0:q1], in1=cosF[0:32, q0:q1], op=ALU.mult), reads=[bkey(b1), "cosF"], writes=["tmpf0"])
                P.op("dve", lambda e, b2=b2: e.tensor_tensor(out=tmpf[0:32, 1, q0:q1], in0=ps[b2][0:32, q0:q1], in1=sinF[0:32, q0:q1], op=ALU.mult), reads=[bkey(b2), "sinF"], writes=["tmpf1"])
                P.op("pool", lambda e: e.tensor_tensor(out=qpe[0:32, q0:q1], in0=tmpf[0:32, 0, q0:q1], in1=tmpf[0:32, 1, q0:q1], op=ALU.add), reads=["tmpf0", "tmpf1"], writes=["qpe%d" % qb_])

            def head_out(hh, q0, q1):
                hp, m = hh % 2, hh // 2
                P.op("act", lambda e: e.activation(out=OLs[:, 0, q0:q1], in_=ps[5][:, q0:q1], func=AF.Copy), reads=[bkey(5)], writes=["OLs0"])
                P.op("dve", lambda e: e.tensor_copy(out=OLs[:, 1, q0:q1], in_=ps[6][:, q0:q1]), reads=[bkey(6)], writes=["OLs1"])
                P.op("dve", lambda e: e.reciprocal(out=rden[hp * 64:(hp + 1) * 64, q0:q1], in_=ps[7][hp * 64:(hp + 1) * 64, q0:q1]), reads=[bkey(7)], writes=["rden"])
                bank = dbank()

                def fo(e, bank=bank):
                    e.matmul(ps[bank][hp * 64:(hp + 1) * 64, q0:q1], wuvb[:, hh, 0, :], OLs[:, 0, q0:q1], start=True, stop=False)
                    return e.matmul(ps[bank][hp * 64:(hp + 1) * 64, q0:q1], wuvb[:, hh, 1, :], OLs[:, 1, q0:q1], start=False, stop=True)
                P.op("pe", fo, reads=["wuvb", "OLs0", "OLs1"], writes=[bkey(bank)])
                P.op("dve", lambda e, bank=bank: e.tensor_tensor(out=mix[hp * 64:(hp + 1) * 64, 4 + m, q0:q1], in0=ps[bank][hp * 64:(hp + 1) * 64, q0:q1], in1=rden[hp * 64:(hp + 1) * 64, q0:q1], op=ALU.mult),
                     reads=[bkey(bank), "rden"], writes=["mix%d" % (4 + m)])

            tu = 0
            if isp:
                nt = 4 * qb + 4
                head_q(0, 0, 512, 0)
                for hh in range(8):
                    gens = []
                    for kt in range(nt):
                        j = kt - 4 * qb
                        q0 = 128 * j if j > 0 else 0
                        mask = mlm[:, 0:512 - q0] if j >= 0 else None
                        gens.append(mla_tile(ckvT[:, 0, kt * 128:(kt + 1) * 128], ckvT[:, 1, kt * 128:(kt + 1) * 128], kpeT[0:32, kt * 128:(kt + 1) * 128],
                                             ckvtok[:, kt, :], 128, q0, 512, mask, kt == 0, kt == nt - 1, hh % 2, tu, hh % 2))
                        tu += 1
                    hook = (lambda hh=hh: head_q(hh + 1, 0, 512, (hh + 1) % 2)) if hh < 7 else None
                    run_pipelined(gens, 1, hook)
                    head_out(hh, 0, 512)
            else:
                for ss in range(int(os.environ.get("KLIM_MLA", SPB))):
                    for qtr in range(4):
                        P.dma(lambda e, ss=ss, qtr=qtr: e.dma_start(out=cstg[:, 0:2048].rearrange("p (a d) -> p a d", a=8), in_=cckv[ss, qtr * 1024:(qtr + 1) * 1024, :].rearrange("(a p) d -> p a d", p=128)),
                              writes=["cstg"], semkey="cstg")
                        P.op("dve", lambda e, qtr=qtr: e.tensor_copy(out=ckvtok_s[:, qtr * 8:(qtr + 1) * 8, :], in_=cstg[:, 0:2048].rearrange("p (a d) -> p a d", a=8)), reads=["cstg"], writes=["ckvtok"])
                    P.dma(lambda e, ss=ss: e.dma_start(out=cstg[:, 0:1024].rearrange("p (a d) -> p a d", a=32), in_=ckpe[ss, :, :].rearrange("(a p) d -> p a d", p=128)),
                          writes=["cstg"], semkey="cstg")
                    P.op("dve", lambda e: e.tensor_copy(out=kpbs[:, :, :], in_=cstg[:, 0:1024].rearrange("p (a d) -> p a d", a=32)), reads=["cstg"], writes=["kpbs"])
                    for kt in range(32):
                        for cc in range(2):
                            if (kt * 2 + cc) % 4 == 0:
                                bank = dbank()
                            i4 = (kt * 2 + cc) % 4
                            P.op("pe", lambda e, kt=kt, cc=cc, bank=bank, i4=i4: e.transpose(psb[bank][:, i4 * 128:(i4 + 1) * 128], ckvtok_s[:, kt, cc * 128:(cc + 1) * 128], identb[:, :]),
                                 reads=["ckvtok", "identb"], writes=[bkey(bank)])
                            P.op("act", lambda e, kt=kt, cc=cc, bank=bank, i4=i4: e.activation(out=ckvT_s[:, cc, kt * 128:(kt + 1) * 128], in_=psb[bank][:, i4 * 128:(i4 + 1) * 128], func=AF.Copy),
                                 reads=[bkey(bank)], writes=["ckvT"])
                    for g in range(8):
                        bank = dbank()
                        P.op("pe", lambda e, g=g, bank=bank: [e.transpose(psb[bank][0:32, i * 128:(i + 1) * 128], kpbs[:, 4 * g + i, :], identb[:, :]) for i in range(4)][-1],
                             reads=["kpbs", "identb"], writes=[bkey(bank)])
                        P.op("dve", lambda e, g=g, bank=bank: e.tensor_copy(out=kpeT_s[0:32, g * 512:(g + 1) * 512], in_=psb[bank][0:32, 0:512]), reads=[bkey(bank)], writes=["kpeT"])
                    P.dma(lambda e, ss=ss: e.dma_start(out=ckvtok_s[0:16, 32, :], in_=ckvn[ss * 16:(ss + 1) * 16, :]), reads=["ckvtok"], writes=["ckvtok"], semkey="ckvnn")
                    P.op("act", lambda e, ss=ss: e.activation(out=ckvT_s[:, 0, 4096:4112], in_=sq[:, 0, ss * 16:(ss + 1) * 16], func=AF.Copy), reads=["sq0"], writes=["ckvT"])
                    P.op("act", lambda e, ss=ss: e.activation(out=ckvT_s[:, 1, 4096:4112], in_=sq[:, 0, 64 + ss * 16:64 + (ss + 1) * 16], func=AF.Copy), reads=["sq0"], writes=["ckvT"])
                    P.op("act", lambda e, ss=ss: e.activation(out=kpeT_s[0:32, 4096:4112], in_=sq[0:32, 0, 128 + ss * 16:128 + (ss + 1) * 16], func=AF.Copy), reads=["sq0"], writes=["kpeT"])
                    q0, q1 = ss * 16, ss * 16 + 16
                    for hh in range(8):
                        if hh == 0:
                            head_q(0, q0, q1, 0)
                        gens = []
                        for kt in range(33):
                            nk = 16 if kt == 32 else 128
                            k0 = kt * 128
                            gens.append(mla_tile(ckvT_s[:, 0, k0:k0 + nk], ckvT_s[:, 1, k0:k0 + nk], kpeT_s[0:32, k0:k0 + nk],
                                                 ckvtok_s[0:nk, kt, :], nk, q0, q1, None, kt == 0, kt == 32, hh % 2, tu, hh % 2))
                            tu += 1
                        hook = (lambda hh=hh, q0=q0, q1=q1: head_q(hh + 1, q0, q1, (hh + 1) % 2)) if hh < 7 else None
                        run_pipelined(gens, 1, hook)
                        head_out(hh, q0, q1)

            if float(os.environ.get("KSTOP", 99)) <= 6:
                return
            wout_and_post(SLAB_OOUT, 16 + 8, U)
            ffn(1, U)

            if float(os.environ.get("KSTOP", 99)) <= 7:
                return
            for tt in range(NTT):
                for half in range(2):
                    bank = dbank()
                    P.op("pe", lambda e, half=half, bank=bank, tt=tt: [e.transpose(ps[bank][0:TT, c4 * 128:(c4 + 1) * 128], xT[:, half * 4 + c4, tt * TT:(tt + 1) * TT], identf[:, :]) for c4 in range(4)][-1],
                         reads=["xT", "identf"], writes=[bkey(bank)])
                    P.op("act", lambda e, half=half, bank=bank: e.activation(out=yout[0:TT, half * 512:(half + 1) * 512], in_=ps[bank][0:TT, :], func=AF.Copy),
                         reads=[bkey(bank)], writes=["yout"])
                dst = y_p[s, row0 + tt * 128: row0 + tt * 128 + 128, :] if isp else y_s[0:64, :]
                P.dma(lambda e, dst=dst: e.dma_start(out=dst, in_=yout[0:TT, :]), reads=["yout"], writes=["out_y"], semkey="yout")
            barrier()

        setup()
        barrier()
        for (kind, s, qb) in units:
            unit(kind, s, qb)
        P.emit(st)
    return nc


def _consts():
    kl = np.arange(128)[:, None]
    x = np.arange(512)[None, :]
    cst = np.zeros((128, 1408), np.float32)
    cst[:, 0:128] = np.eye(128, dtype=np.float32)
    jj = np.arange(128)[:, None]; kk = np.arange(128)[None, :]
    cst[:, 128:256] = -(jj >= kk).astype(np.float32)
    cst[:, 256:768] = (kl < x).astype(np.float32)
    cst[:, 768:1280] = ((kl // 64) <= (x // 64)).astype(np.float32)
    tt = np.arange(128)[:, None]; ss = np.arange(128)[None, :]
    cst[:, 1280:1408] = (ss <= tt).astype(np.float32)
    half = 16
    inv = (10000.0 ** (-(np.arange(half, dtype=np.float32) / np.float32(half)))).astype(np.float32)

    def cs(pos):
        ang = (pos.astype(np.float32)[:, None] * inv[None, :]).astype(np.float32)
        return np.cos(ang.astype(np.float64)).astype(np.float32), np.sin(ang.astype(np.float64)).astype(np.float32)
    cp, sp_ = cs(np.arange(T))
    rope_t = np.zeros((128, 512), np.float32)
    rope_t[:, 0:256] = cp.reshape(16, 128, 16).transpose(1, 0, 2).reshape(128, 256)
    rope_t[:, 256:512] = sp_.reshape(16, 128, 16).transpose(1, 0, 2).reshape(128, 256)
    cs_, ss_ = cs(PAST + np.arange(DS))
    rope_ts = np.zeros((64, 32), np.float32)
    rope_ts[:, 0:16] = np.tile(cs_, (4, 1))
    rope_ts[:, 16:32] = np.tile(ss_, (4, 1))
    rope_f = np.zeros((2, 32, T), np.float32)
    rope_f[0, 0:16] = cp.T; rope_f[0, 16:32] = cp.T
    rope_f[1, 0:16] = -sp_.T; rope_f[1, 16:32] = sp_.T
    rope_fs = np.zeros((2, 32, 64), np.float32)
    rope_fs[0, 0:16] = np.tile(cs_.T, (1, 4)); rope_fs[0, 16:32] = np.tile(cs_.T, (1, 4))
    rope_fs[1, 0:16] = -np.tile(ss_.T, (1, 4)); rope_fs[1, 16:32] = np.tile(ss_.T, (1, 4))
    return cst, rope_t, rope_ts, rope_f, rope_fs


_CACHE = {}


def kernel(x_prompt, x_sample, cache_sb_k, cache_sb_v, state_conv, cache_mla_ckv, cache_mla_kpe,
           mix_pre_g, mix_post_g, ffn_pre_g, ffn_post_g, even_w_in, even_w_conv, even_w_out,
           odd_w_in, sgu_ln_g, sgu_ln_b, sgu_w_s, sgu_b_s, mla_q_norm_g, mla_kv_norm_g,
           mla_w_uq, mla_w_uk, mla_w_uv, odd_w_out, ffn_w_up, ffn_w_down):
    f = lambda a: np.ascontiguousarray(np.asarray(a, dtype=np.float32))
    dbg = os.environ.get("KDEBUG", "")
    if dbg:
        units = []
        for tok in dbg.split(","):
            if tok == "s":
                units.append(("s", 0, 0))
            else:
                a, b = tok.split(":")
                units.append(("p", int(a), int(b)))
    else:
        units = [("p", s, qb) for s in range(SPB) for qb in range(4)] + [("s", 0, 0)]
    key = tuple(units)
    if key not in _CACHE:
        _CACHE[key] = build_program(units)
    nc = _CACHE[key]
    cst, rope_t, rope_ts, rope_f, rope_fs = _consts()
    grow = np.concatenate([f(mix_pre_g).reshape(16, 128), f(mix_post_g).reshape(16, 128), f(ffn_pre_g).reshape(16, 128),
                           f(ffn_post_g).reshape(16, 128), f(even_w_conv).reshape(12, 128), f(mla_q_norm_g).reshape(3, 128)], axis=0)
    shared = dict(
        ein=f(even_w_in)[0], eout=f(even_w_out)[0], oin=f(odd_w_in)[0], oout=f(odd_w_out)[0],
        fup=f(ffn_w_up), fdn=f(ffn_w_down), grow=f(grow),
        lng=f(sgu_ln_g).reshape(1, 512), lnb=f(sgu_ln_b).reshape(1, 512), kvg=f(mla_kv_norm_g).reshape(1, 256),
        bsd=f(sgu_b_s).reshape(1, 512), bss=f(f(sgu_b_s)[0][:, 0:16]).reshape(1, 4, 16),
        wsd=f(sgu_w_s)[0], wuq=f(mla_w_uq)[0], wuk=f(mla_w_uk)[0], wuv=f(mla_w_uv)[0],
        cst=cst, rope_t=rope_t, rope_ts=rope_ts, rope_f=rope_f, rope_fs=rope_fs)
    xp = f(x_prompt); xs = f(x_sample)
    csk = f(cache_sb_k)[0].reshape(32, PAST, 512); csv = f(cache_sb_v)[0].reshape(32, PAST, 512)
    sc = f(state_conv)[0]; cckv = f(cache_mla_ckv)[0]; ckpe = f(cache_mla_kpe)[0]
    in_maps = []
    for c in range(NCORES):
        sl = slice(c * SPB, (c + 1) * SPB)
        m = dict(shared)
        m.update(xp=xp[sl], xs=f(xs[sl].reshape(SPB * DS, D)), csk=csk[sl], csv=csv[sl], sconv=sc[sl], cckv=cckv[sl], ckpe=ckpe[sl])
        in_maps.append(m)
    res = run_bass_kernel_spmd(nc, in_maps, core_ids=list(range(NCORES)))
    R = res.results
    cat = lambda name: np.concatenate([np.asarray(R[c][name], dtype=np.float32) for c in range(NCORES)], axis=0)
    y_p = cat("y_p")
    y_s = cat("y_s").reshape(32, DS, D)
    sbk_p = cat("sbk_p").reshape(1, 32, T, 8, 64)
    sbv_p = cat("sbv_p").reshape(1, 32, T, 8, 64)
    conv_p = cat("conv_p").reshape(1, 32, 2, 512)
    ckv_p = cat("ckv_p").reshape(1, 32, T, 256)
    kpe_p = cat("kpe_p").reshape(1, 32, T, 32)
    sbk_s = cat("sbk_s").reshape(1, 32, DS, 8, 64)
    sbv_s = cat("sbv_s").reshape(1, 32, DS, 8, 64)
    conv_s = cat("conv_s").reshape(1, 32, 2, 512)
    ckv_s = cat("ckv_s").reshape(1, 32, DS, 256)
    kpe_s = cat("kpe_s").reshape(1, 32, DS, 32)
    sguv_s = cat("sguv_s").reshape(1, 32, DS, 512)
    return (y_p, y_s, sbk_p, sbv_p, conv_p, ckv_p, kpe_p, sbk_s, sbv_s, conv_s, ckv_s, kpe_s, sguv_s)
```

```python
import os
import math
import numpy as np
from contextlib import ExitStack
import concourse.bass as bass
import concourse.mybir as mybir
from concourse.bass_utils import run_bass_kernel_spmd

F32 = mybir.dt.float32
BF16 = mybir.dt.bfloat16
AF = mybir.ActivationFunctionType
ALU = mybir.AluOpType

NCORES = 8
SPB = 4
T = 2048
D = 1024
PAST = 4096
DS = 16
EPS = 1e-6
SB_SCALE = 0.125
MLA_SCALE = 1.0 / math.sqrt(96.0)

COMPUTE = ("pe", "act", "dve", "pool")
ALLENG = ("pe", "act", "dve", "pool", "sp")
SAME_ENGINE_SYNC = True


class Prog:
    def __init__(self, nc):
        self.nc = nc
        self.ops = []

    def op(self, eng, fn, reads=(), writes=()):
        self.ops.append(dict(eng=eng, fn=fn, reads=tuple(reads), writes=tuple(writes), dma=False, bar=False))

    def dma(self, fn, reads=(), writes=(), semkey=None, queue="sp"):
        self.ops.append(dict(eng=queue, fn=fn, reads=tuple(reads), writes=tuple(writes), dma=True, bar=False,
                             semkey=semkey))

    def barrier(self, fn):
        self.ops.append(dict(eng="dve", fn=fn, reads=(), writes=(), dma=False, bar=True))

    def resolve(self):
        ops = self.ops
        last_w = {}
        readers = {}
        last_dma_by_sem = {}
        eng_pos = {}
        last_on_eng = {}
        pending = {}
        for i, o in enumerate(ops):
            e = o["eng"]
            o["pos"] = eng_pos.get(e, 0)
            eng_pos[e] = o["pos"] + 1
            deps = set()
            raw = set()
            if o["bar"]:
                for e2, j in last_on_eng.items():
                    deps.add(j)
                for sk, j in last_dma_by_sem.items():
                    deps.add(j)
                raw = set(deps)
                for e2 in ALLENG:
                    pending[e2] = i
                pending.pop("dve", None)
            else:
                if e in pending:
                    deps.add(pending.pop(e))
                for k in o["reads"]:
                    if k in last_w:
                        deps.add(last_w[k]); raw.add(last_w[k])
                    if k.startswith("ps") or k.startswith("acc"):
                        for e2, j in readers.get(k, {}).items():
                            if e2 != e and not isinstance(j, list):
                                deps.add(j)
                for k in o["writes"]:
                    if k in last_w:
                        deps.add(last_w[k])
                    for j in readers.get(k, {}).values():
                        if isinstance(j, list):
                            deps.update(j)
                        else:
                            deps.add(j)
                if o["dma"]:
                    sk = o["semkey"]
                    if sk in last_dma_by_sem:
                        deps.add(last_dma_by_sem[sk])
                    last_dma_by_sem[sk] = i
            deps.discard(i)
            o["deps"] = deps
            o["raw"] = raw
            if not o["dma"]:
                last_on_eng[e] = i
            for k in o["reads"]:
                r = readers.setdefault(k, {})
                if o["dma"]:
                    r.setdefault("dma", []).append(i)
                else:
                    r[e] = i
            for k in o["writes"]:
                last_w[k] = i
                readers[k] = {}
        waited = {}
        waited_dma = {}
        dma_count = {}
        for i, o in enumerate(ops):
            if o["dma"]:
                sk = o["semkey"]
                dma_count[sk] = dma_count.get(sk, 0) + 1
                o["dma_n"] = dma_count[sk]
            o["needs_inc"] = False
        for i, o in enumerate(ops):
            e = o["eng"]
            w = waited.setdefault(e, {})
            wd = waited_dma.setdefault(e, {})
            waits = []
            for j in sorted(o["deps"]):
                p = ops[j]
                if p["dma"]:
                    sk = p["semkey"]
                    if wd.get(sk, 0) >= p["dma_n"]:
                        continue
                    wd[sk] = p["dma_n"]
                    waits.append(("d", sk, p["dma_n"]))
                else:
                    pe_ = p["eng"]
                    if pe_ == e:
                        if e == "pe" or e == "sp":
                            continue
                        if (not SAME_ENGINE_SYNC) or (j not in o["raw"]):
                            continue
                    if w.get(pe_, -1) >= p["pos"]:
                        continue
                    w[pe_] = p["pos"]
                    p["needs_inc"] = True
                    waits.append(("c", pe_, j))
            o["waits"] = waits
        last_idx = {}
        for i, o in enumerate(ops):
            if not o["dma"]:
                last_idx[o["eng"]] = i
        for e_, i in last_idx.items():
            if e_ in COMPUTE:
                ops[i]["needs_inc"] = True
        cnt = {}
        for o in ops:
            if o["dma"]:
                continue
            e = o["eng"]
            if o["needs_inc"]:
                cnt[e] = cnt.get(e, 0) + 1
            o["count"] = cnt.get(e, 0)
        self.sem_keys = sorted({o["semkey"] for o in ops if o["dma"]}, key=str)
        self.final_dma = dict(dma_count)
        return cnt

    def emit(self, stack):
        nc = self.nc
        ops = self.ops
        cnt = self.resolve()
        esem = {e: stack.enter_context(nc.semaphore("s_" + e)) for e in COMPUTE}
        dsem = {k: stack.enter_context(nc.semaphore("d_%d" % i)) for i, k in enumerate(self.sem_keys)}
        block = stack.enter_context(nc.Block())
        by_eng = {}
        for o in ops:
            by_eng.setdefault(o["eng"], []).append(o)

        def run(engname, eng):
            for o in by_eng.get(engname, []):
                for wt in o["waits"]:
                    if wt[0] == "d":
                        eng.wait_ge(dsem[wt[1]], 16 * wt[2])
                    else:
                        eng.wait_ge(esem[wt[1]], ops[wt[2]]["count"])
                ins = o["fn"](eng)
                if o["dma"]:
                    ins.then_inc(dsem[o["semkey"]], 16)
                elif o["needs_inc"]:
                    ins.then_inc(esem[engname], 1)
            if engname == "sp":
                for k, n in self.final_dma.items():
                    eng.wait_ge(dsem[k], 16 * n)
                for e2 in COMPUTE:
                    if cnt.get(e2, 0) > 0:
                        eng.wait_ge(esem[e2], cnt[e2])

        @block.sync
        def _(sync):
            run("sp", sync)

        @block.tensor
        def _(tensor):
            run("pe", tensor)

        @block.scalar
        def _(scalar):
            run("act", scalar)

        @block.vector
        def _(vector):
            run("dve", vector)

        @block.gpsimd
        def _(gpsimd):
            run("pool", gpsimd)


SLAB_EIN = 0
SLAB_EOUT = 7
SLAB_FUP0 = 9
SLAB_FDN0 = 17
SLAB_OIN = 25
SLAB_OOUT = 29
SLAB_FUP1 = 31
SLAB_FDN1 = 39
NSLAB = 47


def build_program(units):
    nc = bass.Bass("TRN2", target_bir_lowering=False)

    def din(name, shape):
        return nc.dram_tensor(name, list(shape), F32, kind="ExternalInput").ap()

    def dout(name, shape):
        return nc.dram_tensor(name, list(shape), F32, kind="ExternalOutput").ap()

    xp = din("xp", [SPB, T, D]); xs = din("xs", [SPB * DS, D])
    csk = din("csk", [SPB, PAST, 512]); csv = din("csv", [SPB, PAST, 512])
    sconv = din("sconv", [SPB, 2, 512])
    cckv = din("cckv", [SPB, PAST, 256]); ckpe = din("ckpe", [SPB, PAST, 32])
    ein = din("ein", [D, 3072]); eout = din("eout", [D, D]); oin = din("oin", [D, 1696]); oout = din("oout", [D, D])
    fup = din("fup", [2, D, 4096]); fdn = din("fdn", [2, 4096, D])
    grow = din("grow", [79, 128])
    lng = din("lng", [1, 512]); lnb = din("lnb", [1, 512]); kvg = din("kvg", [1, 256]); bsd = din("bsd", [1, 512])
    bss = din("bss", [1, 4, 16])
    wsd = din("wsd", [4, 128, 128])
    wuq = din("wuq", [384, 768]); wuk = din("wuk", [8, 64, 256]); wuv = din("wuv", [8, 256, 64])
    cst = din("cst", [128, 1408])
    rope_t = din("rope_t", [128, 512]); rope_ts = din("rope_ts", [64, 32])
    rope_f = din("rope_f", [2, 32, T]); rope_fs = din("rope_fs", [2, 32, 64])

    y_p = dout("y_p", [SPB, T, D]); y_s = dout("y_s", [SPB * DS, D])
    sbk_p = dout("sbk_p", [SPB, T, 512]); sbv_p = dout("sbv_p", [SPB, T, 512])
    conv_p = dout("conv_p", [SPB, 2, 512])
    ckv_p = dout("ckv_p", [SPB, T, 256]); kpe_p = dout("kpe_p", [SPB, T, 32])
    sbk_s = dout("sbk_s", [SPB * DS, 512]); sbv_s = dout("sbv_s", [SPB * DS, 512])
    conv_s = dout("conv_s", [SPB, 2, 512])
    ckv_s = dout("ckv_s", [SPB * DS, 256]); kpe_s = dout("kpe_s", [SPB * DS, 32])
    sguv_s = dout("sguv_s", [SPB * DS, 512])

    wsc = nc.dram_tensor("wsc", [NSLAB, 128, 4096], BF16, kind="Internal").ap()

    P = Prog(nc)
    with ExitStack() as st:
        def sb(name, shape, dt):
            return st.enter_context(nc.sbuf_tensor(name, list(shape), dt))

        xT = sb("xT", [128, 8, 512], F32)
        identf = sb("identf", [128, 128], F32)
        identb = sb("identb", [128, 128], BF16)
        ntri = sb("ntri", [128, 128], BF16)
        nones = sb("nones", [128, 128], BF16)
        onesb = sb("onesb", [128, 128], BF16)
        sbm = sb("sbm", [128, 512], BF16)
        mlm = sb("mlm", [128, 512], BF16)
        gcol = sb("gcol", [128, 80], F32)
        lnG = sb("lnG", [128, 512], F32); lnB = sb("lnB", [128, 512], F32)
        kvG = sb("kvG", [128, 256], F32); bsB = sb("bsB", [128, 512], F32); bsBs = sb("bsBs", [128, 4, 64], F32)
        wsT = sb("wsT", [128, 4, 128], BF16); wsbd = sb("wsbd", [128, 4, 64], BF16)
        wuqn = sb("wuqn", [128, 3, 512], BF16); wuqp = sb("wuqp", [128, 3, 256], BF16)
        wuqs = sb("wuqs", [128, 3, 256], BF16)
        wukp = sb("wukp", [128, 4, 256], BF16); wuvb = sb("wuvb", [128, 8, 2, 64], BF16)
        ropeT = sb("ropeT", [128, 512], F32); ropeTs = sb("ropeTs", [128, 32], F32)
        h = sb("h", [128, 8, 512], BF16)
        mix = sb("mix", [128, 8, 512], BF16)
        o = sb("o", [128, 8, 512], F32)
        rstd = sb("rstd", [128, 512], F32)
        sq = sb("sq", [128, 2, 512], BF16)
        io_b = [sb("io0", [128, 1024], F32), sb("io1", [128, 1024], F32)]
        tmpf = sb("tmpf", [128, 2, 512], F32)
        cstate = sb("cstate", [128, 4, 2], F32)
        small4 = sb("small4", [128, 4, 32], F32)
        bart = sb("bart", [128, 4], F32)
        slab = sb("slab", [128, 2, 4096], BF16)
        AA = sb("arenaA", [128, 21248], BF16)
        AC = sb("arenaC", [128, 26624], BF16)
        ps = [st.enter_context(nc.psum_tensor("ps%d" % i, [128, 512], F32)) for i in range(8)]
        psb = [p_[:, :].bitcast(BF16) for p_ in ps]

        def carve(arena, off_bytes, nelem, dt):
            a0 = off_bytes // 2
            if dt == BF16:
                return arena[:, a0:a0 + nelem]
            return arena[:, a0:a0 + 2 * nelem].bitcast(F32)

        qT = carve(AA, 0, 2048, BF16).rearrange("p (m t) -> p m t", m=4)
        e_t = carve(AA, 4096, 512, F32)
        sp_t = carve(AA, 6144, 1024, BF16).rearrange("p (a t) -> p a t", a=2)
        spm_t = carve(AA, 8192, 1024, BF16).rearrange("p (a t) -> p a t", a=2)
        w_t = carve(AA, 10240, 1024, BF16).rearrange("p (a t) -> p a t", a=2)
        S_b = [carve(AA, 12288, 512, BF16), carve(AA, 22528, 512, BF16)]
        cin = carve(AA, 13312, 520, F32)
        kvout_b = [carve(AA, 15872, 1024, F32).rearrange("p (a t) -> p a t", a=2), carve(AA, 24576, 1024, F32).rearrange("p (a t) -> p a t", a=2)]
        ksn = carve(AA, 19968, 256, BF16).rearrange("p (m t) -> p m t", m=4)
        vnew = carve(AA, 20480, 512, BF16)
        qs_b = [carve(AA, 21504, 16, BF16), carve(AA, 21568, 16, BF16)]
        cqT = carve(AA, 0, 1536, BF16).rearrange("p (m t) -> p m t", m=3)
        qnT = carve(AA, 3072, 2048, BF16).rearrange("p (m t) -> p m t", m=4)
        uT = carve(AA, 7168, 2048, F32).rearrange("p (m t) -> p m t", m=4)
        vnb = carve(AA, 15360, 2048, BF16).rearrange("p (a t) -> p a t", a=4)
        ckst_b = [carve(AA, 19456, 288, F32), carve(AA, 40960, 288, F32)]
        qlat_b = [carve(AA, 20992, 1024, BF16).rearrange("p (a t) -> p a t", a=2), carve(AA, 37888, 1024, BF16).rearrange("p (a t) -> p a t", a=2)]
        qpe_b = [carve(AA, 23040, 512, BF16), carve(AA, 39936, 512, BF16)]
        OLs = carve(AA, 24064, 1024, BF16).rearrange("p (a t) -> p a t", a=2)
        rden = carve(AA, 26112, 512, F32)
        cosF = carve(AA, 28160, 512, F32)
        sinF = carve(AA, 30208, 512, F32)
        p_t = carve(AA, 32256, 1024, BF16).rearrange("p (a t) -> p a t", a=2)
        vnf = carve(AA, 34304, 512, F32)
        kpb_b = [carve(AA, 36352, 32, BF16), carve(AA, 42112, 32, BF16)]
        ckvn = carve(AA, 36416, 256, BF16)
        h1 = carve(AA, 0, 16384, BF16).rearrange("p (j t) -> p j t", j=32)
        cstS = carve(AA, 0, 1408, F32)
        growS = carve(AA, 5632, 128, F32)
        wsS = carve(AA, 6144, 512, F32).rearrange("p (g s) -> p g s", g=4)
        wuqS = carve(AA, 8192, 2304, F32).rearrange("p (k n) -> p k n", k=3)
        wukS = carve(AA, 17408, 1024, F32).rearrange("p (j c) -> p j c", j=4)
        wuvS = carve(AA, 21504, 1024, F32).rearrange("p (a b) -> p a b", a=16)

        ksT = carve(AC, 0, 8192, BF16).rearrange("p (m t) -> p m t", m=4)
        vtok = carve(AC, 16384, 8192, BF16).rearrange("p (a t) -> p a t", a=16)
        ckvT = carve(AC, 32768, 4096, BF16).rearrange("p (a t) -> p a t", a=2)
        ckvtok = carve(AC, 40960, 4096, BF16).rearrange("p (a t) -> p a t", a=16)
        kpeT = carve(AC, 49152, 2048, BF16)
        kld = carve(AC, 0, 2048, F32).rearrange("p (a d) -> p a d", a=32)
        vld = carve(AC, 8192, 2048, F32).rearrange("p (a d) -> p a d", a=32)
        kTs_b = [carve(AC, 16384, 4112, BF16), carve(AC, 24640, 4112, BF16)]
        vh_b = [carve(AC, 32896, 33 * 64, BF16).rearrange("p (a d) -> p a d", a=33), carve(AC, 37120, 33 * 64, BF16).rearrange("p (a d) -> p a d", a=33)]
        kbf_b = [carve(AC, 41344, 2048, BF16).rearrange("p (a d) -> p a d", a=32), carve(AC, 45440, 2048, BF16).rearrange("p (a d) -> p a d", a=32)]
        cstg = carve(AC, 0, 2048, F32)
        ckvtok_s = carve(AC, 8192, 33 * 256, BF16).rearrange("p (a d) -> p a d", a=33)
        ckvT_s = carve(AC, 25088, 2 * 4112, BF16).rearrange("p (a t) -> p a t", a=2)
        kpeT_s = carve(AC, 41536, 4112, BF16)
        kpbs = carve(AC, 49792, 1024, BF16).rearrange("p (a d) -> p a d", a=32)
        stg = carve(AC, 0, 8192, F32).rearrange("p (a n) -> p a n", a=2)
        sbf = carve(AC, 32768, 8192, BF16).rearrange("p (a n) -> p a n", a=2)

        state = dict(dbank=0, zb=0, bb=0, slabk=0, slab_issued=0, tmp=0)

        def dbank():
            b = state["dbank"] % 3
            state["dbank"] += 1
            return b

        def bkey(b):
            return "ps%d" % b

        def barrier():
            P.barrier(lambda e: e.memset(bart[:, 0:4], 0.0))

        def setup():
            P.dma(lambda e: e.dma_start(out=cstS[:, :], in_=cst[:, :]), writes=["cstS"], semkey="su0")
            P.op("act", lambda e: e.activation(out=identf[:, :], in_=cstS[:, 0:128], func=AF.Copy), reads=["cstS"], writes=["identf"])
            P.op("dve", lambda e: e.tensor_copy(out=identb[:, :], in_=cstS[:, 0:128]), reads=["cstS"], writes=["identb"])
            P.op("dve", lambda e: e.tensor_copy(out=ntri[:, :], in_=cstS[:, 128:256]), reads=["cstS"], writes=["ntri"])
            P.op("dve", lambda e: e.tensor_copy(out=sbm[:, :], in_=cstS[:, 256:768]), reads=["cstS"], writes=["sbm"])
            P.op("dve", lambda e: e.tensor_copy(out=mlm[:, :], in_=cstS[:, 768:1280]), reads=["cstS"], writes=["mlm"])
            P.op("pool", lambda e: e.memset(nones[:, :], -1.0), writes=["nones"])
            P.op("pool", lambda e: e.memset(onesb[:, :], 1.0), writes=["onesb"])
            P.op("pool", lambda e: e.memset(wsbd[:, :, :], 0.0), writes=["wsbd"])
            P.op("pool", lambda e: e.memset(cstate[:, :, :], 0.0), writes=["cstate"])
            P.dma(lambda e: e.dma_start(out=growS[0:79, :], in_=grow[:, :]), writes=["growS"], semkey="su1")
            P.op("pe", lambda e: e.transpose(ps[0][:, 0:79], growS[0:79, :], identf[0:79, 0:79]), reads=["growS", "identf"], writes=[bkey(0)])
            P.op("act", lambda e: e.activation(out=gcol[:, 0:79], in_=ps[0][:, 0:79], func=AF.Copy), reads=[bkey(0)], writes=["gcol"])
            P.dma(lambda e: e.dma_start(out=lnG[:, :], in_=lng[0:1, :].broadcast_to([128, 512])), writes=["lnG"], semkey="su2")
            P.dma(lambda e: e.dma_start(out=lnB[:, :], in_=lnb[0:1, :].broadcast_to([128, 512])), writes=["lnB"], semkey="su3")
            P.dma(lambda e: e.dma_start(out=kvG[:, :], in_=kvg[0:1, :].broadcast_to([128, 256])), writes=["kvG"], semkey="su4")
            P.dma(lambda e: e.dma_start(out=bsB[:, :], in_=bsd[0:1, :].broadcast_to([128, 512])), writes=["bsB"], semkey="su5")
            for j in range(4):
                P.dma(lambda e, j=j: e.dma_start(out=bsBs[:, :, j * 16:(j + 1) * 16], in_=bss[0:1, :, :].broadcast_to([128, 4, 16])),
                      writes=["bsBs"], semkey="su6")
            P.dma(lambda e: e.dma_start(out=ropeT[:, :], in_=rope_t[:, :]), writes=["ropeT"], semkey="su7")
            P.dma(lambda e: e.dma_start(out=ropeTs[0:64, :], in_=rope_ts[:, :]), writes=["ropeTs"], semkey="su8")
            P.dma(lambda e: e.dma_start(out=wsS[:, :, :], in_=wsd.rearrange("g t s -> t g s")), writes=["wsS"], semkey="su9")
            for g in range(4):
                P.op("dve", lambda e, g=g: e.tensor_tensor(out=wsS[:, g, :], in0=wsS[:, g, :], in1=cstS[:, 1280:1408], op=ALU.mult),
                     reads=["wsS", "cstS"], writes=["wsS"])
            P.op("pe", lambda e: [e.transpose(ps[1][:, g * 128:(g + 1) * 128], wsS[:, g, :], identf[:, :]) for g in range(4)][-1],
                 reads=["wsS", "identf"], writes=[bkey(1)])
            P.op("act", lambda e: e.activation(out=wsT[:, :, :], in_=ps[1][:, :].rearrange("p (g t) -> p g t", g=4), func=AF.Copy),
                 reads=[bkey(1)], writes=["wsT"])
            for j in range(4):
                P.dma(lambda e, j=j: e.dma_start(out=wsbd[16 * j:16 * j + 16, :, 16 * j:16 * j + 16], in_=wsT[0:16, :, 0:16]),
                      reads=["wsT", "wsbd"], writes=["wsbd"], semkey="su10")
            P.dma(lambda e: e.dma_start(out=wuqS[:, :, :], in_=wuq.rearrange("(k p) n -> p k n", p=128)), writes=["wuqS"], semkey="su11")
            for kc in range(3):
                src = wuqS[:, kc, :].rearrange("p (h d) -> p h d", d=96)
                P.op("dve", lambda e, kc=kc, src=src: e.tensor_copy(out=wuqn[:, kc, :].rearrange("p (h d) -> p h d", d=64), in_=src[:, :, 0:64]),
                     reads=["wuqS"], writes=["wuqn"])
                P.op("dve", lambda e, kc=kc, src=src: e.tensor_copy(out=wuqp[:, kc, :].rearrange("p (h d) -> p h d", d=32), in_=src[:, :, 64:96]),
                     reads=["wuqS"], writes=["wuqp"])
                P.op("pool", lambda e, kc=kc, src=src: e.tensor_copy(out=wuqs[:, kc, :].rearrange("p (h d) -> p h d", d=32)[:, :, 0:16], in_=src[:, :, 80:96]),
                     reads=["wuqS"], writes=["wuqs"])
                P.op("pool", lambda e, kc=kc, src=src: e.tensor_copy(out=wuqs[:, kc, :].rearrange("p (h d) -> p h d", d=32)[:, :, 16:32], in_=src[:, :, 64:80]),
                     reads=["wuqS"], writes=["wuqs"])
            for two in range(2):
                P.dma(lambda e, two=two: e.dma_start(out=wukS[two * 64:(two + 1) * 64, :, :], in_=wuk.rearrange("(j two) n c -> two n j c", two=2)[two]),
                      writes=["wukS"], semkey="su12")
            P.op("dve", lambda e: e.tensor_copy(out=wukp[:, :, :], in_=wukS[:, :, :]), reads=["wukS"], writes=["wukp"])
            P.dma(lambda e: e.dma_start(out=wuvS[:, :, :], in_=wuv.rearrange("h (cc p) v -> p (h cc) v", p=128)), writes=["wuvS"], semkey="su13")
            P.op("dve", lambda e: e.tensor_copy(out=wuvb[:, :, :, :].rearrange("p h c v -> p (h c) v"), in_=wuvS[:, :, :]), reads=["wuvS"], writes=["wuvb"])

            def Wv(W, c0, c1):
                return W.rearrange("(kc p) n -> p kc n", p=128)[:, :, c0:c1]

            slabs = []
            for c0 in (0, 512, 1024):
                slabs.append(([(Wv(ein, c0, c0 + 512), 0, 512)], 8, 512))
            for c in range(4):
                slabs.append(([(Wv(ein, 1536 + 128 * c, 1536 + 128 * c + 128), 0, 128),
                               (Wv(ein, 2048 + 128 * c, 2048 + 128 * c + 128), 128, 128),
                               (Wv(ein, 2560 + 128 * c, 2560 + 128 * c + 128), 256, 128)], 8, 384))
            for c0 in (0, 512):
                slabs.append(([(Wv(eout, c0, c0 + 512), 0, 512)], 8, 512))
            for j in range(8):
                slabs.append(([(Wv(fup[0], 512 * j, 512 * j + 512), 0, 512)], 8, 512))
            for n in range(8):
                slabs.append(([(Wv(fdn[0], 128 * n, 128 * n + 128), 0, 128)], 32, 128))
            slabs.append(([(Wv(oin, 0, 512), 0, 512)], 8, 512))
            slabs.append(([(Wv(oin, 1024, 1408), 0, 384)], 8, 384))
            slabs.append(([(Wv(oin, 512, 1024), 0, 512)], 8, 512))
            slabs.append(([(Wv(oin, 1408, 1696), 0, 288)], 8, 288))
            for c0 in (0, 512):
                slabs.append(([(Wv(oout, c0, c0 + 512), 0, 512)], 8, 512))
            for j in range(8):
                slabs.append(([(Wv(fup[1], 512 * j, 512 * j + 512), 0, 512)], 8, 512))
            for n in range(8):
                slabs.append(([(Wv(fdn[1], 128 * n, 128 * n + 128), 0, 128)], 32, 128))
            assert len(slabs) == NSLAB
            engs = ["dve", "act", "pool"]
            for i, (pieces, kc, nct) in enumerate(slabs):
                b = i % 2
                n = kc * nct
                sv = stg[:, b, 0:n].rearrange("p (k n) -> p k n", k=kc)
                for pi, (src, coff, ncol) in enumerate(pieces):
                    P.dma(lambda e, sv=sv, src=src, coff=coff, ncol=ncol: e.dma_start(out=sv[:, :, coff:coff + ncol], in_=src),
                          writes=["stg%d" % b], semkey="stg%d_%d" % (b, pi))
                en = engs[i % 3]
                if en == "act":
                    P.op("act", lambda e, b=b, n=n: e.activation(out=sbf[:, b, 0:n], in_=stg[:, b, 0:n], func=AF.Copy),
                         reads=["stg%d" % b], writes=["sbf%d" % b])
                else:
                    P.op(en, lambda e, b=b, n=n: e.tensor_copy(out=sbf[:, b, 0:n], in_=stg[:, b, 0:n]),
                         reads=["stg%d" % b], writes=["sbf%d" % b])
                P.dma(lambda e, i=i, b=b, n=n: e.dma_start(out=wsc[i, :, 0:n], in_=sbf[:, b, 0:n]),
                      reads=["sbf%d" % b], writes=["wsc%d" % i], semkey="sbfo%d" % b)

        unit_slab_seq = list(range(NSLAB))
        total_slabs = len(units) * NSLAB

        slab_n = {3: 3072, 4: 3072, 5: 3072, 6: 3072, SLAB_OIN + 1: 3072, SLAB_OIN + 3: 2304}

        def issue_slab(k):
            idx = unit_slab_seq[k % NSLAB]
            b = k % 2
            n = slab_n.get(idx, 4096)
            P.dma(lambda e, idx=idx, b=b, n=n: e.dma_start(out=slab[:, b, 0:n], in_=wsc[idx, :, 0:n]),
                  reads=["wsc%d" % idx], writes=["slab%d" % b], semkey="slab%d" % b)

        def get_slab(expect):
            k = state["slabk"]
            assert unit_slab_seq[k % NSLAB] == expect, (k, expect)
            while state["slab_issued"] <= min(k + 1, total_slabs - 1):
                issue_slab(state["slab_issued"])
                state["slab_issued"] += 1
            state["slabk"] += 1
            b = k % 2
            return slab[:, b, :], "slab%d" % b

        def rms_stats(srcs, keys, Dn, U):
            bank = dbank()
            n = len(srcs)
            for c in range(n):
                P.op("act", lambda e, c=c: e.activation(out=sq[:, c % 2, 0:U], in_=srcs[c], func=AF.Square),
                     reads=[keys[c]], writes=["sq%d" % (c % 2)])
                P.op("pe", lambda e, c=c: e.matmul(ps[bank][:, 0:U], onesb[:, :], sq[:, c % 2, 0:U], start=(c == 0), stop=(c == n - 1)),
                     reads=["sq%d" % (c % 2), "onesb"], writes=[bkey(bank)])
            P.op("act", lambda e: e.activation(out=rstd[:, 0:U], in_=ps[bank][:, 0:U], func=AF.Ln, scale=1.0 / Dn, bias=EPS),
                 reads=[bkey(bank)], writes=["rstd"])
            P.op("act", lambda e: e.activation(out=rstd[:, 0:U], in_=rstd[:, 0:U], func=AF.Exp, scale=-0.5),
                 reads=["rstd"], writes=["rstd"])

        def pre_norm(gi, U):
            rms_stats([xT[:, c, 0:U] for c in range(8)], ["xT%d" % c for c in range(8)], 1024, U)
            for c in range(8):
                P.op("dve", lambda e, c=c: e.scalar_tensor_tensor(out=h[:, c, 0:U], in0=xT[:, c, 0:U], scalar=gcol[:, gi + c:gi + c + 1],
                                                                 in1=rstd[:, 0:U], op0=ALU.mult, op1=ALU.mult),
                     reads=["xT%d" % c, "rstd", "gcol"], writes=["h"])

        def post_norm_residual(gi, U):
            rms_stats([o[:, c, 0:U] for c in range(8)], ["o%d" % c for c in range(8)], 1024, U)
            for c in range(8):
                P.op("dve", lambda e, c=c: e.scalar_tensor_tensor(out=o[:, c, 0:U], in0=o[:, c, 0:U], scalar=gcol[:, gi + c:gi + c + 1],
                                                                 in1=rstd[:, 0:U], op0=ALU.mult, op1=ALU.mult),
                     reads=["o%d" % c, "rstd", "gcol"], writes=["o%d" % c])
                P.op("pool", lambda e, c=c: e.tensor_tensor(out=xT[:, c, 0:U], in0=xT[:, c, 0:U], in1=o[:, c, 0:U], op=ALU.add),
                     reads=["xT%d" % c, "o%d" % c], writes=["xT%d" % c])

        def proj_fm(sl, slk, col0, ncol, rhs_fn, rkeys, nkc, U, stride):
            bank = dbank()
            sv = sl[:, 0:nkc * stride].rearrange("p (k n) -> p k n", k=nkc)

            def fn(e):
                ins = None
                for kc in range(nkc):
                    ins = e.matmul(ps[bank][0:ncol, 0:U], sv[:, kc, col0:col0 + ncol], rhs_fn(kc), start=(kc == 0), stop=(kc == nkc - 1))
                return ins
            P.op("pe", fn, reads=[slk] + list(rkeys), writes=[bkey(bank)])
            return bank

        def proj_tm(sl, slk, ncol, tt, TT, stride):
            bank = dbank()
            sv = sl[:, 0:8 * stride].rearrange("p (k n) -> p k n", k=8)

            def fn(e):
                ins = None
                for kc in range(8):
                    ins = e.matmul(ps[bank][0:TT, 0:ncol], h[:, kc, tt * TT:(tt + 1) * TT], sv[:, kc, 0:ncol], start=(kc == 0), stop=(kc == 7))
                return ins
            P.op("pe", fn, reads=[slk, "h"], writes=[bkey(bank)])
            return bank

        def wout_and_post(slab0, gi, U):
            for sl_i in range(2):
                sl, slk = get_slab(slab0 + sl_i)
                for mm in range(4):
                    n = sl_i * 4 + mm
                    bank = proj_fm(sl, slk, mm * 128, 128, lambda kc: mix[:, kc, 0:U], ["mix%d" % c for c in range(8)], 8, U, 512)
                    P.op("dve", lambda e, n=n, bank=bank: e.tensor_copy(out=o[:, n, 0:U], in_=ps[bank][:, 0:U]),
                         reads=[bkey(bank)], writes=["o%d" % n])
            post_norm_residual(gi, U)

        def ffn(layer, U):
            pre_norm(32 + layer * 8, U)
            barrier()
            s_up = SLAB_FUP0 if layer == 0 else SLAB_FUP1
            s_dn = SLAB_FDN0 if layer == 0 else SLAB_FDN1
            for j in range(8):
                sl, slk = get_slab(s_up + j)
                for mm in range(4):
                    bank = proj_fm(sl, slk, mm * 128, 128, lambda kc: h[:, kc, 0:U], ["h"], 8, U, 512)
                    tb = state["tmp"] % 2
                    state["tmp"] += 1
                    P.op("act", lambda e, bank=bank, tb=tb: e.activation(out=tmpf[:, tb, 0:U], in_=ps[bank][:, 0:U], func=AF.Relu),
                         reads=[bkey(bank)], writes=["tmpf%d" % tb])
                    P.op("dve", lambda e, bank=bank, tb=tb, jj=4 * j + mm: e.tensor_tensor(out=h1[:, jj, 0:U], in0=tmpf[:, tb, 0:U], in1=ps[bank][:, 0:U], op=ALU.mult),
                         reads=[bkey(bank), "tmpf%d" % tb], writes=["h1_%d" % (4 * j + mm)])
            for n in range(8):
                sl, slk = get_slab(s_dn + n)
                bank = proj_fm(sl, slk, 0, 128, lambda kc: h1[:, kc, 0:U], ["h1_%d" % c for c in range(32)], 32, U, 128)
                P.op("dve", lambda e, n=n, bank=bank: e.tensor_copy(out=o[:, n, 0:U], in_=ps[bank][:, 0:U]),
                     reads=[bkey(bank)], writes=["o%d" % n])
            post_norm_residual(48 + layer * 8, U)
            barrier()

        def sb_tile(kAP, qAP, vAP, nk, ncols, S_ap, acc_ap, acc_key, mask, first, last, rk, tu, Sk="S0", vk="vtok"):
            zb = 3 + (tu % 2)
            bb = 5 + (tu % 2)
            a = tu % 2
            P.op("pe", lambda e: e.matmul(ps[zb][0:nk, 0:ncols], kAP, qAP, start=True, stop=True), reads=rk, writes=[bkey(zb)])
            P.op("act", lambda e: e.activation(out=e_t[0:nk, 0:ncols], in_=ps[zb][0:nk, 0:ncols], func=AF.Exp), reads=[bkey(zb)], writes=["e_t"])
            P.op("act", lambda e: e.activation(out=sp_t[0:nk, a, 0:ncols], in_=e_t[0:nk, 0:ncols], func=AF.Ln, bias=1.0), reads=["e_t"], writes=["sp%d" % a])
            if mask is not None:
                P.op("dve", lambda e: e.tensor_tensor(out=spm_t[0:nk, a, 0:ncols], in0=sp_t[0:nk, a, 0:ncols], in1=mask, op=ALU.mult),
                     reads=["sp%d" % a, "sbm"], writes=["spm%d" % a])
                spm = spm_t[0:nk, a, 0:ncols]
                spk = "spm%d" % a
            else:
                spm = sp_t[0:nk, a, 0:ncols]
                spk = "sp%d" % a
            yield

            def fnB(e):
                e.matmul(ps[bb][0:nk, 0:ncols], ntri[0:nk, 0:nk], spm, start=True, stop=False)
                if not first:
                    e.matmul(ps[bb][0:nk, 0:ncols], nones[0:128, 0:nk], S_ap, start=False, stop=False)
                return e.matmul(ps[bb][0:nk, 0:ncols], kAP, qAP, start=False, stop=True)
            P.op("pe", fnB, reads=[spk, Sk, "ntri", "nones"] + list(rk), writes=[bkey(bb)])
            if not last:
                P.op("pool", lambda e: e.tensor_tensor(out=S_ap[0:nk, :], in0=S_ap[0:nk, :], in1=spm, op=ALU.add), reads=[spk, Sk], writes=[Sk])
            P.op("act", lambda e: e.activation(out=w_t[0:nk, a, 0:ncols], in_=ps[bb][0:nk, 0:ncols], func=AF.Exp), reads=[bkey(bb)], writes=["w%d" % a])
            if mask is not None:
                P.op("dve", lambda e: e.tensor_tensor(out=w_t[0:nk, a, 0:ncols], in0=w_t[0:nk, a, 0:ncols], in1=mask, op=ALU.mult),
                     reads=["w%d" % a, "sbm"], writes=["w%d" % a])
            yield
            P.op("pe", lambda e: e.matmul(acc_ap, vAP, w_t[0:nk, a, 0:ncols], start=first, stop=last), reads=["w%d" % a, vk], writes=[acc_key])

        def wrap(g, pre=None, post=None):
            if pre is not None:
                pre()
            try:
                while True:
                    next(g)
                    yield
            except StopIteration:
                pass
            if post is not None:
                post()

        def run_pipelined(gens, hook_step=None, hook=None):
            pending = list(gens)
            active = []
            step = 0
            while pending or active:
                while pending and not hasattr(pending[0], "__next__"):
                    pending.pop(0)()
                if pending:
                    active.append(pending.pop(0))
                nxt = []
                for g in reversed(active):
                    try:
                        next(g)
                        nxt.append(g)
                    except StopIteration:
                        pass
                active = list(reversed(nxt))
                if hook is not None and step == hook_step:
                    hook()
                step += 1
            if hook is not None and step <= hook_step:
                hook()

        def mla_tile(cT0, cT1, kpT, ctok, nk, q0, q1, mask, first, last, hp, tu, qb_=0):
            zb = 3 + (tu % 2)
            a = tu % 2
            qlat = qlat_b[qb_]
            qpe = qpe_b[qb_]

            def fnZ(e):
                e.matmul(ps[zb][0:nk, q0:q1], cT0, qlat[:, 0, q0:q1], start=True, stop=False)
                e.matmul(ps[zb][0:nk, q0:q1], cT1, qlat[:, 1, q0:q1], start=False, stop=False)
                return e.matmul(ps[zb][0:nk, q0:q1], kpT, qpe[0:32, q0:q1], start=False, stop=True)
            P.op("pe", fnZ, reads=["ckvT", "kpeT", "qlat%d" % qb_, "qpe%d" % qb_], writes=[bkey(zb)])
            P.op("act", lambda e: e.activation(out=p_t[0:nk, a, q0:q1], in_=ps[zb][0:nk, q0:q1], func=AF.Exp, scale=MLA_SCALE),
                 reads=[bkey(zb)], writes=["p%d" % a])
            if mask is not None:
                P.op("dve", lambda e: e.tensor_tensor(out=p_t[0:nk, a, q0:q1], in0=p_t[0:nk, a, q0:q1], in1=mask, op=ALU.mult),
                     reads=["p%d" % a, "mlm"], writes=["p%d" % a])
            yield

            def fnO(e):
                e.matmul(ps[5][:, q0:q1], ctok[:, 0:128], p_t[0:nk, a, q0:q1], start=first, stop=last)
                e.matmul(ps[6][:, q0:q1], ctok[:, 128:256], p_t[0:nk, a, q0:q1], start=first, stop=last)
                return e.matmul(ps[7][:, q0:q1], onesb[0:nk, :], p_t[0:nk, a, q0:q1], start=first, stop=last)
            P.op("pe", fnO, reads=["p%d" % a, "ckvtok", "onesb"], writes=[bkey(5), bkey(6), bkey(7)])

        def unit(kind, s, qb):
            isp = (kind == "p")
            U = 512 if isp else 64
            TT = 128 if isp else 64
            NTT = U // TT
            row0 = qb * 512

            for tt in range(NTT):
                src = xp[s, row0 + tt * 128: row0 + tt * 128 + 128, :] if isp else xs[0:64, :]
                xin = io_b[tt % 2]
                xk = "io%d" % (tt % 2)
                P.dma(lambda e, src=src, xin=xin: e.dma_start(out=xin[0:TT, :], in_=src), writes=[xk], semkey=xk)
                for half in range(2):
                    bank = dbank()
                    P.op("pe", lambda e, half=half, bank=bank, xin=xin: [e.transpose(ps[bank][:, c4 * TT:(c4 + 1) * TT], xin[0:TT, (half * 4 + c4) * 128:(half * 4 + c4 + 1) * 128], identf[0:TT, 0:TT]) for c4 in range(4)][-1],
                         reads=[xk, "identf"], writes=[bkey(bank)])
                    P.op("act", lambda e, half=half, bank=bank, tt=tt: e.activation(out=xT[:, half * 4:half * 4 + 4, tt * TT:(tt + 1) * TT],
                                                                                in_=ps[bank][:, 0:4 * TT].rearrange("p (a t) -> p a t", a=4), func=AF.Copy),
                         reads=[bkey(bank)], writes=["xT%d" % (half * 4 + c4) for c4 in range(4)])
            if float(os.environ.get("KSTOP", 99)) <= 1:
                return
            pre_norm(0, U)
            if float(os.environ.get("KSTOP", 99)) <= 1.2:
                return
            sl, slk = get_slab(SLAB_EIN + 0)
            for m in range(4):
                bank = proj_fm(sl, slk, m * 128, 128, lambda kc: h[:, kc, 0:U], ["h"], 8, U, 512)
                P.op("act", lambda e, m=m, bank=bank: e.activation(out=qT[:, m, 0:U], in_=ps[bank][:, 0:U], func=AF.Copy),
                     reads=[bkey(bank)], writes=["qT"])
            if float(os.environ.get("KSTOP", 99)) <= 1.4:
                return
            sl, slk = get_slab(SLAB_EIN + 1)
            for m in range(4):
                bank = proj_fm(sl, slk, m * 128, 128, lambda kc: h[:, kc, 0:U], ["h"], 8, U, 512)
                if isp:
                    P.op("act", lambda e, m=m, bank=bank: e.activation(out=ksT[:, m, row0:row0 + 512], in_=ps[bank][:, 0:U], func=AF.Copy, scale=SB_SCALE),
                         reads=[bkey(bank)], writes=["ksT"])
                else:
                    P.op("act", lambda e, m=m, bank=bank: e.activation(out=ksn[:, m, 0:U], in_=ps[bank][:, 0:U], func=AF.Copy, scale=SB_SCALE),
                         reads=[bkey(bank)], writes=["ksn"])
            for tt in range(NTT):
                bank = proj_tm(sl, slk, 512, tt, TT, 512)
                kvout = kvout_b[tt % 2]
                kk0 = "kvout0_%d" % (tt % 2)
                P.op("dve", lambda e, bank=bank, kvout=kvout: e.tensor_copy(out=kvout[0:TT, 0, :], in_=ps[bank][0:TT, :]), reads=[bkey(bank)], writes=[kk0])
                dst = sbk_p[s, row0 + tt * 128: row0 + tt * 128 + 128, :] if isp else sbk_s[0:64, :]
                P.dma(lambda e, dst=dst, kvout=kvout: e.dma_start(out=dst, in_=kvout[0:TT, 0, :]), reads=[kk0], writes=[], semkey=kk0)
            if float(os.environ.get("KSTOP", 99)) <= 1.6:
                return
            sl, slk = get_slab(SLAB_EIN + 2)
            for tt in range(NTT):
                bank = proj_tm(sl, slk, 512, tt, TT, 512)
                kvout = kvout_b[tt % 2]
                kk1 = "kvout1_%d" % (tt % 2)
                P.op("dve", lambda e, bank=bank, kvout=kvout: e.tensor_copy(out=kvout[0:TT, 1, :], in_=ps[bank][0:TT, :]), reads=[bkey(bank)], writes=[kk1])
                dst = sbv_p[s, row0 + tt * 128: row0 + tt * 128 + 128, :] if isp else sbv_s[0:64, :]
                P.dma(lambda e, dst=dst, kvout=kvout: e.dma_start(out=dst, in_=kvout[0:TT, 1, :]), reads=[kk1], writes=[], semkey=kk1)
                if isp:
                    P.op("act", lambda e, bank=bank, tt=tt: e.activation(out=vtok[:, qb * 4 + tt, :], in_=ps[bank][:, :], func=AF.Copy),
                         reads=[bkey(bank)], writes=["vtok"])
                elif not os.environ.get("KNOVNEW"):
                    P.op("act", lambda e, bank=bank: e.activation(out=vnew[0:64, :], in_=ps[bank][0:64, :], func=AF.Copy),
                         reads=[bkey(bank)], writes=["vnew"])
            if float(os.environ.get("KSTOP", 99)) <= 1.8:
                return
            if isp and qb == 0:
                P.op("pool", lambda e: e.memset(cstate[:, :, :], 0.0), writes=["cstate"])
            for c in range(4):
                if os.environ.get("KNOCONV") and not isp:
                    state["slabk"] += 1
                    continue
                sl, slk = get_slab(SLAB_EIN + 3 + c)
                b0 = proj_fm(sl, slk, 0, 128, lambda kc: h[:, kc, 0:U], ["h"], 8, U, 384)
                b1 = proj_fm(sl, slk, 128, 128, lambda kc: h[:, kc, 0:U], ["h"], 8, U, 384)
                b2 = proj_fm(sl, slk, 256, 128, lambda kc: h[:, kc, 0:U], ["h"], 8, U, 384)
                if isp:
                    cv = cin[:, 0:514]
                    cur = cv[:, 2:514]
                    t0, t1_, t2 = cv[:, 0:512], cv[:, 1:513], cv[:, 2:514]
                    tv = tmpf[:, 1, 0:512]
                    g1 = tmpf[:, 0, 0:512]
                    pv = lambda b: ps[b][:, 0:512]
                    P.op("dve", lambda e, c=c: e.tensor_copy(out=cin[:, 0:2], in_=cstate[:, c, :]), reads=["cstate"], writes=["cin"])
                else:
                    cv = cin[:, 0:72].rearrange("p (s t) -> p s t", s=4)
                    cur = cv[:, :, 2:18]
                    t0, t1_, t2 = cv[:, :, 0:16], cv[:, :, 1:17], cv[:, :, 2:18]
                    tv = tmpf[:, 1, 0:64].rearrange("p (s t) -> p s t", s=4)
                    g1 = tmpf[:, 0, 0:64].rearrange("p (s t) -> p s t", s=4)
                    pv = lambda b: ps[b][:, 0:64].rearrange("p (s t) -> p s t", s=4)
                    for s4 in range(4):
                        for j2 in range(2):
                            P.dma(lambda e, c=c, cv=cv, s4=s4, j2=j2: e.dma_start(out=cv[:, s4, j2:j2 + 1], in_=sconv[s4, j2:j2 + 1, c * 128:(c + 1) * 128].rearrange("j p -> p j")),
                                  reads=[], writes=["cinp%d" % (s4 * 2 + j2)], semkey="cinld%d" % (s4 * 2 + j2))
                P.op("act", lambda e, g1=g1, b1=b1, pv=pv: e.activation(out=g1, in_=pv(b1), func=AF.Copy), reads=[bkey(b1)], writes=["tmpf0"])
                P.op("dve", lambda e, cur=cur, g1=g1, b2=b2, pv=pv: e.tensor_tensor(out=cur, in0=pv(b2), in1=g1, op=ALU.mult),
                     reads=[bkey(b2), "tmpf0"], writes=["cin"])
                wi = 64 + c
                cpk = [] if isp else ["cinp%d" % i for i in range(8)]
                P.op("dve", lambda e, tv=tv, t0=t0, wi=wi: e.tensor_scalar(out=tv, in0=t0, scalar1=gcol[:, wi:wi + 1], scalar2=None, op0=ALU.mult), reads=["cin", "gcol"] + cpk, writes=["tmpf1"])
                P.op("dve", lambda e, tv=tv, t1_=t1_, wi=wi: e.scalar_tensor_tensor(out=tv, in0=t1_, scalar=gcol[:, wi + 4:wi + 5], in1=tv, op0=ALU.mult, op1=ALU.add), reads=["cin", "tmpf1", "gcol"] + cpk, writes=["tmpf1"])
                P.op("dve", lambda e, tv=tv, t2=t2, wi=wi: e.scalar_tensor_tensor(out=tv, in0=t2, scalar=gcol[:, wi + 8:wi + 9], in1=tv, op0=ALU.mult, op1=ALU.add), reads=["cin", "tmpf1", "gcol"] + cpk, writes=["tmpf1"])
                if isp:
                    P.op("dve", lambda e, c=c, tv=tv, b0=b0: e.tensor_tensor(out=mix[:, 4 + c, 0:512], in0=tv, in1=ps[b0][:, 0:512], op=ALU.mult), reads=["tmpf1", bkey(b0)], writes=["mix%d" % (4 + c)])
                    P.op("pool", lambda e, c=c: e.tensor_copy(out=cstate[:, c, :], in_=cin[:, 512:514]), reads=["cin"], writes=["cstate"])
                    if qb == 3:
                        for j2 in range(2):
                            P.dma(lambda e, c=c, j2=j2: e.dma_start(out=conv_p[s, j2:j2 + 1, c * 128:(c + 1) * 128].rearrange("j p -> p j"), in_=cin[:, 512 + j2:513 + j2]),
                                  reads=["cin"], writes=[], semkey="convo%d" % j2)
                else:
                    P.op("dve", lambda e, c=c, tv=tv, b0=b0, pv=pv: e.tensor_tensor(out=mix[:, 4 + c, 0:64].rearrange("p (s t) -> p s t", s=4), in0=tv, in1=pv(b0), op=ALU.mult),
                         reads=["tmpf1", bkey(b0)], writes=["mix%d" % (4 + c)])
                    for s4 in range(4):
                        for j2 in range(2):
                            P.dma(lambda e, c=c, cv=cv, s4=s4, j2=j2: e.dma_start(out=conv_s[s4, j2:j2 + 1, c * 128:(c + 1) * 128].rearrange("j p -> p j"), in_=cv[:, s4, 16 + j2:17 + j2]),
                                  reads=["cin"], writes=[], semkey="convo%d" % (s4 * 2 + j2))

            if float(os.environ.get("KSTOP", 99)) <= 2:
                return
            tu = 0
            if isp:
                nt = 4 * qb + 4
                gens = []
                for hh in range(8):
                    hp, m = hh % 2, hh // 2
                    S_t = S_b[hp]
                    Sk = "S%d" % hp
                    pre = lambda S_t=S_t, Sk=Sk: P.op("pool", lambda e: e.memset(S_t[:, 0:512], 0.0), writes=[Sk])
                    post = lambda hp=hp, m=m: P.op("act", lambda e: e.activation(out=mix[hp * 64:(hp + 1) * 64, m, 0:512], in_=ps[7][hp * 64:(hp + 1) * 64, 0:512], func=AF.Copy),
                                                   reads=["acc%d" % hp], writes=["mix%d" % m])
                    for idx, kt in enumerate(range(nt - 1, -1, -1)):
                        j = kt - 4 * qb
                        q0 = 128 * j if j > 0 else 0
                        mask = sbm[:, 0:512 - q0] if j >= 0 else None
                        g = sb_tile(ksT[hp * 64:(hp + 1) * 64, m, kt * 128:(kt + 1) * 128], qT[hp * 64:(hp + 1) * 64, m, q0:512],
                                    vtok[:, kt, hh * 64:(hh + 1) * 64], 128, 512 - q0, S_t[:, q0:512],
                                    ps[7][hp * 64:(hp + 1) * 64, q0:512], "acc%d" % hp, mask, idx == 0, idx == nt - 1, ["ksT", "qT"], tu, Sk)
                        gens.append(wrap(g, pre if idx == 0 else None, post if idx == nt - 1 else None))
                        tu += 1
                run_pipelined(gens)
            else:
                items = []
                heads = [(ss, hh) for ss in range(SPB) for hh in range(8)]

                def prologue(ss, hh, b):
                    hp, m = hh % 2, hh // 2
                    kTs, vh, kbf, qs_t = kTs_b[b], vh_b[b], kbf_b[b], qs_b[b]
                    P.dma(lambda e: e.dma_start(out=kld[:, :, :], in_=csk[ss, :, hh * 64:(hh + 1) * 64].rearrange("(a p) d -> p a d", p=128)),
                          writes=["kld"], semkey="kld")
                    P.dma(lambda e: e.dma_start(out=vld[:, :, :], in_=csv[ss, :, hh * 64:(hh + 1) * 64].rearrange("(a p) d -> p a d", p=128)),
                          writes=["vld"], semkey="vld")
                    P.op("dve", lambda e: e.tensor_copy(out=kbf[:, :, :], in_=kld[:, :, :]), reads=["kld"], writes=["kbf%d" % b])
                    P.op("pool", lambda e: e.tensor_copy(out=vh[:, 0:32, :], in_=vld[:, :, :]), reads=["vld"], writes=["vh%d" % b])
                    for g in range(8):
                        bank = dbank()
                        P.op("pe", lambda e, g=g, bank=bank: [e.transpose(psb[bank][0:64, i * 128:(i + 1) * 128], kbf[:, 4 * g + i, :], identb[:, :]) for i in range(4)][-1],
                             reads=["kbf%d" % b, "identb"], writes=[bkey(bank)])
                        P.op("dve", lambda e, g=g, bank=bank: e.tensor_scalar(out=kTs[0:64, g * 512:(g + 1) * 512], in0=psb[bank][0:64, 0:512], scalar1=SB_SCALE, scalar2=None, op0=ALU.mult),
                             reads=[bkey(bank)], writes=["kTs%d" % b])
                    P.op("dve", lambda e: e.tensor_copy(out=kTs[0:64, 4096:4112], in_=ksn[hp * 64:(hp + 1) * 64, m, ss * 16:(ss + 1) * 16]),
                         reads=["ksn"], writes=["kTs%d" % b])
                    P.op("dve", lambda e: e.tensor_copy(out=qs_t[0:64, 0:16], in_=qT[hp * 64:(hp + 1) * 64, m, ss * 16:(ss + 1) * 16]),
                         reads=["qT"], writes=["qs%d" % b])
                    P.dma(lambda e: e.dma_start(out=vh[0:16, 32, :], in_=vnew[ss * 16:(ss + 1) * 16, hh * 64:(hh + 1) * 64]),
                          reads=["vnew", "vh%d" % b], writes=["vh%d" % b], semkey="vhn")

                for hi, (ss, hh) in enumerate(heads):
                    b = hi % 2
                    hp, m = hh % 2, hh // 2
                    kTs, vh, qs_t = kTs_b[b], vh_b[b], qs_b[b]
                    S_t = S_b[b]
                    Sk = "S%d" % b
                    if hi == 0:
                        items.append(lambda: prologue(heads[0][0], heads[0][1], 0))
                    pre = lambda S_t=S_t, Sk=Sk: P.op("pool", lambda e: e.memset(S_t[:, 0:16], 0.0), writes=[Sk])
                    post = lambda hp=hp, m=m, ss=ss: P.op("act", lambda e: e.activation(out=mix[hp * 64:(hp + 1) * 64, m, ss * 16:(ss + 1) * 16], in_=ps[7][hp * 64:(hp + 1) * 64, 0:16], func=AF.Copy),
                                                          reads=["acc%d" % hp], writes=["mix%d" % m])
                    for idx, kt in enumerate(range(32, -1, -1)):
                        nk = 16 if kt == 32 else 128
                        mask = sbm[0:16, 0:16] if kt == 32 else None
                        k0 = kt * 128
                        g = sb_tile(kTs[0:64, k0:k0 + nk], qs_t[0:64, 0:16], vh[0:nk, kt, :], nk, 16, S_t[:, 0:16],
                                    ps[7][hp * 64:(hp + 1) * 64, 0:16], "acc%d" % hp, mask, idx == 0, idx == 32, ["kTs%d" % b, "qs%d" % b], tu, Sk, "vh%d" % b)
                        items.append(wrap(g, pre if idx == 0 else None, post if idx == 32 else None))
                        tu += 1
                        if idx == 3 and hi + 1 < len(heads):
                            items.append(lambda hi=hi: prologue(heads[hi + 1][0], heads[hi + 1][1], (hi + 1) % 2))
                run_pipelined(items)
            if float(os.environ.get("KSTOP", 99)) <= 3:
                return
            wout_and_post(SLAB_EOUT, 16 + 0, U)
            ffn(0, U)

            if float(os.environ.get("KSTOP", 99)) <= 4:
                return
            pre_norm(8, U)
            sl, slk = get_slab(SLAB_OIN + 0)
            for m in range(4):
                bank = proj_fm(sl, slk, m * 128, 128, lambda kc: h[:, kc, 0:U], ["h"], 8, U, 512)
                P.op("act", lambda e, m=m, bank=bank: e.activation(out=uT[:, m, 0:U], in_=ps[bank][:, 0:U], func=AF.Copy), reads=[bkey(bank)], writes=["uT"])
            sl, slk = get_slab(SLAB_OIN + 1)
            for m in range(3):
                bank = proj_fm(sl, slk, m * 128, 128, lambda kc: h[:, kc, 0:U], ["h"], 8, U, 384)
                P.op("dve", lambda e, m=m, bank=bank: e.tensor_copy(out=o[:, m, 0:U], in_=ps[bank][:, 0:U]), reads=[bkey(bank)], writes=["o%d" % m])
            rms_stats([o[:, m, 0:U] for m in range(3)], ["o0", "o1", "o2"], 384, U)
            for m in range(3):
                P.op("dve", lambda e, m=m: e.scalar_tensor_tensor(out=cqT[:, m, 0:U], in0=o[:, m, 0:U], scalar=gcol[:, 76 + m:77 + m], in1=rstd[:, 0:U], op0=ALU.mult, op1=ALU.mult),
                     reads=["o%d" % m, "rstd", "gcol"], writes=["cqT"])
            sl, slk = get_slab(SLAB_OIN + 2)
            for tt in range(NTT):
                bank = proj_tm(sl, slk, 512, tt, TT, 512)
                sm = small4[:, tt, :]
                k_ = "sm%d_" % tt
                ts = tt % 2
                tk = "tmpf%d" % ts
                P.op("dve", lambda e, bank=bank, sm=sm: e.bn_stats(out=sm[0:TT, 0:6], in_=ps[bank][0:TT, 0:512]), reads=[bkey(bank)], writes=[k_ + "bn"])
                P.op("dve", lambda e, sm=sm: e.bn_aggr(out=sm[0:TT, 6:8], in_=sm[0:TT, 0:6]), reads=[k_ + "bn"], writes=[k_ + "mv"])
                P.op("act", lambda e, sm=sm: e.activation(out=sm[0:TT, 8:9], in_=sm[0:TT, 7:8], func=AF.Ln, bias=EPS), reads=[k_ + "mv"], writes=[k_ + "rv"])
                P.op("act", lambda e, sm=sm: e.activation(out=sm[0:TT, 8:9], in_=sm[0:TT, 8:9], func=AF.Exp, scale=-0.5), reads=[k_ + "rv"], writes=[k_ + "rv"])
                P.op("dve", lambda e, bank=bank, sm=sm, ts=ts: e.tensor_scalar(out=tmpf[0:TT, ts, :], in0=ps[bank][0:TT, 0:512], scalar1=sm[0:TT, 6:7], scalar2=sm[0:TT, 8:9], op0=ALU.subtract, op1=ALU.mult),
                     reads=[bkey(bank), k_ + "mv", k_ + "rv"], writes=[tk])
                P.op("pool", lambda e, ts=ts: e.tensor_tensor(out=tmpf[0:TT, ts, :], in0=tmpf[0:TT, ts, :], in1=lnG[0:TT, :], op=ALU.mult), reads=[tk, "lnG"], writes=[tk])
                if isp:
                    P.op("pool", lambda e, tt=tt, ts=ts: e.tensor_tensor(out=vnb[0:TT, tt, :], in0=tmpf[0:TT, ts, :], in1=lnB[0:TT, :], op=ALU.add), reads=[tk, "lnB"], writes=["vnb"])
                else:
                    P.op("pool", lambda e, ts=ts: e.tensor_tensor(out=vnf[0:TT, :], in0=tmpf[0:TT, ts, :], in1=lnB[0:TT, :], op=ALU.add), reads=[tk, "lnB"], writes=["vnf"])
                    P.op("pool", lambda e, tt=tt: e.tensor_copy(out=vnb[0:TT, tt, :], in_=vnf[0:TT, :]), reads=["vnf"], writes=["vnb"])
                    P.dma(lambda e: e.dma_start(out=sguv_s[0:64, :], in_=vnf[0:64, :]), reads=["vnf"], writes=[], semkey="vnfo")
            for g in range(4):
                bank = dbank()

                def fng(e, g=g, bank=bank):
                    ins = None
                    for tt in range(NTT):
                        rhs = wsT[:, g, :] if isp else wsbd[0:64, g, :]
                        ins = e.matmul(ps[bank][:, tt * TT:(tt + 1) * TT], vnb[0:TT, tt, g * 128:(g + 1) * 128], rhs, start=True, stop=True)
                    return ins
                P.op("pe", fng, reads=["vnb", "wsT", "wsbd"], writes=[bkey(bank)])
                if isp:
                    for tt in range(NTT):
                        P.op("dve", lambda e, g=g, tt=tt, bank=bank: e.tensor_tensor(out=tmpf[:, 1, tt * 128:(tt + 1) * 128], in0=ps[bank][:, tt * 128:(tt + 1) * 128], in1=bsB[:, g * 128:(g + 1) * 128], op=ALU.add),
                             reads=[bkey(bank), "bsB"], writes=["tmpf1"])
                else:
                    P.op("dve", lambda e, g=g, bank=bank: e.tensor_tensor(out=tmpf[:, 1, 0:64], in0=ps[bank][:, 0:64], in1=bsBs[:, g, :], op=ALU.add),
                         reads=[bkey(bank), "bsBs"], writes=["tmpf1"])
                P.op("pool", lambda e, g=g: e.tensor_tensor(out=mix[:, g, 0:U], in0=tmpf[:, 1, 0:U], in1=uT[:, g, 0:U], op=ALU.mult), reads=["tmpf1", "uT"], writes=["mix%d" % g])
            sl, slk = get_slab(SLAB_OIN + 3)
            for tt in range(NTT):
                bank = proj_tm(sl, slk, 288, tt, TT, 288)
                tile_i = qb * 4 + tt
                sm = small4[:, tt, :]
                k_ = "sm%d_" % tt
                ts = tt % 2
                tk = "tmpf%d" % ts
                ckst = ckst_b[ts]
                ck = "ckst%d" % ts
                kpb = kpb_b[ts]
                kk = "kpb%d" % ts
                P.op("act", lambda e, bank=bank, sm=sm, ts=ts: e.activation(out=tmpf[0:TT, ts, 0:256], in_=ps[bank][0:TT, 0:256], func=AF.Square, accum_out=sm[0:TT, 10:11]),
                     reads=[bkey(bank)], writes=[tk, k_ + "ss"])
                P.op("act", lambda e, sm=sm: e.activation(out=sm[0:TT, 11:12], in_=sm[0:TT, 10:11], func=AF.Ln, scale=1.0 / 256, bias=EPS), reads=[k_ + "ss"], writes=[k_ + "rk"])
                P.op("act", lambda e, sm=sm: e.activation(out=sm[0:TT, 11:12], in_=sm[0:TT, 11:12], func=AF.Exp, scale=-0.5), reads=[k_ + "rk"], writes=[k_ + "rk"])
                P.op("dve", lambda e, bank=bank, sm=sm, ckst=ckst: e.scalar_tensor_tensor(out=ckst[0:TT, 0:256], in0=ps[bank][0:TT, 0:256], scalar=sm[0:TT, 11:12], in1=kvG[0:TT, :], op0=ALU.mult, op1=ALU.mult),
                     reads=[bkey(bank), k_ + "rk", "kvG"], writes=[ck])
                if isp:
                    cosv = ropeT[0:TT, tile_i * 16:(tile_i + 1) * 16]
                    sinv = ropeT[0:TT, 256 + tile_i * 16:256 + (tile_i + 1) * 16]
                else:
                    cosv = ropeTs[0:TT, 0:16]
                    sinv = ropeTs[0:TT, 16:32]
                x1 = ps[bank][0:TT, 256:272]
                x2 = ps[bank][0:TT, 272:288]
                t16 = sm[0:TT, 16:32]
                P.op("dve", lambda e, x1=x1, cosv=cosv, ckst=ckst: e.tensor_tensor(out=ckst[0:TT, 256:272], in0=x1, in1=cosv, op=ALU.mult), reads=[bkey(bank), "ropeT", ck], writes=[ck])
                P.op("dve", lambda e, x2=x2, sinv=sinv, t16=t16: e.tensor_tensor(out=t16, in0=x2, in1=sinv, op=ALU.mult), reads=[bkey(bank), "ropeT"], writes=[k_ + "t16"])
                P.op("dve", lambda e, ckst=ckst, t16=t16: e.tensor_tensor(out=ckst[0:TT, 256:272], in0=ckst[0:TT, 256:272], in1=t16, op=ALU.subtract), reads=[ck, k_ + "t16"], writes=[ck])
                P.op("dve", lambda e, x1=x1, sinv=sinv, ckst=ckst: e.tensor_tensor(out=ckst[0:TT, 272:288], in0=x1, in1=sinv, op=ALU.mult), reads=[bkey(bank), "ropeT", ck], writes=[ck])
                P.op("dve", lambda e, x2=x2, cosv=cosv, t16=t16: e.tensor_tensor(out=t16, in0=x2, in1=cosv, op=ALU.mult), reads=[bkey(bank), "ropeT", k_ + "t16"], writes=[k_ + "t16"])
                P.op("dve", lambda e, ckst=ckst, t16=t16: e.tensor_tensor(out=ckst[0:TT, 272:288], in0=ckst[0:TT, 272:288], in1=t16, op=ALU.add), reads=[ck, k_ + "t16"], writes=[ck])
                if isp:
                    r0 = row0 + tt * 128
                    P.dma(lambda e, r0=r0, ckst=ckst: e.dma_start(out=ckv_p[s, r0:r0 + 128, :], in_=ckst[0:128, 0:256]), reads=[ck], writes=[], semkey="ckvo%d" % ts)
                    P.dma(lambda e, r0=r0, ckst=ckst: e.dma_start(out=kpe_p[s, r0:r0 + 128, :], in_=ckst[0:128, 256:288]), reads=[ck], writes=[], semkey="kpeo%d" % ts)
                    ctk = ckvtok[:, tile_i, :]
                else:
                    P.dma(lambda e, ckst=ckst: e.dma_start(out=ckv_s[0:64, :], in_=ckst[0:64, 0:256]), reads=[ck], writes=[], semkey="ckvo%d" % ts)
                    P.dma(lambda e, ckst=ckst: e.dma_start(out=kpe_s[0:64, :], in_=ckst[0:64, 256:288]), reads=[ck], writes=[], semkey="kpeo%d" % ts)
                    ctk = ckvn[:, :]
                P.op("pool", lambda e, ctk=ctk, ckst=ckst: e.tensor_copy(out=ctk[0:TT, :], in_=ckst[0:TT, 0:256]), reads=[ck], writes=["ckvtok"])
                P.op("pool", lambda e, ckst=ckst, kpb=kpb: e.tensor_copy(out=kpb[0:TT, :], in_=ckst[0:TT, 256:288]), reads=[ck], writes=[kk])
                bank2 = dbank()

                def fnt(e, ctk=ctk, bank2=bank2, kpb=kpb):
                    e.transpose(psb[bank2][:, 0:TT], ctk[0:TT, 0:128], identb[0:TT, 0:TT])
                    e.transpose(psb[bank2][:, 128:128 + TT], ctk[0:TT, 128:256], identb[0:TT, 0:TT])
                    return e.transpose(psb[bank2][0:32, 256:256 + TT], kpb[0:TT, 0:32], identb[0:TT, 0:TT])
                P.op("pe", fnt, reads=["ckvtok", kk, "identb"], writes=[bkey(bank2)])
                if isp:
                    c0 = tile_i * 128
                    P.op("act", lambda e, c0=c0, bank2=bank2: e.activation(out=ckvT[:, :, c0:c0 + 128], in_=psb[bank2][:, 0:256].rearrange("p (a t) -> p a t", a=2), func=AF.Copy),
                         reads=[bkey(bank2)], writes=["ckvT"])
                    P.op("dve", lambda e, c0=c0, bank2=bank2: e.tensor_copy(out=kpeT[0:32, c0:c0 + 128], in_=psb[bank2][0:32, 256:384]), reads=[bkey(bank2)], writes=["kpeT"])
                else:
                    P.op("act", lambda e, bank2=bank2: e.activation(out=sq[:, 0, 0:64], in_=psb[bank2][:, 0:64], func=AF.Copy), reads=[bkey(bank2)], writes=["sq0"])
                    P.op("act", lambda e, bank2=bank2: e.activation(out=sq[:, 0, 64:128], in_=psb[bank2][:, 128:192], func=AF.Copy), reads=[bkey(bank2)], writes=["sq0"])
                    P.op("act", lambda e, bank2=bank2: e.activation(out=sq[0:32, 0, 128:192], in_=psb[bank2][0:32, 256:320], func=AF.Copy), reads=[bkey(bank2)], writes=["sq0"])
            if float(os.environ.get("KSTOP", 99)) <= 5:
                return
            for m in range(4):
                bank = proj_fm(wuqn[:, :, :].rearrange("p k n -> p (k n)"), "wuqn", m * 128, 128, lambda kc: cqT[:, kc, 0:U], ["cqT"], 3, U, 512)
                P.op("act", lambda e, m=m, bank=bank: e.activation(out=qnT[:, m, 0:U], in_=ps[bank][:, 0:U], func=AF.Copy), reads=[bkey(bank)], writes=["qnT"])
            if isp:
                P.dma(lambda e: e.dma_start(out=cosF[0:32, 0:512], in_=rope_f[0, :, row0:row0 + 512]), writes=["cosF"], semkey="cosF")
                P.dma(lambda e: e.dma_start(out=sinF[0:32, 0:512], in_=rope_f[1, :, row0:row0 + 512]), writes=["sinF"], semkey="sinF")
            else:
                P.dma(lambda e: e.dma_start(out=cosF[0:32, 0:64], in_=rope_fs[0, :, :]), writes=["cosF"], semkey="cosF")
                P.dma(lambda e: e.dma_start(out=sinF[0:32, 0:64], in_=rope_fs[1, :, :]), writes=["sinF"], semkey="sinF")

            def head_q(hh, q0, q1, qb_=0):
                hp, m = hh % 2, hh // 2
                qlat = qlat_b[qb_]
                qpe = qpe_b[qb_]
                for cc in range(2):
                    bank = dbank()
                    P.op("pe", lambda e, cc=cc, bank=bank: e.matmul(ps[bank][:, q0:q1], wukp[hp * 64:(hp + 1) * 64, m, cc * 128:(cc + 1) * 128], qnT[hp * 64:(hp + 1) * 64, m, q0:q1], start=True, stop=True),
                         reads=["wukp", "qnT"], writes=[bkey(bank)])
                    if cc == 0:
                        P.op("act", lambda e, bank=bank: e.activation(out=qlat[:, 0, q0:q1], in_=ps[bank][:, q0:q1], func=AF.Copy), reads=[bkey(bank)], writes=["qlat%d" % qb_])
                    else:
                        P.op("dve", lambda e, bank=bank: e.tensor_copy(out=qlat[:, 1, q0:q1], in_=ps[bank][:, q0:q1]), reads=[bkey(bank)], writes=["qlat%d" % qb_])
                b1 = dbank()
                b2 = dbank()

                def fq(e, b1=b1, b2=b2):
                    ins = None
                    for kc in range(3):
                        e.matmul(ps[b1][0:32, q0:q1], wuqp[:, kc, hh * 32:(hh + 1) * 32], cqT[:, kc, q0:q1], start=(kc == 0), stop=(kc == 2))
                    for kc in range(3):
                        ins = e.matmul(ps[b2][0:32, q0:q1], wuqs[:, kc, hh * 32:(hh + 1) * 32], cqT[:, kc, q0:q1], start=(kc == 0), stop=(kc == 2))
                    return ins
                P.op("pe", fq, reads=["wuqp", "wuqs", "cqT"], writes=[bkey(b1), bkey(b2)])
                P.op("dve", lambda e, b1=b1: e.tensor_tensor(out=tmpf[0:32, 0, q0:q1], in0=ps[b1][0:32, q0:q1], in1=cosF[0:32, q0:q1], op=ALU.mult), reads=[bkey(b1), "cosF"], writes=["tmpf0"])
                P.op("dve", lambda e, b2=b2: e.tensor_tensor(out=tmpf[0:32, 1, q0:q1], in0=ps[b2][0:32, q0:q1], in1=sinF[0:32, q0:q1], op=ALU.mult), reads=[bkey(b2), "sinF"], writes=["tmpf1"])
                P.op("pool", lambda e: e.tensor_tensor(out=qpe[0:32, q0:q1], in0=tmpf[0:32, 0, q0:q1], in1=tmpf[0:32, 1, q0:q1], op=ALU.add), reads=["tmpf0", "tmpf1"], writes=["qpe%d" % qb_])

            def head_out(hh, q0, q1):
                hp, m = hh % 2, hh // 2
                P.op("act", lambda e: e.activation(out=OLs[:, 0, q0:q1], in_=ps[5][:, q0:q1], func=AF.Copy), reads=[bkey(5)], writes=["OLs0"])
                P.op("dve", lambda e: e.tensor_copy(out=OLs[:, 1, q0:q1], in_=ps[6][:, q0:q1]), reads=[bkey(6)], writes=["OLs1"])
                P.op("dve", lambda e: e.reciprocal(out=rden[hp * 64:(hp + 1) * 64, q0:q1], in_=ps[7][hp * 64:(hp + 1) * 64, q0:q1]), reads=[bkey(7)], writes=["rden"])
                bank = dbank()

                def fo(e, bank=bank):
                    e.matmul(ps[bank][hp * 64:(hp + 1) * 64, q0:q1], wuvb[:, hh, 0, :], OLs[:, 0, q0:q1], start=True, stop=False)
                    return e.matmul(ps[bank][hp * 64:(hp + 1) * 64, q0:q1], wuvb[:, hh, 1, :], OLs[:, 1, q0:q1], start=False, stop=True)
                P.op("pe", fo, reads=["wuvb", "OLs0", "OLs1"], writes=[bkey(bank)])
                P.op("dve", lambda e, bank=bank: e.tensor_tensor(out=mix[hp * 64:(hp + 1) * 64, 4 + m, q0:q1], in0=ps[bank][hp * 64:(hp + 1) * 64, q0:q1], in1=rden[hp * 64:(hp + 1) * 64, q0:q1], op=ALU.mult),
                     reads=[bkey(bank), "rden"], writes=["mix%d" % (4 + m)])

            tu = 0
            if isp:
                nt = 4 * qb + 4
                head_q(0, 0, 512, 0)
                for hh in range(8):
                    gens = []
                    for kt in range(nt):
                        j = kt - 4 * qb
                        q0 = 128 * j if j > 0 else 0
                        mask = mlm[:, 0:512 - q0] if j >= 0 else None
                        gens.append(mla_tile(ckvT[:, 0, kt * 128:(kt + 1) * 128], ckvT[:, 1, kt * 128:(kt + 1) * 128], kpeT[0:32, kt * 128:(kt + 1) * 128],
                                             ckvtok[:, kt, :], 128, q0, 512, mask, kt == 0, kt == nt - 1, hh % 2, tu, hh % 2))
                        tu += 1
                    hook = (lambda hh=hh: head_q(hh + 1, 0, 512, (hh + 1) % 2)) if hh < 7 else None
                    run_pipelined(gens, 1, hook)
                    head_out(hh, 0, 512)
            else:
                for ss in range(int(os.environ.get("KLIM_MLA", SPB))):
                    for qtr in range(4):
                        P.dma(lambda e, ss=ss, qtr=qtr: e.dma_start(out=cstg[:, 0:2048].rearrange("p (a d) -> p a d", a=8), in_=cckv[ss, qtr * 1024:(qtr + 1) * 1024, :].rearrange("(a p) d -> p a d", p=128)),
                              writes=["cstg"], semkey="cstg")
                        P.op("dve", lambda e, qtr=qtr: e.tensor_copy(out=ckvtok_s[:, qtr * 8:(qtr + 1) * 8, :], in_=cstg[:, 0:2048].rearrange("p (a d) -> p a d", a=8)), reads=["cstg"], writes=["ckvtok"])
                    P.dma(lambda e, ss=ss: e.dma_start(out=cstg[:, 0:1024].rearrange("p (a d) -> p a d", a=32), in_=ckpe[ss, :, :].rearrange("(a p) d -> p a d", p=128)),
                          writes=["cstg"], semkey="cstg")
                    P.op("dve", lambda e: e.tensor_copy(out=kpbs[:, :, :], in_=cstg[:, 0:1024].rearrange("p (a d) -> p a d", a=32)), reads=["cstg"], writes=["kpbs"])
                    for kt in range(32):
                        for cc in range(2):
                            if (kt * 2 + cc) % 4 == 0:
                                bank = dbank()
                            i4 = (kt * 2 + cc) % 4
                            P.op("pe", lambda e, kt=kt, cc=cc, bank=bank, i4=i4: e.transpose(psb[bank][:, i4 * 128:(i4 + 1) * 128], ckvtok_s[:, kt, cc * 128:(cc + 1) * 128], identb[:, :]),
                                 reads=["ckvtok", "identb"], writes=[bkey(bank)])
                            P.op("act", lambda e, kt=kt, cc=cc, bank=bank, i4=i4: e.activation(out=ckvT_s[:, cc, kt * 128:(kt + 1) * 128], in_=psb[bank][:, i4 * 128:(i4 + 1) * 128], func=AF.Copy),
                                 reads=[bkey(bank)], writes=["ckvT"])
                    for g in range(8):
                        bank = dbank()
                        P.op("pe", lambda e, g=g, bank=bank: [e.transpose(psb[bank][0:32, i * 128:(i + 1) * 128], kpbs[:, 4 * g + i, :], identb[:, :]) for i in range(4)][-1],
                             reads=["kpbs", "identb"], writes=[bkey(bank)])
                        P.op("dve", lambda e, g=g, bank=bank: e.tensor_copy(out=kpeT_s[0:32, g * 512:(g + 1) * 512], in_=psb[bank][0:32, 0:512]), reads=[bkey(bank)], writes=["kpeT"])
                    P.dma(lambda e, ss=ss: e.dma_start(out=ckvtok_s[0:16, 32, :], in_=ckvn[ss * 16:(ss + 1) * 16, :]), reads=["ckvtok"], writes=["ckvtok"], semkey="ckvnn")
                    P.op("act", lambda e, ss=ss: e.activation(out=ckvT_s[:, 0, 4096:4112], in_=sq[:, 0, ss * 16:(ss + 1) * 16], func=AF.Copy), reads=["sq0"], writes=["ckvT"])
                    P.op("act", lambda e, ss=ss: e.activation(out=ckvT_s[:, 1, 4096:4112], in_=sq[:, 0, 64 + ss * 16:64 + (ss + 1) * 16], func=AF.Copy), reads=["sq0"], writes=["ckvT"])
                    P.op("act", lambda e, ss=ss: e.activation(out=kpeT_s[0:32, 4096:4112], in_=sq[0:32, 0, 128 + ss * 16:128 + (ss + 1) * 16], func=AF.Copy), reads=["sq0"], writes=["kpeT"])
                    q0, q1 = ss * 16, ss * 16 + 16
                    for hh in range(8):
                        if hh == 0:
                            head_q(0, q0, q1, 0)
                        gens = []
                        for kt in range(33):
                            nk = 16 if kt == 32 else 128
                            k0 = kt * 128
                            gens.append(mla_tile(ckvT_s[:, 0, k0:k0 + nk], ckvT_s[:, 1, k0:k0 + nk], kpeT_s[0:32, k0:k0 + nk],
                                                 ckvtok_s[0:nk, kt, :], nk, q0, q1, None, kt == 0, kt == 32, hh % 2, tu, hh % 2))
                            tu += 1
                        hook = (lambda hh=hh, q0=q0, q1=q1: head_q(hh + 1, q0, q1, (hh + 1) % 2)) if hh < 7 else None
                        run_pipelined(gens, 1, hook)
                        head_out(hh, q0, q1)

            if float(os.environ.get("KSTOP", 99)) <= 6:
                return
            wout_and_post(SLAB_OOUT, 16 + 8, U)
            ffn(1, U)

            if float(os.environ.get("KSTOP", 99)) <= 7:
                return
            for tt in range(NTT):
                yout = io_b[tt % 2]
                yk = "io%d" % (tt % 2)
                for half in range(2):
                    bank = dbank()
                    P.op("pe", lambda e, half=half, bank=bank, tt=tt: [e.transpose(ps[bank][0:TT, c4 * 128:(c4 + 1) * 128], xT[:, half * 4 + c4, tt * TT:(tt + 1) * TT], identf[:, :]) for c4 in range(4)][-1],
                         reads=["xT%d" % (half * 4 + c4) for c4 in range(4)] + ["identf"], writes=[bkey(bank)])
                    P.op("act", lambda e, half=half, bank=bank, yout=yout: e.activation(out=yout[0:TT, half * 512:(half + 1) * 512], in_=ps[bank][0:TT, :], func=AF.Copy),
                         reads=[bkey(bank)], writes=[yk])
                dst = y_p[s, row0 + tt * 128: row0 + tt * 128 + 128, :] if isp else y_s[0:64, :]
                P.dma(lambda e, dst=dst, yout=yout: e.dma_start(out=dst, in_=yout[0:TT, :]), reads=[yk], writes=[], semkey=yk)
            barrier()

        setup()
        barrier()
        for (kind, s, qb) in units:
            unit(kind, s, qb)
        P.emit(st)
    return nc


def _consts():
    kl = np.arange(128)[:, None]
    x = np.arange(512)[None, :]
    cst = np.zeros((128, 1408), np.float32)
    cst[:, 0:128] = np.eye(128, dtype=np.float32)
    jj = np.arange(128)[:, None]; kk = np.arange(128)[None, :]
    cst[:, 128:256] = -(jj >= kk).astype(np.float32)
    cst[:, 256:768] = (kl < x).astype(np.float32)
    cst[:, 768:1280] = ((kl // 64) <= (x // 64)).astype(np.float32)
    tt = np.arange(128)[:, None]; ss = np.arange(128)[None, :]
    cst[:, 1280:1408] = (ss <= tt).astype(np.float32)
    half = 16
    inv = (10000.0 ** (-(np.arange(half, dtype=np.float32) / np.float32(half)))).astype(np.float32)

    def cs(pos):
        ang = (pos.astype(np.float32)[:, None] * inv[None, :]).astype(np.float32)
        return np.cos(ang.astype(np.float64)).astype(np.float32), np.sin(ang.astype(np.float64)).astype(np.float32)
    cp, sp_ = cs(np.arange(T))
    rope_t = np.zeros((128, 512), np.float32)
    rope_t[:, 0:256] = cp.reshape(16, 128, 16).transpose(1, 0, 2).reshape(128, 256)
    rope_t[:, 256:512] = sp_.reshape(16, 128, 16).transpose(1, 0, 2).reshape(128, 256)
    cs_, ss_ = cs(PAST + np.arange(DS))
    rope_ts = np.zeros((64, 32), np.float32)
    rope_ts[:, 0:16] = np.tile(cs_, (4, 1))
    rope_ts[:, 16:32] = np.tile(ss_, (4, 1))
    rope_f = np.zeros((2, 32, T), np.float32)
    rope_f[0, 0:16] = cp.T; rope_f[0, 16:32] = cp.T
    rope_f[1, 0:16] = -sp_.T; rope_f[1, 16:32] = sp_.T
    rope_fs = np.zeros((2, 32, 64), np.float32)
    rope_fs[0, 0:16] = np.tile(cs_.T, (1, 4)); rope_fs[0, 16:32] = np.tile(cs_.T, (1, 4))
    rope_fs[1, 0:16] = -np.tile(ss_.T, (1, 4)); rope_fs[1, 16:32] = np.tile(ss_.T, (1, 4))
    return cst, rope_t, rope_ts, rope_f, rope_fs


_CACHE = {}


def kernel(x_prompt, x_sample, cache_sb_k, cache_sb_v, state_conv, cache_mla_ckv, cache_mla_kpe,
           mix_pre_g, mix_post_g, ffn_pre_g, ffn_post_g, even_w_in, even_w_conv, even_w_out,
           odd_w_in, sgu_ln_g, sgu_ln_b, sgu_w_s, sgu_b_s, mla_q_norm_g, mla_kv_norm_g,
           mla_w_uq, mla_w_uk, mla_w_uv, odd_w_out, ffn_w_up, ffn_w_down):
    f = lambda a: np.ascontiguousarray(np.asarray(a, dtype=np.float32))
    dbg = os.environ.get("KDEBUG", "")
    if dbg:
        units = []
        for tok in dbg.split(","):
            if tok == "s":
                units.append(("s", 0, 0))
            else:
                a, b = tok.split(":")
                units.append(("p", int(a), int(b)))
    else:
        units = [("p", s, qb) for s in range(SPB) for qb in range(4)] + [("s", 0, 0)]
    key = tuple(units)
    if key not in _CACHE:
        _CACHE[key] = build_program(units)
    nc = _CACHE[key]
    cst, rope_t, rope_ts, rope_f, rope_fs = _consts()
    grow = np.concatenate([f(mix_pre_g).reshape(16, 128), f(mix_post_g).reshape(16, 128), f(ffn_pre_g).reshape(16, 128),
                           f(ffn_post_g).reshape(16, 128), f(even_w_conv).reshape(12, 128), f(mla_q_norm_g).reshape(3, 128)], axis=0)
    shared = dict(
        ein=f(even_w_in)[0], eout=f(even_w_out)[0], oin=f(odd_w_in)[0], oout=f(odd_w_out)[0],
        fup=f(ffn_w_up), fdn=f(ffn_w_down), grow=f(grow),
        lng=f(sgu_ln_g).reshape(1, 512), lnb=f(sgu_ln_b).reshape(1, 512), kvg=f(mla_kv_norm_g).reshape(1, 256),
        bsd=f(sgu_b_s).reshape(1, 512), bss=f(f(sgu_b_s)[0][:, 0:16]).reshape(1, 4, 16),
        wsd=f(sgu_w_s)[0], wuq=f(mla_w_uq)[0], wuk=f(mla_w_uk)[0], wuv=f(mla_w_uv)[0],
        cst=cst, rope_t=rope_t, rope_ts=rope_ts, rope_f=rope_f, rope_fs=rope_fs)
    xp = f(x_prompt); xs = f(x_sample)
    csk = f(cache_sb_k)[0].reshape(32, PAST, 512); csv = f(cache_sb_v)[0].reshape(32, PAST, 512)
    sc = f(state_conv)[0]; cckv = f(cache_mla_ckv)[0]; ckpe = f(cache_mla_kpe)[0]
    in_maps = []
    for c in range(NCORES):
        sl = slice(c * SPB, (c + 1) * SPB)
        m = dict(shared)
        m.update(xp=xp[sl], xs=f(xs[sl].reshape(SPB * DS, D)), csk=csk[sl], csv=csv[sl], sconv=sc[sl], cckv=cckv[sl], ckpe=ckpe[sl])
        in_maps.append(m)
    res = run_bass_kernel_spmd(nc, in_maps, core_ids=list(range(NCORES)))
    R = res.results
    cat = lambda name: np.concatenate([np.asarray(R[c][name], dtype=np.float32) for c in range(NCORES)], axis=0)
    y_p = cat("y_p")
    y_s = cat("y_s").reshape(32, DS, D)
    sbk_p = cat("sbk_p").reshape(1, 32, T, 8, 64)
    sbv_p = cat("sbv_p").reshape(1, 32, T, 8, 64)
    conv_p = cat("conv_p").reshape(1, 32, 2, 512)
    ckv_p = cat("ckv_p").reshape(1, 32, T, 256)
    kpe_p = cat("kpe_p").reshape(1, 32, T, 32)
    sbk_s = cat("sbk_s").reshape(1, 32, DS, 8, 64)
    sbv_s = cat("sbv_s").reshape(1, 32, DS, 8, 64)
    conv_s = cat("conv_s").reshape(1, 32, 2, 512)
    ckv_s = cat("ckv_s").reshape(1, 32, DS, 256)
    kpe_s = cat("kpe_s").reshape(1, 32, DS, 32)
    sguv_s = cat("sguv_s").reshape(1, 32, DS, 512)
    return (y_p, y_s, sbk_p, sbv_p, conv_p, ckv_p, kpe_p, sbk_s, sbv_s, conv_s, ckv_s, kpe_s, sguv_s)
```

```python
import os
import math
import numpy as np
from contextlib import ExitStack
import concourse.bass as bass
import concourse.mybir as mybir
from concourse.bass_utils import run_bass_kernel_spmd

F32 = mybir.dt.float32
BF16 = mybir.dt.bfloat16
AF = mybir.ActivationFunctionType
ALU = mybir.AluOpType

NCORES = 8
SPB = 4
T = 2048
D = 1024
PAST = 4096
DS = 16
EPS = 1e-6
SB_SCALE = 0.125
MLA_SCALE = 1.0 / math.sqrt(96.0)

COMPUTE = ("pe", "act", "dve", "pool")
ALLENG = ("pe", "act", "dve", "pool", "sp")
SAME_ENGINE_SYNC = True


class Prog:
    def __init__(self, nc):
        self.nc = nc
        self.ops = []

    def op(self, eng, fn, reads=(), writes=()):
        self.ops.append(dict(eng=eng, fn=fn, reads=tuple(reads), writes=tuple(writes), dma=False, bar=False))

    def dma(self, fn, reads=(), writes=(), semkey=None, queue="sp"):
        self.ops.append(dict(eng=queue, fn=fn, reads=tuple(reads), writes=tuple(writes), dma=True, bar=False,
                             semkey=semkey))

    def barrier(self, fn):
        self.ops.append(dict(eng="dve", fn=fn, reads=(), writes=(), dma=False, bar=True))

    def resolve(self):
        ops = self.ops
        last_w = {}
        readers = {}
        last_dma_by_sem = {}
        eng_pos = {}
        last_on_eng = {}
        pending = {}
        for i, o in enumerate(ops):
            e = o["eng"]
            o["pos"] = eng_pos.get(e, 0)
            eng_pos[e] = o["pos"] + 1
            deps = set()
            raw = set()
            if o["bar"]:
                for e2, j in last_on_eng.items():
                    deps.add(j)
                for sk, j in last_dma_by_sem.items():
                    deps.add(j)
                raw = set(deps)
                for e2 in ALLENG:
                    pending[e2] = i
                pending.pop("dve", None)
            else:
                if e in pending:
                    deps.add(pending.pop(e))
                for k in o["reads"]:
                    if k in last_w:
                        deps.add(last_w[k]); raw.add(last_w[k])
                    if k.startswith("ps") or k.startswith("acc"):
                        for e2, j in readers.get(k, {}).items():
                            if e2 != e and not isinstance(j, list):
                                deps.add(j)
                for k in o["writes"]:
                    if k in last_w:
                        deps.add(last_w[k])
                    for j in readers.get(k, {}).values():
                        if isinstance(j, list):
                            deps.update(j)
                        else:
                            deps.add(j)
                if o["dma"]:
                    sk = o["semkey"]
                    if sk in last_dma_by_sem:
                        deps.add(last_dma_by_sem[sk])
                    last_dma_by_sem[sk] = i
            deps.discard(i)
            o["deps"] = deps
            o["raw"] = raw
            if not o["dma"]:
                last_on_eng[e] = i
            for k in o["reads"]:
                r = readers.setdefault(k, {})
                if o["dma"]:
                    r.setdefault("dma", []).append(i)
                else:
                    r[e] = i
            for k in o["writes"]:
                last_w[k] = i
                readers[k] = {}
        waited = {}
        waited_dma = {}
        dma_count = {}
        for i, o in enumerate(ops):
            if o["dma"]:
                sk = o["semkey"]
                dma_count[sk] = dma_count.get(sk, 0) + 1
                o["dma_n"] = dma_count[sk]
            o["needs_inc"] = False
        for i, o in enumerate(ops):
            e = o["eng"]
            w = waited.setdefault(e, {})
            wd = waited_dma.setdefault(e, {})
            waits = []
            for j in sorted(o["deps"]):
                p = ops[j]
                if p["dma"]:
                    sk = p["semkey"]
                    if wd.get(sk, 0) >= p["dma_n"]:
                        continue
                    wd[sk] = p["dma_n"]
                    waits.append(("d", sk, p["dma_n"]))
                else:
                    pe_ = p["eng"]
                    if pe_ == e:
                        if e == "pe" or e == "sp":
                            continue
                        if (not SAME_ENGINE_SYNC) or (j not in o["raw"]):
                            continue
                    if w.get(pe_, -1) >= p["pos"]:
                        continue
                    w[pe_] = p["pos"]
                    p["needs_inc"] = True
                    waits.append(("c", pe_, j))
            o["waits"] = waits
        last_idx = {}
        for i, o in enumerate(ops):
            if not o["dma"]:
                last_idx[o["eng"]] = i
        for e_, i in last_idx.items():
            if e_ in COMPUTE:
                ops[i]["needs_inc"] = True
        cnt = {}
        for o in ops:
            if o["dma"]:
                continue
            e = o["eng"]
            if o["needs_inc"]:
                cnt[e] = cnt.get(e, 0) + 1
            o["count"] = cnt.get(e, 0)
        self.sem_keys = sorted({o["semkey"] for o in ops if o["dma"]}, key=str)
        self.final_dma = dict(dma_count)
        return cnt

    def emit(self, stack):
        nc = self.nc
        ops = self.ops
        cnt = self.resolve()
        esem = {e: stack.enter_context(nc.semaphore("s_" + e)) for e in COMPUTE}
        dsem = {k: stack.enter_context(nc.semaphore("d_%d" % i)) for i, k in enumerate(self.sem_keys)}
        block = stack.enter_context(nc.Block())
        by_eng = {}
        for o in ops:
            by_eng.setdefault(o["eng"], []).append(o)

        def run(engname, eng):
            for o in by_eng.get(engname, []):
                for wt in o["waits"]:
                    if wt[0] == "d":
                        eng.wait_ge(dsem[wt[1]], 16 * wt[2])
                    else:
                        eng.wait_ge(esem[wt[1]], ops[wt[2]]["count"])
                ins = o["fn"](eng)
                if o["dma"]:
                    ins.then_inc(dsem[o["semkey"]], 16)
                elif o["needs_inc"]:
                    ins.then_inc(esem[engname], 1)
            if engname == "sp":
                for k, n in self.final_dma.items():
                    eng.wait_ge(dsem[k], 16 * n)
                for e2 in COMPUTE:
                    if cnt.get(e2, 0) > 0:
                        eng.wait_ge(esem[e2], cnt[e2])

        @block.sync
        def _(sync):
            run("sp", sync)

        @block.tensor
        def _(tensor):
            run("pe", tensor)

        @block.scalar
        def _(scalar):
            run("act", scalar)

        @block.vector
        def _(vector):
            run("dve", vector)

        @block.gpsimd
        def _(gpsimd):
            run("pool", gpsimd)


SLAB_EIN = 0
SLAB_EOUT = 7
SLAB_FUP0 = 9
SLAB_FDN0 = 17
SLAB_OIN = 25
SLAB_OOUT = 29
SLAB_FUP1 = 31
SLAB_FDN1 = 39
NSLAB = 47


def build_program(units):
    nc = bass.Bass("TRN2", target_bir_lowering=False)

    def din(name, shape):
        return nc.dram_tensor(name, list(shape), F32, kind="ExternalInput").ap()

    def dout(name, shape):
        return nc.dram_tensor(name, list(shape), F32, kind="ExternalOutput").ap()

    xp = din("xp", [SPB, T, D]); xs = din("xs", [SPB * DS, D])
    csk = din("csk", [SPB, PAST, 512]); csv = din("csv", [SPB, PAST, 512])
    sconv = din("sconv", [SPB, 2, 512])
    cckv = din("cckv", [SPB, PAST, 256]); ckpe = din("ckpe", [SPB, PAST, 32])
    ein = din("ein", [D, 3072]); eout = din("eout", [D, D]); oin = din("oin", [D, 1696]); oout = din("oout", [D, D])
    fup = din("fup", [2, D, 4096]); fdn = din("fdn", [2, 4096, D])
    grow = din("grow", [79, 128])
    lng = din("lng", [1, 512]); lnb = din("lnb", [1, 512]); kvg = din("kvg", [1, 256]); bsd = din("bsd", [1, 512])
    bss = din("bss", [1, 4, 16])
    wsd = din("wsd", [4, 128, 128])
    wuq = din("wuq", [384, 768]); wuk = din("wuk", [8, 64, 256]); wuv = din("wuv", [8, 256, 64])
    cst = din("cst", [128, 1408])
    rope_t = din("rope_t", [128, 512]); rope_ts = din("rope_ts", [64, 32])
    rope_f = din("rope_f", [2, 32, T]); rope_fs = din("rope_fs", [2, 32, 64])

    y_p = dout("y_p", [SPB, T, D]); y_s = dout("y_s", [SPB * DS, D])
    sbk_p = dout("sbk_p", [SPB, T, 512]); sbv_p = dout("sbv_p", [SPB, T, 512])
    conv_p = dout("conv_p", [SPB, 2, 512])
    ckv_p = dout("ckv_p", [SPB, T, 256]); kpe_p = dout("kpe_p", [SPB, T, 32])
    sbk_s = dout("sbk_s", [SPB * DS, 512]); sbv_s = dout("sbv_s", [SPB * DS, 512])
    conv_s = dout("conv_s", [SPB, 2, 512])
    ckv_s = dout("ckv_s", [SPB * DS, 256]); kpe_s = dout("kpe_s", [SPB * DS, 32])
    sguv_s = dout("sguv_s", [SPB * DS, 512])

    wsc = nc.dram_tensor("wsc", [NSLAB, 128, 4096], BF16, kind="Internal").ap()

    P = Prog(nc)
    with ExitStack() as st:
        def sb(name, shape, dt):
            return st.enter_context(nc.sbuf_tensor(name, list(shape), dt))

        xT = sb("xT", [128, 8, 512], F32)
        identf = sb("identf", [128, 128], F32)
        identb = sb("identb", [128, 128], BF16)
        ntri = sb("ntri", [128, 128], BF16)
        nones = sb("nones", [128, 128], BF16)
        onesb = sb("onesb", [128, 128], BF16)
        sbm = sb("sbm", [128, 512], BF16)
        mlm = sb("mlm", [128, 512], BF16)
        gcol = sb("gcol", [128, 80], F32)
        lnG = sb("lnG", [128, 512], F32); lnB = sb("lnB", [128, 512], F32)
        kvG = sb("kvG", [128, 256], F32); bsB = sb("bsB", [128, 512], F32); bsBs = sb("bsBs", [128, 4, 64], F32)
        wsT = sb("wsT", [128, 4, 128], BF16); wsbd = sb("wsbd", [128, 4, 64], BF16)
        wuqn = sb("wuqn", [128, 3, 512], BF16); wuqp = sb("wuqp", [128, 3, 256], BF16)
        wuqs = sb("wuqs", [128, 3, 256], BF16)
        wukp = sb("wukp", [128, 4, 256], BF16); wuvb = sb("wuvb", [128, 8, 2, 64], BF16)
        ropeT = sb("ropeT", [128, 512], F32); ropeTs = sb("ropeTs", [128, 32], F32)
        h = sb("h", [128, 8, 512], BF16)
        mix = sb("mix", [128, 8, 512], BF16)
        o = sb("o", [128, 8, 512], F32)
        rstd = sb("rstd", [128, 512], F32)
        sq = sb("sq", [128, 2, 512], BF16)
        io_b = [sb("io0", [128, 1024], F32), sb("io1", [128, 1024], F32)]
        tmpf = sb("tmpf", [128, 2, 512], F32)
        cstate = sb("cstate", [128, 4, 2], F32)
        small4 = sb("small4", [128, 4, 32], F32)
        bart = sb("bart", [128, 4], F32)
        slab = sb("slab", [128, 2, 4096], BF16)
        AA = sb("arenaA", [128, 21248], BF16)
        AC = sb("arenaC", [128, 26624], BF16)
        ps = [st.enter_context(nc.psum_tensor("ps%d" % i, [128, 512], F32)) for i in range(8)]
        psb = [p_[:, :].bitcast(BF16) for p_ in ps]

        def carve(arena, off_bytes, nelem, dt):
            a0 = off_bytes // 2
            if dt == BF16:
                return arena[:, a0:a0 + nelem]
            return arena[:, a0:a0 + 2 * nelem].bitcast(F32)

        qT = carve(AA, 0, 2048, BF16).rearrange("p (m t) -> p m t", m=4)
        e_t = carve(AA, 4096, 512, F32)
        sp_t = carve(AA, 6144, 1024, BF16).rearrange("p (a t) -> p a t", a=2)
        spm_t = carve(AA, 8192, 1024, BF16).rearrange("p (a t) -> p a t", a=2)
        w_t = carve(AA, 10240, 1024, BF16).rearrange("p (a t) -> p a t", a=2)
        S_b = [carve(AA, 12288, 512, BF16), carve(AA, 22528, 512, BF16)]
        cin = carve(AA, 13312, 520, F32)
        kvout_b = [carve(AA, 15872, 1024, F32).rearrange("p (a t) -> p a t", a=2), carve(AA, 24576, 1024, F32).rearrange("p (a t) -> p a t", a=2)]
        ksn = carve(AA, 19968, 256, BF16).rearrange("p (m t) -> p m t", m=4)
        vnew = carve(AA, 20480, 512, BF16)
        qs_b = [carve(AA, 21504, 16, BF16), carve(AA, 21568, 16, BF16)]
        SA_b = [carve(AA, 28672, 16, BF16), carve(AA, 28736, 16, BF16)]
        SB_b = [carve(AA, 28800, 16, BF16), carve(AA, 28864, 16, BF16)]
        fE_b = [carve(AA, 28928, 16, F32), carve(AA, 29056, 16, F32)]
        tE_b = [carve(AA, 29184, 16, F32), carve(AA, 29312, 16, F32)]
        cqT = carve(AA, 0, 1536, BF16).rearrange("p (m t) -> p m t", m=3)
        qnT = carve(AA, 3072, 2048, BF16).rearrange("p (m t) -> p m t", m=4)
        uT = carve(AA, 7168, 2048, F32).rearrange("p (m t) -> p m t", m=4)
        vnb = carve(AA, 15360, 2048, BF16).rearrange("p (a t) -> p a t", a=4)
        ckst_b = [carve(AA, 19456, 288, F32), carve(AA, 40960, 288, F32)]
        qlat_b = [carve(AA, 20992, 1024, BF16).rearrange("p (a t) -> p a t", a=2), carve(AA, 37888, 1024, BF16).rearrange("p (a t) -> p a t", a=2)]
        qpe_b = [carve(AA, 23040, 512, BF16), carve(AA, 39936, 512, BF16)]
        OLs = carve(AA, 24064, 1024, BF16).rearrange("p (a t) -> p a t", a=2)
        rden = carve(AA, 26112, 512, F32)
        cosF = carve(AA, 28160, 512, F32)
        sinF = carve(AA, 30208, 512, F32)
        p_t = carve(AA, 32256, 1024, BF16).rearrange("p (a t) -> p a t", a=2)
        vnf = carve(AA, 34304, 512, F32)
        kpb_b = [carve(AA, 36352, 32, BF16), carve(AA, 42112, 32, BF16)]
        ckvn = carve(AA, 36416, 256, BF16)
        h1 = carve(AA, 0, 16384, BF16).rearrange("p (j t) -> p j t", j=32)
        cstS = carve(AA, 0, 1408, F32)
        growS = carve(AA, 5632, 128, F32)
        wsS = carve(AA, 6144, 512, F32).rearrange("p (g s) -> p g s", g=4)
        wuqS = carve(AA, 8192, 2304, F32).rearrange("p (k n) -> p k n", k=3)
        wukS = carve(AA, 17408, 1024, F32).rearrange("p (j c) -> p j c", j=4)
        wuvS = carve(AA, 21504, 1024, F32).rearrange("p (a b) -> p a b", a=16)

        ksT = carve(AC, 0, 8192, BF16).rearrange("p (m t) -> p m t", m=4)
        vtok = carve(AC, 16384, 8192, BF16).rearrange("p (a t) -> p a t", a=16)
        ckvT = carve(AC, 32768, 4096, BF16).rearrange("p (a t) -> p a t", a=2)
        ckvtok = carve(AC, 40960, 4096, BF16).rearrange("p (a t) -> p a t", a=16)
        kpeT = carve(AC, 49152, 2048, BF16)
        kld = carve(AC, 0, 2048, F32).rearrange("p (a d) -> p a d", a=32)
        vld = carve(AC, 8192, 2048, F32).rearrange("p (a d) -> p a d", a=32)
        kTs_b = [carve(AC, 16384, 4112, BF16), carve(AC, 24640, 4112, BF16)]
        vh_b = [carve(AC, 32896, 33 * 64, BF16).rearrange("p (a d) -> p a d", a=33), carve(AC, 37120, 33 * 64, BF16).rearrange("p (a d) -> p a d", a=33)]
        kbf_b = [carve(AC, 41344, 2048, BF16).rearrange("p (a d) -> p a d", a=32), carve(AC, 45440, 2048, BF16).rearrange("p (a d) -> p a d", a=32)]
        cstg = carve(AC, 0, 2048, F32)
        ckvtok_s = carve(AC, 8192, 33 * 256, BF16).rearrange("p (a d) -> p a d", a=33)
        ckvT_s = carve(AC, 25088, 2 * 4112, BF16).rearrange("p (a t) -> p a t", a=2)
        kpeT_s = carve(AC, 41536, 4112, BF16)
        kpbs = carve(AC, 49792, 1024, BF16).rearrange("p (a d) -> p a d", a=32)
        stg = carve(AC, 0, 8192, F32).rearrange("p (a n) -> p a n", a=2)
        sbf = carve(AC, 32768, 8192, BF16).rearrange("p (a n) -> p a n", a=2)

        state = dict(dbank=0, zb=0, bb=0, slabk=0, slab_issued=0, tmp=0)

        def dbank():
            b = state["dbank"] % 3
            state["dbank"] += 1
            return b

        def bkey(b):
            return "ps%d" % b

        def barrier():
            P.barrier(lambda e: e.memset(bart[:, 0:4], 0.0))

        def setup():
            P.dma(lambda e: e.dma_start(out=cstS[:, :], in_=cst[:, :]), writes=["cstS"], semkey="su0")
            P.op("act", lambda e: e.activation(out=identf[:, :], in_=cstS[:, 0:128], func=AF.Copy), reads=["cstS"], writes=["identf"])
            P.op("dve", lambda e: e.tensor_copy(out=identb[:, :], in_=cstS[:, 0:128]), reads=["cstS"], writes=["identb"])
            P.op("dve", lambda e: e.tensor_copy(out=ntri[:, :], in_=cstS[:, 128:256]), reads=["cstS"], writes=["ntri"])
            P.op("dve", lambda e: e.tensor_copy(out=sbm[:, :], in_=cstS[:, 256:768]), reads=["cstS"], writes=["sbm"])
            P.op("dve", lambda e: e.tensor_copy(out=mlm[:, :], in_=cstS[:, 768:1280]), reads=["cstS"], writes=["mlm"])
            P.op("pool", lambda e: e.memset(nones[:, :], -1.0), writes=["nones"])
            P.op("pool", lambda e: e.memset(onesb[:, :], 1.0), writes=["onesb"])
            P.op("pool", lambda e: e.memset(wsbd[:, :, :], 0.0), writes=["wsbd"])
            P.op("pool", lambda e: e.memset(cstate[:, :, :], 0.0), writes=["cstate"])
            P.dma(lambda e: e.dma_start(out=growS[0:79, :], in_=grow[:, :]), writes=["growS"], semkey="su1")
            P.op("pe", lambda e: e.transpose(ps[0][:, 0:79], growS[0:79, :], identf[0:79, 0:79]), reads=["growS", "identf"], writes=[bkey(0)])
            P.op("act", lambda e: e.activation(out=gcol[:, 0:79], in_=ps[0][:, 0:79], func=AF.Copy), reads=[bkey(0)], writes=["gcol"])
            P.dma(lambda e: e.dma_start(out=lnG[:, :], in_=lng[0:1, :].broadcast_to([128, 512])), writes=["lnG"], semkey="su2")
            P.dma(lambda e: e.dma_start(out=lnB[:, :], in_=lnb[0:1, :].broadcast_to([128, 512])), writes=["lnB"], semkey="su3")
            P.dma(lambda e: e.dma_start(out=kvG[:, :], in_=kvg[0:1, :].broadcast_to([128, 256])), writes=["kvG"], semkey="su4")
            P.dma(lambda e: e.dma_start(out=bsB[:, :], in_=bsd[0:1, :].broadcast_to([128, 512])), writes=["bsB"], semkey="su5")
            for j in range(4):
                P.dma(lambda e, j=j: e.dma_start(out=bsBs[:, :, j * 16:(j + 1) * 16], in_=bss[0:1, :, :].broadcast_to([128, 4, 16])),
                      writes=["bsBs"], semkey="su6")
            P.dma(lambda e: e.dma_start(out=ropeT[:, :], in_=rope_t[:, :]), writes=["ropeT"], semkey="su7")
            P.dma(lambda e: e.dma_start(out=ropeTs[0:64, :], in_=rope_ts[:, :]), writes=["ropeTs"], semkey="su8")
            P.dma(lambda e: e.dma_start(out=wsS[:, :, :], in_=wsd.rearrange("g t s -> t g s")), writes=["wsS"], semkey="su9")
            for g in range(4):
                P.op("dve", lambda e, g=g: e.tensor_tensor(out=wsS[:, g, :], in0=wsS[:, g, :], in1=cstS[:, 1280:1408], op=ALU.mult),
                     reads=["wsS", "cstS"], writes=["wsS"])
            P.op("pe", lambda e: [e.transpose(ps[1][:, g * 128:(g + 1) * 128], wsS[:, g, :], identf[:, :]) for g in range(4)][-1],
                 reads=["wsS", "identf"], writes=[bkey(1)])
            P.op("act", lambda e: e.activation(out=wsT[:, :, :], in_=ps[1][:, :].rearrange("p (g t) -> p g t", g=4), func=AF.Copy),
                 reads=[bkey(1)], writes=["wsT"])
            for j in range(4):
                P.dma(lambda e, j=j: e.dma_start(out=wsbd[16 * j:16 * j + 16, :, 16 * j:16 * j + 16], in_=wsT[0:16, :, 0:16]),
                      reads=["wsT", "wsbd"], writes=["wsbd"], semkey="su10")
            P.dma(lambda e: e.dma_start(out=wuqS[:, :, :], in_=wuq.rearrange("(k p) n -> p k n", p=128)), writes=["wuqS"], semkey="su11")
            for kc in range(3):
                src = wuqS[:, kc, :].rearrange("p (h d) -> p h d", d=96)
                P.op("dve", lambda e, kc=kc, src=src: e.tensor_copy(out=wuqn[:, kc, :].rearrange("p (h d) -> p h d", d=64), in_=src[:, :, 0:64]),
                     reads=["wuqS"], writes=["wuqn"])
                P.op("dve", lambda e, kc=kc, src=src: e.tensor_copy(out=wuqp[:, kc, :].rearrange("p (h d) -> p h d", d=32), in_=src[:, :, 64:96]),
                     reads=["wuqS"], writes=["wuqp"])
                P.op("pool", lambda e, kc=kc, src=src: e.tensor_copy(out=wuqs[:, kc, :].rearrange("p (h d) -> p h d", d=32)[:, :, 0:16], in_=src[:, :, 80:96]),
                     reads=["wuqS"], writes=["wuqs"])
                P.op("pool", lambda e, kc=kc, src=src: e.tensor_copy(out=wuqs[:, kc, :].rearrange("p (h d) -> p h d", d=32)[:, :, 16:32], in_=src[:, :, 64:80]),
                     reads=["wuqS"], writes=["wuqs"])
            for two in range(2):
                P.dma(lambda e, two=two: e.dma_start(out=wukS[two * 64:(two + 1) * 64, :, :], in_=wuk.rearrange("(j two) n c -> two n j c", two=2)[two]),
                      writes=["wukS"], semkey="su12")
            P.op("dve", lambda e: e.tensor_copy(out=wukp[:, :, :], in_=wukS[:, :, :]), reads=["wukS"], writes=["wukp"])
            P.dma(lambda e: e.dma_start(out=wuvS[:, :, :], in_=wuv.rearrange("h (cc p) v -> p (h cc) v", p=128)), writes=["wuvS"], semkey="su13")
            P.op("dve", lambda e: e.tensor_copy(out=wuvb[:, :, :, :].rearrange("p h c v -> p (h c) v"), in_=wuvS[:, :, :]), reads=["wuvS"], writes=["wuvb"])

            def Wv(W, c0, c1):
                return W.rearrange("(kc p) n -> p kc n", p=128)[:, :, c0:c1]

            slabs = []
            for c0 in (0, 512, 1024):
                slabs.append(([(Wv(ein, c0, c0 + 512), 0, 512)], 8, 512))
            for c in range(4):
                slabs.append(([(Wv(ein, 1536 + 128 * c, 1536 + 128 * c + 128), 0, 128),
                               (Wv(ein, 2048 + 128 * c, 2048 + 128 * c + 128), 128, 128),
                               (Wv(ein, 2560 + 128 * c, 2560 + 128 * c + 128), 256, 128)], 8, 384))
            for c0 in (0, 512):
                slabs.append(([(Wv(eout, c0, c0 + 512), 0, 512)], 8, 512))
            for j in range(8):
                slabs.append(([(Wv(fup[0], 512 * j, 512 * j + 512), 0, 512)], 8, 512))
            for n in range(8):
                slabs.append(([(Wv(fdn[0], 128 * n, 128 * n + 128), 0, 128)], 32, 128))
            slabs.append(([(Wv(oin, 0, 512), 0, 512)], 8, 512))
            slabs.append(([(Wv(oin, 1024, 1408), 0, 384)], 8, 384))
            slabs.append(([(Wv(oin, 512, 1024), 0, 512)], 8, 512))
            slabs.append(([(Wv(oin, 1408, 1696), 0, 288)], 8, 288))
            for c0 in (0, 512):
                slabs.append(([(Wv(oout, c0, c0 + 512), 0, 512)], 8, 512))
            for j in range(8):
                slabs.append(([(Wv(fup[1], 512 * j, 512 * j + 512), 0, 512)], 8, 512))
            for n in range(8):
                slabs.append(([(Wv(fdn[1], 128 * n, 128 * n + 128), 0, 128)], 32, 128))
            assert len(slabs) == NSLAB
            engs = ["dve", "act", "pool"]
            for i, (pieces, kc, nct) in enumerate(slabs):
                b = i % 2
                n = kc * nct
                sv = stg[:, b, 0:n].rearrange("p (k n) -> p k n", k=kc)
                for pi, (src, coff, ncol) in enumerate(pieces):
                    P.dma(lambda e, sv=sv, src=src, coff=coff, ncol=ncol: e.dma_start(out=sv[:, :, coff:coff + ncol], in_=src),
                          writes=["stg%d" % b], semkey="stg%d_%d" % (b, pi))
                en = engs[i % 3]
                if en == "act":
                    P.op("act", lambda e, b=b, n=n: e.activation(out=sbf[:, b, 0:n], in_=stg[:, b, 0:n], func=AF.Copy),
                         reads=["stg%d" % b], writes=["sbf%d" % b])
                else:
                    P.op(en, lambda e, b=b, n=n: e.tensor_copy(out=sbf[:, b, 0:n], in_=stg[:, b, 0:n]),
                         reads=["stg%d" % b], writes=["sbf%d" % b])
                P.dma(lambda e, i=i, b=b, n=n: e.dma_start(out=wsc[i, :, 0:n], in_=sbf[:, b, 0:n]),
                      reads=["sbf%d" % b], writes=["wsc%d" % i], semkey="sbfo%d" % b)

        unit_slab_seq = list(range(NSLAB))
        total_slabs = len(units) * NSLAB

        slab_n = {3: 3072, 4: 3072, 5: 3072, 6: 3072, SLAB_OIN + 1: 3072, SLAB_OIN + 3: 2304}

        def issue_slab(k):
            idx = unit_slab_seq[k % NSLAB]
            b = k % 2
            n = slab_n.get(idx, 4096)
            P.dma(lambda e, idx=idx, b=b, n=n: e.dma_start(out=slab[:, b, 0:n], in_=wsc[idx, :, 0:n]),
                  reads=["wsc%d" % idx], writes=["slab%d" % b], semkey="slab%d" % b)

        def get_slab(expect):
            k = state["slabk"]
            assert unit_slab_seq[k % NSLAB] == expect, (k, expect)
            while state["slab_issued"] <= min(k + 1, total_slabs - 1):
                issue_slab(state["slab_issued"])
                state["slab_issued"] += 1
            state["slabk"] += 1
            b = k % 2
            return slab[:, b, :], "slab%d" % b

        def rms_stats(srcs, keys, Dn, U):
            bank = dbank()
            n = len(srcs)
            for c in range(n):
                P.op("act", lambda e, c=c: e.activation(out=sq[:, c % 2, 0:U], in_=srcs[c], func=AF.Square),
                     reads=[keys[c]], writes=["sq%d" % (c % 2)])
                P.op("pe", lambda e, c=c: e.matmul(ps[bank][:, 0:U], onesb[:, :], sq[:, c % 2, 0:U], start=(c == 0), stop=(c == n - 1)),
                     reads=["sq%d" % (c % 2), "onesb"], writes=[bkey(bank)])
            P.op("act", lambda e: e.activation(out=rstd[:, 0:U], in_=ps[bank][:, 0:U], func=AF.Ln, scale=1.0 / Dn, bias=EPS),
                 reads=[bkey(bank)], writes=["rstd"])
            P.op("act", lambda e: e.activation(out=rstd[:, 0:U], in_=rstd[:, 0:U], func=AF.Exp, scale=-0.5),
                 reads=["rstd"], writes=["rstd"])

        def pre_norm(gi, U):
            rms_stats([xT[:, c, 0:U] for c in range(8)], ["xT%d" % c for c in range(8)], 1024, U)
            for c in range(8):
                P.op("dve", lambda e, c=c: e.scalar_tensor_tensor(out=h[:, c, 0:U], in0=xT[:, c, 0:U], scalar=gcol[:, gi + c:gi + c + 1],
                                                                 in1=rstd[:, 0:U], op0=ALU.mult, op1=ALU.mult),
                     reads=["xT%d" % c, "rstd", "gcol"], writes=["h"])

        def post_norm_residual(gi, U):
            rms_stats([o[:, c, 0:U] for c in range(8)], ["o%d" % c for c in range(8)], 1024, U)
            for c in range(8):
                P.op("dve", lambda e, c=c: e.scalar_tensor_tensor(out=o[:, c, 0:U], in0=o[:, c, 0:U], scalar=gcol[:, gi + c:gi + c + 1],
                                                                 in1=rstd[:, 0:U], op0=ALU.mult, op1=ALU.mult),
                     reads=["o%d" % c, "rstd", "gcol"], writes=["o%d" % c])
                P.op("pool", lambda e, c=c: e.tensor_tensor(out=xT[:, c, 0:U], in0=xT[:, c, 0:U], in1=o[:, c, 0:U], op=ALU.add),
                     reads=["xT%d" % c, "o%d" % c], writes=["xT%d" % c])

        def proj_fm(sl, slk, col0, ncol, rhs_fn, rkeys, nkc, U, stride):
            bank = dbank()
            sv = sl[:, 0:nkc * stride].rearrange("p (k n) -> p k n", k=nkc)

            def fn(e):
                ins = None
                for kc in range(nkc):
                    ins = e.matmul(ps[bank][0:ncol, 0:U], sv[:, kc, col0:col0 + ncol], rhs_fn(kc), start=(kc == 0), stop=(kc == nkc - 1))
                return ins
            P.op("pe", fn, reads=[slk] + list(rkeys), writes=[bkey(bank)])
            return bank

        def proj_tm(sl, slk, ncol, tt, TT, stride):
            bank = dbank()
            sv = sl[:, 0:8 * stride].rearrange("p (k n) -> p k n", k=8)

            def fn(e):
                ins = None
                for kc in range(8):
                    ins = e.matmul(ps[bank][0:TT, 0:ncol], h[:, kc, tt * TT:(tt + 1) * TT], sv[:, kc, 0:ncol], start=(kc == 0), stop=(kc == 7))
                return ins
            P.op("pe", fn, reads=[slk, "h"], writes=[bkey(bank)])
            return bank

        def wout_and_post(slab0, gi, U):
            for sl_i in range(2):
                sl, slk = get_slab(slab0 + sl_i)
                for mm in range(4):
                    n = sl_i * 4 + mm
                    bank = proj_fm(sl, slk, mm * 128, 128, lambda kc: mix[:, kc, 0:U], ["mix%d" % c for c in range(8)], 8, U, 512)
                    P.op("dve", lambda e, n=n, bank=bank: e.tensor_copy(out=o[:, n, 0:U], in_=ps[bank][:, 0:U]),
                         reads=[bkey(bank)], writes=["o%d" % n])
            post_norm_residual(gi, U)

        def ffn(layer, U):
            pre_norm(32 + layer * 8, U)
            barrier()
            s_up = SLAB_FUP0 if layer == 0 else SLAB_FUP1
            s_dn = SLAB_FDN0 if layer == 0 else SLAB_FDN1
            for j in range(8):
                sl, slk = get_slab(s_up + j)
                for mm in range(4):
                    bank = proj_fm(sl, slk, mm * 128, 128, lambda kc: h[:, kc, 0:U], ["h"], 8, U, 512)
                    tb = state["tmp"] % 2
                    state["tmp"] += 1
                    P.op("act", lambda e, bank=bank, tb=tb: e.activation(out=tmpf[:, tb, 0:U], in_=ps[bank][:, 0:U], func=AF.Relu),
                         reads=[bkey(bank)], writes=["tmpf%d" % tb])
                    P.op("dve", lambda e, bank=bank, tb=tb, jj=4 * j + mm: e.tensor_tensor(out=h1[:, jj, 0:U], in0=tmpf[:, tb, 0:U], in1=ps[bank][:, 0:U], op=ALU.mult),
                         reads=[bkey(bank), "tmpf%d" % tb], writes=["h1_%d" % (4 * j + mm)])
            for n in range(8):
                sl, slk = get_slab(s_dn + n)
                bank = proj_fm(sl, slk, 0, 128, lambda kc: h1[:, kc, 0:U], ["h1_%d" % c for c in range(32)], 32, U, 128)
                P.op("dve", lambda e, n=n, bank=bank: e.tensor_copy(out=o[:, n, 0:U], in_=ps[bank][:, 0:U]),
                     reads=[bkey(bank)], writes=["o%d" % n])
            post_norm_residual(48 + layer * 8, U)
            barrier()

        def sb_tile(kAP, qAP, vAP, nk, ncols, S_ap, acc_ap, acc_key, mask, first, last, rk, tu, Sk="S0", vk="vtok", acc_start=None, s_add=None):
            if acc_start is None:
                acc_start = first
            if s_add is None:
                s_add = not last
            zb = 3 + (tu % 2)
            bb = 5 + (tu % 2)
            a = tu % 2
            P.op("pe", lambda e: e.matmul(ps[zb][0:nk, 0:ncols], kAP, qAP, start=True, stop=True), reads=rk, writes=[bkey(zb)])
            P.op("act", lambda e: e.activation(out=e_t[0:nk, 0:ncols], in_=ps[zb][0:nk, 0:ncols], func=AF.Exp), reads=[bkey(zb)], writes=["e_t"])
            P.op("act", lambda e: e.activation(out=sp_t[0:nk, a, 0:ncols], in_=e_t[0:nk, 0:ncols], func=AF.Ln, bias=1.0), reads=["e_t"], writes=["sp%d" % a])
            if mask is not None:
                P.op("dve", lambda e: e.tensor_tensor(out=spm_t[0:nk, a, 0:ncols], in0=sp_t[0:nk, a, 0:ncols], in1=mask, op=ALU.mult),
                     reads=["sp%d" % a, "sbm"], writes=["spm%d" % a])
                spm = spm_t[0:nk, a, 0:ncols]
                spk = "spm%d" % a
            else:
                spm = sp_t[0:nk, a, 0:ncols]
                spk = "sp%d" % a
            yield

            def fnB(e):
                e.matmul(ps[bb][0:nk, 0:ncols], ntri[0:nk, 0:nk], spm, start=True, stop=False)
                if not first:
                    e.matmul(ps[bb][0:nk, 0:ncols], nones[0:128, 0:nk], S_ap, start=False, stop=False)
                return e.matmul(ps[bb][0:nk, 0:ncols], kAP, qAP, start=False, stop=True)
            P.op("pe", fnB, reads=[spk, Sk, "ntri", "nones"] + list(rk), writes=[bkey(bb)])
            if s_add:
                P.op("pool", lambda e: e.tensor_tensor(out=S_ap[0:nk, :], in0=S_ap[0:nk, :], in1=spm, op=ALU.add), reads=[spk, Sk], writes=[Sk])
            P.op("act", lambda e: e.activation(out=w_t[0:nk, a, 0:ncols], in_=ps[bb][0:nk, 0:ncols], func=AF.Exp), reads=[bkey(bb)], writes=["w%d" % a])
            if mask is not None:
                P.op("dve", lambda e: e.tensor_tensor(out=w_t[0:nk, a, 0:ncols], in0=w_t[0:nk, a, 0:ncols], in1=mask, op=ALU.mult),
                     reads=["w%d" % a, "sbm"], writes=["w%d" % a])
            yield
            P.op("pe", lambda e: e.matmul(acc_ap, vAP, w_t[0:nk, a, 0:ncols], start=acc_start, stop=last), reads=["w%d" % a, vk], writes=[acc_key])

        def wrap(g, pre=None, post=None):
            if pre is not None:
                pre()
            try:
                while True:
                    next(g)
                    yield
            except StopIteration:
                pass
            if post is not None:
                post()

        def run_pipelined(gens, hook_step=None, hook=None):
            pending = list(gens)
            active = []
            step = 0
            while pending or active:
                while pending and not hasattr(pending[0], "__next__"):
                    pending.pop(0)()
                if pending:
                    active.append(pending.pop(0))
                nxt = []
                for g in reversed(active):
                    try:
                        next(g)
                        nxt.append(g)
                    except StopIteration:
                        pass
                active = list(reversed(nxt))
                if hook is not None and step == hook_step:
                    hook()
                step += 1
            if hook is not None and step <= hook_step:
                hook()

        def mla_tile(cT0, cT1, kpT, ctok, nk, q0, q1, mask, first, last, hp, tu, qb_=0):
            zb = 3 + (tu % 2)
            a = tu % 2
            qlat = qlat_b[qb_]
            qpe = qpe_b[qb_]

            def fnZ(e):
                e.matmul(ps[zb][0:nk, q0:q1], cT0, qlat[:, 0, q0:q1], start=True, stop=False)
                e.matmul(ps[zb][0:nk, q0:q1], cT1, qlat[:, 1, q0:q1], start=False, stop=False)
                return e.matmul(ps[zb][0:nk, q0:q1], kpT, qpe[0:32, q0:q1], start=False, stop=True)
            P.op("pe", fnZ, reads=["ckvT", "kpeT", "qlat%d" % qb_, "qpe%d" % qb_], writes=[bkey(zb)])
            P.op("act", lambda e: e.activation(out=p_t[0:nk, a, q0:q1], in_=ps[zb][0:nk, q0:q1], func=AF.Exp, scale=MLA_SCALE),
                 reads=[bkey(zb)], writes=["p%d" % a])
            if mask is not None:
                P.op("dve", lambda e: e.tensor_tensor(out=p_t[0:nk, a, q0:q1], in0=p_t[0:nk, a, q0:q1], in1=mask, op=ALU.mult),
                     reads=["p%d" % a, "mlm"], writes=["p%d" % a])
            yield

            def fnO(e):
                e.matmul(ps[5][:, q0:q1], ctok[:, 0:128], p_t[0:nk, a, q0:q1], start=first, stop=last)
                e.matmul(ps[6][:, q0:q1], ctok[:, 128:256], p_t[0:nk, a, q0:q1], start=first, stop=last)
                return e.matmul(ps[7][:, q0:q1], onesb[0:nk, :], p_t[0:nk, a, q0:q1], start=first, stop=last)
            P.op("pe", fnO, reads=["p%d" % a, "ckvtok", "onesb"], writes=[bkey(5), bkey(6), bkey(7)])

        def unit(kind, s, qb):
            isp = (kind == "p")
            U = 512 if isp else 64
            TT = 128 if isp else 64
            NTT = U // TT
            row0 = qb * 512

            for tt in range(NTT):
                src = xp[s, row0 + tt * 128: row0 + tt * 128 + 128, :] if isp else xs[0:64, :]
                xin = io_b[tt % 2]
                xk = "io%d" % (tt % 2)
                P.dma(lambda e, src=src, xin=xin: e.dma_start(out=xin[0:TT, :], in_=src), writes=[xk], semkey=xk)
                for half in range(2):
                    bank = dbank()
                    P.op("pe", lambda e, half=half, bank=bank, xin=xin: [e.transpose(ps[bank][:, c4 * TT:(c4 + 1) * TT], xin[0:TT, (half * 4 + c4) * 128:(half * 4 + c4 + 1) * 128], identf[0:TT, 0:TT]) for c4 in range(4)][-1],
                         reads=[xk, "identf"], writes=[bkey(bank)])
                    P.op("act", lambda e, half=half, bank=bank, tt=tt: e.activation(out=xT[:, half * 4:half * 4 + 4, tt * TT:(tt + 1) * TT],
                                                                                in_=ps[bank][:, 0:4 * TT].rearrange("p (a t) -> p a t", a=4), func=AF.Copy),
                         reads=[bkey(bank)], writes=["xT%d" % (half * 4 + c4) for c4 in range(4)])
            if float(os.environ.get("KSTOP", 99)) <= 1:
                return
            pre_norm(0, U)
            if float(os.environ.get("KSTOP", 99)) <= 1.2:
                return
            sl, slk = get_slab(SLAB_EIN + 0)
            for m in range(4):
                bank = proj_fm(sl, slk, m * 128, 128, lambda kc: h[:, kc, 0:U], ["h"], 8, U, 512)
                P.op("act", lambda e, m=m, bank=bank: e.activation(out=qT[:, m, 0:U], in_=ps[bank][:, 0:U], func=AF.Copy),
                     reads=[bkey(bank)], writes=["qT"])
            if float(os.environ.get("KSTOP", 99)) <= 1.4:
                return
            sl, slk = get_slab(SLAB_EIN + 1)
            for m in range(4):
                bank = proj_fm(sl, slk, m * 128, 128, lambda kc: h[:, kc, 0:U], ["h"], 8, U, 512)
                if isp:
                    P.op("act", lambda e, m=m, bank=bank: e.activation(out=ksT[:, m, row0:row0 + 512], in_=ps[bank][:, 0:U], func=AF.Copy, scale=SB_SCALE),
                         reads=[bkey(bank)], writes=["ksT"])
                else:
                    P.op("act", lambda e, m=m, bank=bank: e.activation(out=ksn[:, m, 0:U], in_=ps[bank][:, 0:U], func=AF.Copy, scale=SB_SCALE),
                         reads=[bkey(bank)], writes=["ksn"])
            for tt in range(NTT):
                bank = proj_tm(sl, slk, 512, tt, TT, 512)
                kvout = kvout_b[tt % 2]
                kk0 = "kvout0_%d" % (tt % 2)
                P.op("dve", lambda e, bank=bank, kvout=kvout: e.tensor_copy(out=kvout[0:TT, 0, :], in_=ps[bank][0:TT, :]), reads=[bkey(bank)], writes=[kk0])
                dst = sbk_p[s, row0 + tt * 128: row0 + tt * 128 + 128, :] if isp else sbk_s[0:64, :]
                P.dma(lambda e, dst=dst, kvout=kvout: e.dma_start(out=dst, in_=kvout[0:TT, 0, :]), reads=[kk0], writes=[], semkey=kk0)
            if float(os.environ.get("KSTOP", 99)) <= 1.6:
                return
            sl, slk = get_slab(SLAB_EIN + 2)
            for tt in range(NTT):
                bank = proj_tm(sl, slk, 512, tt, TT, 512)
                kvout = kvout_b[tt % 2]
                kk1 = "kvout1_%d" % (tt % 2)
                P.op("dve", lambda e, bank=bank, kvout=kvout: e.tensor_copy(out=kvout[0:TT, 1, :], in_=ps[bank][0:TT, :]), reads=[bkey(bank)], writes=[kk1])
                dst = sbv_p[s, row0 + tt * 128: row0 + tt * 128 + 128, :] if isp else sbv_s[0:64, :]
                P.dma(lambda e, dst=dst, kvout=kvout: e.dma_start(out=dst, in_=kvout[0:TT, 1, :]), reads=[kk1], writes=[], semkey=kk1)
                if isp:
                    P.op("act", lambda e, bank=bank, tt=tt: e.activation(out=vtok[:, qb * 4 + tt, :], in_=ps[bank][:, :], func=AF.Copy),
                         reads=[bkey(bank)], writes=["vtok"])
                elif not os.environ.get("KNOVNEW"):
                    P.op("act", lambda e, bank=bank: e.activation(out=vnew[0:64, :], in_=ps[bank][0:64, :], func=AF.Copy),
                         reads=[bkey(bank)], writes=["vnew"])
            if float(os.environ.get("KSTOP", 99)) <= 1.8:
                return
            if isp and qb == 0:
                P.op("pool", lambda e: e.memset(cstate[:, :, :], 0.0), writes=["cstate"])
            for c in range(4):
                if os.environ.get("KNOCONV") and not isp:
                    state["slabk"] += 1
                    continue
                sl, slk = get_slab(SLAB_EIN + 3 + c)
                b0 = proj_fm(sl, slk, 0, 128, lambda kc: h[:, kc, 0:U], ["h"], 8, U, 384)
                b1 = proj_fm(sl, slk, 128, 128, lambda kc: h[:, kc, 0:U], ["h"], 8, U, 384)
                b2 = proj_fm(sl, slk, 256, 128, lambda kc: h[:, kc, 0:U], ["h"], 8, U, 384)
                if isp:
                    cv = cin[:, 0:514]
                    cur = cv[:, 2:514]
                    t0, t1_, t2 = cv[:, 0:512], cv[:, 1:513], cv[:, 2:514]
                    tv = tmpf[:, 1, 0:512]
                    g1 = tmpf[:, 0, 0:512]
                    pv = lambda b: ps[b][:, 0:512]
                    P.op("dve", lambda e, c=c: e.tensor_copy(out=cin[:, 0:2], in_=cstate[:, c, :]), reads=["cstate"], writes=["cin"])
                else:
                    cv = cin[:, 0:72].rearrange("p (s t) -> p s t", s=4)
                    cur = cv[:, :, 2:18]
                    t0, t1_, t2 = cv[:, :, 0:16], cv[:, :, 1:17], cv[:, :, 2:18]
                    tv = tmpf[:, 1, 0:64].rearrange("p (s t) -> p s t", s=4)
                    g1 = tmpf[:, 0, 0:64].rearrange("p (s t) -> p s t", s=4)
                    pv = lambda b: ps[b][:, 0:64].rearrange("p (s t) -> p s t", s=4)
                    for s4 in range(4):
                        for j2 in range(2):
                            P.dma(lambda e, c=c, cv=cv, s4=s4, j2=j2: e.dma_start(out=cv[:, s4, j2:j2 + 1], in_=sconv[s4, j2:j2 + 1, c * 128:(c + 1) * 128].rearrange("j p -> p j")),
                                  reads=[], writes=["cinp%d" % (s4 * 2 + j2)], semkey="cinld%d" % (s4 * 2 + j2))
                P.op("act", lambda e, g1=g1, b1=b1, pv=pv: e.activation(out=g1, in_=pv(b1), func=AF.Copy), reads=[bkey(b1)], writes=["tmpf0"])
                P.op("dve", lambda e, cur=cur, g1=g1, b2=b2, pv=pv: e.tensor_tensor(out=cur, in0=pv(b2), in1=g1, op=ALU.mult),
                     reads=[bkey(b2), "tmpf0"], writes=["cin"])
                wi = 64 + c
                cpk = [] if isp else ["cinp%d" % i for i in range(8)]
                P.op("dve", lambda e, tv=tv, t0=t0, wi=wi: e.tensor_scalar(out=tv, in0=t0, scalar1=gcol[:, wi:wi + 1], scalar2=None, op0=ALU.mult), reads=["cin", "gcol"] + cpk, writes=["tmpf1"])
                P.op("dve", lambda e, tv=tv, t1_=t1_, wi=wi: e.scalar_tensor_tensor(out=tv, in0=t1_, scalar=gcol[:, wi + 4:wi + 5], in1=tv, op0=ALU.mult, op1=ALU.add), reads=["cin", "tmpf1", "gcol"] + cpk, writes=["tmpf1"])
                P.op("dve", lambda e, tv=tv, t2=t2, wi=wi: e.scalar_tensor_tensor(out=tv, in0=t2, scalar=gcol[:, wi + 8:wi + 9], in1=tv, op0=ALU.mult, op1=ALU.add), reads=["cin", "tmpf1", "gcol"] + cpk, writes=["tmpf1"])
                if isp:
                    P.op("dve", lambda e, c=c, tv=tv, b0=b0: e.tensor_tensor(out=mix[:, 4 + c, 0:512], in0=tv, in1=ps[b0][:, 0:512], op=ALU.mult), reads=["tmpf1", bkey(b0)], writes=["mix%d" % (4 + c)])
                    P.op("pool", lambda e, c=c: e.tensor_copy(out=cstate[:, c, :], in_=cin[:, 512:514]), reads=["cin"], writes=["cstate"])
                    if qb == 3:
                        for j2 in range(2):
                            P.dma(lambda e, c=c, j2=j2: e.dma_start(out=conv_p[s, j2:j2 + 1, c * 128:(c + 1) * 128].rearrange("j p -> p j"), in_=cin[:, 512 + j2:513 + j2]),
                                  reads=["cin"], writes=[], semkey="convo%d" % j2)
                else:
                    P.op("dve", lambda e, c=c, tv=tv, b0=b0, pv=pv: e.tensor_tensor(out=mix[:, 4 + c, 0:64].rearrange("p (s t) -> p s t", s=4), in0=tv, in1=pv(b0), op=ALU.mult),
                         reads=["tmpf1", bkey(b0)], writes=["mix%d" % (4 + c)])
                    for s4 in range(4):
                        for j2 in range(2):
                            P.dma(lambda e, c=c, cv=cv, s4=s4, j2=j2: e.dma_start(out=conv_s[s4, j2:j2 + 1, c * 128:(c + 1) * 128].rearrange("j p -> p j"), in_=cv[:, s4, 16 + j2:17 + j2]),
                                  reads=["cin"], writes=[], semkey="convo%d" % (s4 * 2 + j2))

            if float(os.environ.get("KSTOP", 99)) <= 2:
                return
            tu = 0
            if isp:
                nt = 4 * qb + 4
                gens = []
                for hh in range(8):
                    hp, m = hh % 2, hh // 2
                    S_t = S_b[hp]
                    Sk = "S%d" % hp
                    pre = lambda S_t=S_t, Sk=Sk: P.op("pool", lambda e: e.memset(S_t[:, 0:512], 0.0), writes=[Sk])
                    post = lambda hp=hp, m=m: P.op("act", lambda e: e.activation(out=mix[hp * 64:(hp + 1) * 64, m, 0:512], in_=ps[7][hp * 64:(hp + 1) * 64, 0:512], func=AF.Copy),
                                                   reads=["acc%d" % hp], writes=["mix%d" % m])
                    for idx, kt in enumerate(range(nt - 1, -1, -1)):
                        j = kt - 4 * qb
                        q0 = 128 * j if j > 0 else 0
                        mask = sbm[:, 0:512 - q0] if j >= 0 else None
                        g = sb_tile(ksT[hp * 64:(hp + 1) * 64, m, kt * 128:(kt + 1) * 128], qT[hp * 64:(hp + 1) * 64, m, q0:512],
                                    vtok[:, kt, hh * 64:(hh + 1) * 64], 128, 512 - q0, S_t[:, q0:512],
                                    ps[7][hp * 64:(hp + 1) * 64, q0:512], "acc%d" % hp, mask, idx == 0, idx == nt - 1, ["ksT", "qT"], tu, Sk)
                        gens.append(wrap(g, pre if idx == 0 else None, post if idx == nt - 1 else None))
                        tu += 1
                run_pipelined(gens)
            else:
                items = []
                heads = [(ss, hh) for ss in range(SPB) for hh in range(8)]

                def prologue(ss, hh, b):
                    hp, m = hh % 2, hh // 2
                    kTs, vh, kbf, qs_t = kTs_b[b], vh_b[b], kbf_b[b], qs_b[b]
                    P.dma(lambda e: e.dma_start(out=kld[:, :, :], in_=csk[ss, :, hh * 64:(hh + 1) * 64].rearrange("(a p) d -> p a d", p=128)),
                          writes=["kld"], semkey="kld")
                    P.dma(lambda e: e.dma_start(out=vld[:, :, :], in_=csv[ss, :, hh * 64:(hh + 1) * 64].rearrange("(a p) d -> p a d", p=128)),
                          writes=["vld"], semkey="vld")
                    P.op("dve", lambda e: e.tensor_copy(out=kbf[:, :, :], in_=kld[:, :, :]), reads=["kld"], writes=["kbf%d" % b])
                    P.op("pool", lambda e: e.tensor_copy(out=vh[:, 0:32, :], in_=vld[:, :, :]), reads=["vld"], writes=["vh%d" % b])
                    for g in range(8):
                        bank = dbank()
                        P.op("pe", lambda e, g=g, bank=bank: [e.transpose(psb[bank][0:64, i * 128:(i + 1) * 128], kbf[:, 4 * g + i, :], identb[:, :]) for i in range(4)][-1],
                             reads=["kbf%d" % b, "identb"], writes=[bkey(bank)])
                        P.op("dve", lambda e, g=g, bank=bank: e.tensor_scalar(out=kTs[0:64, g * 512:(g + 1) * 512], in0=psb[bank][0:64, 0:512], scalar1=SB_SCALE, scalar2=None, op0=ALU.mult),
                             reads=[bkey(bank)], writes=["kTs%d" % b])
                    P.op("dve", lambda e: e.tensor_copy(out=kTs[0:64, 4096:4112], in_=ksn[hp * 64:(hp + 1) * 64, m, ss * 16:(ss + 1) * 16]),
                         reads=["ksn"], writes=["kTs%d" % b])
                    P.op("dve", lambda e: e.tensor_copy(out=qs_t[0:64, 0:16], in_=qT[hp * 64:(hp + 1) * 64, m, ss * 16:(ss + 1) * 16]),
                         reads=["qT"], writes=["qs%d" % b])
                    P.dma(lambda e: e.dma_start(out=vh[0:16, 32, :], in_=vnew[ss * 16:(ss + 1) * 16, hh * 64:(hh + 1) * 64]),
                          reads=["vnew", "vh%d" % b], writes=["vh%d" % b], semkey="vhn")

                for hi, (ss, hh) in enumerate(heads):
                    b = hi % 2
                    hp, m = hh % 2, hh // 2
                    kTs, vh, qs_t = kTs_b[b], vh_b[b], qs_b[b]
                    S_A, S_B = SA_b[b], SB_b[b]
                    fE, tE = fE_b[b], tE_b[b]
                    if hi == 0:
                        items.append(lambda: prologue(heads[0][0], heads[0][1], 0))

                    def pre(S_A=S_A, S_B=S_B, b=b):
                        P.op("pool", lambda e: e.memset(S_A[:, 0:16], 0.0), writes=["SA%d" % b])
                        P.op("pool", lambda e: e.memset(S_B[:, 0:16], 0.0), writes=["SB%d" % b])

                    def post(hp=hp, m=m, ss=ss, S_A=S_A, fE=fE, tE=tE, b=b):
                        bank = dbank()
                        P.op("pe", lambda e: e.matmul(ps[bank][hp * 64:(hp + 1) * 64, 0:16], nones[0:128, 0:64], S_A[:, 0:16], start=True, stop=True),
                             reads=["SA%d" % b, "nones"], writes=[bkey(bank)])
                        P.op("act", lambda e: e.activation(out=fE[hp * 64:(hp + 1) * 64, 0:16], in_=ps[bank][hp * 64:(hp + 1) * 64, 0:16], func=AF.Exp),
                             reads=[bkey(bank)], writes=["fE%d" % b])
                        P.op("dve", lambda e: e.tensor_tensor(out=tE[hp * 64:(hp + 1) * 64, 0:16], in0=ps[7][hp * 64:(hp + 1) * 64, 16:32], in1=fE[hp * 64:(hp + 1) * 64, 0:16], op=ALU.mult),
                             reads=["acc%d" % hp, "fE%d" % b], writes=["tE%d" % b])
                        P.op("dve", lambda e: e.tensor_tensor(out=mix[hp * 64:(hp + 1) * 64, m, ss * 16:(ss + 1) * 16], in0=ps[7][hp * 64:(hp + 1) * 64, 0:16], in1=tE[hp * 64:(hp + 1) * 64, 0:16], op=ALU.add),
                             reads=["acc%d" % hp, "tE%d" % b], writes=["mix%d" % m])

                    chA = list(range(32, 16, -1))
                    chB = list(range(16, -1, -1))
                    seq = []
                    for i in range(17):
                        if i < len(chA):
                            seq.append(("A", i, chA[i]))
                        seq.append(("B", i, chB[i]))
                    for si, (ch, ci, kt) in enumerate(seq):
                        nk = 16 if kt == 32 else 128
                        mask = sbm[0:16, 0:16] if kt == 32 else None
                        k0 = kt * 128
                        isA = (ch == "A")
                        S_c = S_A if isA else S_B
                        Sk = ("SA%d" if isA else "SB%d") % b
                        clen = len(chA) if isA else len(chB)
                        acc_ap = ps[7][hp * 64:(hp + 1) * 64, 0:16] if isA else ps[7][hp * 64:(hp + 1) * 64, 16:32]
                        g = sb_tile(kTs[0:64, k0:k0 + nk], qs_t[0:64, 0:16], vh[0:nk, kt, :], nk, 16, S_c[:, 0:16],
                                    acc_ap, "acc%d" % hp, mask, ci == 0, ci == clen - 1, ["kTs%d" % b, "qs%d" % b], tu, Sk, "vh%d" % b,
                                    acc_start=(si == 0), s_add=(True if isA else (ci != clen - 1)))
                        items.append(wrap(g, pre if si == 0 else None, post if si == len(seq) - 1 else None))
                        tu += 1
                        if si == 6 and hi + 1 < len(heads):
                            items.append(lambda hi=hi: prologue(heads[hi + 1][0], heads[hi + 1][1], (hi + 1) % 2))
                run_pipelined(items)
            if float(os.environ.get("KSTOP", 99)) <= 3:
                return
            wout_and_post(SLAB_EOUT, 16 + 0, U)
            ffn(0, U)

            if float(os.environ.get("KSTOP", 99)) <= 4:
                return
            pre_norm(8, U)
            sl, slk = get_slab(SLAB_OIN + 0)
            for m in range(4):
                bank = proj_fm(sl, slk, m * 128, 128, lambda kc: h[:, kc, 0:U], ["h"], 8, U, 512)
                P.op("act", lambda e, m=m, bank=bank: e.activation(out=uT[:, m, 0:U], in_=ps[bank][:, 0:U], func=AF.Copy), reads=[bkey(bank)], writes=["uT"])
            sl, slk = get_slab(SLAB_OIN + 1)
            for m in range(3):
                bank = proj_fm(sl, slk, m * 128, 128, lambda kc: h[:, kc, 0:U], ["h"], 8, U, 384)
                P.op("dve", lambda e, m=m, bank=bank: e.tensor_copy(out=o[:, m, 0:U], in_=ps[bank][:, 0:U]), reads=[bkey(bank)], writes=["o%d" % m])
            rms_stats([o[:, m, 0:U] for m in range(3)], ["o0", "o1", "o2"], 384, U)
            for m in range(3):
                P.op("dve", lambda e, m=m: e.scalar_tensor_tensor(out=cqT[:, m, 0:U], in0=o[:, m, 0:U], scalar=gcol[:, 76 + m:77 + m], in1=rstd[:, 0:U], op0=ALU.mult, op1=ALU.mult),
                     reads=["o%d" % m, "rstd", "gcol"], writes=["cqT"])
            sl, slk = get_slab(SLAB_OIN + 2)
            for tt in range(NTT):
                bank = proj_tm(sl, slk, 512, tt, TT, 512)
                sm = small4[:, tt, :]
                k_ = "sm%d_" % tt
                ts = tt % 2
                tk = "tmpf%d" % ts
                P.op("dve", lambda e, bank=bank, sm=sm: e.bn_stats(out=sm[0:TT, 0:6], in_=ps[bank][0:TT, 0:512]), reads=[bkey(bank)], writes=[k_ + "bn"])
                P.op("dve", lambda e, sm=sm: e.bn_aggr(out=sm[0:TT, 6:8], in_=sm[0:TT, 0:6]), reads=[k_ + "bn"], writes=[k_ + "mv"])
                P.op("act", lambda e, sm=sm: e.activation(out=sm[0:TT, 8:9], in_=sm[0:TT, 7:8], func=AF.Ln, bias=EPS), reads=[k_ + "mv"], writes=[k_ + "rv"])
                P.op("act", lambda e, sm=sm: e.activation(out=sm[0:TT, 8:9], in_=sm[0:TT, 8:9], func=AF.Exp, scale=-0.5), reads=[k_ + "rv"], writes=[k_ + "rv"])
                P.op("dve", lambda e, bank=bank, sm=sm, ts=ts: e.tensor_scalar(out=tmpf[0:TT, ts, :], in0=ps[bank][0:TT, 0:512], scalar1=sm[0:TT, 6:7], scalar2=sm[0:TT, 8:9], op0=ALU.subtract, op1=ALU.mult),
                     reads=[bkey(bank), k_ + "mv", k_ + "rv"], writes=[tk])
                P.op("pool", lambda e, ts=ts: e.tensor_tensor(out=tmpf[0:TT, ts, :], in0=tmpf[0:TT, ts, :], in1=lnG[0:TT, :], op=ALU.mult), reads=[tk, "lnG"], writes=[tk])
                if isp:
                    P.op("pool", lambda e, tt=tt, ts=ts: e.tensor_tensor(out=vnb[0:TT, tt, :], in0=tmpf[0:TT, ts, :], in1=lnB[0:TT, :], op=ALU.add), reads=[tk, "lnB"], writes=["vnb"])
                else:
                    P.op("pool", lambda e, ts=ts: e.tensor_tensor(out=vnf[0:TT, :], in0=tmpf[0:TT, ts, :], in1=lnB[0:TT, :], op=ALU.add), reads=[tk, "lnB"], writes=["vnf"])
                    P.op("pool", lambda e, tt=tt: e.tensor_copy(out=vnb[0:TT, tt, :], in_=vnf[0:TT, :]), reads=["vnf"], writes=["vnb"])
                    P.dma(lambda e: e.dma_start(out=sguv_s[0:64, :], in_=vnf[0:64, :]), reads=["vnf"], writes=[], semkey="vnfo")
            for g in range(4):
                bank = dbank()

                def fng(e, g=g, bank=bank):
                    ins = None
                    for tt in range(NTT):
                        rhs = wsT[:, g, :] if isp else wsbd[0:64, g, :]
                        ins = e.matmul(ps[bank][:, tt * TT:(tt + 1) * TT], vnb[0:TT, tt, g * 128:(g + 1) * 128], rhs, start=True, stop=True)
                    return ins
                P.op("pe", fng, reads=["vnb", "wsT", "wsbd"], writes=[bkey(bank)])
                if isp:
                    for tt in range(NTT):
                        P.op("dve", lambda e, g=g, tt=tt, bank=bank: e.tensor_tensor(out=tmpf[:, 1, tt * 128:(tt + 1) * 128], in0=ps[bank][:, tt * 128:(tt + 1) * 128], in1=bsB[:, g * 128:(g + 1) * 128], op=ALU.add),
                             reads=[bkey(bank), "bsB"], writes=["tmpf1"])
                else:
                    P.op("dve", lambda e, g=g, bank=bank: e.tensor_tensor(out=tmpf[:, 1, 0:64], in0=ps[bank][:, 0:64], in1=bsBs[:, g, :], op=ALU.add),
                         reads=[bkey(bank), "bsBs"], writes=["tmpf1"])
                P.op("pool", lambda e, g=g: e.tensor_tensor(out=mix[:, g, 0:U], in0=tmpf[:, 1, 0:U], in1=uT[:, g, 0:U], op=ALU.mult), reads=["tmpf1", "uT"], writes=["mix%d" % g])
            sl, slk = get_slab(SLAB_OIN + 3)
            for tt in range(NTT):
                bank = proj_tm(sl, slk, 288, tt, TT, 288)
                tile_i = qb * 4 + tt
                sm = small4[:, tt, :]
                k_ = "sm%d_" % tt
                ts = tt % 2
                tk = "tmpf%d" % ts
                ckst = ckst_b[ts]
                ck = "ckst%d" % ts
                kpb = kpb_b[ts]
                kk = "kpb%d" % ts
                P.op("act", lambda e, bank=bank, sm=sm, ts=ts: e.activation(out=tmpf[0:TT, ts, 0:256], in_=ps[bank][0:TT, 0:256], func=AF.Square, accum_out=sm[0:TT, 10:11]),
                     reads=[bkey(bank)], writes=[tk, k_ + "ss"])
                P.op("act", lambda e, sm=sm: e.activation(out=sm[0:TT, 11:12], in_=sm[0:TT, 10:11], func=AF.Ln, scale=1.0 / 256, bias=EPS), reads=[k_ + "ss"], writes=[k_ + "rk"])
                P.op("act", lambda e, sm=sm: e.activation(out=sm[0:TT, 11:12], in_=sm[0:TT, 11:12], func=AF.Exp, scale=-0.5), reads=[k_ + "rk"], writes=[k_ + "rk"])
                P.op("dve", lambda e, bank=bank, sm=sm, ckst=ckst: e.scalar_tensor_tensor(out=ckst[0:TT, 0:256], in0=ps[bank][0:TT, 0:256], scalar=sm[0:TT, 11:12], in1=kvG[0:TT, :], op0=ALU.mult, op1=ALU.mult),
                     reads=[bkey(bank), k_ + "rk", "kvG"], writes=[ck])
                if isp:
                    cosv = ropeT[0:TT, tile_i * 16:(tile_i + 1) * 16]
                    sinv = ropeT[0:TT, 256 + tile_i * 16:256 + (tile_i + 1) * 16]
                else:
                    cosv = ropeTs[0:TT, 0:16]
                    sinv = ropeTs[0:TT, 16:32]
                x1 = ps[bank][0:TT, 256:272]
                x2 = ps[bank][0:TT, 272:288]
                t16 = sm[0:TT, 16:32]
                P.op("dve", lambda e, x1=x1, cosv=cosv, ckst=ckst: e.tensor_tensor(out=ckst[0:TT, 256:272], in0=x1, in1=cosv, op=ALU.mult), reads=[bkey(bank), "ropeT", ck], writes=[ck])
                P.op("dve", lambda e, x2=x2, sinv=sinv, t16=t16: e.tensor_tensor(out=t16, in0=x2, in1=sinv, op=ALU.mult), reads=[bkey(bank), "ropeT"], writes=[k_ + "t16"])
                P.op("dve", lambda e, ckst=ckst, t16=t16: e.tensor_tensor(out=ckst[0:TT, 256:272], in0=ckst[0:TT, 256:272], in1=t16, op=ALU.subtract), reads=[ck, k_ + "t16"], writes=[ck])
                P.op("dve", lambda e, x1=x1, sinv=sinv, ckst=ckst: e.tensor_tensor(out=ckst[0:TT, 272:288], in0=x1, in1=sinv, op=ALU.mult), reads=[bkey(bank), "ropeT", ck], writes=[ck])
                P.op("dve", lambda e, x2=x2, cosv=cosv, t16=t16: e.tensor_tensor(out=t16, in0=x2, in1=cosv, op=ALU.mult), reads=[bkey(bank), "ropeT", k_ + "t16"], writes=[k_ + "t16"])
                P.op("dve", lambda e, ckst=ckst, t16=t16: e.tensor_tensor(out=ckst[0:TT, 272:288], in0=ckst[0:TT, 272:288], in1=t16, op=ALU.add), reads=[ck, k_ + "t16"], writes=[ck])
                if isp:
                    r0 = row0 + tt * 128
                    P.dma(lambda e, r0=r0, ckst=ckst: e.dma_start(out=ckv_p[s, r0:r0 + 128, :], in_=ckst[0:128, 0:256]), reads=[ck], writes=[], semkey="ckvo%d" % ts)
                    P.dma(lambda e, r0=r0, ckst=ckst: e.dma_start(out=kpe_p[s, r0:r0 + 128, :], in_=ckst[0:128, 256:288]), reads=[ck], writes=[], semkey="kpeo%d" % ts)
                    ctk = ckvtok[:, tile_i, :]
                else:
                    P.dma(lambda e, ckst=ckst: e.dma_start(out=ckv_s[0:64, :], in_=ckst[0:64, 0:256]), reads=[ck], writes=[], semkey="ckvo%d" % ts)
                    P.dma(lambda e, ckst=ckst: e.dma_start(out=kpe_s[0:64, :], in_=ckst[0:64, 256:288]), reads=[ck], writes=[], semkey="kpeo%d" % ts)
                    ctk = ckvn[:, :]
                P.op("pool", lambda e, ctk=ctk, ckst=ckst: e.tensor_copy(out=ctk[0:TT, :], in_=ckst[0:TT, 0:256]), reads=[ck], writes=["ckvtok"])
                P.op("pool", lambda e, ckst=ckst, kpb=kpb: e.tensor_copy(out=kpb[0:TT, :], in_=ckst[0:TT, 256:288]), reads=[ck], writes=[kk])
                bank2 = dbank()

                def fnt(e, ctk=ctk, bank2=bank2, kpb=kpb):
                    e.transpose(psb[bank2][:, 0:TT], ctk[0:TT, 0:128], identb[0:TT, 0:TT])
                    e.transpose(psb[bank2][:, 128:128 + TT], ctk[0:TT, 128:256], identb[0:TT, 0:TT])
                    return e.transpose(psb[bank2][0:32, 256:256 + TT], kpb[0:TT, 0:32], identb[0:TT, 0:TT])
                P.op("pe", fnt, reads=["ckvtok", kk, "identb"], writes=[bkey(bank2)])
                if isp:
                    c0 = tile_i * 128
                    P.op("act", lambda e, c0=c0, bank2=bank2: e.activation(out=ckvT[:, :, c0:c0 + 128], in_=psb[bank2][:, 0:256].rearrange("p (a t) -> p a t", a=2), func=AF.Copy),
                         reads=[bkey(bank2)], writes=["ckvT"])
                    P.op("dve", lambda e, c0=c0, bank2=bank2: e.tensor_copy(out=kpeT[0:32, c0:c0 + 128], in_=psb[bank2][0:32, 256:384]), reads=[bkey(bank2)], writes=["kpeT"])
                else:
                    P.op("act", lambda e, bank2=bank2: e.activation(out=sq[:, 0, 0:64], in_=psb[bank2][:, 0:64], func=AF.Copy), reads=[bkey(bank2)], writes=["sq0"])
                    P.op("act", lambda e, bank2=bank2: e.activation(out=sq[:, 0, 64:128], in_=psb[bank2][:, 128:192], func=AF.Copy), reads=[bkey(bank2)], writes=["sq0"])
                    P.op("act", lambda e, bank2=bank2: e.activation(out=sq[0:32, 0, 128:192], in_=psb[bank2][0:32, 256:320], func=AF.Copy), reads=[bkey(bank2)], writes=["sq0"])
            if float(os.environ.get("KSTOP", 99)) <= 5:
                return
            for m in range(4):
                bank = proj_fm(wuqn[:, :, :].rearrange("p k n -> p (k n)"), "wuqn", m * 128, 128, lambda kc: cqT[:, kc, 0:U], ["cqT"], 3, U, 512)
                P.op("act", lambda e, m=m, bank=bank: e.activation(out=qnT[:, m, 0:U], in_=ps[bank][:, 0:U], func=AF.Copy), reads=[bkey(bank)], writes=["qnT"])
            if isp:
                P.dma(lambda e: e.dma_start(out=cosF[0:32, 0:512], in_=rope_f[0, :, row0:row0 + 512]), writes=["cosF"], semkey="cosF")
                P.dma(lambda e: e.dma_start(out=sinF[0:32, 0:512], in_=rope_f[1, :, row0:row0 + 512]), writes=["sinF"], semkey="sinF")
            else:
                P.dma(lambda e: e.dma_start(out=cosF[0:32, 0:64], in_=rope_fs[0, :, :]), writes=["cosF"], semkey="cosF")
                P.dma(lambda e: e.dma_start(out=sinF[0:32, 0:64], in_=rope_fs[1, :, :]), writes=["sinF"], semkey="sinF")

            def head_q(hh, q0, q1, qb_=0):
                hp, m = hh % 2, hh // 2
                qlat = qlat_b[qb_]
                qpe = qpe_b[qb_]
                for cc in range(2):
                    bank = dbank()
                    P.op("pe", lambda e, cc=cc, bank=bank: e.matmul(ps[bank][:, q0:q1], wukp[hp * 64:(hp + 1) * 64, m, cc * 128:(cc + 1) * 128], qnT[hp * 64:(hp + 1) * 64, m, q0:q1], start=True, stop=True),
                         reads=["wukp", "qnT"], writes=[bkey(bank)])
                    if cc == 0:
                        P.op("act", lambda e, bank=bank: e.activation(out=qlat[:, 0, q0:q1], in_=ps[bank][:, q0:q1], func=AF.Copy), reads=[bkey(bank)], writes=["qlat%d" % qb_])
                    else:
                        P.op("dve", lambda e, bank=bank: e.tensor_copy(out=qlat[:, 1, q0:q1], in_=ps[bank][:, q0:q1]), reads=[bkey(bank)], writes=["qlat%d" % qb_])
                b1 = dbank()
                b2 = dbank()

                def fq(e, b1=b1, b2=b2):
                    ins = None
                    for kc in range(3):
                        e.matmul(ps[b1][0:32, q0:q1], wuqp[:, kc, hh * 32:(hh + 1) * 32], cqT[:, kc, q0:q1], start=(kc == 0), stop=(kc == 2))
                    for kc in range(3):
                        ins = e.matmul(ps[b2][0:32, q0:q1], wuqs[:, kc, hh * 32:(hh + 1) * 32], cqT[:, kc, q0:q1], start=(kc == 0), stop=(kc == 2))
                    return ins
                P.op("pe", fq, reads=["wuqp", "wuqs", "cqT"], writes=[bkey(b1), bkey(b2)])
                P.op("dve", lambda e, b1=b1: e.tensor_tensor(out=tmpf[0:32, 0, q0:q1], in0=ps[b1][0:32, q0:q1], in1=cosF[0:32, q0:q1], op=ALU.mult), reads=[bkey(b1), "cosF"], writes=["tmpf0"])
                P.op("dve", lambda e, b2=b2: e.tensor_tensor(out=tmpf[0:32, 1, q0:q1], in0=ps[b2][0:32, q0:q1], in1=sinF[0:32, q0:q1], op=ALU.mult), reads=[bkey(b2), "sinF"], writes=["tmpf1"])
                P.op("pool", lambda e: e.tensor_tensor(out=qpe[0:32, q0:q1], in0=tmpf[0:32, 0, q0:q1], in1=tmpf[0:32, 1, q0:q1], op=ALU.add), reads=["tmpf0", "tmpf1"], writes=["qpe%d" % qb_])

            def head_out(hh, q0, q1):
                hp, m = hh % 2, hh // 2
                P.op("act", lambda e: e.activation(out=OLs[:, 0, q0:q1], in_=ps[5][:, q0:q1], func=AF.Copy), reads=[bkey(5)], writes=["OLs0"])
                P.op("dve", lambda e: e.tensor_copy(out=OLs[:, 1, q0:q1], in_=ps[6][:, q0:q1]), reads=[bkey(6)], writes=["OLs1"])
                P.op("dve", lambda e: e.reciprocal(out=rden[hp * 64:(hp + 1) * 64, q0:q1], in_=ps[7][hp * 64:(hp + 1) * 64, q0:q1]), reads=[bkey(7)], writes=["rden"])
                bank = dbank()

                def fo(e, bank=bank):
                    e.matmul(ps[bank][hp * 64:(hp + 1) * 64, q0:q1], wuvb[:, hh, 0, :], OLs[:, 0, q0:q1], start=True, stop=False)
                    return e.matmul(ps[bank][hp * 64:(hp + 1) * 64, q0:q1], wuvb[:, hh, 1, :], OLs[:, 1, q0:q1], start=False, stop=True)
                P.op("pe", fo, reads=["wuvb", "OLs0", "OLs1"], writes=[bkey(bank)])
                P.op("dve", lambda e, bank=bank: e.tensor_tensor(out=mix[hp * 64:(hp + 1) * 64, 4 + m, q0:q1], in0=ps[bank][hp * 64:(hp + 1) * 64, q0:q1], in1=rden[hp * 64:(hp + 1) * 64, q0:q1], op=ALU.mult),
                     reads=[bkey(bank), "rden"], writes=["mix%d" % (4 + m)])

            tu = 0
            if isp:
                nt = 4 * qb + 4
                head_q(0, 0, 512, 0)
                for hh in range(8):
                    gens = []
                    for kt in range(nt):
                        j = kt - 4 * qb
                        q0 = 128 * j if j > 0 else 0
                        mask = mlm[:, 0:512 - q0] if j >= 0 else None
                        gens.append(mla_tile(ckvT[:, 0, kt * 128:(kt + 1) * 128], ckvT[:, 1, kt * 128:(kt + 1) * 128], kpeT[0:32, kt * 128:(kt + 1) * 128],
                                             ckvtok[:, kt, :], 128, q0, 512, mask, kt == 0, kt == nt - 1, hh % 2, tu, hh % 2))
                        tu += 1
                    hook = (lambda hh=hh: head_q(hh + 1, 0, 512, (hh + 1) % 2)) if hh < 7 else None
                    run_pipelined(gens, 1, hook)
                    head_out(hh, 0, 512)
            else:
                for ss in range(int(os.environ.get("KLIM_MLA", SPB))):
                    for qtr in range(4):
                        P.dma(lambda e, ss=ss, qtr=qtr: e.dma_start(out=cstg[:, 0:2048].rearrange("p (a d) -> p a d", a=8), in_=cckv[ss, qtr * 1024:(qtr + 1) * 1024, :].rearrange("(a p) d -> p a d", p=128)),
                              writes=["cstg"], semkey="cstg")
                        P.op("dve", lambda e, qtr=qtr: e.tensor_copy(out=ckvtok_s[:, qtr * 8:(qtr + 1) * 8, :], in_=cstg[:, 0:2048].rearrange("p (a d) -> p a d", a=8)), reads=["cstg"], writes=["ckvtok"])
                    P.dma(lambda e, ss=ss: e.dma_start(out=cstg[:, 0:1024].rearrange("p (a d) -> p a d", a=32), in_=ckpe[ss, :, :].rearrange("(a p) d -> p a d", p=128)),
                          writes=["cstg"], semkey="cstg")
                    P.op("dve", lambda e: e.tensor_copy(out=kpbs[:, :, :], in_=cstg[:, 0:1024].rearrange("p (a d) -> p a d", a=32)), reads=["cstg"], writes=["kpbs"])
                    for kt in range(32):
                        for cc in range(2):
                            if (kt * 2 + cc) % 4 == 0:
                                bank = dbank()
                            i4 = (kt * 2 + cc) % 4
                            P.op("pe", lambda e, kt=kt, cc=cc, bank=bank, i4=i4: e.transpose(psb[bank][:, i4 * 128:(i4 + 1) * 128], ckvtok_s[:, kt, cc * 128:(cc + 1) * 128], identb[:, :]),
                                 reads=["ckvtok", "identb"], writes=[bkey(bank)])
                            P.op("act", lambda e, kt=kt, cc=cc, bank=bank, i4=i4: e.activation(out=ckvT_s[:, cc, kt * 128:(kt + 1) * 128], in_=psb[bank][:, i4 * 128:(i4 + 1) * 128], func=AF.Copy),
                                 reads=[bkey(bank)], writes=["ckvT"])
                    for g in range(8):
                        bank = dbank()
                        P.op("pe", lambda e, g=g, bank=bank: [e.transpose(psb[bank][0:32, i * 128:(i + 1) * 128], kpbs[:, 4 * g + i, :], identb[:, :]) for i in range(4)][-1],
                             reads=["kpbs", "identb"], writes=[bkey(bank)])
                        P.op("dve", lambda e, g=g, bank=bank: e.tensor_copy(out=kpeT_s[0:32, g * 512:(g + 1) * 512], in_=psb[bank][0:32, 0:512]), reads=[bkey(bank)], writes=["kpeT"])
                    P.dma(lambda e, ss=ss: e.dma_start(out=ckvtok_s[0:16, 32, :], in_=ckvn[ss * 16:(ss + 1) * 16, :]), reads=["ckvtok"], writes=["ckvtok"], semkey="ckvnn")
                    P.op("act", lambda e, ss=ss: e.activation(out=ckvT_s[:, 0, 4096:4112], in_=sq[:, 0, ss * 16:(ss + 1) * 16], func=AF.Copy), reads=["sq0"], writes=["ckvT"])
                    P.op("act", lambda e, ss=ss: e.activation(out=ckvT_s[:, 1, 4096:4112], in_=sq[:, 0, 64 + ss * 16:64 + (ss + 1) * 16], func=AF.Copy), reads=["sq0"], writes=["ckvT"])
                    P.op("act", lambda e, ss=ss: e.activation(out=kpeT_s[0:32, 4096:4112], in_=sq[0:32, 0, 128 + ss * 16:128 + (ss + 1) * 16], func=AF.Copy), reads=["sq0"], writes=["kpeT"])
                    q0, q1 = ss * 16, ss * 16 + 16
                    for hh in range(8):
                        if hh == 0:
                            head_q(0, q0, q1, 0)
                        gens = []
                        for kt in range(33):
                            nk = 16 if kt == 32 else 128
                            k0 = kt * 128
                            gens.append(mla_tile(ckvT_s[:, 0, k0:k0 + nk], ckvT_s[:, 1, k0:k0 + nk], kpeT_s[0:32, k0:k0 + nk],
                                                 ckvtok_s[0:nk, kt, :], nk, q0, q1, None, kt == 0, kt == 32, hh % 2, tu, hh % 2))
                            tu += 1
                        hook = (lambda hh=hh, q0=q0, q1=q1: head_q(hh + 1, q0, q1, (hh + 1) % 2)) if hh < 7 else None
                        run_pipelined(gens, 1, hook)
                        head_out(hh, q0, q1)

            if float(os.environ.get("KSTOP", 99)) <= 6:
                return
            wout_and_post(SLAB_OOUT, 16 + 8, U)
            ffn(1, U)

            if float(os.environ.get("KSTOP", 99)) <= 7:
                return
            for tt in range(NTT):
                yout = io_b[tt % 2]
                yk = "io%d" % (tt % 2)
                for half in range(2):
                    bank = dbank()
                    P.op("pe", lambda e, half=half, bank=bank, tt=tt: [e.transpose(ps[bank][0:TT, c4 * 128:(c4 + 1) * 128], xT[:, half * 4 + c4, tt * TT:(tt + 1) * TT], identf[:, :]) for c4 in range(4)][-1],
                         reads=["xT%d" % (half * 4 + c4) for c4 in range(4)] + ["identf"], writes=[bkey(bank)])
                    P.op("act", lambda e, half=half, bank=bank, yout=yout: e.activation(out=yout[0:TT, half * 512:(half + 1) * 512], in_=ps[bank][0:TT, :], func=AF.Copy),
                         reads=[bkey(bank)], writes=[yk])
                dst = y_p[s, row0 + tt * 128: row0 + tt * 128 + 128, :] if isp else y_s[0:64, :]
                P.dma(lambda e, dst=dst, yout=yout: e.dma_start(out=dst, in_=yout[0:TT, :]), reads=[yk], writes=[], semkey=yk)
            barrier()

        setup()
        barrier()
        for (kind, s, qb) in units:
            unit(kind, s, qb)
        P.emit(st)
    return nc


def _consts():
    kl = np.arange(128)[:, None]
    x = np.arange(512)[None, :]
    cst = np.zeros((128, 1408), np.float32)
    cst[:, 0:128] = np.eye(128, dtype=np.float32)
    jj = np.arange(128)[:, None]; kk = np.arange(128)[None, :]
    cst[:, 128:256] = -(jj >= kk).astype(np.float32)
    cst[:, 256:768] = (kl < x).astype(np.float32)
    cst[:, 768:1280] = ((kl // 64) <= (x // 64)).astype(np.float32)
    tt = np.arange(128)[:, None]; ss = np.arange(128)[None, :]
    cst[:, 1280:1408] = (ss <= tt).astype(np.float32)
    half = 16
    inv = (10000.0 ** (-(np.arange(half, dtype=np.float32) / np.float32(half)))).astype(np.float32)

    def cs(pos):
        ang = (pos.astype(np.float32)[:, None] * inv[None, :]).astype(np.float32)
        return np.cos(ang.astype(np.float64)).astype(np.float32), np.sin(ang.astype(np.float64)).astype(np.float32)
    cp, sp_ = cs(np.arange(T))
    rope_t = np.zeros((128, 512), np.float32)
    rope_t[:, 0:256] = cp.reshape(16, 128, 16).transpose(1, 0, 2).reshape(128, 256)
    rope_t[:, 256:512] = sp_.reshape(16, 128, 16).transpose(1, 0, 2).reshape(128, 256)
    cs_, ss_ = cs(PAST + np.arange(DS))
    rope_ts = np.zeros((64, 32), np.float32)
    rope_ts[:, 0:16] = np.tile(cs_, (4, 1))
    rope_ts[:, 16:32] = np.tile(ss_, (4, 1))
    rope_f = np.zeros((2, 32, T), np.float32)
    rope_f[0, 0:16] = cp.T; rope_f[0, 16:32] = cp.T
    rope_f[1, 0:16] = -sp_.T; rope_f[1, 16:32] = sp_.T
    rope_fs = np.zeros((2, 32, 64), np.float32)
    rope_fs[0, 0:16] = np.tile(cs_.T, (1, 4)); rope_fs[0, 16:32] = np.tile(cs_.T, (1, 4))
    rope_fs[1, 0:16] = -np.tile(ss_.T, (1, 4)); rope_fs[1, 16:32] = np.tile(ss_.T, (1, 4))
    return cst, rope_t, rope_ts, rope_f, rope_fs


_CACHE = {}


def kernel(x_prompt, x_sample, cache_sb_k, cache_sb_v, state_conv, cache_mla_ckv, cache_mla_kpe,
           mix_pre_g, mix_post_g, ffn_pre_g, ffn_post_g, even_w_in, even_w_conv, even_w_out,
           odd_w_in, sgu_ln_g, sgu_ln_b, sgu_w_s, sgu_b_s, mla_q_norm_g, mla_kv_norm_g,
           mla_w_uq, mla_w_uk, mla_w_uv, odd_w_out, ffn_w_up, ffn_w_down):
    f = lambda a: np.ascontiguousarray(np.asarray(a, dtype=np.float32))
    dbg = os.environ.get("KDEBUG", "")
    if dbg:
        units = []
        for tok in dbg.split(","):
            if tok == "s":
                units.append(("s", 0, 0))
            else:
                a, b = tok.split(":")
                units.append(("p", int(a), int(b)))
    else:
        units = [("p", s, qb) for s in range(SPB) for qb in range(4)] + [("s", 0, 0)]
    key = tuple(units)
    if key not in _CACHE:
        _CACHE[key] = build_program(units)
    nc = _CACHE[key]
    cst, rope_t, rope_ts, rope_f, rope_fs = _consts()
    grow = np.concatenate([f(mix_pre_g).reshape(16, 128), f(mix_post_g).reshape(16, 128), f(ffn_pre_g).reshape(16, 128),
                           f(ffn_post_g).reshape(16, 128), f(even_w_conv).reshape(12, 128), f(mla_q_norm_g).reshape(3, 128)], axis=0)
    shared = dict(
        ein=f(even_w_in)[0], eout=f(even_w_out)[0], oin=f(odd_w_in)[0], oout=f(odd_w_out)[0],
        fup=f(ffn_w_up), fdn=f(ffn_w_down), grow=f(grow),
        lng=f(sgu_ln_g).reshape(1, 512), lnb=f(sgu_ln_b).reshape(1, 512), kvg=f(mla_kv_norm_g).reshape(1, 256),
        bsd=f(sgu_b_s).reshape(1, 512), bss=f(f(sgu_b_s)[0][:, 0:16]).reshape(1, 4, 16),
        wsd=f(sgu_w_s)[0], wuq=f(mla_w_uq)[0], wuk=f(mla_w_uk)[0], wuv=f(mla_w_uv)[0],
        cst=cst, rope_t=rope_t, rope_ts=rope_ts, rope_f=rope_f, rope_fs=rope_fs)
    xp = f(x_prompt); xs = f(x_sample)
    csk = f(cache_sb_k)[0].reshape(32, PAST, 512); csv = f(cache_sb_v)[0].reshape(32, PAST, 512)
    sc = f(state_conv)[0]; cckv = f(cache_mla_ckv)[0]; ckpe = f(cache_mla_kpe)[0]
    in_maps = []
    for c in range(NCORES):
        sl = slice(c * SPB, (c + 1) * SPB)
        m = dict(shared)
        m.update(xp=xp[sl], xs=f(xs[sl].reshape(SPB * DS, D)), csk=csk[sl], csv=csv[sl], sconv=sc[sl], cckv=cckv[sl], ckpe=ckpe[sl])
        in_maps.append(m)
    res = run_bass_kernel_spmd(nc, in_maps, core_ids=list(range(NCORES)))
    R = res.results
    cat = lambda name: np.concatenate([np.asarray(R[c][name], dtype=np.float32) for c in range(NCORES)], axis=0)
    y_p = cat("y_p")
    y_s = cat("y_s").reshape(32, DS, D)
    sbk_p = cat("sbk_p").reshape(1, 32, T, 8, 64)
    sbv_p = cat("sbv_p").reshape(1, 32, T, 8, 64)
    conv_p = cat("conv_p").reshape(1, 32, 2, 512)
    ckv_p = cat("ckv_p").reshape(1, 32, T, 256)
    kpe_p = cat("kpe_p").reshape(1, 32, T, 32)
    sbk_s = cat("sbk_s").reshape(1, 32, DS, 8, 64)
    sbv_s = cat("sbv_s").reshape(1, 32, DS, 8, 64)
    conv_s = cat("conv_s").reshape(1, 32, 2, 512)
    ckv_s = cat("ckv_s").reshape(1, 32, DS, 256)
    kpe_s = cat("kpe_s").reshape(1, 32, DS, 32)
    sguv_s = cat("sguv_s").reshape(1, 32, DS, 512)
    return (y_p, y_s, sbk_p, sbv_p, conv_p, ckv_p, kpe_p, sbk_s, sbv_s, conv_s, ckv_s, kpe_s, sguv_s)
```

```python
import os
import math
import numpy as np
from contextlib import ExitStack
import concourse.bass as bass
import concourse.mybir as mybir
from concourse.bass_utils import run_bass_kernel_spmd

F32 = mybir.dt.float32
BF16 = mybir.dt.bfloat16
AF = mybir.ActivationFunctionType
ALU = mybir.AluOpType

NCORES = 8
SPB = 4
T = 2048
D = 1024
PAST = 4096
DS = 16
EPS = 1e-6
SB_SCALE = 0.125
MLA_SCALE = 1.0 / math.sqrt(96.0)

COMPUTE = ("pe", "act", "dve", "pool")
ALLENG = ("pe", "act", "dve", "pool", "sp")
SAME_ENGINE_SYNC = True


class Prog:
    def __init__(self, nc):
        self.nc = nc
        self.ops = []

    def op(self, eng, fn, reads=(), writes=()):
        self.ops.append(dict(eng=eng, fn=fn, reads=tuple(reads), writes=tuple(writes), dma=False, bar=False))

    def dma(self, fn, reads=(), writes=(), semkey=None, queue="sp"):
        self.ops.append(dict(eng=queue, fn=fn, reads=tuple(reads), writes=tuple(writes), dma=True, bar=False,
                             semkey=semkey))

    def barrier(self, fn):
        self.ops.append(dict(eng="dve", fn=fn, reads=(), writes=(), dma=False, bar=True))

    def resolve(self):
        ops = self.ops
        last_w = {}
        readers = {}
        last_dma_by_sem = {}
        eng_pos = {}
        last_on_eng = {}
        pending = {}
        for i, o in enumerate(ops):
            e = o["eng"]
            o["pos"] = eng_pos.get(e, 0)
            eng_pos[e] = o["pos"] + 1
            deps = set()
            raw = set()
            if o["bar"]:
                for e2, j in last_on_eng.items():
                    deps.add(j)
                for sk, j in last_dma_by_sem.items():
                    deps.add(j)
                raw = set(deps)
                for e2 in ALLENG:
                    pending[e2] = i
                pending.pop("dve", None)
            else:
                if e in pending:
                    deps.add(pending.pop(e))
                for k in o["reads"]:
                    if k in last_w:
                        deps.add(last_w[k]); raw.add(last_w[k])
                    if k.startswith("ps") or k.startswith("acc"):
                        for e2, j in readers.get(k, {}).items():
                            if e2 != e and not isinstance(j, list):
                                deps.add(j)
                for k in o["writes"]:
                    if k in last_w:
                        deps.add(last_w[k])
                    for j in readers.get(k, {}).values():
                        if isinstance(j, list):
                            deps.update(j)
                        else:
                            deps.add(j)
                if o["dma"]:
                    sk = o["semkey"]
                    if sk in last_dma_by_sem:
                        deps.add(last_dma_by_sem[sk])
                    last_dma_by_sem[sk] = i
            deps.discard(i)
            o["deps"] = deps
            o["raw"] = raw
            if not o["dma"]:
                last_on_eng[e] = i
            for k in o["reads"]:
                r = readers.setdefault(k, {})
                if o["dma"]:
                    r.setdefault("dma", []).append(i)
                else:
                    r[e] = i
            for k in o["writes"]:
                last_w[k] = i
                readers[k] = {}
        waited = {}
        waited_dma = {}
        dma_count = {}
        for i, o in enumerate(ops):
            if o["dma"]:
                sk = o["semkey"]
                dma_count[sk] = dma_count.get(sk, 0) + 1
                o["dma_n"] = dma_count[sk]
            o["needs_inc"] = False
        for i, o in enumerate(ops):
            e = o["eng"]
            w = waited.setdefault(e, {})
            wd = waited_dma.setdefault(e, {})
            waits = []
            for j in sorted(o["deps"]):
                p = ops[j]
                if p["dma"]:
                    sk = p["semkey"]
                    if wd.get(sk, 0) >= p["dma_n"]:
                        continue
                    wd[sk] = p["dma_n"]
                    waits.append(("d", sk, p["dma_n"]))
                else:
                    pe_ = p["eng"]
                    if pe_ == e:
                        if e == "pe" or e == "sp":
                            continue
                        if (not SAME_ENGINE_SYNC) or (j not in o["raw"]):
                            continue
                    if w.get(pe_, -1) >= p["pos"]:
                        continue
                    w[pe_] = p["pos"]
                    p["needs_inc"] = True
                    waits.append(("c", pe_, j))
            o["waits"] = waits
        last_idx = {}
        for i, o in enumerate(ops):
            if not o["dma"]:
                last_idx[o["eng"]] = i
        for e_, i in last_idx.items():
            if e_ in COMPUTE:
                ops[i]["needs_inc"] = True
        cnt = {}
        for o in ops:
            if o["dma"]:
                continue
            e = o["eng"]
            if o["needs_inc"]:
                cnt[e] = cnt.get(e, 0) + 1
            o["count"] = cnt.get(e, 0)
        self.sem_keys = sorted({o["semkey"] for o in ops if o["dma"]}, key=str)
        self.final_dma = dict(dma_count)
        return cnt

    def emit(self, stack):
        nc = self.nc
        ops = self.ops
        cnt = self.resolve()
        esem = {e: stack.enter_context(nc.semaphore("s_" + e)) for e in COMPUTE}
        dsem = {k: stack.enter_context(nc.semaphore("d_%d" % i)) for i, k in enumerate(self.sem_keys)}
        block = stack.enter_context(nc.Block())
        by_eng = {}
        for o in ops:
            by_eng.setdefault(o["eng"], []).append(o)

        def run(engname, eng):
            for o in by_eng.get(engname, []):
                for wt in o["waits"]:
                    if wt[0] == "d":
                        eng.wait_ge(dsem[wt[1]], 16 * wt[2])
                    else:
                        eng.wait_ge(esem[wt[1]], ops[wt[2]]["count"])
                ins = o["fn"](eng)
                if o["dma"]:
                    ins.then_inc(dsem[o["semkey"]], 16)
                elif o["needs_inc"]:
                    ins.then_inc(esem[engname], 1)
            if engname == "sp":
                for k, n in self.final_dma.items():
                    eng.wait_ge(dsem[k], 16 * n)
                for e2 in COMPUTE:
                    if cnt.get(e2, 0) > 0:
                        eng.wait_ge(esem[e2], cnt[e2])

        @block.sync
        def _(sync):
            run("sp", sync)

        @block.tensor
        def _(tensor):
            run("pe", tensor)

        @block.scalar
        def _(scalar):
            run("act", scalar)

        @block.vector
        def _(vector):
            run("dve", vector)

        @block.gpsimd
        def _(gpsimd):
            run("pool", gpsimd)


SLAB_EIN = 0
SLAB_EOUT = 7
SLAB_FUP0 = 9
SLAB_FDN0 = 17
SLAB_OIN = 25
SLAB_OOUT = 29
SLAB_FUP1 = 31
SLAB_FDN1 = 39
NSLAB = 47


def build_program(units):
    nc = bass.Bass("TRN2", target_bir_lowering=False)

    def din(name, shape):
        return nc.dram_tensor(name, list(shape), F32, kind="ExternalInput").ap()

    def dout(name, shape):
        return nc.dram_tensor(name, list(shape), F32, kind="ExternalOutput").ap()

    xp = din("xp", [SPB, T, D]); xs = din("xs", [SPB * DS, D])
    csk = din("csk", [SPB, PAST, 512]); csv = din("csv", [SPB, PAST, 512])
    sconv = din("sconv", [SPB, 2, 512])
    cckv = din("cckv", [SPB, PAST, 256]); ckpe = din("ckpe", [SPB, PAST, 32])
    ein = din("ein", [D, 3072]); eout = din("eout", [D, D]); oin = din("oin", [D, 1696]); oout = din("oout", [D, D])
    fup = din("fup", [2, D, 4096]); fdn = din("fdn", [2, 4096, D])
    grow = din("grow", [79, 128])
    lng = din("lng", [1, 512]); lnb = din("lnb", [1, 512]); kvg = din("kvg", [1, 256]); bsd = din("bsd", [1, 512])
    bss = din("bss", [1, 4, 16])
    wsd = din("wsd", [4, 128, 128])
    wuq = din("wuq", [384, 768]); wuk = din("wuk", [8, 64, 256]); wuv = din("wuv", [8, 256, 64])
    cst = din("cst", [128, 1408])
    rope_t = din("rope_t", [128, 512]); rope_ts = din("rope_ts", [64, 32])
    rope_f = din("rope_f", [2, 32, T]); rope_fs = din("rope_fs", [2, 32, 64])

    y_p = dout("y_p", [SPB, T, D]); y_s = dout("y_s", [SPB * DS, D])
    sbk_p = dout("sbk_p", [SPB, T, 512]); sbv_p = dout("sbv_p", [SPB, T, 512])
    conv_p = dout("conv_p", [SPB, 2, 512])
    ckv_p = dout("ckv_p", [SPB, T, 256]); kpe_p = dout("kpe_p", [SPB, T, 32])
    sbk_s = dout("sbk_s", [SPB * DS, 512]); sbv_s = dout("sbv_s", [SPB * DS, 512])
    conv_s = dout("conv_s", [SPB, 2, 512])
    ckv_s = dout("ckv_s", [SPB * DS, 256]); kpe_s = dout("kpe_s", [SPB * DS, 32])
    sguv_s = dout("sguv_s", [SPB * DS, 512])

    wsc = nc.dram_tensor("wsc", [NSLAB, 128, 4096], BF16, kind="Internal").ap()

    P = Prog(nc)
    with ExitStack() as st:
        def sb(name, shape, dt):
            return st.enter_context(nc.sbuf_tensor(name, list(shape), dt))

        xT = sb("xT", [128, 8, 512], F32)
        identf = sb("identf", [128, 128], F32)
        identb = sb("identb", [128, 128], BF16)
        ntri = sb("ntri", [128, 128], BF16)
        nones = sb("nones", [128, 128], BF16)
        onesb = sb("onesb", [128, 128], BF16)
        sbm = sb("sbm", [128, 512], BF16)
        mlm = sb("mlm", [128, 512], BF16)
        gcol = sb("gcol", [128, 80], F32)
        lnG = sb("lnG", [128, 512], F32); lnB = sb("lnB", [128, 512], F32)
        kvG = sb("kvG", [128, 256], F32); bsB = sb("bsB", [128, 512], F32); bsBs = sb("bsBs", [128, 4, 64], F32)
        wsT = sb("wsT", [128, 4, 128], BF16); wsbd = sb("wsbd", [128, 4, 64], BF16)
        wuqn = sb("wuqn", [128, 3, 512], BF16); wuqp = sb("wuqp", [128, 3, 256], BF16)
        wuqs = sb("wuqs", [128, 3, 256], BF16)
        wukp = sb("wukp", [128, 4, 256], BF16); wuvb = sb("wuvb", [128, 8, 2, 64], BF16)
        ropeT = sb("ropeT", [128, 512], F32); ropeTs = sb("ropeTs", [128, 32], F32)
        h = sb("h", [128, 8, 512], BF16)
        mix = sb("mix", [128, 8, 512], BF16)
        o = sb("o", [128, 8, 512], F32)
        rstd = sb("rstd", [128, 512], F32)
        sq = sb("sq", [128, 2, 512], BF16)
        io_b = [sb("io0", [128, 1024], F32), sb("io1", [128, 1024], F32)]
        tmpf = sb("tmpf", [128, 2, 512], F32)
        cstate = sb("cstate", [128, 4, 2], F32)
        small4 = sb("small4", [128, 4, 32], F32)
        bart = sb("bart", [128, 4], F32)
        slab = sb("slab", [128, 2, 4096], BF16)
        AA = sb("arenaA", [128, 21248], BF16)
        AC = sb("arenaC", [128, 26624], BF16)
        ps = [st.enter_context(nc.psum_tensor("ps%d" % i, [128, 512], F32)) for i in range(8)]
        psb = [p_[:, :].bitcast(BF16) for p_ in ps]

        def carve(arena, off_bytes, nelem, dt):
            a0 = off_bytes // 2
            if dt == BF16:
                return arena[:, a0:a0 + nelem]
            return arena[:, a0:a0 + 2 * nelem].bitcast(F32)

        qT = carve(AA, 0, 2048, BF16).rearrange("p (m t) -> p m t", m=4)
        e_t = carve(AA, 4096, 512, F32)
        sp_t = carve(AA, 6144, 1024, BF16).rearrange("p (a t) -> p a t", a=2)
        spm_t = carve(AA, 8192, 1024, BF16).rearrange("p (a t) -> p a t", a=2)
        w_t = carve(AA, 10240, 1024, BF16).rearrange("p (a t) -> p a t", a=2)
        S_b = [carve(AA, 12288, 512, BF16), carve(AA, 22528, 512, BF16)]
        cin = carve(AA, 13312, 520, F32)
        kvout_b = [carve(AA, 15872, 1024, F32).rearrange("p (a t) -> p a t", a=2), carve(AA, 24576, 1024, F32).rearrange("p (a t) -> p a t", a=2)]
        ksn = carve(AA, 19968, 256, BF16).rearrange("p (m t) -> p m t", m=4)
        vnew = carve(AA, 20480, 512, BF16)
        qs_b = [carve(AA, 21504, 16, BF16), carve(AA, 21568, 16, BF16)]
        SA_b = [carve(AA, 28672, 16, BF16), carve(AA, 28736, 16, BF16)]
        SB_b = [carve(AA, 28800, 16, BF16), carve(AA, 28864, 16, BF16)]
        fE_b = [carve(AA, 28928, 16, F32), carve(AA, 29056, 16, F32)]
        tE_b = [carve(AA, 29184, 16, F32), carve(AA, 29312, 16, F32)]
        cqT = carve(AA, 0, 1536, BF16).rearrange("p (m t) -> p m t", m=3)
        qnT = carve(AA, 3072, 2048, BF16).rearrange("p (m t) -> p m t", m=4)
        uT = carve(AA, 7168, 2048, F32).rearrange("p (m t) -> p m t", m=4)
        vnb = carve(AA, 15360, 2048, BF16).rearrange("p (a t) -> p a t", a=4)
        ckst_b = [carve(AA, 19456, 288, F32), carve(AA, 40960, 288, F32)]
        qlat_b = [carve(AA, 20992, 1024, BF16).rearrange("p (a t) -> p a t", a=2), carve(AA, 37888, 1024, BF16).rearrange("p (a t) -> p a t", a=2)]
        qpe_b = [carve(AA, 23040, 512, BF16), carve(AA, 39936, 512, BF16)]
        OLs = carve(AA, 24064, 1024, BF16).rearrange("p (a t) -> p a t", a=2)
        rden = carve(AA, 26112, 512, F32)
        cosF = carve(AA, 28160, 512, F32)
        sinF = carve(AA, 30208, 512, F32)
        p_t = carve(AA, 32256, 1024, BF16).rearrange("p (a t) -> p a t", a=2)
        vnf = carve(AA, 34304, 512, F32)
        kpb_b = [carve(AA, 36352, 32, BF16), carve(AA, 42112, 32, BF16)]
        ckvn = carve(AA, 36416, 256, BF16)
        h1 = carve(AA, 0, 16384, BF16).rearrange("p (j t) -> p j t", j=32)
        cstS = carve(AA, 0, 1408, F32)
        growS = carve(AA, 5632, 128, F32)
        wsS = carve(AA, 6144, 512, F32).rearrange("p (g s) -> p g s", g=4)
        wuqS = carve(AA, 8192, 2304, F32).rearrange("p (k n) -> p k n", k=3)
        wukS = carve(AA, 17408, 1024, F32).rearrange("p (j c) -> p j c", j=4)
        wuvS = carve(AA, 21504, 1024, F32).rearrange("p (a b) -> p a b", a=16)

        ksT = carve(AC, 0, 8192, BF16).rearrange("p (m t) -> p m t", m=4)
        vtok = carve(AC, 16384, 8192, BF16).rearrange("p (a t) -> p a t", a=16)
        ckvT = carve(AC, 32768, 4096, BF16).rearrange("p (a t) -> p a t", a=2)
        ckvtok = carve(AC, 40960, 4096, BF16).rearrange("p (a t) -> p a t", a=16)
        kpeT = carve(AC, 49152, 2048, BF16)
        kld = carve(AC, 0, 2048, F32).rearrange("p (a d) -> p a d", a=32)
        vld = carve(AC, 8192, 2048, F32).rearrange("p (a d) -> p a d", a=32)
        kTs_b = [carve(AC, 16384, 4112, BF16), carve(AC, 24640, 4112, BF16)]
        vh_b = [carve(AC, 32896, 33 * 64, BF16).rearrange("p (a d) -> p a d", a=33), carve(AC, 37120, 33 * 64, BF16).rearrange("p (a d) -> p a d", a=33)]
        kbf_b = [carve(AC, 41344, 2048, BF16).rearrange("p (a d) -> p a d", a=32), carve(AC, 45440, 2048, BF16).rearrange("p (a d) -> p a d", a=32)]
        cstg = carve(AC, 0, 2048, F32)
        ckvtok_s = carve(AC, 8192, 33 * 256, BF16).rearrange("p (a d) -> p a d", a=33)
        ckvT_s = carve(AC, 25088, 2 * 4112, BF16).rearrange("p (a t) -> p a t", a=2)
        kpeT_s = carve(AC, 41536, 4112, BF16)
        kpbs = carve(AC, 49792, 1024, BF16).rearrange("p (a d) -> p a d", a=32)
        stg = carve(AC, 0, 8192, F32).rearrange("p (a n) -> p a n", a=2)
        sbf = carve(AC, 32768, 8192, BF16).rearrange("p (a n) -> p a n", a=2)

        state = dict(dbank=0, zb=0, bb=0, slabk=0, slab_issued=0, tmp=0)

        def dbank():
            b = state["dbank"] % 3
            state["dbank"] += 1
            return b

        def bkey(b):
            return "ps%d" % b

        def barrier():
            P.barrier(lambda e: e.memset(bart[:, 0:4], 0.0))

        def setup():
            P.dma(lambda e: e.dma_start(out=cstS[:, :], in_=cst[:, :]), writes=["cstS"], semkey="su0")
            P.op("act", lambda e: e.activation(out=identf[:, :], in_=cstS[:, 0:128], func=AF.Copy), reads=["cstS"], writes=["identf"])
            P.op("dve", lambda e: e.tensor_copy(out=identb[:, :], in_=cstS[:, 0:128]), reads=["cstS"], writes=["identb"])
            P.op("dve", lambda e: e.tensor_copy(out=ntri[:, :], in_=cstS[:, 128:256]), reads=["cstS"], writes=["ntri"])
            P.op("dve", lambda e: e.tensor_copy(out=sbm[:, :], in_=cstS[:, 256:768]), reads=["cstS"], writes=["sbm"])
            P.op("dve", lambda e: e.tensor_copy(out=mlm[:, :], in_=cstS[:, 768:1280]), reads=["cstS"], writes=["mlm"])
            P.op("pool", lambda e: e.memset(nones[:, :], -1.0), writes=["nones"])
            P.op("pool", lambda e: e.memset(onesb[:, :], 1.0), writes=["onesb"])
            P.op("pool", lambda e: e.memset(wsbd[:, :, :], 0.0), writes=["wsbd"])
            P.op("pool", lambda e: e.memset(cstate[:, :, :], 0.0), writes=["cstate"])
            P.dma(lambda e: e.dma_start(out=growS[0:79, :], in_=grow[:, :]), writes=["growS"], semkey="su1")
            P.op("pe", lambda e: e.transpose(ps[0][:, 0:79], growS[0:79, :], identf[0:79, 0:79]), reads=["growS", "identf"], writes=[bkey(0)])
            P.op("act", lambda e: e.activation(out=gcol[:, 0:79], in_=ps[0][:, 0:79], func=AF.Copy), reads=[bkey(0)], writes=["gcol"])
            P.dma(lambda e: e.dma_start(out=lnG[:, :], in_=lng[0:1, :].broadcast_to([128, 512])), writes=["lnG"], semkey="su2")
            P.dma(lambda e: e.dma_start(out=lnB[:, :], in_=lnb[0:1, :].broadcast_to([128, 512])), writes=["lnB"], semkey="su3")
            P.dma(lambda e: e.dma_start(out=kvG[:, :], in_=kvg[0:1, :].broadcast_to([128, 256])), writes=["kvG"], semkey="su4")
            P.dma(lambda e: e.dma_start(out=bsB[:, :], in_=bsd[0:1, :].broadcast_to([128, 512])), writes=["bsB"], semkey="su5")
            for j in range(4):
                P.dma(lambda e, j=j: e.dma_start(out=bsBs[:, :, j * 16:(j + 1) * 16], in_=bss[0:1, :, :].broadcast_to([128, 4, 16])),
                      writes=["bsBs"], semkey="su6")
            P.dma(lambda e: e.dma_start(out=ropeT[:, :], in_=rope_t[:, :]), writes=["ropeT"], semkey="su7")
            P.dma(lambda e: e.dma_start(out=ropeTs[0:64, :], in_=rope_ts[:, :]), writes=["ropeTs"], semkey="su8")
            P.dma(lambda e: e.dma_start(out=wsS[:, :, :], in_=wsd.rearrange("g t s -> t g s")), writes=["wsS"], semkey="su9")
            for g in range(4):
                P.op("dve", lambda e, g=g: e.tensor_tensor(out=wsS[:, g, :], in0=wsS[:, g, :], in1=cstS[:, 1280:1408], op=ALU.mult),
                     reads=["wsS", "cstS"], writes=["wsS"])
            P.op("pe", lambda e: [e.transpose(ps[1][:, g * 128:(g + 1) * 128], wsS[:, g, :], identf[:, :]) for g in range(4)][-1],
                 reads=["wsS", "identf"], writes=[bkey(1)])
            P.op("act", lambda e: e.activation(out=wsT[:, :, :], in_=ps[1][:, :].rearrange("p (g t) -> p g t", g=4), func=AF.Copy),
                 reads=[bkey(1)], writes=["wsT"])
            for j in range(4):
                P.dma(lambda e, j=j: e.dma_start(out=wsbd[16 * j:16 * j + 16, :, 16 * j:16 * j + 16], in_=wsT[0:16, :, 0:16]),
                      reads=["wsT", "wsbd"], writes=["wsbd"], semkey="su10")
            P.dma(lambda e: e.dma_start(out=wuqS[:, :, :], in_=wuq.rearrange("(k p) n -> p k n", p=128)), writes=["wuqS"], semkey="su11")
            for kc in range(3):
                src = wuqS[:, kc, :].rearrange("p (h d) -> p h d", d=96)
                P.op("dve", lambda e, kc=kc, src=src: e.tensor_copy(out=wuqn[:, kc, :].rearrange("p (h d) -> p h d", d=64), in_=src[:, :, 0:64]),
                     reads=["wuqS"], writes=["wuqn"])
                P.op("dve", lambda e, kc=kc, src=src: e.tensor_copy(out=wuqp[:, kc, :].rearrange("p (h d) -> p h d", d=32), in_=src[:, :, 64:96]),
                     reads=["wuqS"], writes=["wuqp"])
                P.op("pool", lambda e, kc=kc, src=src: e.tensor_copy(out=wuqs[:, kc, :].rearrange("p (h d) -> p h d", d=32)[:, :, 0:16], in_=src[:, :, 80:96]),
                     reads=["wuqS"], writes=["wuqs"])
                P.op("pool", lambda e, kc=kc, src=src: e.tensor_copy(out=wuqs[:, kc, :].rearrange("p (h d) -> p h d", d=32)[:, :, 16:32], in_=src[:, :, 64:80]),
                     reads=["wuqS"], writes=["wuqs"])
            for two in range(2):
                P.dma(lambda e, two=two: e.dma_start(out=wukS[two * 64:(two + 1) * 64, :, :], in_=wuk.rearrange("(j two) n c -> two n j c", two=2)[two]),
                      writes=["wukS"], semkey="su12")
            P.op("dve", lambda e: e.tensor_copy(out=wukp[:, :, :], in_=wukS[:, :, :]), reads=["wukS"], writes=["wukp"])
            P.dma(lambda e: e.dma_start(out=wuvS[:, :, :], in_=wuv.rearrange("h (cc p) v -> p (h cc) v", p=128)), writes=["wuvS"], semkey="su13")
            P.op("dve", lambda e: e.tensor_copy(out=wuvb[:, :, :, :].rearrange("p h c v -> p (h c) v"), in_=wuvS[:, :, :]), reads=["wuvS"], writes=["wuvb"])

            def Wv(W, c0, c1):
                return W.rearrange("(kc p) n -> p kc n", p=128)[:, :, c0:c1]

            slabs = []
            for c0 in (0, 512, 1024):
                slabs.append(([(Wv(ein, c0, c0 + 512), 0, 512)], 8, 512))
            for c in range(4):
                slabs.append(([(Wv(ein, 1536 + 128 * c, 1536 + 128 * c + 128), 0, 128),
                               (Wv(ein, 2048 + 128 * c, 2048 + 128 * c + 128), 128, 128),
                               (Wv(ein, 2560 + 128 * c, 2560 + 128 * c + 128), 256, 128)], 8, 384))
            for c0 in (0, 512):
                slabs.append(([(Wv(eout, c0, c0 + 512), 0, 512)], 8, 512))
            for j in range(8):
                slabs.append(([(Wv(fup[0], 512 * j, 512 * j + 512), 0, 512)], 8, 512))
            for n in range(8):
                slabs.append(([(Wv(fdn[0], 128 * n, 128 * n + 128), 0, 128)], 32, 128))
            slabs.append(([(Wv(oin, 0, 512), 0, 512)], 8, 512))
            slabs.append(([(Wv(oin, 1024, 1408), 0, 384)], 8, 384))
            slabs.append(([(Wv(oin, 512, 1024), 0, 512)], 8, 512))
            slabs.append(([(Wv(oin, 1408, 1696), 0, 288)], 8, 288))
            for c0 in (0, 512):
                slabs.append(([(Wv(oout, c0, c0 + 512), 0, 512)], 8, 512))
            for j in range(8):
                slabs.append(([(Wv(fup[1], 512 * j, 512 * j + 512), 0, 512)], 8, 512))
            for n in range(8):
                slabs.append(([(Wv(fdn[1], 128 * n, 128 * n + 128), 0, 128)], 32, 128))
            assert len(slabs) == NSLAB
            engs = ["dve", "act", "pool"]
            for i, (pieces, kc, nct) in enumerate(slabs):
                b = i % 2
                n = kc * nct
                sv = stg[:, b, 0:n].rearrange("p (k n) -> p k n", k=kc)
                for pi, (src, coff, ncol) in enumerate(pieces):
                    P.dma(lambda e, sv=sv, src=src, coff=coff, ncol=ncol: e.dma_start(out=sv[:, :, coff:coff + ncol], in_=src),
                          writes=["stg%d" % b], semkey="stg%d_%d" % (b, pi))
                en = engs[i % 3]
                if en == "act":
                    P.op("act", lambda e, b=b, n=n: e.activation(out=sbf[:, b, 0:n], in_=stg[:, b, 0:n], func=AF.Copy),
                         reads=["stg%d" % b], writes=["sbf%d" % b])
                else:
                    P.op(en, lambda e, b=b, n=n: e.tensor_copy(out=sbf[:, b, 0:n], in_=stg[:, b, 0:n]),
                         reads=["stg%d" % b], writes=["sbf%d" % b])
                P.dma(lambda e, i=i, b=b, n=n: e.dma_start(out=wsc[i, :, 0:n], in_=sbf[:, b, 0:n]),
                      reads=["sbf%d" % b], writes=["wsc%d" % i], semkey="sbfo%d" % b)

        unit_slab_seq = list(range(NSLAB))
        total_slabs = len(units) * NSLAB

        slab_n = {3: 3072, 4: 3072, 5: 3072, 6: 3072, SLAB_OIN + 1: 3072, SLAB_OIN + 3: 2304}

        def issue_slab(k):
            idx = unit_slab_seq[k % NSLAB]
            b = k % 2
            n = slab_n.get(idx, 4096)
            P.dma(lambda e, idx=idx, b=b, n=n: e.dma_start(out=slab[:, b, 0:n], in_=wsc[idx, :, 0:n]),
                  reads=["wsc%d" % idx], writes=["slab%d" % b], semkey="slab%d" % b)

        def get_slab(expect):
            k = state["slabk"]
            assert unit_slab_seq[k % NSLAB] == expect, (k, expect)
            while state["slab_issued"] <= min(k + 1, total_slabs - 1):
                issue_slab(state["slab_issued"])
                state["slab_issued"] += 1
            state["slabk"] += 1
            b = k % 2
            return slab[:, b, :], "slab%d" % b

        def rms_stats(srcs, keys, Dn, U):
            bank = dbank()
            n = len(srcs)
            for c in range(n):
                P.op("act", lambda e, c=c: e.activation(out=sq[:, c % 2, 0:U], in_=srcs[c], func=AF.Square),
                     reads=[keys[c]], writes=["sq%d" % (c % 2)])
                P.op("pe", lambda e, c=c: e.matmul(ps[bank][:, 0:U], onesb[:, :], sq[:, c % 2, 0:U], start=(c == 0), stop=(c == n - 1)),
                     reads=["sq%d" % (c % 2), "onesb"], writes=[bkey(bank)])
            P.op("act", lambda e: e.activation(out=rstd[:, 0:U], in_=ps[bank][:, 0:U], func=AF.Ln, scale=1.0 / Dn, bias=EPS),
                 reads=[bkey(bank)], writes=["rstd"])
            P.op("act", lambda e: e.activation(out=rstd[:, 0:U], in_=rstd[:, 0:U], func=AF.Exp, scale=-0.5),
                 reads=["rstd"], writes=["rstd"])

        def pre_norm(gi, U):
            rms_stats([xT[:, c, 0:U] for c in range(8)], ["xT%d" % c for c in range(8)], 1024, U)
            for c in range(8):
                P.op("dve", lambda e, c=c: e.scalar_tensor_tensor(out=h[:, c, 0:U], in0=xT[:, c, 0:U], scalar=gcol[:, gi + c:gi + c + 1],
                                                                 in1=rstd[:, 0:U], op0=ALU.mult, op1=ALU.mult),
                     reads=["xT%d" % c, "rstd", "gcol"], writes=["h"])

        def post_norm_residual(gi, U):
            rms_stats([o[:, c, 0:U] for c in range(8)], ["o%d" % c for c in range(8)], 1024, U)
            for c in range(8):
                P.op("dve", lambda e, c=c: e.scalar_tensor_tensor(out=o[:, c, 0:U], in0=o[:, c, 0:U], scalar=gcol[:, gi + c:gi + c + 1],
                                                                 in1=rstd[:, 0:U], op0=ALU.mult, op1=ALU.mult),
                     reads=["o%d" % c, "rstd", "gcol"], writes=["o%d" % c])
                P.op("pool", lambda e, c=c: e.tensor_tensor(out=xT[:, c, 0:U], in0=xT[:, c, 0:U], in1=o[:, c, 0:U], op=ALU.add),
                     reads=["xT%d" % c, "o%d" % c], writes=["xT%d" % c])

        def proj_fm(sl, slk, col0, ncol, rhs_fn, rkeys, nkc, U, stride):
            bank = dbank()
            sv = sl[:, 0:nkc * stride].rearrange("p (k n) -> p k n", k=nkc)

            def fn(e):
                ins = None
                for kc in range(nkc):
                    ins = e.matmul(ps[bank][0:ncol, 0:U], sv[:, kc, col0:col0 + ncol], rhs_fn(kc), start=(kc == 0), stop=(kc == nkc - 1))
                return ins
            P.op("pe", fn, reads=[slk] + list(rkeys), writes=[bkey(bank)])
            return bank

        def proj_tm(sl, slk, ncol, tt, TT, stride):
            bank = dbank()
            sv = sl[:, 0:8 * stride].rearrange("p (k n) -> p k n", k=8)

            def fn(e):
                ins = None
                for kc in range(8):
                    ins = e.matmul(ps[bank][0:TT, 0:ncol], h[:, kc, tt * TT:(tt + 1) * TT], sv[:, kc, 0:ncol], start=(kc == 0), stop=(kc == 7))
                return ins
            P.op("pe", fn, reads=[slk, "h"], writes=[bkey(bank)])
            return bank

        def wout_and_post(slab0, gi, U):
            for sl_i in range(2):
                sl, slk = get_slab(slab0 + sl_i)
                for mm in range(4):
                    n = sl_i * 4 + mm
                    bank = proj_fm(sl, slk, mm * 128, 128, lambda kc: mix[:, kc, 0:U], ["mix%d" % c for c in range(8)], 8, U, 512)
                    P.op("dve", lambda e, n=n, bank=bank: e.tensor_copy(out=o[:, n, 0:U], in_=ps[bank][:, 0:U]),
                         reads=[bkey(bank)], writes=["o%d" % n])
            post_norm_residual(gi, U)

        def ffn(layer, U):
            pre_norm(32 + layer * 8, U)
            barrier()
            s_up = SLAB_FUP0 if layer == 0 else SLAB_FUP1
            s_dn = SLAB_FDN0 if layer == 0 else SLAB_FDN1
            for j in range(8):
                sl, slk = get_slab(s_up + j)
                for mm in range(4):
                    bank = proj_fm(sl, slk, mm * 128, 128, lambda kc: h[:, kc, 0:U], ["h"], 8, U, 512)
                    tb = state["tmp"] % 2
                    state["tmp"] += 1
                    P.op("act", lambda e, bank=bank, tb=tb: e.activation(out=tmpf[:, tb, 0:U], in_=ps[bank][:, 0:U], func=AF.Relu),
                         reads=[bkey(bank)], writes=["tmpf%d" % tb])
                    P.op("dve", lambda e, bank=bank, tb=tb, jj=4 * j + mm: e.tensor_tensor(out=h1[:, jj, 0:U], in0=tmpf[:, tb, 0:U], in1=ps[bank][:, 0:U], op=ALU.mult),
                         reads=[bkey(bank), "tmpf%d" % tb], writes=["h1_%d" % (4 * j + mm)])
            for n in range(8):
                sl, slk = get_slab(s_dn + n)
                bank = proj_fm(sl, slk, 0, 128, lambda kc: h1[:, kc, 0:U], ["h1_%d" % c for c in range(32)], 32, U, 128)
                P.op("dve", lambda e, n=n, bank=bank: e.tensor_copy(out=o[:, n, 0:U], in_=ps[bank][:, 0:U]),
                     reads=[bkey(bank)], writes=["o%d" % n])
            post_norm_residual(48 + layer * 8, U)
            barrier()

        def sb_tile(kAP, qAP, vAP, nk, ncols, S_ap, acc_ap, acc_key, mask, first, last, rk, tu, Sk="S0", vk="vtok", acc_start=None, s_add=None):
            if acc_start is None:
                acc_start = first
            if s_add is None:
                s_add = not last
            zb = 3 + (tu % 2)
            bb = 5 + (tu % 2)
            a = tu % 2
            P.op("pe", lambda e: e.matmul(ps[zb][0:nk, 0:ncols], kAP, qAP, start=True, stop=True), reads=rk, writes=[bkey(zb)])
            P.op("act", lambda e: e.activation(out=e_t[0:nk, 0:ncols], in_=ps[zb][0:nk, 0:ncols], func=AF.Exp), reads=[bkey(zb)], writes=["e_t"])
            P.op("act", lambda e: e.activation(out=sp_t[0:nk, a, 0:ncols], in_=e_t[0:nk, 0:ncols], func=AF.Ln, bias=1.0), reads=["e_t"], writes=["sp%d" % a])
            if mask is not None:
                P.op("dve", lambda e: e.tensor_tensor(out=spm_t[0:nk, a, 0:ncols], in0=sp_t[0:nk, a, 0:ncols], in1=mask, op=ALU.mult),
                     reads=["sp%d" % a, "sbm"], writes=["spm%d" % a])
                spm = spm_t[0:nk, a, 0:ncols]
                spk = "spm%d" % a
            else:
                spm = sp_t[0:nk, a, 0:ncols]
                spk = "sp%d" % a
            yield

            def fnB(e):
                e.matmul(ps[bb][0:nk, 0:ncols], ntri[0:nk, 0:nk], spm, start=True, stop=False)
                if not first:
                    e.matmul(ps[bb][0:nk, 0:ncols], nones[0:128, 0:nk], S_ap, start=False, stop=False)
                return e.matmul(ps[bb][0:nk, 0:ncols], kAP, qAP, start=False, stop=True)
            P.op("pe", fnB, reads=[spk, Sk, "ntri", "nones"] + list(rk), writes=[bkey(bb)])
            if s_add:
                P.op("pool", lambda e: e.tensor_tensor(out=S_ap[0:nk, :], in0=S_ap[0:nk, :], in1=spm, op=ALU.add), reads=[spk, Sk], writes=[Sk])
            P.op("act", lambda e: e.activation(out=w_t[0:nk, a, 0:ncols], in_=ps[bb][0:nk, 0:ncols], func=AF.Exp), reads=[bkey(bb)], writes=["w%d" % a])
            if mask is not None:
                P.op("dve", lambda e: e.tensor_tensor(out=w_t[0:nk, a, 0:ncols], in0=w_t[0:nk, a, 0:ncols], in1=mask, op=ALU.mult),
                     reads=["w%d" % a, "sbm"], writes=["w%d" % a])
            yield
            P.op("pe", lambda e: e.matmul(acc_ap, vAP, w_t[0:nk, a, 0:ncols], start=acc_start, stop=last), reads=["w%d" % a, vk], writes=[acc_key])

        def wrap(g, pre=None, post=None):
            if pre is not None:
                pre()
            try:
                while True:
                    next(g)
                    yield
            except StopIteration:
                pass
            if post is not None:
                post()

        def run_pipelined(gens, hook_step=None, hook=None):
            pending = list(gens)
            active = []
            step = 0
            while pending or active:
                while pending and not hasattr(pending[0], "__next__"):
                    pending.pop(0)()
                if pending:
                    active.append(pending.pop(0))
                nxt = []
                for g in reversed(active):
                    try:
                        next(g)
                        nxt.append(g)
                    except StopIteration:
                        pass
                active = list(reversed(nxt))
                if hook is not None and step == hook_step:
                    hook()
                step += 1
            if hook is not None and step <= hook_step:
                hook()

        def mla_tile(cT0, cT1, kpT, ctok, nk, q0, q1, mask, first, last, hp, tu, qb_=0):
            zb = 3 + (tu % 2)
            a = tu % 2
            qlat = qlat_b[qb_]
            qpe = qpe_b[qb_]

            def fnZ(e):
                e.matmul(ps[zb][0:nk, q0:q1], cT0, qlat[:, 0, q0:q1], start=True, stop=False)
                e.matmul(ps[zb][0:nk, q0:q1], cT1, qlat[:, 1, q0:q1], start=False, stop=False)
                return e.matmul(ps[zb][0:nk, q0:q1], kpT, qpe[0:32, q0:q1], start=False, stop=True)
            P.op("pe", fnZ, reads=["ckvT", "kpeT", "qlat%d" % qb_, "qpe%d" % qb_], writes=[bkey(zb)])
            P.op("act", lambda e: e.activation(out=p_t[0:nk, a, q0:q1], in_=ps[zb][0:nk, q0:q1], func=AF.Exp, scale=MLA_SCALE),
                 reads=[bkey(zb)], writes=["p%d" % a])
            if mask is not None:
                P.op("dve", lambda e: e.tensor_tensor(out=p_t[0:nk, a, q0:q1], in0=p_t[0:nk, a, q0:q1], in1=mask, op=ALU.mult),
                     reads=["p%d" % a, "mlm"], writes=["p%d" % a])
            yield

            def fnO(e):
                e.matmul(ps[5][:, q0:q1], ctok[:, 0:128], p_t[0:nk, a, q0:q1], start=first, stop=last)
                e.matmul(ps[6][:, q0:q1], ctok[:, 128:256], p_t[0:nk, a, q0:q1], start=first, stop=last)
                return e.matmul(ps[7][:, q0:q1], onesb[0:nk, :], p_t[0:nk, a, q0:q1], start=first, stop=last)
            P.op("pe", fnO, reads=["p%d" % a, "ckvtok", "onesb"], writes=[bkey(5), bkey(6), bkey(7)])

        def unit(kind, s, qb):
            isp = (kind == "p")
            U = 512 if isp else 64
            TT = 128 if isp else 64
            NTT = U // TT
            row0 = qb * 512

            for tt in range(NTT):
                src = xp[s, row0 + tt * 128: row0 + tt * 128 + 128, :] if isp else xs[0:64, :]
                xin = io_b[tt % 2]
                xk = "io%d" % (tt % 2)
                P.dma(lambda e, src=src, xin=xin: e.dma_start(out=xin[0:TT, :], in_=src), writes=[xk], semkey=xk)
                for half in range(2):
                    bank = dbank()
                    P.op("pe", lambda e, half=half, bank=bank, xin=xin: [e.transpose(ps[bank][:, c4 * TT:(c4 + 1) * TT], xin[0:TT, (half * 4 + c4) * 128:(half * 4 + c4 + 1) * 128], identf[0:TT, 0:TT]) for c4 in range(4)][-1],
                         reads=[xk, "identf"], writes=[bkey(bank)])
                    P.op("act", lambda e, half=half, bank=bank, tt=tt: e.activation(out=xT[:, half * 4:half * 4 + 4, tt * TT:(tt + 1) * TT],
                                                                                in_=ps[bank][:, 0:4 * TT].rearrange("p (a t) -> p a t", a=4), func=AF.Copy),
                         reads=[bkey(bank)], writes=["xT%d" % (half * 4 + c4) for c4 in range(4)])
            if float(os.environ.get("KSTOP", 99)) <= 1:
                return
            pre_norm(0, U)
            if float(os.environ.get("KSTOP", 99)) <= 1.2:
                return
            sl, slk = get_slab(SLAB_EIN + 0)
            for m in range(4):
                bank = proj_fm(sl, slk, m * 128, 128, lambda kc: h[:, kc, 0:U], ["h"], 8, U, 512)
                P.op("act", lambda e, m=m, bank=bank: e.activation(out=qT[:, m, 0:U], in_=ps[bank][:, 0:U], func=AF.Copy),
                     reads=[bkey(bank)], writes=["qT"])
            if float(os.environ.get("KSTOP", 99)) <= 1.4:
                return
            sl, slk = get_slab(SLAB_EIN + 1)
            for m in range(4):
                bank = proj_fm(sl, slk, m * 128, 128, lambda kc: h[:, kc, 0:U], ["h"], 8, U, 512)
                if isp:
                    P.op("act", lambda e, m=m, bank=bank: e.activation(out=ksT[:, m, row0:row0 + 512], in_=ps[bank][:, 0:U], func=AF.Copy, scale=SB_SCALE),
                         reads=[bkey(bank)], writes=["ksT"])
                else:
                    P.op("act", lambda e, m=m, bank=bank: e.activation(out=ksn[:, m, 0:U], in_=ps[bank][:, 0:U], func=AF.Copy, scale=SB_SCALE),
                         reads=[bkey(bank)], writes=["ksn"])
            for tt in range(NTT):
                bank = proj_tm(sl, slk, 512, tt, TT, 512)
                kvout = kvout_b[tt % 2]
                kk0 = "kvout0_%d" % (tt % 2)
                P.op("dve", lambda e, bank=bank, kvout=kvout: e.tensor_copy(out=kvout[0:TT, 0, :], in_=ps[bank][0:TT, :]), reads=[bkey(bank)], writes=[kk0])
                dst = sbk_p[s, row0 + tt * 128: row0 + tt * 128 + 128, :] if isp else sbk_s[0:64, :]
                P.dma(lambda e, dst=dst, kvout=kvout: e.dma_start(out=dst, in_=kvout[0:TT, 0, :]), reads=[kk0], writes=[], semkey=kk0)
            if float(os.environ.get("KSTOP", 99)) <= 1.6:
                return
            sl, slk = get_slab(SLAB_EIN + 2)
            for tt in range(NTT):
                bank = proj_tm(sl, slk, 512, tt, TT, 512)
                kvout = kvout_b[tt % 2]
                kk1 = "kvout1_%d" % (tt % 2)
                P.op("dve", lambda e, bank=bank, kvout=kvout: e.tensor_copy(out=kvout[0:TT, 1, :], in_=ps[bank][0:TT, :]), reads=[bkey(bank)], writes=[kk1])
                dst = sbv_p[s, row0 + tt * 128: row0 + tt * 128 + 128, :] if isp else sbv_s[0:64, :]
                P.dma(lambda e, dst=dst, kvout=kvout: e.dma_start(out=dst, in_=kvout[0:TT, 1, :]), reads=[kk1], writes=[], semkey=kk1)
                if isp:
                    P.op("act", lambda e, bank=bank, tt=tt: e.activation(out=vtok[:, qb * 4 + tt, :], in_=ps[bank][:, :], func=AF.Copy),
                         reads=[bkey(bank)], writes=["vtok"])
                elif not os.environ.get("KNOVNEW"):
                    P.op("act", lambda e, bank=bank: e.activation(out=vnew[0:64, :], in_=ps[bank][0:64, :], func=AF.Copy),
                         reads=[bkey(bank)], writes=["vnew"])
            if float(os.environ.get("KSTOP", 99)) <= 1.8:
                return
            if isp and qb == 0:
                P.op("pool", lambda e: e.memset(cstate[:, :, :], 0.0), writes=["cstate"])
            for c in range(4):
                if os.environ.get("KNOCONV") and not isp:
                    state["slabk"] += 1
                    continue
                sl, slk = get_slab(SLAB_EIN + 3 + c)
                b0 = proj_fm(sl, slk, 0, 128, lambda kc: h[:, kc, 0:U], ["h"], 8, U, 384)
                b1 = proj_fm(sl, slk, 128, 128, lambda kc: h[:, kc, 0:U], ["h"], 8, U, 384)
                b2 = proj_fm(sl, slk, 256, 128, lambda kc: h[:, kc, 0:U], ["h"], 8, U, 384)
                if isp:
                    cv = cin[:, 0:514]
                    cur = cv[:, 2:514]
                    t0, t1_, t2 = cv[:, 0:512], cv[:, 1:513], cv[:, 2:514]
                    tv = tmpf[:, 1, 0:512]
                    g1 = tmpf[:, 0, 0:512]
                    pv = lambda b: ps[b][:, 0:512]
                    P.op("dve", lambda e, c=c: e.tensor_copy(out=cin[:, 0:2], in_=cstate[:, c, :]), reads=["cstate"], writes=["cin"])
                else:
                    cv = cin[:, 0:72].rearrange("p (s t) -> p s t", s=4)
                    cur = cv[:, :, 2:18]
                    t0, t1_, t2 = cv[:, :, 0:16], cv[:, :, 1:17], cv[:, :, 2:18]
                    tv = tmpf[:, 1, 0:64].rearrange("p (s t) -> p s t", s=4)
                    g1 = tmpf[:, 0, 0:64].rearrange("p (s t) -> p s t", s=4)
                    pv = lambda b: ps[b][:, 0:64].rearrange("p (s t) -> p s t", s=4)
                    for s4 in range(4):
                        for j2 in range(2):
                            P.dma(lambda e, c=c, cv=cv, s4=s4, j2=j2: e.dma_start(out=cv[:, s4, j2:j2 + 1], in_=sconv[s4, j2:j2 + 1, c * 128:(c + 1) * 128].rearrange("j p -> p j")),
                                  reads=[], writes=["cinp%d" % (s4 * 2 + j2)], semkey="cinld%d" % (s4 * 2 + j2))
                P.op("act", lambda e, g1=g1, b1=b1, pv=pv: e.activation(out=g1, in_=pv(b1), func=AF.Copy), reads=[bkey(b1)], writes=["tmpf0"])
                P.op("dve", lambda e, cur=cur, g1=g1, b2=b2, pv=pv: e.tensor_tensor(out=cur, in0=pv(b2), in1=g1, op=ALU.mult),
                     reads=[bkey(b2), "tmpf0"], writes=["cin"])
                wi = 64 + c
                cpk = [] if isp else ["cinp%d" % i for i in range(8)]
                P.op("dve", lambda e, tv=tv, t0=t0, wi=wi: e.tensor_scalar(out=tv, in0=t0, scalar1=gcol[:, wi:wi + 1], scalar2=None, op0=ALU.mult), reads=["cin", "gcol"] + cpk, writes=["tmpf1"])
                P.op("dve", lambda e, tv=tv, t1_=t1_, wi=wi: e.scalar_tensor_tensor(out=tv, in0=t1_, scalar=gcol[:, wi + 4:wi + 5], in1=tv, op0=ALU.mult, op1=ALU.add), reads=["cin", "tmpf1", "gcol"] + cpk, writes=["tmpf1"])
                P.op("dve", lambda e, tv=tv, t2=t2, wi=wi: e.scalar_tensor_tensor(out=tv, in0=t2, scalar=gcol[:, wi + 8:wi + 9], in1=tv, op0=ALU.mult, op1=ALU.add), reads=["cin", "tmpf1", "gcol"] + cpk, writes=["tmpf1"])
                if isp:
                    P.op("dve", lambda e, c=c, tv=tv, b0=b0: e.tensor_tensor(out=mix[:, 4 + c, 0:512], in0=tv, in1=ps[b0][:, 0:512], op=ALU.mult), reads=["tmpf1", bkey(b0)], writes=["mix%d" % (4 + c)])
                    P.op("pool", lambda e, c=c: e.tensor_copy(out=cstate[:, c, :], in_=cin[:, 512:514]), reads=["cin"], writes=["cstate"])
                    if qb == 3:
                        for j2 in range(2):
                            P.dma(lambda e, c=c, j2=j2: e.dma_start(out=conv_p[s, j2:j2 + 1, c * 128:(c + 1) * 128].rearrange("j p -> p j"), in_=cin[:, 512 + j2:513 + j2]),
                                  reads=["cin"], writes=[], semkey="convo%d" % j2)
                else:
                    P.op("dve", lambda e, c=c, tv=tv, b0=b0, pv=pv: e.tensor_tensor(out=mix[:, 4 + c, 0:64].rearrange("p (s t) -> p s t", s=4), in0=tv, in1=pv(b0), op=ALU.mult),
                         reads=["tmpf1", bkey(b0)], writes=["mix%d" % (4 + c)])
                    for s4 in range(4):
                        for j2 in range(2):
                            P.dma(lambda e, c=c, cv=cv, s4=s4, j2=j2: e.dma_start(out=conv_s[s4, j2:j2 + 1, c * 128:(c + 1) * 128].rearrange("j p -> p j"), in_=cv[:, s4, 16 + j2:17 + j2]),
                                  reads=["cin"], writes=[], semkey="convo%d" % (s4 * 2 + j2))

            if float(os.environ.get("KSTOP", 99)) <= 2:
                return
            tu = 0
            if isp:
                nt = 4 * qb + 4
                gens = []
                for hh in range(8):
                    hp, m = hh % 2, hh // 2
                    S_t = S_b[hp]
                    Sk = "S%d" % hp
                    pre = lambda S_t=S_t, Sk=Sk: P.op("pool", lambda e: e.memset(S_t[:, 0:512], 0.0), writes=[Sk])
                    post = lambda hp=hp, m=m: P.op("act", lambda e: e.activation(out=mix[hp * 64:(hp + 1) * 64, m, 0:512], in_=ps[7][hp * 64:(hp + 1) * 64, 0:512], func=AF.Copy),
                                                   reads=["acc%d" % hp], writes=["mix%d" % m])
                    for idx, kt in enumerate(range(nt - 1, -1, -1)):
                        j = kt - 4 * qb
                        q0 = 128 * j if j > 0 else 0
                        mask = sbm[:, 0:512 - q0] if j >= 0 else None
                        g = sb_tile(ksT[hp * 64:(hp + 1) * 64, m, kt * 128:(kt + 1) * 128], qT[hp * 64:(hp + 1) * 64, m, q0:512],
                                    vtok[:, kt, hh * 64:(hh + 1) * 64], 128, 512 - q0, S_t[:, q0:512],
                                    ps[7][hp * 64:(hp + 1) * 64, q0:512], "acc%d" % hp, mask, idx == 0, idx == nt - 1, ["ksT", "qT"], tu, Sk)
                        gens.append(wrap(g, pre if idx == 0 else None, post if idx == nt - 1 else None))
                        tu += 1
                run_pipelined(gens)
            else:
                items = []
                heads = [(ss, hh) for ss in range(SPB) for hh in range(8)]

                def prologue(ss, hh, b):
                    hp, m = hh % 2, hh // 2
                    kTs, vh, kbf, qs_t = kTs_b[b], vh_b[b], kbf_b[b], qs_b[b]
                    P.dma(lambda e: e.dma_start(out=kld[:, :, :], in_=csk[ss, :, hh * 64:(hh + 1) * 64].rearrange("(a p) d -> p a d", p=128)),
                          writes=["kld"], semkey="kld")
                    P.dma(lambda e: e.dma_start(out=vld[:, :, :], in_=csv[ss, :, hh * 64:(hh + 1) * 64].rearrange("(a p) d -> p a d", p=128)),
                          writes=["vld"], semkey="vld")
                    P.op("dve", lambda e: e.tensor_copy(out=kbf[:, :, :], in_=kld[:, :, :]), reads=["kld"], writes=["kbf%d" % b])
                    P.op("pool", lambda e: e.tensor_copy(out=vh[:, 0:32, :], in_=vld[:, :, :]), reads=["vld"], writes=["vh%d" % b])
                    for g in range(8):
                        bank = dbank()
                        P.op("pe", lambda e, g=g, bank=bank: [e.transpose(psb[bank][0:64, i * 128:(i + 1) * 128], kbf[:, 4 * g + i, :], identb[:, :]) for i in range(4)][-1],
                             reads=["kbf%d" % b, "identb"], writes=[bkey(bank)])
                        P.op("dve", lambda e, g=g, bank=bank: e.tensor_scalar(out=kTs[0:64, g * 512:(g + 1) * 512], in0=psb[bank][0:64, 0:512], scalar1=SB_SCALE, scalar2=None, op0=ALU.mult),
                             reads=[bkey(bank)], writes=["kTs%d" % b])
                    P.op("dve", lambda e: e.tensor_copy(out=kTs[0:64, 4096:4112], in_=ksn[hp * 64:(hp + 1) * 64, m, ss * 16:(ss + 1) * 16]),
                         reads=["ksn"], writes=["kTs%d" % b])
                    P.op("dve", lambda e: e.tensor_copy(out=qs_t[0:64, 0:16], in_=qT[hp * 64:(hp + 1) * 64, m, ss * 16:(ss + 1) * 16]),
                         reads=["qT"], writes=["qs%d" % b])
                    P.dma(lambda e: e.dma_start(out=vh[0:16, 32, :], in_=vnew[ss * 16:(ss + 1) * 16, hh * 64:(hh + 1) * 64]),
                          reads=["vnew", "vh%d" % b], writes=["vh%d" % b], semkey="vhn")

                for hi, (ss, hh) in enumerate(heads):
                    b = hi % 2
                    hp, m = hh % 2, hh // 2
                    kTs, vh, qs_t = kTs_b[b], vh_b[b], qs_b[b]
                    S_A, S_B = SA_b[b], SB_b[b]
                    fE, tE = fE_b[b], tE_b[b]
                    if hi == 0:
                        items.append(lambda: prologue(heads[0][0], heads[0][1], 0))

                    def pre(S_A=S_A, S_B=S_B, b=b):
                        P.op("pool", lambda e: e.memset(S_A[:, 0:16], 0.0), writes=["SA%d" % b])
                        P.op("pool", lambda e: e.memset(S_B[:, 0:16], 0.0), writes=["SB%d" % b])

                    def post(hp=hp, m=m, ss=ss, S_A=S_A, fE=fE, tE=tE, b=b):
                        bank = dbank()
                        P.op("pe", lambda e: e.matmul(ps[bank][hp * 64:(hp + 1) * 64, 0:16], nones[0:128, 0:64], S_A[:, 0:16], start=True, stop=True),
                             reads=["SA%d" % b, "nones"], writes=[bkey(bank)])
                        P.op("act", lambda e: e.activation(out=fE[hp * 64:(hp + 1) * 64, 0:16], in_=ps[bank][hp * 64:(hp + 1) * 64, 0:16], func=AF.Exp),
                             reads=[bkey(bank)], writes=["fE%d" % b])
                        P.op("dve", lambda e: e.tensor_tensor(out=tE[hp * 64:(hp + 1) * 64, 0:16], in0=ps[7][hp * 64:(hp + 1) * 64, 16:32], in1=fE[hp * 64:(hp + 1) * 64, 0:16], op=ALU.mult),
                             reads=["acc%d" % hp, "fE%d" % b], writes=["tE%d" % b])
                        P.op("dve", lambda e: e.tensor_tensor(out=mix[hp * 64:(hp + 1) * 64, m, ss * 16:(ss + 1) * 16], in0=ps[7][hp * 64:(hp + 1) * 64, 0:16], in1=tE[hp * 64:(hp + 1) * 64, 0:16], op=ALU.add),
                             reads=["acc%d" % hp, "tE%d" % b], writes=["mix%d" % m])

                    chA = list(range(32, 16, -1))
                    chB = list(range(16, -1, -1))
                    seq = []
                    for i in range(17):
                        if i < len(chA):
                            seq.append(("A", i, chA[i]))
                        seq.append(("B", i, chB[i]))
                    for si, (ch, ci, kt) in enumerate(seq):
                        nk = 16 if kt == 32 else 128
                        mask = sbm[0:16, 0:16] if kt == 32 else None
                        k0 = kt * 128
                        isA = (ch == "A")
                        S_c = S_A if isA else S_B
                        Sk = ("SA%d" if isA else "SB%d") % b
                        clen = len(chA) if isA else len(chB)
                        acc_ap = ps[7][hp * 64:(hp + 1) * 64, 0:16] if isA else ps[7][hp * 64:(hp + 1) * 64, 16:32]
                        g = sb_tile(kTs[0:64, k0:k0 + nk], qs_t[0:64, 0:16], vh[0:nk, kt, :], nk, 16, S_c[:, 0:16],
                                    acc_ap, "acc%d" % hp, mask, ci == 0, ci == clen - 1, ["kTs%d" % b, "qs%d" % b], tu, Sk, "vh%d" % b,
                                    acc_start=(si == 0), s_add=(True if isA else (ci != clen - 1)))
                        items.append(wrap(g, pre if si == 0 else None, post if si == len(seq) - 1 else None))
                        tu += 1
                        if si == 6 and hi + 1 < len(heads):
                            items.append(lambda hi=hi: prologue(heads[hi + 1][0], heads[hi + 1][1], (hi + 1) % 2))
                run_pipelined(items)
            if float(os.environ.get("KSTOP", 99)) <= 3:
                return
            wout_and_post(SLAB_EOUT, 16 + 0, U)
            ffn(0, U)

            if float(os.environ.get("KSTOP", 99)) <= 4:
                return
            pre_norm(8, U)
            sl, slk = get_slab(SLAB_OIN + 0)
            for m in range(4):
                bank = proj_fm(sl, slk, m * 128, 128, lambda kc: h[:, kc, 0:U], ["h"], 8, U, 512)
                P.op("act", lambda e, m=m, bank=bank: e.activation(out=uT[:, m, 0:U], in_=ps[bank][:, 0:U], func=AF.Copy), reads=[bkey(bank)], writes=["uT"])
            sl, slk = get_slab(SLAB_OIN + 1)
            for m in range(3):
                bank = proj_fm(sl, slk, m * 128, 128, lambda kc: h[:, kc, 0:U], ["h"], 8, U, 384)
                P.op("dve", lambda e, m=m, bank=bank: e.tensor_copy(out=o[:, m, 0:U], in_=ps[bank][:, 0:U]), reads=[bkey(bank)], writes=["o%d" % m])
            rms_stats([o[:, m, 0:U] for m in range(3)], ["o0", "o1", "o2"], 384, U)
            for m in range(3):
                P.op("dve", lambda e, m=m: e.scalar_tensor_tensor(out=cqT[:, m, 0:U], in0=o[:, m, 0:U], scalar=gcol[:, 76 + m:77 + m], in1=rstd[:, 0:U], op0=ALU.mult, op1=ALU.mult),
                     reads=["o%d" % m, "rstd", "gcol"], writes=["cqT"])
            sl, slk = get_slab(SLAB_OIN + 2)
            for tt in range(NTT):
                bank = proj_tm(sl, slk, 512, tt, TT, 512)
                sm = small4[:, tt, :]
                k_ = "sm%d_" % tt
                ts = tt % 2
                tk = "tmpf%d" % ts
                P.op("dve", lambda e, bank=bank, sm=sm: e.bn_stats(out=sm[0:TT, 0:6], in_=ps[bank][0:TT, 0:512]), reads=[bkey(bank)], writes=[k_ + "bn"])
                P.op("dve", lambda e, sm=sm: e.bn_aggr(out=sm[0:TT, 6:8], in_=sm[0:TT, 0:6]), reads=[k_ + "bn"], writes=[k_ + "mv"])
                P.op("act", lambda e, sm=sm: e.activation(out=sm[0:TT, 8:9], in_=sm[0:TT, 7:8], func=AF.Ln, bias=EPS), reads=[k_ + "mv"], writes=[k_ + "rv"])
                P.op("act", lambda e, sm=sm: e.activation(out=sm[0:TT, 8:9], in_=sm[0:TT, 8:9], func=AF.Exp, scale=-0.5), reads=[k_ + "rv"], writes=[k_ + "rv"])
                P.op("dve", lambda e, bank=bank, sm=sm, ts=ts: e.tensor_scalar(out=tmpf[0:TT, ts, :], in0=ps[bank][0:TT, 0:512], scalar1=sm[0:TT, 6:7], scalar2=sm[0:TT, 8:9], op0=ALU.subtract, op1=ALU.mult),
                     reads=[bkey(bank), k_ + "mv", k_ + "rv"], writes=[tk])
                P.op("pool", lambda e, ts=ts: e.tensor_tensor(out=tmpf[0:TT, ts, :], in0=tmpf[0:TT, ts, :], in1=lnG[0:TT, :], op=ALU.mult), reads=[tk, "lnG"], writes=[tk])
                if isp:
                    P.op("pool", lambda e, tt=tt, ts=ts: e.tensor_tensor(out=vnb[0:TT, tt, :], in0=tmpf[0:TT, ts, :], in1=lnB[0:TT, :], op=ALU.add), reads=[tk, "lnB"], writes=["vnb"])
                else:
                    P.op("pool", lambda e, ts=ts: e.tensor_tensor(out=vnf[0:TT, :], in0=tmpf[0:TT, ts, :], in1=lnB[0:TT, :], op=ALU.add), reads=[tk, "lnB"], writes=["vnf"])
                    P.op("pool", lambda e, tt=tt: e.tensor_copy(out=vnb[0:TT, tt, :], in_=vnf[0:TT, :]), reads=["vnf"], writes=["vnb"])
                    P.dma(lambda e: e.dma_start(out=sguv_s[0:64, :], in_=vnf[0:64, :]), reads=["vnf"], writes=[], semkey="vnfo")
            for g in range(4):
                bank = dbank()

                def fng(e, g=g, bank=bank):
                    ins = None
                    for tt in range(NTT):
                        rhs = wsT[:, g, :] if isp else wsbd[0:64, g, :]
                        ins = e.matmul(ps[bank][:, tt * TT:(tt + 1) * TT], vnb[0:TT, tt, g * 128:(g + 1) * 128], rhs, start=True, stop=True)
                    return ins
                P.op("pe", fng, reads=["vnb", "wsT", "wsbd"], writes=[bkey(bank)])
                if isp:
                    for tt in range(NTT):
                        P.op("dve", lambda e, g=g, tt=tt, bank=bank: e.tensor_tensor(out=tmpf[:, 1, tt * 128:(tt + 1) * 128], in0=ps[bank][:, tt * 128:(tt + 1) * 128], in1=bsB[:, g * 128:(g + 1) * 128], op=ALU.add),
                             reads=[bkey(bank), "bsB"], writes=["tmpf1"])
                else:
                    P.op("dve", lambda e, g=g, bank=bank: e.tensor_tensor(out=tmpf[:, 1, 0:64], in0=ps[bank][:, 0:64], in1=bsBs[:, g, :], op=ALU.add),
                         reads=[bkey(bank), "bsBs"], writes=["tmpf1"])
                P.op("pool", lambda e, g=g: e.tensor_tensor(out=mix[:, g, 0:U], in0=tmpf[:, 1, 0:U], in1=uT[:, g, 0:U], op=ALU.mult), reads=["tmpf1", "uT"], writes=["mix%d" % g])
            sl, slk = get_slab(SLAB_OIN + 3)
            for tt in range(NTT):
                bank = proj_tm(sl, slk, 288, tt, TT, 288)
                tile_i = qb * 4 + tt
                sm = small4[:, tt, :]
                k_ = "sm%d_" % tt
                ts = tt % 2
                tk = "tmpf%d" % ts
                ckst = ckst_b[ts]
                ck = "ckst%d" % ts
                kpb = kpb_b[ts]
                kk = "kpb%d" % ts
                P.op("act", lambda e, bank=bank, sm=sm, ts=ts: e.activation(out=tmpf[0:TT, ts, 0:256], in_=ps[bank][0:TT, 0:256], func=AF.Square, accum_out=sm[0:TT, 10:11]),
                     reads=[bkey(bank)], writes=[tk, k_ + "ss"])
                P.op("act", lambda e, sm=sm: e.activation(out=sm[0:TT, 11:12], in_=sm[0:TT, 10:11], func=AF.Ln, scale=1.0 / 256, bias=EPS), reads=[k_ + "ss"], writes=[k_ + "rk"])
                P.op("act", lambda e, sm=sm: e.activation(out=sm[0:TT, 11:12], in_=sm[0:TT, 11:12], func=AF.Exp, scale=-0.5), reads=[k_ + "rk"], writes=[k_ + "rk"])
                P.op("dve", lambda e, bank=bank, sm=sm, ckst=ckst: e.scalar_tensor_tensor(out=ckst[0:TT, 0:256], in0=ps[bank][0:TT, 0:256], scalar=sm[0:TT, 11:12], in1=kvG[0:TT, :], op0=ALU.mult, op1=ALU.mult),
                     reads=[bkey(bank), k_ + "rk", "kvG"], writes=[ck])
                if isp:
                    cosv = ropeT[0:TT, tile_i * 16:(tile_i + 1) * 16]
                    sinv = ropeT[0:TT, 256 + tile_i * 16:256 + (tile_i + 1) * 16]
                else:
                    cosv = ropeTs[0:TT, 0:16]
                    sinv = ropeTs[0:TT, 16:32]
                x1 = ps[bank][0:TT, 256:272]
                x2 = ps[bank][0:TT, 272:288]
                t16 = sm[0:TT, 16:32]
                P.op("dve", lambda e, x1=x1, cosv=cosv, ckst=ckst: e.tensor_tensor(out=ckst[0:TT, 256:272], in0=x1, in1=cosv, op=ALU.mult), reads=[bkey(bank), "ropeT", ck], writes=[ck])
                P.op("dve", lambda e, x2=x2, sinv=sinv, t16=t16: e.tensor_tensor(out=t16, in0=x2, in1=sinv, op=ALU.mult), reads=[bkey(bank), "ropeT"], writes=[k_ + "t16"])
                P.op("dve", lambda e, ckst=ckst, t16=t16: e.tensor_tensor(out=ckst[0:TT, 256:272], in0=ckst[0:TT, 256:272], in1=t16, op=ALU.subtract), reads=[ck, k_ + "t16"], writes=[ck])
                P.op("dve", lambda e, x1=x1, sinv=sinv, ckst=ckst: e.tensor_tensor(out=ckst[0:TT, 272:288], in0=x1, in1=sinv, op=ALU.mult), reads=[bkey(bank), "ropeT", ck], writes=[ck])
                P.op("dve", lambda e, x2=x2, cosv=cosv, t16=t16: e.tensor_tensor(out=t16, in0=x2, in1=cosv, op=ALU.mult), reads=[bkey(bank), "ropeT", k_ + "t16"], writes=[k_ + "t16"])
                P.op("dve", lambda e, ckst=ckst, t16=t16: e.tensor_tensor(out=ckst[0:TT, 272:288], in0=ckst[0:TT, 272:288], in1=t16, op=ALU.add), reads=[ck, k_ + "t16"], writes=[ck])
                if isp:
                    r0 = row0 + tt * 128
                    P.dma(lambda e, r0=r0, ckst=ckst: e.dma_start(out=ckv_p[s, r0:r0 + 128, :], in_=ckst[0:128, 0:256]), reads=[ck], writes=[], semkey="ckvo%d" % ts)
                    P.dma(lambda e, r0=r0, ckst=ckst: e.dma_start(out=kpe_p[s, r0:r0 + 128, :], in_=ckst[0:128, 256:288]), reads=[ck], writes=[], semkey="kpeo%d" % ts)
                    ctk = ckvtok[:, tile_i, :]
                else:
                    P.dma(lambda e, ckst=ckst: e.dma_start(out=ckv_s[0:64, :], in_=ckst[0:64, 0:256]), reads=[ck], writes=[], semkey="ckvo%d" % ts)
                    P.dma(lambda e, ckst=ckst: e.dma_start(out=kpe_s[0:64, :], in_=ckst[0:64, 256:288]), reads=[ck], writes=[], semkey="kpeo%d" % ts)
                    ctk = ckvn[:, :]
                P.op("pool", lambda e, ctk=ctk, ckst=ckst: e.tensor_copy(out=ctk[0:TT, :], in_=ckst[0:TT, 0:256]), reads=[ck], writes=["ckvtok"])
                P.op("pool", lambda e, ckst=ckst, kpb=kpb: e.tensor_copy(out=kpb[0:TT, :], in_=ckst[0:TT, 256:288]), reads=[ck], writes=[kk])
                bank2 = dbank()

                def fnt(e, ctk=ctk, bank2=bank2, kpb=kpb):
                    e.transpose(psb[bank2][:, 0:TT], ctk[0:TT, 0:128], identb[0:TT, 0:TT])
                    e.transpose(psb[bank2][:, 128:128 + TT], ctk[0:TT, 128:256], identb[0:TT, 0:TT])
                    return e.transpose(psb[bank2][0:32, 256:256 + TT], kpb[0:TT, 0:32], identb[0:TT, 0:TT])
                P.op("pe", fnt, reads=["ckvtok", kk, "identb"], writes=[bkey(bank2)])
                if isp:
                    c0 = tile_i * 128
                    P.op("act", lambda e, c0=c0, bank2=bank2: e.activation(out=ckvT[:, :, c0:c0 + 128], in_=psb[bank2][:, 0:256].rearrange("p (a t) -> p a t", a=2), func=AF.Copy),
                         reads=[bkey(bank2)], writes=["ckvT"])
                    P.op("dve", lambda e, c0=c0, bank2=bank2: e.tensor_copy(out=kpeT[0:32, c0:c0 + 128], in_=psb[bank2][0:32, 256:384]), reads=[bkey(bank2)], writes=["kpeT"])
                else:
                    P.op("act", lambda e, bank2=bank2: e.activation(out=sq[:, 0, 0:64], in_=psb[bank2][:, 0:64], func=AF.Copy), reads=[bkey(bank2)], writes=["sq0"])
                    P.op("act", lambda e, bank2=bank2: e.activation(out=sq[:, 0, 64:128], in_=psb[bank2][:, 128:192], func=AF.Copy), reads=[bkey(bank2)], writes=["sq0"])
                    P.op("act", lambda e, bank2=bank2: e.activation(out=sq[0:32, 0, 128:192], in_=psb[bank2][0:32, 256:320], func=AF.Copy), reads=[bkey(bank2)], writes=["sq0"])
            if float(os.environ.get("KSTOP", 99)) <= 5:
                return
            for m in range(4):
                bank = proj_fm(wuqn[:, :, :].rearrange("p k n -> p (k n)"), "wuqn", m * 128, 128, lambda kc: cqT[:, kc, 0:U], ["cqT"], 3, U, 512)
                P.op("act", lambda e, m=m, bank=bank: e.activation(out=qnT[:, m, 0:U], in_=ps[bank][:, 0:U], func=AF.Copy), reads=[bkey(bank)], writes=["qnT"])
            if isp:
                P.dma(lambda e: e.dma_start(out=cosF[0:32, 0:512], in_=rope_f[0, :, row0:row0 + 512]), writes=["cosF"], semkey="cosF")
                P.dma(lambda e: e.dma_start(out=sinF[0:32, 0:512], in_=rope_f[1, :, row0:row0 + 512]), writes=["sinF"], semkey="sinF")
            else:
                P.dma(lambda e: e.dma_start(out=cosF[0:32, 0:64], in_=rope_fs[0, :, :]), writes=["cosF"], semkey="cosF")
                P.dma(lambda e: e.dma_start(out=sinF[0:32, 0:64], in_=rope_fs[1, :, :]), writes=["sinF"], semkey="sinF")

            def head_q(hh, q0, q1, qb_=0, d0=None):
                hp, m = hh % 2, hh // 2
                qlat = qlat_b[qb_]
                qpe = qpe_b[qb_]
                if d0 is None:
                    d0 = q0
                d1 = d0 + (q1 - q0)
                for cc in range(2):
                    bank = dbank()
                    P.op("pe", lambda e, cc=cc, bank=bank: e.matmul(ps[bank][:, q0:q1], wukp[hp * 64:(hp + 1) * 64, m, cc * 128:(cc + 1) * 128], qnT[hp * 64:(hp + 1) * 64, m, q0:q1], start=True, stop=True),
                         reads=["wukp", "qnT"], writes=[bkey(bank)])
                    if cc == 0:
                        P.op("act", lambda e, bank=bank: e.activation(out=qlat[:, 0, d0:d1], in_=ps[bank][:, q0:q1], func=AF.Copy), reads=[bkey(bank)], writes=["qlat%d" % qb_])
                    else:
                        P.op("dve", lambda e, bank=bank: e.tensor_copy(out=qlat[:, 1, d0:d1], in_=ps[bank][:, q0:q1]), reads=[bkey(bank)], writes=["qlat%d" % qb_])
                b1 = dbank()
                b2 = dbank()

                def fq(e, b1=b1, b2=b2):
                    ins = None
                    for kc in range(3):
                        e.matmul(ps[b1][0:32, q0:q1], wuqp[:, kc, hh * 32:(hh + 1) * 32], cqT[:, kc, q0:q1], start=(kc == 0), stop=(kc == 2))
                    for kc in range(3):
                        ins = e.matmul(ps[b2][0:32, q0:q1], wuqs[:, kc, hh * 32:(hh + 1) * 32], cqT[:, kc, q0:q1], start=(kc == 0), stop=(kc == 2))
                    return ins
                P.op("pe", fq, reads=["wuqp", "wuqs", "cqT"], writes=[bkey(b1), bkey(b2)])
                P.op("dve", lambda e, b1=b1: e.tensor_tensor(out=tmpf[0:32, 0, q0:q1], in0=ps[b1][0:32, q0:q1], in1=cosF[0:32, q0:q1], op=ALU.mult), reads=[bkey(b1), "cosF"], writes=["tmpf0"])
                P.op("dve", lambda e, b2=b2: e.tensor_tensor(out=tmpf[0:32, 1, q0:q1], in0=ps[b2][0:32, q0:q1], in1=sinF[0:32, q0:q1], op=ALU.mult), reads=[bkey(b2), "sinF"], writes=["tmpf1"])
                P.op("pool", lambda e: e.tensor_tensor(out=qpe[0:32, d0:d1], in0=tmpf[0:32, 0, q0:q1], in1=tmpf[0:32, 1, q0:q1], op=ALU.add), reads=["tmpf0", "tmpf1"], writes=["qpe%d" % qb_])

            def head_out(hh, q0, q1, s0=None):
                hp, m = hh % 2, hh // 2
                if s0 is None:
                    s0 = q0
                s1 = s0 + (q1 - q0)
                P.op("act", lambda e: e.activation(out=OLs[:, 0, s0:s1], in_=ps[5][:, s0:s1], func=AF.Copy), reads=[bkey(5)], writes=["OLs0"])
                P.op("dve", lambda e: e.tensor_copy(out=OLs[:, 1, s0:s1], in_=ps[6][:, s0:s1]), reads=[bkey(6)], writes=["OLs1"])
                P.op("dve", lambda e: e.reciprocal(out=rden[hp * 64:(hp + 1) * 64, s0:s1], in_=ps[7][hp * 64:(hp + 1) * 64, s0:s1]), reads=[bkey(7)], writes=["rden"])
                bank = dbank()

                def fo(e, bank=bank):
                    e.matmul(ps[bank][hp * 64:(hp + 1) * 64, s0:s1], wuvb[:, hh, 0, :], OLs[:, 0, s0:s1], start=True, stop=False)
                    return e.matmul(ps[bank][hp * 64:(hp + 1) * 64, s0:s1], wuvb[:, hh, 1, :], OLs[:, 1, s0:s1], start=False, stop=True)
                P.op("pe", fo, reads=["wuvb", "OLs0", "OLs1"], writes=[bkey(bank)])
                P.op("dve", lambda e, bank=bank: e.tensor_tensor(out=mix[hp * 64:(hp + 1) * 64, 4 + m, q0:q1], in0=ps[bank][hp * 64:(hp + 1) * 64, s0:s1], in1=rden[hp * 64:(hp + 1) * 64, s0:s1], op=ALU.mult),
                     reads=[bkey(bank), "rden"], writes=["mix%d" % (4 + m)])

            tu = 0
            if isp:
                nt = 4 * qb + 4
                head_q(0, 0, 512, 0)
                for hh in range(8):
                    gens = []
                    for kt in range(nt):
                        j = kt - 4 * qb
                        q0 = 128 * j if j > 0 else 0
                        mask = mlm[:, 0:512 - q0] if j >= 0 else None
                        gens.append(mla_tile(ckvT[:, 0, kt * 128:(kt + 1) * 128], ckvT[:, 1, kt * 128:(kt + 1) * 128], kpeT[0:32, kt * 128:(kt + 1) * 128],
                                             ckvtok[:, kt, :], 128, q0, 512, mask, kt == 0, kt == nt - 1, hh % 2, tu, hh % 2))
                        tu += 1
                    hook = (lambda hh=hh: head_q(hh + 1, 0, 512, (hh + 1) % 2)) if hh < 7 else None
                    run_pipelined(gens, 1, hook)
                    head_out(hh, 0, 512)
            else:
                for ss in range(int(os.environ.get("KLIM_MLA", SPB))):
                    for qtr in range(4):
                        P.dma(lambda e, ss=ss, qtr=qtr: e.dma_start(out=cstg[:, 0:2048].rearrange("p (a d) -> p a d", a=8), in_=cckv[ss, qtr * 1024:(qtr + 1) * 1024, :].rearrange("(a p) d -> p a d", p=128)),
                              writes=["cstg"], semkey="cstg")
                        P.op("dve", lambda e, qtr=qtr: e.tensor_copy(out=ckvtok_s[:, qtr * 8:(qtr + 1) * 8, :], in_=cstg[:, 0:2048].rearrange("p (a d) -> p a d", a=8)), reads=["cstg"], writes=["ckvtok"])
                    P.dma(lambda e, ss=ss: e.dma_start(out=cstg[:, 0:1024].rearrange("p (a d) -> p a d", a=32), in_=ckpe[ss, :, :].rearrange("(a p) d -> p a d", p=128)),
                          writes=["cstg"], semkey="cstg")
                    P.op("dve", lambda e: e.tensor_copy(out=kpbs[:, :, :], in_=cstg[:, 0:1024].rearrange("p (a d) -> p a d", a=32)), reads=["cstg"], writes=["kpbs"])
                    for kt in range(32):
                        for cc in range(2):
                            if (kt * 2 + cc) % 4 == 0:
                                bank = dbank()
                            i4 = (kt * 2 + cc) % 4
                            P.op("pe", lambda e, kt=kt, cc=cc, bank=bank, i4=i4: e.transpose(psb[bank][:, i4 * 128:(i4 + 1) * 128], ckvtok_s[:, kt, cc * 128:(cc + 1) * 128], identb[:, :]),
                                 reads=["ckvtok", "identb"], writes=[bkey(bank)])
                            P.op("act", lambda e, kt=kt, cc=cc, bank=bank, i4=i4: e.activation(out=ckvT_s[:, cc, kt * 128:(kt + 1) * 128], in_=psb[bank][:, i4 * 128:(i4 + 1) * 128], func=AF.Copy),
                                 reads=[bkey(bank)], writes=["ckvT"])
                    for g in range(8):
                        bank = dbank()
                        P.op("pe", lambda e, g=g, bank=bank: [e.transpose(psb[bank][0:32, i * 128:(i + 1) * 128], kpbs[:, 4 * g + i, :], identb[:, :]) for i in range(4)][-1],
                             reads=["kpbs", "identb"], writes=[bkey(bank)])
                        P.op("dve", lambda e, g=g, bank=bank: e.tensor_copy(out=kpeT_s[0:32, g * 512:(g + 1) * 512], in_=psb[bank][0:32, 0:512]), reads=[bkey(bank)], writes=["kpeT"])
                    P.dma(lambda e, ss=ss: e.dma_start(out=ckvtok_s[0:16, 32, :], in_=ckvn[ss * 16:(ss + 1) * 16, :]), reads=["ckvtok"], writes=["ckvtok"], semkey="ckvnn")
                    P.op("act", lambda e, ss=ss: e.activation(out=ckvT_s[:, 0, 4096:4112], in_=sq[:, 0, ss * 16:(ss + 1) * 16], func=AF.Copy), reads=["sq0"], writes=["ckvT"])
                    P.op("act", lambda e, ss=ss: e.activation(out=ckvT_s[:, 1, 4096:4112], in_=sq[:, 0, 64 + ss * 16:64 + (ss + 1) * 16], func=AF.Copy), reads=["sq0"], writes=["ckvT"])
                    P.op("act", lambda e, ss=ss: e.activation(out=kpeT_s[0:32, 4096:4112], in_=sq[0:32, 0, 128 + ss * 16:128 + (ss + 1) * 16], func=AF.Copy), reads=["sq0"], writes=["kpeT"])
                    q0, q1 = ss * 16, ss * 16 + 16
                    for hh in range(8):
                        head_q(hh, q0, q1, 0, d0=hh * 16)
                    gens = []
                    for kt in range(33):
                        nk = 16 if kt == 32 else 128
                        k0 = kt * 128
                        gens.append(mla_tile(ckvT_s[:, 0, k0:k0 + nk], ckvT_s[:, 1, k0:k0 + nk], kpeT_s[0:32, k0:k0 + nk],
                                             ckvtok_s[0:nk, kt, :], nk, 0, 128, None, kt == 0, kt == 32, 0, tu, 0))
                        tu += 1
                    run_pipelined(gens)
                    for hh in range(8):
                        head_out(hh, q0, q1, s0=hh * 16)
            if float(os.environ.get("KSTOP", 99)) <= 6:
                return
            wout_and_post(SLAB_OOUT, 16 + 8, U)
            ffn(1, U)

            if float(os.environ.get("KSTOP", 99)) <= 7:
                return
            for tt in range(NTT):
                yout = io_b[tt % 2]
                yk = "io%d" % (tt % 2)
                for half in range(2):
                    bank = dbank()
                    P.op("pe", lambda e, half=half, bank=bank, tt=tt: [e.transpose(ps[bank][0:TT, c4 * 128:(c4 + 1) * 128], xT[:, half * 4 + c4, tt * TT:(tt + 1) * TT], identf[:, :]) for c4 in range(4)][-1],
                         reads=["xT%d" % (half * 4 + c4) for c4 in range(4)] + ["identf"], writes=[bkey(bank)])
                    P.op("act", lambda e, half=half, bank=bank, yout=yout: e.activation(out=yout[0:TT, half * 512:(half + 1) * 512], in_=ps[bank][0:TT, :], func=AF.Copy),
                         reads=[bkey(bank)], writes=[yk])
                dst = y_p[s, row0 + tt * 128: row0 + tt * 128 + 128, :] if isp else y_s[0:64, :]
                P.dma(lambda e, dst=dst, yout=yout: e.dma_start(out=dst, in_=yout[0:TT, :]), reads=[yk], writes=[], semkey=yk)
            barrier()

        setup()
        barrier()
        for (kind, s, qb) in units:
            unit(kind, s, qb)
        P.emit(st)
    return nc


def _consts():
    kl = np.arange(128)[:, None]
    x = np.arange(512)[None, :]
    cst = np.zeros((128, 1408), np.float32)
    cst[:, 0:128] = np.eye(128, dtype=np.float32)
    jj = np.arange(128)[:, None]; kk = np.arange(128)[None, :]
    cst[:, 128:256] = -(jj >= kk).astype(np.float32)
    cst[:, 256:768] = (kl < x).astype(np.float32)
    cst[:, 768:1280] = ((kl // 64) <= (x // 64)).astype(np.float32)
    tt = np.arange(128)[:, None]; ss = np.arange(128)[None, :]
    cst[:, 1280:1408] = (ss <= tt).astype(np.float32)
    half = 16
    inv = (10000.0 ** (-(np.arange(half, dtype=np.float32) / np.float32(half)))).astype(np.float32)

    def cs(pos):
        ang = (pos.astype(np.float32)[:, None] * inv[None, :]).astype(np.float32)
        return np.cos(ang.astype(np.float64)).astype(np.float32), np.sin(ang.astype(np.float64)).astype(np.float32)
    cp, sp_ = cs(np.arange(T))
    rope_t = np.zeros((128, 512), np.float32)
    rope_t[:, 0:256] = cp.reshape(16, 128, 16).transpose(1, 0, 2).reshape(128, 256)
    rope_t[:, 256:512] = sp_.reshape(16, 128, 16).transpose(1, 0, 2).reshape(128, 256)
    cs_, ss_ = cs(PAST + np.arange(DS))
    rope_ts = np.zeros((64, 32), np.float32)
    rope_ts[:, 0:16] = np.tile(cs_, (4, 1))
    rope_ts[:, 16:32] = np.tile(ss_, (4, 1))
    rope_f = np.zeros((2, 32, T), np.float32)
    rope_f[0, 0:16] = cp.T; rope_f[0, 16:32] = cp.T
    rope_f[1, 0:16] = -sp_.T; rope_f[1, 16:32] = sp_.T
    rope_fs = np.zeros((2, 32, 64), np.float32)
    rope_fs[0, 0:16] = np.tile(cs_.T, (1, 4)); rope_fs[0, 16:32] = np.tile(cs_.T, (1, 4))
    rope_fs[1, 0:16] = -np.tile(ss_.T, (1, 4)); rope_fs[1, 16:32] = np.tile(ss_.T, (1, 4))
    return cst, rope_t, rope_ts, rope_f, rope_fs


_CACHE = {}


def kernel(x_prompt, x_sample, cache_sb_k, cache_sb_v, state_conv, cache_mla_ckv, cache_mla_kpe,
           mix_pre_g, mix_post_g, ffn_pre_g, ffn_post_g, even_w_in, even_w_conv, even_w_out,
           odd_w_in, sgu_ln_g, sgu_ln_b, sgu_w_s, sgu_b_s, mla_q_norm_g, mla_kv_norm_g,
           mla_w_uq, mla_w_uk, mla_w_uv, odd_w_out, ffn_w_up, ffn_w_down):
    f = lambda a: np.ascontiguousarray(np.asarray(a, dtype=np.float32))
    dbg = os.environ.get("KDEBUG", "")
    if dbg:
        units = []
        for tok in dbg.split(","):
            if tok == "s":
                units.append(("s", 0, 0))
            else:
                a, b = tok.split(":")
                units.append(("p", int(a), int(b)))
    else:
        units = [("p", s, qb) for s in range(SPB) for qb in range(4)] + [("s", 0, 0)]
    key = tuple(units)
    if key not in _CACHE:
        _CACHE[key] = build_program(units)
    nc = _CACHE[key]
    cst, rope_t, rope_ts, rope_f, rope_fs = _consts()
    grow = np.concatenate([f(mix_pre_g).reshape(16, 128), f(mix_post_g).reshape(16, 128), f(ffn_pre_g).reshape(16, 128),
                           f(ffn_post_g).reshape(16, 128), f(even_w_conv).reshape(12, 128), f(mla_q_norm_g).reshape(3, 128)], axis=0)
    shared = dict(
        ein=f(even_w_in)[0], eout=f(even_w_out)[0], oin=f(odd_w_in)[0], oout=f(odd_w_out)[0],
        fup=f(ffn_w_up), fdn=f(ffn_w_down), grow=f(grow),
        lng=f(sgu_ln_g).reshape(1, 512), lnb=f(sgu_ln_b).reshape(1, 512), kvg=f(mla_kv_norm_g).reshape(1, 256),
        bsd=f(sgu_b_s).reshape(1, 512), bss=f(f(sgu_b_s)[0][:, 0:16]).reshape(1, 4, 16),
        wsd=f(sgu_w_s)[0], wuq=f(mla_w_uq)[0], wuk=f(mla_w_uk)[0], wuv=f(mla_w_uv)[0],
        cst=cst, rope_t=rope_t, rope_ts=rope_ts, rope_f=rope_f, rope_fs=rope_fs)
    xp = f(x_prompt); xs = f(x_sample)
    csk = f(cache_sb_k)[0].reshape(32, PAST, 512); csv = f(cache_sb_v)[0].reshape(32, PAST, 512)
    sc = f(state_conv)[0]; cckv = f(cache_mla_ckv)[0]; ckpe = f(cache_mla_kpe)[0]
    in_maps = []
    for c in range(NCORES):
        sl = slice(c * SPB, (c + 1) * SPB)
        m = dict(shared)
        m.update(xp=xp[sl], xs=f(xs[sl].reshape(SPB * DS, D)), csk=csk[sl], csv=csv[sl], sconv=sc[sl], cckv=cckv[sl], ckpe=ckpe[sl])
        in_maps.append(m)
    res = run_bass_kernel_spmd(nc, in_maps, core_ids=list(range(NCORES)))
    R = res.results
    cat = lambda name: np.concatenate([np.asarray(R[c][name], dtype=np.float32) for c in range(NCORES)], axis=0)
    y_p = cat("y_p")
    y_s = cat("y_s").reshape(32, DS, D)
    sbk_p = cat("sbk_p").reshape(1, 32, T, 8, 64)
    sbv_p = cat("sbv_p").reshape(1, 32, T, 8, 64)
    conv_p = cat("conv_p").reshape(1, 32, 2, 512)
    ckv_p = cat("ckv_p").reshape(1, 32, T, 256)
    kpe_p = cat("kpe_p").reshape(1, 32, T, 32)
    sbk_s = cat("sbk_s").reshape(1, 32, DS, 8, 64)
    sbv_s = cat("sbv_s").reshape(1, 32, DS, 8, 64)
    conv_s = cat("conv_s").reshape(1, 32, 2, 512)
    ckv_s = cat("ckv_s").reshape(1, 32, DS, 256)
    kpe_s = cat("kpe_s").reshape(1, 32, DS, 32)
    sguv_s = cat("sguv_s").reshape(1, 32, DS, 512)
    return (y_p, y_s, sbk_p, sbv_p, conv_p, ckv_p, kpe_p, sbk_s, sbv_s, conv_s, ckv_s, kpe_s, sguv_s)
```

```python
import os
import math
import numpy as np
from contextlib import ExitStack
import concourse.bass as bass
import concourse.mybir as mybir
from concourse.bass_utils import run_bass_kernel_spmd

F32 = mybir.dt.float32
BF16 = mybir.dt.bfloat16
AF = mybir.ActivationFunctionType
ALU = mybir.AluOpType

NCORES = 8
SPB = 4
T = 2048
D = 1024
PAST = 4096
DS = 16
EPS = 1e-6
SB_SCALE = 0.125
MLA_SCALE = 1.0 / math.sqrt(96.0)

COMPUTE = ("pe", "act", "dve", "pool")
ALLENG = ("pe", "act", "dve", "pool", "sp")
SAME_ENGINE_SYNC = True


class Prog:
    def __init__(self, nc):
        self.nc = nc
        self.ops = []

    def op(self, eng, fn, reads=(), writes=()):
        self.ops.append(dict(eng=eng, fn=fn, reads=tuple(reads), writes=tuple(writes), dma=False, bar=False))

    def dma(self, fn, reads=(), writes=(), semkey=None, queue="sp"):
        self.ops.append(dict(eng=queue, fn=fn, reads=tuple(reads), writes=tuple(writes), dma=True, bar=False,
                             semkey=semkey))

    def barrier(self, fn):
        self.ops.append(dict(eng="dve", fn=fn, reads=(), writes=(), dma=False, bar=True))

    def resolve(self):
        ops = self.ops
        last_w = {}
        readers = {}
        last_dma_by_sem = {}
        eng_pos = {}
        last_on_eng = {}
        pending = {}
        for i, o in enumerate(ops):
            e = o["eng"]
            o["pos"] = eng_pos.get(e, 0)
            eng_pos[e] = o["pos"] + 1
            deps = set()
            raw = set()
            if o["bar"]:
                for e2, j in last_on_eng.items():
                    deps.add(j)
                for sk, j in last_dma_by_sem.items():
                    deps.add(j)
                raw = set(deps)
                for e2 in ALLENG:
                    pending[e2] = i
                pending.pop("dve", None)
            else:
                if e in pending:
                    deps.add(pending.pop(e))
                for k in o["reads"]:
                    if k in last_w:
                        deps.add(last_w[k]); raw.add(last_w[k])
                    if k.startswith("ps") or k.startswith("acc"):
                        for e2, j in readers.get(k, {}).items():
                            if e2 != e and not isinstance(j, list):
                                deps.add(j)
                for k in o["writes"]:
                    if k in last_w:
                        deps.add(last_w[k])
                    for j in readers.get(k, {}).values():
                        if isinstance(j, list):
                            deps.update(j)
                        else:
                            deps.add(j)
                if o["dma"]:
                    sk = o["semkey"]
                    if sk in last_dma_by_sem:
                        deps.add(last_dma_by_sem[sk])
                    last_dma_by_sem[sk] = i
            deps.discard(i)
            o["deps"] = deps
            o["raw"] = raw
            if not o["dma"]:
                last_on_eng[e] = i
            for k in o["reads"]:
                r = readers.setdefault(k, {})
                if o["dma"]:
                    r.setdefault("dma", []).append(i)
                else:
                    r[e] = i
            for k in o["writes"]:
                last_w[k] = i
                readers[k] = {}
        waited = {}
        waited_dma = {}
        dma_count = {}
        for i, o in enumerate(ops):
            if o["dma"]:
                sk = o["semkey"]
                dma_count[sk] = dma_count.get(sk, 0) + 1
                o["dma_n"] = dma_count[sk]
            o["needs_inc"] = False
        for i, o in enumerate(ops):
            e = o["eng"]
            w = waited.setdefault(e, {})
            wd = waited_dma.setdefault(e, {})
            waits = []
            for j in sorted(o["deps"]):
                p = ops[j]
                if p["dma"]:
                    sk = p["semkey"]
                    if wd.get(sk, 0) >= p["dma_n"]:
                        continue
                    wd[sk] = p["dma_n"]
                    waits.append(("d", sk, p["dma_n"]))
                else:
                    pe_ = p["eng"]
                    if pe_ == e:
                        if e == "pe" or e == "sp":
                            continue
                        if (not SAME_ENGINE_SYNC) or (j not in o["raw"]):
                            continue
                    if w.get(pe_, -1) >= p["pos"]:
                        continue
                    w[pe_] = p["pos"]
                    p["needs_inc"] = True
                    waits.append(("c", pe_, j))
            o["waits"] = waits
        last_idx = {}
        for i, o in enumerate(ops):
            if not o["dma"]:
                last_idx[o["eng"]] = i
        for e_, i in last_idx.items():
            if e_ in COMPUTE:
                ops[i]["needs_inc"] = True
        cnt = {}
        for o in ops:
            if o["dma"]:
                continue
            e = o["eng"]
            if o["needs_inc"]:
                cnt[e] = cnt.get(e, 0) + 1
            o["count"] = cnt.get(e, 0)
        self.sem_keys = sorted({o["semkey"] for o in ops if o["dma"]}, key=str)
        self.final_dma = dict(dma_count)
        return cnt

    def emit(self, stack):
        nc = self.nc
        ops = self.ops
        cnt = self.resolve()
        esem = {e: stack.enter_context(nc.semaphore("s_" + e)) for e in COMPUTE}
        dsem = {k: stack.enter_context(nc.semaphore("d_%d" % i)) for i, k in enumerate(self.sem_keys)}
        block = stack.enter_context(nc.Block())
        by_eng = {}
        for o in ops:
            by_eng.setdefault(o["eng"], []).append(o)

        def run(engname, eng):
            for o in by_eng.get(engname, []):
                for wt in o["waits"]:
                    if wt[0] == "d":
                        eng.wait_ge(dsem[wt[1]], 16 * wt[2])
                    else:
                        eng.wait_ge(esem[wt[1]], ops[wt[2]]["count"])
                ins = o["fn"](eng)
                if o["dma"]:
                    ins.then_inc(dsem[o["semkey"]], 16)
                elif o["needs_inc"]:
                    ins.then_inc(esem[engname], 1)
            if engname == "sp":
                for k, n in self.final_dma.items():
                    eng.wait_ge(dsem[k], 16 * n)
                for e2 in COMPUTE:
                    if cnt.get(e2, 0) > 0:
                        eng.wait_ge(esem[e2], cnt[e2])

        @block.sync
        def _(sync):
            run("sp", sync)

        @block.tensor
        def _(tensor):
            run("pe", tensor)

        @block.scalar
        def _(scalar):
            run("act", scalar)

        @block.vector
        def _(vector):
            run("dve", vector)

        @block.gpsimd
        def _(gpsimd):
            run("pool", gpsimd)


SLAB_EIN = 0
SLAB_EOUT = 7
SLAB_FUP0 = 9
SLAB_FDN0 = 17
SLAB_OIN = 25
SLAB_OOUT = 29
SLAB_FUP1 = 31
SLAB_FDN1 = 39
NSLAB = 47


def build_program(units):
    nc = bass.Bass("TRN2", target_bir_lowering=False)

    def din(name, shape):
        return nc.dram_tensor(name, list(shape), F32, kind="ExternalInput").ap()

    def dout(name, shape):
        return nc.dram_tensor(name, list(shape), F32, kind="ExternalOutput").ap()

    xp = din("xp", [SPB, T, D]); xs = din("xs", [SPB * DS, D])
    csk = din("csk", [SPB, PAST, 512]); csv = din("csv", [SPB, PAST, 512])
    sconv = din("sconv", [SPB, 2, 512])
    cckv = din("cckv", [SPB, PAST, 256]); ckpe = din("ckpe", [SPB, PAST, 32])
    ein = din("ein", [D, 3072]); eout = din("eout", [D, D]); oin = din("oin", [D, 1696]); oout = din("oout", [D, D])
    fup = din("fup", [2, D, 4096]); fdn = din("fdn", [2, 4096, D])
    grow = din("grow", [79, 128])
    lng = din("lng", [1, 512]); lnb = din("lnb", [1, 512]); kvg = din("kvg", [1, 256]); bsd = din("bsd", [1, 512])
    bss = din("bss", [1, 4, 16])
    wsd = din("wsd", [4, 128, 128])
    wuq = din("wuq", [384, 768]); wuk = din("wuk", [8, 64, 256]); wuv = din("wuv", [8, 256, 64])
    cst = din("cst", [128, 1408])
    rope_t = din("rope_t", [128, 512]); rope_ts = din("rope_ts", [64, 32])
    rope_f = din("rope_f", [2, 32, T]); rope_fs = din("rope_fs", [2, 32, 64])

    y_p = dout("y_p", [SPB, T, D]); y_s = dout("y_s", [SPB * DS, D])
    sbk_p = dout("sbk_p", [SPB, T, 512]); sbv_p = dout("sbv_p", [SPB, T, 512])
    conv_p = dout("conv_p", [SPB, 2, 512])
    ckv_p = dout("ckv_p", [SPB, T, 256]); kpe_p = dout("kpe_p", [SPB, T, 32])
    sbk_s = dout("sbk_s", [SPB * DS, 512]); sbv_s = dout("sbv_s", [SPB * DS, 512])
    conv_s = dout("conv_s", [SPB, 2, 512])
    ckv_s = dout("ckv_s", [SPB * DS, 256]); kpe_s = dout("kpe_s", [SPB * DS, 32])
    sguv_s = dout("sguv_s", [SPB * DS, 512])

    wsc = nc.dram_tensor("wsc", [NSLAB, 128, 4096], BF16, kind="Internal").ap()

    P = Prog(nc)
    with ExitStack() as st:
        def sb(name, shape, dt):
            return st.enter_context(nc.sbuf_tensor(name, list(shape), dt))

        xT = sb("xT", [128, 8, 512], F32)
        identf = sb("identf", [128, 128], F32)
        identb = sb("identb", [128, 128], BF16)
        ntri = sb("ntri", [128, 128], BF16)
        nones = sb("nones", [128, 128], BF16)
        onesb = sb("onesb", [128, 128], BF16)
        sbm = sb("sbm", [128, 512], BF16)
        mlm = sb("mlm", [128, 512], BF16)
        gcol = sb("gcol", [128, 80], F32)
        lnG = sb("lnG", [128, 512], F32); lnB = sb("lnB", [128, 512], F32)
        kvG = sb("kvG", [128, 256], F32); bsB = sb("bsB", [128, 512], F32); bsBs = sb("bsBs", [128, 4, 64], F32)
        wsT = sb("wsT", [128, 4, 128], BF16); wsbd = sb("wsbd", [128, 4, 64], BF16)
        wuqn = sb("wuqn", [128, 3, 512], BF16); wuqp = sb("wuqp", [128, 3, 256], BF16)
        wuqs = sb("wuqs", [128, 3, 256], BF16)
        wukp = sb("wukp", [128, 4, 256], BF16); wuvb = sb("wuvb", [128, 8, 2, 64], BF16)
        ropeT = sb("ropeT", [128, 512], F32); ropeTs = sb("ropeTs", [128, 32], F32)
        h = sb("h", [128, 8, 512], BF16)
        mix = sb("mix", [128, 8, 512], BF16)
        o = sb("o", [128, 8, 512], F32)
        rstd = sb("rstd", [128, 512], F32)
        sq = sb("sq", [128, 2, 512], BF16)
        io_b = [sb("io0", [128, 1024], F32), sb("io1", [128, 1024], F32)]
        tmpf = sb("tmpf", [128, 2, 512], F32)
        cstate = sb("cstate", [128, 4, 2], F32)
        small4 = sb("small4", [128, 4, 32], F32)
        bart = sb("bart", [128, 4], F32)
        slab = sb("slab", [128, 2, 4096], BF16)
        AA = sb("arenaA", [128, 21248], BF16)
        AC = sb("arenaC", [128, 26624], BF16)
        ps = [st.enter_context(nc.psum_tensor("ps%d" % i, [128, 512], F32)) for i in range(8)]
        psb = [p_[:, :].bitcast(BF16) for p_ in ps]

        def carve(arena, off_bytes, nelem, dt):
            a0 = off_bytes // 2
            if dt == BF16:
                return arena[:, a0:a0 + nelem]
            return arena[:, a0:a0 + 2 * nelem].bitcast(F32)

        qT = carve(AA, 0, 2048, BF16).rearrange("p (m t) -> p m t", m=4)
        e_t = carve(AA, 4096, 512, F32)
        sp_t = carve(AA, 6144, 1024, BF16).rearrange("p (a t) -> p a t", a=2)
        spm_t = carve(AA, 8192, 1024, BF16).rearrange("p (a t) -> p a t", a=2)
        w_t = carve(AA, 10240, 1024, BF16).rearrange("p (a t) -> p a t", a=2)
        S_b = [carve(AA, 12288, 512, BF16), carve(AA, 22528, 512, BF16)]
        cin = carve(AA, 13312, 520, F32)
        kvout_b = [carve(AA, 15872, 1024, F32).rearrange("p (a t) -> p a t", a=2), carve(AA, 24576, 1024, F32).rearrange("p (a t) -> p a t", a=2)]
        ksn = carve(AA, 19968, 256, BF16).rearrange("p (m t) -> p m t", m=4)
        vnew = carve(AA, 20480, 512, BF16)
        qs_b = [carve(AA, 21504, 16, BF16), carve(AA, 21568, 16, BF16)]
        SA_b = [carve(AA, 28672, 16, BF16), carve(AA, 28736, 16, BF16)]
        SB_b = [carve(AA, 28800, 16, BF16), carve(AA, 28864, 16, BF16)]
        fE_b = [carve(AA, 28928, 16, F32), carve(AA, 29056, 16, F32)]
        tE_b = [carve(AA, 29184, 16, F32), carve(AA, 29312, 16, F32)]
        cqT = carve(AA, 0, 1536, BF16).rearrange("p (m t) -> p m t", m=3)
        qnT = carve(AA, 3072, 2048, BF16).rearrange("p (m t) -> p m t", m=4)
        uT = carve(AA, 7168, 2048, F32).rearrange("p (m t) -> p m t", m=4)
        vnb = carve(AA, 15360, 2048, BF16).rearrange("p (a t) -> p a t", a=4)
        ckst_b = [carve(AA, 19456, 288, F32), carve(AA, 40960, 288, F32)]
        qlat_b = [carve(AA, 20992, 1024, BF16).rearrange("p (a t) -> p a t", a=2), carve(AA, 37888, 1024, BF16).rearrange("p (a t) -> p a t", a=2)]
        qpe_b = [carve(AA, 23040, 512, BF16), carve(AA, 39936, 512, BF16)]
        OLs = carve(AA, 24064, 1024, BF16).rearrange("p (a t) -> p a t", a=2)
        rden = carve(AA, 26112, 512, F32)
        cosF = carve(AA, 28160, 512, F32)
        sinF = carve(AA, 30208, 512, F32)
        p_t = carve(AA, 32256, 1024, BF16).rearrange("p (a t) -> p a t", a=2)
        vnf = carve(AA, 34304, 512, F32)
        kpb_b = [carve(AA, 36352, 32, BF16), carve(AA, 42112, 32, BF16)]
        ckvn = carve(AA, 36416, 256, BF16)
        h1 = carve(AA, 0, 16384, BF16).rearrange("p (j t) -> p j t", j=32)
        cstS = carve(AA, 0, 1408, F32)
        growS = carve(AA, 5632, 128, F32)
        wsS = carve(AA, 6144, 512, F32).rearrange("p (g s) -> p g s", g=4)
        wuqS = carve(AA, 8192, 2304, F32).rearrange("p (k n) -> p k n", k=3)
        wukS = carve(AA, 17408, 1024, F32).rearrange("p (j c) -> p j c", j=4)
        wuvS = carve(AA, 21504, 1024, F32).rearrange("p (a b) -> p a b", a=16)

        ksT = carve(AC, 0, 8192, BF16).rearrange("p (m t) -> p m t", m=4)
        vtok = carve(AC, 16384, 8192, BF16).rearrange("p (a t) -> p a t", a=16)
        ckvT = carve(AC, 32768, 4096, BF16).rearrange("p (a t) -> p a t", a=2)
        ckvtok = carve(AC, 40960, 4096, BF16).rearrange("p (a t) -> p a t", a=16)
        kpeT = carve(AC, 49152, 2048, BF16)
        kld = carve(AC, 0, 2048, F32).rearrange("p (a d) -> p a d", a=32)
        vld = carve(AC, 8192, 2048, F32).rearrange("p (a d) -> p a d", a=32)
        kTs_b = [carve(AC, 16384, 4112, BF16), carve(AC, 24640, 4112, BF16)]
        vh_b = [carve(AC, 32896, 33 * 64, BF16).rearrange("p (a d) -> p a d", a=33), carve(AC, 37120, 33 * 64, BF16).rearrange("p (a d) -> p a d", a=33)]
        kbf_b = [carve(AC, 41344, 2048, BF16).rearrange("p (a d) -> p a d", a=32), carve(AC, 45440, 2048, BF16).rearrange("p (a d) -> p a d", a=32)]
        cstg = carve(AC, 0, 2048, F32)
        ckvtok_s = carve(AC, 8192, 33 * 256, BF16).rearrange("p (a d) -> p a d", a=33)
        ckvT_s = carve(AC, 25088, 2 * 4112, BF16).rearrange("p (a t) -> p a t", a=2)
        kpeT_s = carve(AC, 41536, 4112, BF16)
        kpbs = carve(AC, 49792, 1024, BF16).rearrange("p (a d) -> p a d", a=32)
        stg = carve(AC, 0, 8192, F32).rearrange("p (a n) -> p a n", a=2)
        sbf = carve(AC, 32768, 8192, BF16).rearrange("p (a n) -> p a n", a=2)

        state = dict(dbank=0, zb=0, bb=0, slabk=0, slab_issued=0, tmp=0)

        def dbank():
            b = state["dbank"] % 3
            state["dbank"] += 1
            return b

        def bkey(b):
            return "ps%d" % b

        def barrier():
            P.barrier(lambda e: e.memset(bart[:, 0:4], 0.0))

        def setup():
            P.dma(lambda e: e.dma_start(out=cstS[:, :], in_=cst[:, :]), writes=["cstS"], semkey="su0")
            P.op("act", lambda e: e.activation(out=identf[:, :], in_=cstS[:, 0:128], func=AF.Copy), reads=["cstS"], writes=["identf"])
            P.op("dve", lambda e: e.tensor_copy(out=identb[:, :], in_=cstS[:, 0:128]), reads=["cstS"], writes=["identb"])
            P.op("dve", lambda e: e.tensor_copy(out=ntri[:, :], in_=cstS[:, 128:256]), reads=["cstS"], writes=["ntri"])
            P.op("dve", lambda e: e.tensor_copy(out=sbm[:, :], in_=cstS[:, 256:768]), reads=["cstS"], writes=["sbm"])
            P.op("dve", lambda e: e.tensor_copy(out=mlm[:, :], in_=cstS[:, 768:1280]), reads=["cstS"], writes=["mlm"])
            P.op("pool", lambda e: e.memset(nones[:, :], -1.0), writes=["nones"])
            P.op("pool", lambda e: e.memset(onesb[:, :], 1.0), writes=["onesb"])
            P.op("pool", lambda e: e.memset(wsbd[:, :, :], 0.0), writes=["wsbd"])
            P.op("pool", lambda e: e.memset(cstate[:, :, :], 0.0), writes=["cstate"])
            P.dma(lambda e: e.dma_start(out=growS[0:79, :], in_=grow[:, :]), writes=["growS"], semkey="su1")
            P.op("pe", lambda e: e.transpose(ps[0][:, 0:79], growS[0:79, :], identf[0:79, 0:79]), reads=["growS", "identf"], writes=[bkey(0)])
            P.op("act", lambda e: e.activation(out=gcol[:, 0:79], in_=ps[0][:, 0:79], func=AF.Copy), reads=[bkey(0)], writes=["gcol"])
            P.dma(lambda e: e.dma_start(out=lnG[:, :], in_=lng[0:1, :].broadcast_to([128, 512])), writes=["lnG"], semkey="su2")
            P.dma(lambda e: e.dma_start(out=lnB[:, :], in_=lnb[0:1, :].broadcast_to([128, 512])), writes=["lnB"], semkey="su3")
            P.dma(lambda e: e.dma_start(out=kvG[:, :], in_=kvg[0:1, :].broadcast_to([128, 256])), writes=["kvG"], semkey="su4")
            P.dma(lambda e: e.dma_start(out=bsB[:, :], in_=bsd[0:1, :].broadcast_to([128, 512])), writes=["bsB"], semkey="su5")
            for j in range(4):
                P.dma(lambda e, j=j: e.dma_start(out=bsBs[:, :, j * 16:(j + 1) * 16], in_=bss[0:1, :, :].broadcast_to([128, 4, 16])),
                      writes=["bsBs"], semkey="su6")
            P.dma(lambda e: e.dma_start(out=ropeT[:, :], in_=rope_t[:, :]), writes=["ropeT"], semkey="su7")
            P.dma(lambda e: e.dma_start(out=ropeTs[0:64, :], in_=rope_ts[:, :]), writes=["ropeTs"], semkey="su8")
            P.dma(lambda e: e.dma_start(out=wsS[:, :, :], in_=wsd.rearrange("g t s -> t g s")), writes=["wsS"], semkey="su9")
            for g in range(4):
                P.op("dve", lambda e, g=g: e.tensor_tensor(out=wsS[:, g, :], in0=wsS[:, g, :], in1=cstS[:, 1280:1408], op=ALU.mult),
                     reads=["wsS", "cstS"], writes=["wsS"])
            P.op("pe", lambda e: [e.transpose(ps[1][:, g * 128:(g + 1) * 128], wsS[:, g, :], identf[:, :]) for g in range(4)][-1],
                 reads=["wsS", "identf"], writes=[bkey(1)])
            P.op("act", lambda e: e.activation(out=wsT[:, :, :], in_=ps[1][:, :].rearrange("p (g t) -> p g t", g=4), func=AF.Copy),
                 reads=[bkey(1)], writes=["wsT"])
            for j in range(4):
                P.dma(lambda e, j=j: e.dma_start(out=wsbd[16 * j:16 * j + 16, :, 16 * j:16 * j + 16], in_=wsT[0:16, :, 0:16]),
                      reads=["wsT", "wsbd"], writes=["wsbd"], semkey="su10")
            P.dma(lambda e: e.dma_start(out=wuqS[:, :, :], in_=wuq.rearrange("(k p) n -> p k n", p=128)), writes=["wuqS"], semkey="su11")
            for kc in range(3):
                src = wuqS[:, kc, :].rearrange("p (h d) -> p h d", d=96)
                P.op("dve", lambda e, kc=kc, src=src: e.tensor_copy(out=wuqn[:, kc, :].rearrange("p (h d) -> p h d", d=64), in_=src[:, :, 0:64]),
                     reads=["wuqS"], writes=["wuqn"])
                P.op("dve", lambda e, kc=kc, src=src: e.tensor_copy(out=wuqp[:, kc, :].rearrange("p (h d) -> p h d", d=32), in_=src[:, :, 64:96]),
                     reads=["wuqS"], writes=["wuqp"])
                P.op("pool", lambda e, kc=kc, src=src: e.tensor_copy(out=wuqs[:, kc, :].rearrange("p (h d) -> p h d", d=32)[:, :, 0:16], in_=src[:, :, 80:96]),
                     reads=["wuqS"], writes=["wuqs"])
                P.op("pool", lambda e, kc=kc, src=src: e.tensor_copy(out=wuqs[:, kc, :].rearrange("p (h d) -> p h d", d=32)[:, :, 16:32], in_=src[:, :, 64:80]),
                     reads=["wuqS"], writes=["wuqs"])
            for two in range(2):
                P.dma(lambda e, two=two: e.dma_start(out=wukS[two * 64:(two + 1) * 64, :, :], in_=wuk.rearrange("(j two) n c -> two n j c", two=2)[two]),
                      writes=["wukS"], semkey="su12")
            P.op("dve", lambda e: e.tensor_copy(out=wukp[:, :, :], in_=wukS[:, :, :]), reads=["wukS"], writes=["wukp"])
            P.dma(lambda e: e.dma_start(out=wuvS[:, :, :], in_=wuv.rearrange("h (cc p) v -> p (h cc) v", p=128)), writes=["wuvS"], semkey="su13")
            P.op("dve", lambda e: e.tensor_copy(out=wuvb[:, :, :, :].rearrange("p h c v -> p (h c) v"), in_=wuvS[:, :, :]), reads=["wuvS"], writes=["wuvb"])

            def Wv(W, c0, c1):
                return W.rearrange("(kc p) n -> p kc n", p=128)[:, :, c0:c1]

            slabs = []
            for c0 in (0, 512, 1024):
                slabs.append(([(Wv(ein, c0, c0 + 512), 0, 512)], 8, 512))
            for c in range(4):
                slabs.append(([(Wv(ein, 1536 + 128 * c, 1536 + 128 * c + 128), 0, 128),
                               (Wv(ein, 2048 + 128 * c, 2048 + 128 * c + 128), 128, 128),
                               (Wv(ein, 2560 + 128 * c, 2560 + 128 * c + 128), 256, 128)], 8, 384))
            for c0 in (0, 512):
                slabs.append(([(Wv(eout, c0, c0 + 512), 0, 512)], 8, 512))
            for j in range(8):
                slabs.append(([(Wv(fup[0], 512 * j, 512 * j + 512), 0, 512)], 8, 512))
            for n in range(8):
                slabs.append(([(Wv(fdn[0], 128 * n, 128 * n + 128), 0, 128)], 32, 128))
            slabs.append(([(Wv(oin, 0, 512), 0, 512)], 8, 512))
            slabs.append(([(Wv(oin, 1024, 1408), 0, 384)], 8, 384))
            slabs.append(([(Wv(oin, 512, 1024), 0, 512)], 8, 512))
            slabs.append(([(Wv(oin, 1408, 1696), 0, 288)], 8, 288))
            for c0 in (0, 512):
                slabs.append(([(Wv(oout, c0, c0 + 512), 0, 512)], 8, 512))
            for j in range(8):
                slabs.append(([(Wv(fup[1], 512 * j, 512 * j + 512), 0, 512)], 8, 512))
            for n in range(8):
                slabs.append(([(Wv(fdn[1], 128 * n, 128 * n + 128), 0, 128)], 32, 128))
            assert len(slabs) == NSLAB
            engs = ["dve", "act", "pool"]

            def load_slab(i):
                pieces, kc, nct = slabs[i]
                b = i % 2
                n = kc * nct
                sv = stg[:, b, 0:n].rearrange("p (k n) -> p k n", k=kc)
                for pi, (src, coff, ncol) in enumerate(pieces):
                    P.dma(lambda e, sv=sv, src=src, coff=coff, ncol=ncol: e.dma_start(out=sv[:, :, coff:coff + ncol], in_=src),
                          writes=["stg%d" % b], semkey="stg%d_%d" % (b, pi))

            load_slab(0)
            load_slab(1)
            for i, (pieces, kc, nct) in enumerate(slabs):
                b = i % 2
                n = kc * nct
                en = engs[i % 3]
                if en == "act":
                    P.op("act", lambda e, b=b, n=n: e.activation(out=sbf[:, b, 0:n], in_=stg[:, b, 0:n], func=AF.Copy),
                         reads=["stg%d" % b], writes=["sbf%d" % b])
                else:
                    P.op(en, lambda e, b=b, n=n: e.tensor_copy(out=sbf[:, b, 0:n], in_=stg[:, b, 0:n]),
                         reads=["stg%d" % b], writes=["sbf%d" % b])
                P.dma(lambda e, i=i, b=b, n=n: e.dma_start(out=wsc[i, :, 0:n], in_=sbf[:, b, 0:n]),
                      reads=["sbf%d" % b], writes=["wsc%d" % i], semkey="sbfo%d" % b)
                if i + 2 < len(slabs):
                    load_slab(i + 2)

        unit_slab_seq = list(range(NSLAB))
        total_slabs = len(units) * NSLAB

        slab_n = {3: 3072, 4: 3072, 5: 3072, 6: 3072, SLAB_OIN + 1: 3072, SLAB_OIN + 3: 2304}

        def issue_slab(k):
            idx = unit_slab_seq[k % NSLAB]
            b = k % 2
            n = slab_n.get(idx, 4096)
            P.dma(lambda e, idx=idx, b=b, n=n: e.dma_start(out=slab[:, b, 0:n], in_=wsc[idx, :, 0:n]),
                  reads=["wsc%d" % idx], writes=["slab%d" % b], semkey="slab%d" % b)

        def get_slab(expect):
            k = state["slabk"]
            assert unit_slab_seq[k % NSLAB] == expect, (k, expect)
            while state["slab_issued"] <= min(k + 1, total_slabs - 1):
                issue_slab(state["slab_issued"])
                state["slab_issued"] += 1
            state["slabk"] += 1
            b = k % 2
            return slab[:, b, :], "slab%d" % b

        def rms_stats(srcs, keys, Dn, U):
            bank = dbank()
            n = len(srcs)
            for c in range(n):
                P.op("act", lambda e, c=c: e.activation(out=sq[:, c % 2, 0:U], in_=srcs[c], func=AF.Square),
                     reads=[keys[c]], writes=["sq%d" % (c % 2)])
                P.op("pe", lambda e, c=c: e.matmul(ps[bank][:, 0:U], onesb[:, :], sq[:, c % 2, 0:U], start=(c == 0), stop=(c == n - 1)),
                     reads=["sq%d" % (c % 2), "onesb"], writes=[bkey(bank)])
            P.op("act", lambda e: e.activation(out=rstd[:, 0:U], in_=ps[bank][:, 0:U], func=AF.Ln, scale=1.0 / Dn, bias=EPS),
                 reads=[bkey(bank)], writes=["rstd"])
            P.op("act", lambda e: e.activation(out=rstd[:, 0:U], in_=rstd[:, 0:U], func=AF.Exp, scale=-0.5),
                 reads=["rstd"], writes=["rstd"])

        def pre_norm(gi, U):
            rms_stats([xT[:, c, 0:U] for c in range(8)], ["xT%d" % c for c in range(8)], 1024, U)
            for c in range(8):
                P.op("dve", lambda e, c=c: e.scalar_tensor_tensor(out=h[:, c, 0:U], in0=xT[:, c, 0:U], scalar=gcol[:, gi + c:gi + c + 1],
                                                                 in1=rstd[:, 0:U], op0=ALU.mult, op1=ALU.mult),
                     reads=["xT%d" % c, "rstd", "gcol"], writes=["h"])

        def post_norm_residual(gi, U):
            rms_stats([o[:, c, 0:U] for c in range(8)], ["o%d" % c for c in range(8)], 1024, U)
            for c in range(8):
                P.op("dve", lambda e, c=c: e.scalar_tensor_tensor(out=o[:, c, 0:U], in0=o[:, c, 0:U], scalar=gcol[:, gi + c:gi + c + 1],
                                                                 in1=rstd[:, 0:U], op0=ALU.mult, op1=ALU.mult),
                     reads=["o%d" % c, "rstd", "gcol"], writes=["o%d" % c])
                P.op("pool", lambda e, c=c: e.tensor_tensor(out=xT[:, c, 0:U], in0=xT[:, c, 0:U], in1=o[:, c, 0:U], op=ALU.add),
                     reads=["xT%d" % c, "o%d" % c], writes=["xT%d" % c])

        def proj_fm(sl, slk, col0, ncol, rhs_fn, rkeys, nkc, U, stride):
            bank = dbank()
            sv = sl[:, 0:nkc * stride].rearrange("p (k n) -> p k n", k=nkc)

            def fn(e):
                ins = None
                for kc in range(nkc):
                    ins = e.matmul(ps[bank][0:ncol, 0:U], sv[:, kc, col0:col0 + ncol], rhs_fn(kc), start=(kc == 0), stop=(kc == nkc - 1))
                return ins
            P.op("pe", fn, reads=[slk] + list(rkeys), writes=[bkey(bank)])
            return bank

        def proj_tm(sl, slk, ncol, tt, TT, stride):
            bank = dbank()
            sv = sl[:, 0:8 * stride].rearrange("p (k n) -> p k n", k=8)

            def fn(e):
                ins = None
                for kc in range(8):
                    ins = e.matmul(ps[bank][0:TT, 0:ncol], h[:, kc, tt * TT:(tt + 1) * TT], sv[:, kc, 0:ncol], start=(kc == 0), stop=(kc == 7))
                return ins
            P.op("pe", fn, reads=[slk, "h"], writes=[bkey(bank)])
            return bank

        def wout_and_post(slab0, gi, U):
            for sl_i in range(2):
                sl, slk = get_slab(slab0 + sl_i)
                for mm in range(4):
                    n = sl_i * 4 + mm
                    bank = proj_fm(sl, slk, mm * 128, 128, lambda kc: mix[:, kc, 0:U], ["mix%d" % c for c in range(8)], 8, U, 512)
                    P.op("dve", lambda e, n=n, bank=bank: e.tensor_copy(out=o[:, n, 0:U], in_=ps[bank][:, 0:U]),
                         reads=[bkey(bank)], writes=["o%d" % n])
            post_norm_residual(gi, U)

        def ffn(layer, U):
            pre_norm(32 + layer * 8, U)
            barrier()
            s_up = SLAB_FUP0 if layer == 0 else SLAB_FUP1
            s_dn = SLAB_FDN0 if layer == 0 else SLAB_FDN1
            for j in range(8):
                sl, slk = get_slab(s_up + j)
                for mm in range(4):
                    bank = proj_fm(sl, slk, mm * 128, 128, lambda kc: h[:, kc, 0:U], ["h"], 8, U, 512)
                    tb = state["tmp"] % 2
                    state["tmp"] += 1
                    P.op("act", lambda e, bank=bank, tb=tb: e.activation(out=tmpf[:, tb, 0:U], in_=ps[bank][:, 0:U], func=AF.Relu),
                         reads=[bkey(bank)], writes=["tmpf%d" % tb])
                    P.op("dve", lambda e, bank=bank, tb=tb, jj=4 * j + mm: e.tensor_tensor(out=h1[:, jj, 0:U], in0=tmpf[:, tb, 0:U], in1=ps[bank][:, 0:U], op=ALU.mult),
                         reads=[bkey(bank), "tmpf%d" % tb], writes=["h1_%d" % (4 * j + mm)])
            for n in range(8):
                sl, slk = get_slab(s_dn + n)
                bank = proj_fm(sl, slk, 0, 128, lambda kc: h1[:, kc, 0:U], ["h1_%d" % c for c in range(32)], 32, U, 128)
                P.op("dve", lambda e, n=n, bank=bank: e.tensor_copy(out=o[:, n, 0:U], in_=ps[bank][:, 0:U]),
                     reads=[bkey(bank)], writes=["o%d" % n])
            post_norm_residual(48 + layer * 8, U)
            barrier()

        def sb_tile(kAP, qAP, vAP, nk, ncols, S_ap, acc_ap, acc_key, mask, first, last, rk, tu, Sk="S0", vk="vtok", acc_start=None, s_add=None):
            if acc_start is None:
                acc_start = first
            if s_add is None:
                s_add = not last
            zb = 3 + (tu % 2)
            bb = 5 + (tu % 2)
            a = tu % 2
            P.op("pe", lambda e: e.matmul(ps[zb][0:nk, 0:ncols], kAP, qAP, start=True, stop=True), reads=rk, writes=[bkey(zb)])
            P.op("act", lambda e: e.activation(out=e_t[0:nk, 0:ncols], in_=ps[zb][0:nk, 0:ncols], func=AF.Exp), reads=[bkey(zb)], writes=["e_t"])
            P.op("act", lambda e: e.activation(out=sp_t[0:nk, a, 0:ncols], in_=e_t[0:nk, 0:ncols], func=AF.Ln, bias=1.0), reads=["e_t"], writes=["sp%d" % a])
            if mask is not None:
                P.op("dve", lambda e: e.tensor_tensor(out=spm_t[0:nk, a, 0:ncols], in0=sp_t[0:nk, a, 0:ncols], in1=mask, op=ALU.mult),
                     reads=["sp%d" % a, "sbm"], writes=["spm%d" % a])
                spm = spm_t[0:nk, a, 0:ncols]
                spk = "spm%d" % a
            else:
                spm = sp_t[0:nk, a, 0:ncols]
                spk = "sp%d" % a
            yield

            def fnB(e):
                e.matmul(ps[bb][0:nk, 0:ncols], ntri[0:nk, 0:nk], spm, start=True, stop=False)
                if not first:
                    e.matmul(ps[bb][0:nk, 0:ncols], nones[0:128, 0:nk], S_ap, start=False, stop=False)
                return e.matmul(ps[bb][0:nk, 0:ncols], kAP, qAP, start=False, stop=True)
            P.op("pe", fnB, reads=[spk, Sk, "ntri", "nones"] + list(rk), writes=[bkey(bb)])
            if s_add:
                P.op("pool", lambda e: e.tensor_tensor(out=S_ap[0:nk, :], in0=S_ap[0:nk, :], in1=spm, op=ALU.add), reads=[spk, Sk], writes=[Sk])
            P.op("act", lambda e: e.activation(out=w_t[0:nk, a, 0:ncols], in_=ps[bb][0:nk, 0:ncols], func=AF.Exp), reads=[bkey(bb)], writes=["w%d" % a])
            if mask is not None:
                P.op("dve", lambda e: e.tensor_tensor(out=w_t[0:nk, a, 0:ncols], in0=w_t[0:nk, a, 0:ncols], in1=mask, op=ALU.mult),
                     reads=["w%d" % a, "sbm"], writes=["w%d" % a])
            yield
            P.op("pe", lambda e: e.matmul(acc_ap, vAP, w_t[0:nk, a, 0:ncols], start=acc_start, stop=last), reads=["w%d" % a, vk], writes=[acc_key])

        def wrap(g, pre=None, post=None):
            if pre is not None:
                pre()
            try:
                while True:
                    next(g)
                    yield
            except StopIteration:
                pass
            if post is not None:
                post()

        def run_pipelined(gens, hook_step=None, hook=None):
            pending = list(gens)
            active = []
            step = 0
            while pending or active:
                while pending and not hasattr(pending[0], "__next__"):
                    pending.pop(0)()
                if pending:
                    active.append(pending.pop(0))
                nxt = []
                for g in reversed(active):
                    try:
                        next(g)
                        nxt.append(g)
                    except StopIteration:
                        pass
                active = list(reversed(nxt))
                if hook is not None and step == hook_step:
                    hook()
                step += 1
            if hook is not None and step <= hook_step:
                hook()

        def mla_tile(cT0, cT1, kpT, ctok, nk, q0, q1, mask, first, last, hp, tu, qb_=0):
            zb = 3 + (tu % 2)
            a = tu % 2
            qlat = qlat_b[qb_]
            qpe = qpe_b[qb_]

            def fnZ(e):
                e.matmul(ps[zb][0:nk, q0:q1], cT0, qlat[:, 0, q0:q1], start=True, stop=False)
                e.matmul(ps[zb][0:nk, q0:q1], cT1, qlat[:, 1, q0:q1], start=False, stop=False)
                return e.matmul(ps[zb][0:nk, q0:q1], kpT, qpe[0:32, q0:q1], start=False, stop=True)
            P.op("pe", fnZ, reads=["ckvT", "kpeT", "qlat%d" % qb_, "qpe%d" % qb_], writes=[bkey(zb)])
            P.op("act", lambda e: e.activation(out=p_t[0:nk, a, q0:q1], in_=ps[zb][0:nk, q0:q1], func=AF.Exp, scale=MLA_SCALE),
                 reads=[bkey(zb)], writes=["p%d" % a])
            if mask is not None:
                P.op("dve", lambda e: e.tensor_tensor(out=p_t[0:nk, a, q0:q1], in0=p_t[0:nk, a, q0:q1], in1=mask, op=ALU.mult),
                     reads=["p%d" % a, "mlm"], writes=["p%d" % a])
            yield

            def fnO(e):
                e.matmul(ps[5][:, q0:q1], ctok[:, 0:128], p_t[0:nk, a, q0:q1], start=first, stop=last)
                e.matmul(ps[6][:, q0:q1], ctok[:, 128:256], p_t[0:nk, a, q0:q1], start=first, stop=last)
                return e.matmul(ps[7][:, q0:q1], onesb[0:nk, :], p_t[0:nk, a, q0:q1], start=first, stop=last)
            P.op("pe", fnO, reads=["p%d" % a, "ckvtok", "onesb"], writes=[bkey(5), bkey(6), bkey(7)])

        def unit(kind, s, qb):
            isp = (kind == "p")
            U = 512 if isp else 64
            TT = 128 if isp else 64
            NTT = U // TT
            row0 = qb * 512

            for tt in range(NTT):
                src = xp[s, row0 + tt * 128: row0 + tt * 128 + 128, :] if isp else xs[0:64, :]
                xin = io_b[tt % 2]
                xk = "io%d" % (tt % 2)
                P.dma(lambda e, src=src, xin=xin: e.dma_start(out=xin[0:TT, :], in_=src), writes=[xk], semkey=xk)
                for half in range(2):
                    bank = dbank()
                    P.op("pe", lambda e, half=half, bank=bank, xin=xin: [e.transpose(ps[bank][:, c4 * TT:(c4 + 1) * TT], xin[0:TT, (half * 4 + c4) * 128:(half * 4 + c4 + 1) * 128], identf[0:TT, 0:TT]) for c4 in range(4)][-1],
                         reads=[xk, "identf"], writes=[bkey(bank)])
                    P.op("act", lambda e, half=half, bank=bank, tt=tt: e.activation(out=xT[:, half * 4:half * 4 + 4, tt * TT:(tt + 1) * TT],
                                                                                in_=ps[bank][:, 0:4 * TT].rearrange("p (a t) -> p a t", a=4), func=AF.Copy),
                         reads=[bkey(bank)], writes=["xT%d" % (half * 4 + c4) for c4 in range(4)])
            if float(os.environ.get("KSTOP", 99)) <= 1:
                return
            pre_norm(0, U)
            if float(os.environ.get("KSTOP", 99)) <= 1.2:
                return
            sl, slk = get_slab(SLAB_EIN + 0)
            for m in range(4):
                bank = proj_fm(sl, slk, m * 128, 128, lambda kc: h[:, kc, 0:U], ["h"], 8, U, 512)
                P.op("act", lambda e, m=m, bank=bank: e.activation(out=qT[:, m, 0:U], in_=ps[bank][:, 0:U], func=AF.Copy),
                     reads=[bkey(bank)], writes=["qT"])
            if float(os.environ.get("KSTOP", 99)) <= 1.4:
                return
            sl, slk = get_slab(SLAB_EIN + 1)
            for m in range(4):
                bank = proj_fm(sl, slk, m * 128, 128, lambda kc: h[:, kc, 0:U], ["h"], 8, U, 512)
                if isp:
                    P.op("act", lambda e, m=m, bank=bank: e.activation(out=ksT[:, m, row0:row0 + 512], in_=ps[bank][:, 0:U], func=AF.Copy, scale=SB_SCALE),
                         reads=[bkey(bank)], writes=["ksT"])
                else:
                    P.op("act", lambda e, m=m, bank=bank: e.activation(out=ksn[:, m, 0:U], in_=ps[bank][:, 0:U], func=AF.Copy, scale=SB_SCALE),
                         reads=[bkey(bank)], writes=["ksn"])
            for tt in range(NTT):
                bank = proj_tm(sl, slk, 512, tt, TT, 512)
                kvout = kvout_b[tt % 2]
                kk0 = "kvout0_%d" % (tt % 2)
                P.op("dve", lambda e, bank=bank, kvout=kvout: e.tensor_copy(out=kvout[0:TT, 0, :], in_=ps[bank][0:TT, :]), reads=[bkey(bank)], writes=[kk0])
                dst = sbk_p[s, row0 + tt * 128: row0 + tt * 128 + 128, :] if isp else sbk_s[0:64, :]
                P.dma(lambda e, dst=dst, kvout=kvout: e.dma_start(out=dst, in_=kvout[0:TT, 0, :]), reads=[kk0], writes=[], semkey=kk0)
            if float(os.environ.get("KSTOP", 99)) <= 1.6:
                return
            sl, slk = get_slab(SLAB_EIN + 2)
            for tt in range(NTT):
                bank = proj_tm(sl, slk, 512, tt, TT, 512)
                kvout = kvout_b[tt % 2]
                kk1 = "kvout1_%d" % (tt % 2)
                P.op("dve", lambda e, bank=bank, kvout=kvout: e.tensor_copy(out=kvout[0:TT, 1, :], in_=ps[bank][0:TT, :]), reads=[bkey(bank)], writes=[kk1])
                dst = sbv_p[s, row0 + tt * 128: row0 + tt * 128 + 128, :] if isp else sbv_s[0:64, :]
                P.dma(lambda e, dst=dst, kvout=kvout: e.dma_start(out=dst, in_=kvout[0:TT, 1, :]), reads=[kk1], writes=[], semkey=kk1)
                if isp:
                    P.op("act", lambda e, bank=bank, tt=tt: e.activation(out=vtok[:, qb * 4 + tt, :], in_=ps[bank][:, :], func=AF.Copy),
                         reads=[bkey(bank)], writes=["vtok"])
                elif not os.environ.get("KNOVNEW"):
                    P.op("act", lambda e, bank=bank: e.activation(out=vnew[0:64, :], in_=ps[bank][0:64, :], func=AF.Copy),
                         reads=[bkey(bank)], writes=["vnew"])
            if float(os.environ.get("KSTOP", 99)) <= 1.8:
                return
            if isp and qb == 0:
                P.op("pool", lambda e: e.memset(cstate[:, :, :], 0.0), writes=["cstate"])
            for c in range(4):
                if os.environ.get("KNOCONV") and not isp:
                    state["slabk"] += 1
                    continue
                sl, slk = get_slab(SLAB_EIN + 3 + c)
                b0 = proj_fm(sl, slk, 0, 128, lambda kc: h[:, kc, 0:U], ["h"], 8, U, 384)
                b1 = proj_fm(sl, slk, 128, 128, lambda kc: h[:, kc, 0:U], ["h"], 8, U, 384)
                b2 = proj_fm(sl, slk, 256, 128, lambda kc: h[:, kc, 0:U], ["h"], 8, U, 384)
                if isp:
                    cv = cin[:, 0:514]
                    cur = cv[:, 2:514]
                    t0, t1_, t2 = cv[:, 0:512], cv[:, 1:513], cv[:, 2:514]
                    tv = tmpf[:, 1, 0:512]
                    g1 = tmpf[:, 0, 0:512]
                    pv = lambda b: ps[b][:, 0:512]
                    P.op("dve", lambda e, c=c: e.tensor_copy(out=cin[:, 0:2], in_=cstate[:, c, :]), reads=["cstate"], writes=["cin"])
                else:
                    cv = cin[:, 0:72].rearrange("p (s t) -> p s t", s=4)
                    cur = cv[:, :, 2:18]
                    t0, t1_, t2 = cv[:, :, 0:16], cv[:, :, 1:17], cv[:, :, 2:18]
                    tv = tmpf[:, 1, 0:64].rearrange("p (s t) -> p s t", s=4)
                    g1 = tmpf[:, 0, 0:64].rearrange("p (s t) -> p s t", s=4)
                    pv = lambda b: ps[b][:, 0:64].rearrange("p (s t) -> p s t", s=4)
                    for s4 in range(4):
                        for j2 in range(2):
                            P.dma(lambda e, c=c, cv=cv, s4=s4, j2=j2: e.dma_start(out=cv[:, s4, j2:j2 + 1], in_=sconv[s4, j2:j2 + 1, c * 128:(c + 1) * 128].rearrange("j p -> p j")),
                                  reads=[], writes=["cinp%d" % (s4 * 2 + j2)], semkey="cinld%d" % (s4 * 2 + j2))
                P.op("act", lambda e, g1=g1, b1=b1, pv=pv: e.activation(out=g1, in_=pv(b1), func=AF.Copy), reads=[bkey(b1)], writes=["tmpf0"])
                P.op("dve", lambda e, cur=cur, g1=g1, b2=b2, pv=pv: e.tensor_tensor(out=cur, in0=pv(b2), in1=g1, op=ALU.mult),
                     reads=[bkey(b2), "tmpf0"], writes=["cin"])
                wi = 64 + c
                cpk = [] if isp else ["cinp%d" % i for i in range(8)]
                P.op("dve", lambda e, tv=tv, t0=t0, wi=wi: e.tensor_scalar(out=tv, in0=t0, scalar1=gcol[:, wi:wi + 1], scalar2=None, op0=ALU.mult), reads=["cin", "gcol"] + cpk, writes=["tmpf1"])
                P.op("dve", lambda e, tv=tv, t1_=t1_, wi=wi: e.scalar_tensor_tensor(out=tv, in0=t1_, scalar=gcol[:, wi + 4:wi + 5], in1=tv, op0=ALU.mult, op1=ALU.add), reads=["cin", "tmpf1", "gcol"] + cpk, writes=["tmpf1"])
                P.op("dve", lambda e, tv=tv, t2=t2, wi=wi: e.scalar_tensor_tensor(out=tv, in0=t2, scalar=gcol[:, wi + 8:wi + 9], in1=tv, op0=ALU.mult, op1=ALU.add), reads=["cin", "tmpf1", "gcol"] + cpk, writes=["tmpf1"])
                if isp:
                    P.op("dve", lambda e, c=c, tv=tv, b0=b0: e.tensor_tensor(out=mix[:, 4 + c, 0:512], in0=tv, in1=ps[b0][:, 0:512], op=ALU.mult), reads=["tmpf1", bkey(b0)], writes=["mix%d" % (4 + c)])
                    P.op("pool", lambda e, c=c: e.tensor_copy(out=cstate[:, c, :], in_=cin[:, 512:514]), reads=["cin"], writes=["cstate"])
                    if qb == 3:
                        for j2 in range(2):
                            P.dma(lambda e, c=c, j2=j2: e.dma_start(out=conv_p[s, j2:j2 + 1, c * 128:(c + 1) * 128].rearrange("j p -> p j"), in_=cin[:, 512 + j2:513 + j2]),
                                  reads=["cin"], writes=[], semkey="convo%d" % j2)
                else:
                    P.op("dve", lambda e, c=c, tv=tv, b0=b0, pv=pv: e.tensor_tensor(out=mix[:, 4 + c, 0:64].rearrange("p (s t) -> p s t", s=4), in0=tv, in1=pv(b0), op=ALU.mult),
                         reads=["tmpf1", bkey(b0)], writes=["mix%d" % (4 + c)])
                    for s4 in range(4):
                        for j2 in range(2):
                            P.dma(lambda e, c=c, cv=cv, s4=s4, j2=j2: e.dma_start(out=conv_s[s4, j2:j2 + 1, c * 128:(c + 1) * 128].rearrange("j p -> p j"), in_=cv[:, s4, 16 + j2:17 + j2]),
                                  reads=["cin"], writes=[], semkey="convo%d" % (s4 * 2 + j2))

            if float(os.environ.get("KSTOP", 99)) <= 2:
                return
            tu = 0
            if isp:
                nt = 4 * qb + 4
                gens = []
                for hh in range(8):
                    hp, m = hh % 2, hh // 2
                    S_t = S_b[hp]
                    Sk = "S%d" % hp
                    pre = lambda S_t=S_t, Sk=Sk: P.op("pool", lambda e: e.memset(S_t[:, 0:512], 0.0), writes=[Sk])
                    post = lambda hp=hp, m=m: P.op("act", lambda e: e.activation(out=mix[hp * 64:(hp + 1) * 64, m, 0:512], in_=ps[7][hp * 64:(hp + 1) * 64, 0:512], func=AF.Copy),
                                                   reads=["acc%d" % hp], writes=["mix%d" % m])
                    for idx, kt in enumerate(range(nt - 1, -1, -1)):
                        j = kt - 4 * qb
                        q0 = 128 * j if j > 0 else 0
                        mask = sbm[:, 0:512 - q0] if j >= 0 else None
                        g = sb_tile(ksT[hp * 64:(hp + 1) * 64, m, kt * 128:(kt + 1) * 128], qT[hp * 64:(hp + 1) * 64, m, q0:512],
                                    vtok[:, kt, hh * 64:(hh + 1) * 64], 128, 512 - q0, S_t[:, q0:512],
                                    ps[7][hp * 64:(hp + 1) * 64, q0:512], "acc%d" % hp, mask, idx == 0, idx == nt - 1, ["ksT", "qT"], tu, Sk)
                        gens.append(wrap(g, pre if idx == 0 else None, post if idx == nt - 1 else None))
                        tu += 1
                run_pipelined(gens)
            else:
                items = []
                heads = [(ss, hh) for ss in range(SPB) for hh in range(8)]

                def prologue(ss, hh, b):
                    hp, m = hh % 2, hh // 2
                    kTs, vh, kbf, qs_t = kTs_b[b], vh_b[b], kbf_b[b], qs_b[b]
                    P.dma(lambda e: e.dma_start(out=kld[:, :, :], in_=csk[ss, :, hh * 64:(hh + 1) * 64].rearrange("(a p) d -> p a d", p=128)),
                          writes=["kld"], semkey="kld")
                    P.dma(lambda e: e.dma_start(out=vld[:, :, :], in_=csv[ss, :, hh * 64:(hh + 1) * 64].rearrange("(a p) d -> p a d", p=128)),
                          writes=["vld"], semkey="vld")
                    P.op("dve", lambda e: e.tensor_copy(out=kbf[:, :, :], in_=kld[:, :, :]), reads=["kld"], writes=["kbf%d" % b])
                    P.op("pool", lambda e: e.tensor_copy(out=vh[:, 0:32, :], in_=vld[:, :, :]), reads=["vld"], writes=["vh%d" % b])
                    for g in range(8):
                        bank = dbank()
                        P.op("pe", lambda e, g=g, bank=bank: [e.transpose(psb[bank][0:64, i * 128:(i + 1) * 128], kbf[:, 4 * g + i, :], identb[:, :]) for i in range(4)][-1],
                             reads=["kbf%d" % b, "identb"], writes=[bkey(bank)])
                        P.op("dve", lambda e, g=g, bank=bank: e.tensor_scalar(out=kTs[0:64, g * 512:(g + 1) * 512], in0=psb[bank][0:64, 0:512], scalar1=SB_SCALE, scalar2=None, op0=ALU.mult),
                             reads=[bkey(bank)], writes=["kTs%d" % b])
                    P.op("dve", lambda e: e.tensor_copy(out=kTs[0:64, 4096:4112], in_=ksn[hp * 64:(hp + 1) * 64, m, ss * 16:(ss + 1) * 16]),
                         reads=["ksn"], writes=["kTs%d" % b])
                    P.op("dve", lambda e: e.tensor_copy(out=qs_t[0:64, 0:16], in_=qT[hp * 64:(hp + 1) * 64, m, ss * 16:(ss + 1) * 16]),
                         reads=["qT"], writes=["qs%d" % b])
                    P.dma(lambda e: e.dma_start(out=vh[0:16, 32, :], in_=vnew[ss * 16:(ss + 1) * 16, hh * 64:(hh + 1) * 64]),
                          reads=["vnew", "vh%d" % b], writes=["vh%d" % b], semkey="vhn")

                for hi, (ss, hh) in enumerate(heads):
                    b = hi % 2
                    hp, m = hh % 2, hh // 2
                    kTs, vh, qs_t = kTs_b[b], vh_b[b], qs_b[b]
                    S_A, S_B = SA_b[b], SB_b[b]
                    fE, tE = fE_b[b], tE_b[b]
                    if hi == 0:
                        items.append(lambda: prologue(heads[0][0], heads[0][1], 0))

                    def pre(S_A=S_A, S_B=S_B, b=b):
                        P.op("pool", lambda e: e.memset(S_A[:, 0:16], 0.0), writes=["SA%d" % b])
                        P.op("pool", lambda e: e.memset(S_B[:, 0:16], 0.0), writes=["SB%d" % b])

                    def post(hp=hp, m=m, ss=ss, S_A=S_A, fE=fE, tE=tE, b=b):
                        bank = dbank()
                        P.op("pe", lambda e: e.matmul(ps[bank][hp * 64:(hp + 1) * 64, 0:16], nones[0:128, 0:64], S_A[:, 0:16], start=True, stop=True),
                             reads=["SA%d" % b, "nones"], writes=[bkey(bank)])
                        P.op("act", lambda e: e.activation(out=fE[hp * 64:(hp + 1) * 64, 0:16], in_=ps[bank][hp * 64:(hp + 1) * 64, 0:16], func=AF.Exp),
                             reads=[bkey(bank)], writes=["fE%d" % b])
                        P.op("dve", lambda e: e.tensor_tensor(out=tE[hp * 64:(hp + 1) * 64, 0:16], in0=ps[7][hp * 64:(hp + 1) * 64, 16:32], in1=fE[hp * 64:(hp + 1) * 64, 0:16], op=ALU.mult),
                             reads=["acc%d" % hp, "fE%d" % b], writes=["tE%d" % b])
                        P.op("dve", lambda e: e.tensor_tensor(out=mix[hp * 64:(hp + 1) * 64, m, ss * 16:(ss + 1) * 16], in0=ps[7][hp * 64:(hp + 1) * 64, 0:16], in1=tE[hp * 64:(hp + 1) * 64, 0:16], op=ALU.add),
                             reads=["acc%d" % hp, "tE%d" % b], writes=["mix%d" % m])

                    chA = list(range(32, 16, -1))
                    chB = list(range(16, -1, -1))
                    seq = []
                    for i in range(17):
                        if i < len(chA):
                            seq.append(("A", i, chA[i]))
                        seq.append(("B", i, chB[i]))
                    for si, (ch, ci, kt) in enumerate(seq):
                        nk = 16 if kt == 32 else 128
                        mask = sbm[0:16, 0:16] if kt == 32 else None
                        k0 = kt * 128
                        isA = (ch == "A")
                        S_c = S_A if isA else S_B
                        Sk = ("SA%d" if isA else "SB%d") % b
                        clen = len(chA) if isA else len(chB)
                        acc_ap = ps[7][hp * 64:(hp + 1) * 64, 0:16] if isA else ps[7][hp * 64:(hp + 1) * 64, 16:32]
                        g = sb_tile(kTs[0:64, k0:k0 + nk], qs_t[0:64, 0:16], vh[0:nk, kt, :], nk, 16, S_c[:, 0:16],
                                    acc_ap, "acc%d" % hp, mask, ci == 0, ci == clen - 1, ["kTs%d" % b, "qs%d" % b], tu, Sk, "vh%d" % b,
                                    acc_start=(si == 0), s_add=(True if isA else (ci != clen - 1)))
                        items.append(wrap(g, pre if si == 0 else None, post if si == len(seq) - 1 else None))
                        tu += 1
                        if si == 6 and hi + 1 < len(heads):
                            items.append(lambda hi=hi: prologue(heads[hi + 1][0], heads[hi + 1][1], (hi + 1) % 2))
                run_pipelined(items)
            if float(os.environ.get("KSTOP", 99)) <= 3:
                return
            wout_and_post(SLAB_EOUT, 16 + 0, U)
            ffn(0, U)

            if float(os.environ.get("KSTOP", 99)) <= 4:
                return
            pre_norm(8, U)
            sl, slk = get_slab(SLAB_OIN + 0)
            for m in range(4):
                bank = proj_fm(sl, slk, m * 128, 128, lambda kc: h[:, kc, 0:U], ["h"], 8, U, 512)
                P.op("act", lambda e, m=m, bank=bank: e.activation(out=uT[:, m, 0:U], in_=ps[bank][:, 0:U], func=AF.Copy), reads=[bkey(bank)], writes=["uT"])
            sl, slk = get_slab(SLAB_OIN + 1)
            for m in range(3):
                bank = proj_fm(sl, slk, m * 128, 128, lambda kc: h[:, kc, 0:U], ["h"], 8, U, 384)
                P.op("dve", lambda e, m=m, bank=bank: e.tensor_copy(out=o[:, m, 0:U], in_=ps[bank][:, 0:U]), reads=[bkey(bank)], writes=["o%d" % m])
            rms_stats([o[:, m, 0:U] for m in range(3)], ["o0", "o1", "o2"], 384, U)
            for m in range(3):
                P.op("dve", lambda e, m=m: e.scalar_tensor_tensor(out=cqT[:, m, 0:U], in0=o[:, m, 0:U], scalar=gcol[:, 76 + m:77 + m], in1=rstd[:, 0:U], op0=ALU.mult, op1=ALU.mult),
                     reads=["o%d" % m, "rstd", "gcol"], writes=["cqT"])
            sl, slk = get_slab(SLAB_OIN + 2)
            for tt in range(NTT):
                bank = proj_tm(sl, slk, 512, tt, TT, 512)
                sm = small4[:, tt, :]
                k_ = "sm%d_" % tt
                ts = tt % 2
                tk = "tmpf%d" % ts
                P.op("dve", lambda e, bank=bank, sm=sm: e.bn_stats(out=sm[0:TT, 0:6], in_=ps[bank][0:TT, 0:512]), reads=[bkey(bank)], writes=[k_ + "bn"])
                P.op("dve", lambda e, sm=sm: e.bn_aggr(out=sm[0:TT, 6:8], in_=sm[0:TT, 0:6]), reads=[k_ + "bn"], writes=[k_ + "mv"])
                P.op("act", lambda e, sm=sm: e.activation(out=sm[0:TT, 8:9], in_=sm[0:TT, 7:8], func=AF.Ln, bias=EPS), reads=[k_ + "mv"], writes=[k_ + "rv"])
                P.op("act", lambda e, sm=sm: e.activation(out=sm[0:TT, 8:9], in_=sm[0:TT, 8:9], func=AF.Exp, scale=-0.5), reads=[k_ + "rv"], writes=[k_ + "rv"])
                P.op("dve", lambda e, bank=bank, sm=sm, ts=ts: e.tensor_scalar(out=tmpf[0:TT, ts, :], in0=ps[bank][0:TT, 0:512], scalar1=sm[0:TT, 6:7], scalar2=sm[0:TT, 8:9], op0=ALU.subtract, op1=ALU.mult),
                     reads=[bkey(bank), k_ + "mv", k_ + "rv"], writes=[tk])
                P.op("pool", lambda e, ts=ts: e.tensor_tensor(out=tmpf[0:TT, ts, :], in0=tmpf[0:TT, ts, :], in1=lnG[0:TT, :], op=ALU.mult), reads=[tk, "lnG"], writes=[tk])
                if isp:
                    P.op("pool", lambda e, tt=tt, ts=ts: e.tensor_tensor(out=vnb[0:TT, tt, :], in0=tmpf[0:TT, ts, :], in1=lnB[0:TT, :], op=ALU.add), reads=[tk, "lnB"], writes=["vnb"])
                else:
                    P.op("pool", lambda e, ts=ts: e.tensor_tensor(out=vnf[0:TT, :], in0=tmpf[0:TT, ts, :], in1=lnB[0:TT, :], op=ALU.add), reads=[tk, "lnB"], writes=["vnf"])
                    P.op("pool", lambda e, tt=tt: e.tensor_copy(out=vnb[0:TT, tt, :], in_=vnf[0:TT, :]), reads=["vnf"], writes=["vnb"])
                    P.dma(lambda e: e.dma_start(out=sguv_s[0:64, :], in_=vnf[0:64, :]), reads=["vnf"], writes=[], semkey="vnfo")
            for g in range(4):
                bank = dbank()

                def fng(e, g=g, bank=bank):
                    ins = None
                    for tt in range(NTT):
                        rhs = wsT[:, g, :] if isp else wsbd[0:64, g, :]
                        ins = e.matmul(ps[bank][:, tt * TT:(tt + 1) * TT], vnb[0:TT, tt, g * 128:(g + 1) * 128], rhs, start=True, stop=True)
                    return ins
                P.op("pe", fng, reads=["vnb", "wsT", "wsbd"], writes=[bkey(bank)])
                if isp:
                    for tt in range(NTT):
                        P.op("dve", lambda e, g=g, tt=tt, bank=bank: e.tensor_tensor(out=tmpf[:, 1, tt * 128:(tt + 1) * 128], in0=ps[bank][:, tt * 128:(tt + 1) * 128], in1=bsB[:, g * 128:(g + 1) * 128], op=ALU.add),
                             reads=[bkey(bank), "bsB"], writes=["tmpf1"])
                else:
                    P.op("dve", lambda e, g=g, bank=bank: e.tensor_tensor(out=tmpf[:, 1, 0:64], in0=ps[bank][:, 0:64], in1=bsBs[:, g, :], op=ALU.add),
                         reads=[bkey(bank), "bsBs"], writes=["tmpf1"])
                P.op("pool", lambda e, g=g: e.tensor_tensor(out=mix[:, g, 0:U], in0=tmpf[:, 1, 0:U], in1=uT[:, g, 0:U], op=ALU.mult), reads=["tmpf1", "uT"], writes=["mix%d" % g])
            sl, slk = get_slab(SLAB_OIN + 3)
            for tt in range(NTT):
                bank = proj_tm(sl, slk, 288, tt, TT, 288)
                tile_i = qb * 4 + tt
                sm = small4[:, tt, :]
                k_ = "sm%d_" % tt
                ts = tt % 2
                tk = "tmpf%d" % ts
                ckst = ckst_b[ts]
                ck = "ckst%d" % ts
                kpb = kpb_b[ts]
                kk = "kpb%d" % ts
                P.op("act", lambda e, bank=bank, sm=sm, ts=ts: e.activation(out=tmpf[0:TT, ts, 0:256], in_=ps[bank][0:TT, 0:256], func=AF.Square, accum_out=sm[0:TT, 10:11]),
                     reads=[bkey(bank)], writes=[tk, k_ + "ss"])
                P.op("act", lambda e, sm=sm: e.activation(out=sm[0:TT, 11:12], in_=sm[0:TT, 10:11], func=AF.Ln, scale=1.0 / 256, bias=EPS), reads=[k_ + "ss"], writes=[k_ + "rk"])
                P.op("act", lambda e, sm=sm: e.activation(out=sm[0:TT, 11:12], in_=sm[0:TT, 11:12], func=AF.Exp, scale=-0.5), reads=[k_ + "rk"], writes=[k_ + "rk"])
                P.op("dve", lambda e, bank=bank, sm=sm, ckst=ckst: e.scalar_tensor_tensor(out=ckst[0:TT, 0:256], in0=ps[bank][0:TT, 0:256], scalar=sm[0:TT, 11:12], in1=kvG[0:TT, :], op0=ALU.mult, op1=ALU.mult),
                     reads=[bkey(bank), k_ + "rk", "kvG"], writes=[ck])
                if isp:
                    cosv = ropeT[0:TT, tile_i * 16:(tile_i + 1) * 16]
                    sinv = ropeT[0:TT, 256 + tile_i * 16:256 + (tile_i + 1) * 16]
                else:
                    cosv = ropeTs[0:TT, 0:16]
                    sinv = ropeTs[0:TT, 16:32]
                x1 = ps[bank][0:TT, 256:272]
                x2 = ps[bank][0:TT, 272:288]
                t16 = sm[0:TT, 16:32]
                P.op("dve", lambda e, x1=x1, cosv=cosv, ckst=ckst: e.tensor_tensor(out=ckst[0:TT, 256:272], in0=x1, in1=cosv, op=ALU.mult), reads=[bkey(bank), "ropeT", ck], writes=[ck])
                P.op("dve", lambda e, x2=x2, sinv=sinv, t16=t16: e.tensor_tensor(out=t16, in0=x2, in1=sinv, op=ALU.mult), reads=[bkey(bank), "ropeT"], writes=[k_ + "t16"])
                P.op("dve", lambda e, ckst=ckst, t16=t16: e.tensor_tensor(out=ckst[0:TT, 256:272], in0=ckst[0:TT, 256:272], in1=t16, op=ALU.subtract), reads=[ck, k_ + "t16"], writes=[ck])
                P.op("dve", lambda e, x1=x1, sinv=sinv, ckst=ckst: e.tensor_tensor(out=ckst[0:TT, 272:288], in0=x1, in1=sinv, op=ALU.mult), reads=[bkey(bank), "ropeT", ck], writes=[ck])
                P.op("dve", lambda e, x2=x2, cosv=cosv, t16=t16: e.tensor_tensor(out=t16, in0=x2, in1=cosv, op=ALU.mult), reads=[bkey(bank), "ropeT", k_ + "t16"], writes=[k_ + "t16"])
                P.op("dve", lambda e, ckst=ckst, t16=t16: e.tensor_tensor(out=ckst[0:TT, 272:288], in0=ckst[0:TT, 272:288], in1=t16, op=ALU.add), reads=[ck, k_ + "t16"], writes=[ck])
                if isp:
                    r0 = row0 + tt * 128
                    P.dma(lambda e, r0=r0, ckst=ckst: e.dma_start(out=ckv_p[s, r0:r0 + 128, :], in_=ckst[0:128, 0:256]), reads=[ck], writes=[], semkey="ckvo%d" % ts)
                    P.dma(lambda e, r0=r0, ckst=ckst: e.dma_start(out=kpe_p[s, r0:r0 + 128, :], in_=ckst[0:128, 256:288]), reads=[ck], writes=[], semkey="kpeo%d" % ts)
                    ctk = ckvtok[:, tile_i, :]
                else:
                    P.dma(lambda e, ckst=ckst: e.dma_start(out=ckv_s[0:64, :], in_=ckst[0:64, 0:256]), reads=[ck], writes=[], semkey="ckvo%d" % ts)
                    P.dma(lambda e, ckst=ckst: e.dma_start(out=kpe_s[0:64, :], in_=ckst[0:64, 256:288]), reads=[ck], writes=[], semkey="kpeo%d" % ts)
                    ctk = ckvn[:, :]
                P.op("pool", lambda e, ctk=ctk, ckst=ckst: e.tensor_copy(out=ctk[0:TT, :], in_=ckst[0:TT, 0:256]), reads=[ck], writes=["ckvtok"])
                P.op("pool", lambda e, ckst=ckst, kpb=kpb: e.tensor_copy(out=kpb[0:TT, :], in_=ckst[0:TT, 256:288]), reads=[ck], writes=[kk])
                bank2 = dbank()

                def fnt(e, ctk=ctk, bank2=bank2, kpb=kpb):
                    e.transpose(psb[bank2][:, 0:TT], ctk[0:TT, 0:128], identb[0:TT, 0:TT])
                    e.transpose(psb[bank2][:, 128:128 + TT], ctk[0:TT, 128:256], identb[0:TT, 0:TT])
                    return e.transpose(psb[bank2][0:32, 256:256 + TT], kpb[0:TT, 0:32], identb[0:TT, 0:TT])
                P.op("pe", fnt, reads=["ckvtok", kk, "identb"], writes=[bkey(bank2)])
                if isp:
                    c0 = tile_i * 128
                    P.op("act", lambda e, c0=c0, bank2=bank2: e.activation(out=ckvT[:, :, c0:c0 + 128], in_=psb[bank2][:, 0:256].rearrange("p (a t) -> p a t", a=2), func=AF.Copy),
                         reads=[bkey(bank2)], writes=["ckvT"])
                    P.op("dve", lambda e, c0=c0, bank2=bank2: e.tensor_copy(out=kpeT[0:32, c0:c0 + 128], in_=psb[bank2][0:32, 256:384]), reads=[bkey(bank2)], writes=["kpeT"])
                else:
                    P.op("act", lambda e, bank2=bank2: e.activation(out=sq[:, 0, 0:64], in_=psb[bank2][:, 0:64], func=AF.Copy), reads=[bkey(bank2)], writes=["sq0"])
                    P.op("act", lambda e, bank2=bank2: e.activation(out=sq[:, 0, 64:128], in_=psb[bank2][:, 128:192], func=AF.Copy), reads=[bkey(bank2)], writes=["sq0"])
                    P.op("act", lambda e, bank2=bank2: e.activation(out=sq[0:32, 0, 128:192], in_=psb[bank2][0:32, 256:320], func=AF.Copy), reads=[bkey(bank2)], writes=["sq0"])
            if float(os.environ.get("KSTOP", 99)) <= 5:
                return
            for m in range(4):
                bank = proj_fm(wuqn[:, :, :].rearrange("p k n -> p (k n)"), "wuqn", m * 128, 128, lambda kc: cqT[:, kc, 0:U], ["cqT"], 3, U, 512)
                P.op("act", lambda e, m=m, bank=bank: e.activation(out=qnT[:, m, 0:U], in_=ps[bank][:, 0:U], func=AF.Copy), reads=[bkey(bank)], writes=["qnT"])
            if isp:
                P.dma(lambda e: e.dma_start(out=cosF[0:32, 0:512], in_=rope_f[0, :, row0:row0 + 512]), writes=["cosF"], semkey="cosF")
                P.dma(lambda e: e.dma_start(out=sinF[0:32, 0:512], in_=rope_f[1, :, row0:row0 + 512]), writes=["sinF"], semkey="sinF")
            else:
                P.dma(lambda e: e.dma_start(out=cosF[0:32, 0:64], in_=rope_fs[0, :, :]), writes=["cosF"], semkey="cosF")
                P.dma(lambda e: e.dma_start(out=sinF[0:32, 0:64], in_=rope_fs[1, :, :]), writes=["sinF"], semkey="sinF")

            def head_q(hh, q0, q1, qb_=0, d0=None):
                hp, m = hh % 2, hh // 2
                qlat = qlat_b[qb_]
                qpe = qpe_b[qb_]
                if d0 is None:
                    d0 = q0
                d1 = d0 + (q1 - q0)
                for cc in range(2):
                    bank = dbank()
                    P.op("pe", lambda e, cc=cc, bank=bank: e.matmul(ps[bank][:, q0:q1], wukp[hp * 64:(hp + 1) * 64, m, cc * 128:(cc + 1) * 128], qnT[hp * 64:(hp + 1) * 64, m, q0:q1], start=True, stop=True),
                         reads=["wukp", "qnT"], writes=[bkey(bank)])
                    if cc == 0:
                        P.op("act", lambda e, bank=bank: e.activation(out=qlat[:, 0, d0:d1], in_=ps[bank][:, q0:q1], func=AF.Copy), reads=[bkey(bank)], writes=["qlat%d" % qb_])
                    else:
                        P.op("dve", lambda e, bank=bank: e.tensor_copy(out=qlat[:, 1, d0:d1], in_=ps[bank][:, q0:q1]), reads=[bkey(bank)], writes=["qlat%d" % qb_])
                b1 = dbank()
                b2 = dbank()

                def fq(e, b1=b1, b2=b2):
                    ins = None
                    for kc in range(3):
                        e.matmul(ps[b1][0:32, q0:q1], wuqp[:, kc, hh * 32:(hh + 1) * 32], cqT[:, kc, q0:q1], start=(kc == 0), stop=(kc == 2))
                    for kc in range(3):
                        ins = e.matmul(ps[b2][0:32, q0:q1], wuqs[:, kc, hh * 32:(hh + 1) * 32], cqT[:, kc, q0:q1], start=(kc == 0), stop=(kc == 2))
                    return ins
                P.op("pe", fq, reads=["wuqp", "wuqs", "cqT"], writes=[bkey(b1), bkey(b2)])
                P.op("dve", lambda e, b1=b1: e.tensor_tensor(out=tmpf[0:32, 0, q0:q1], in0=ps[b1][0:32, q0:q1], in1=cosF[0:32, q0:q1], op=ALU.mult), reads=[bkey(b1), "cosF"], writes=["tmpf0"])
                P.op("dve", lambda e, b2=b2: e.tensor_tensor(out=tmpf[0:32, 1, q0:q1], in0=ps[b2][0:32, q0:q1], in1=sinF[0:32, q0:q1], op=ALU.mult), reads=[bkey(b2), "sinF"], writes=["tmpf1"])
                P.op("pool", lambda e: e.tensor_tensor(out=qpe[0:32, d0:d1], in0=tmpf[0:32, 0, q0:q1], in1=tmpf[0:32, 1, q0:q1], op=ALU.add), reads=["tmpf0", "tmpf1"], writes=["qpe%d" % qb_])

            def head_out(hh, q0, q1, s0=None):
                hp, m = hh % 2, hh // 2
                if s0 is None:
                    s0 = q0
                s1 = s0 + (q1 - q0)
                P.op("act", lambda e: e.activation(out=OLs[:, 0, s0:s1], in_=ps[5][:, s0:s1], func=AF.Copy), reads=[bkey(5)], writes=["OLs0"])
                P.op("dve", lambda e: e.tensor_copy(out=OLs[:, 1, s0:s1], in_=ps[6][:, s0:s1]), reads=[bkey(6)], writes=["OLs1"])
                P.op("dve", lambda e: e.reciprocal(out=rden[hp * 64:(hp + 1) * 64, s0:s1], in_=ps[7][hp * 64:(hp + 1) * 64, s0:s1]), reads=[bkey(7)], writes=["rden"])
                bank = dbank()

                def fo(e, bank=bank):
                    e.matmul(ps[bank][hp * 64:(hp + 1) * 64, s0:s1], wuvb[:, hh, 0, :], OLs[:, 0, s0:s1], start=True, stop=False)
                    return e.matmul(ps[bank][hp * 64:(hp + 1) * 64, s0:s1], wuvb[:, hh, 1, :], OLs[:, 1, s0:s1], start=False, stop=True)
                P.op("pe", fo, reads=["wuvb", "OLs0", "OLs1"], writes=[bkey(bank)])
                P.op("dve", lambda e, bank=bank: e.tensor_tensor(out=mix[hp * 64:(hp + 1) * 64, 4 + m, q0:q1], in0=ps[bank][hp * 64:(hp + 1) * 64, s0:s1], in1=rden[hp * 64:(hp + 1) * 64, s0:s1], op=ALU.mult),
                     reads=[bkey(bank), "rden"], writes=["mix%d" % (4 + m)])

            tu = 0
            if isp:
                nt = 4 * qb + 4
                head_q(0, 0, 512, 0)
                for hh in range(8):
                    gens = []
                    for kt in range(nt):
                        j = kt - 4 * qb
                        q0 = 128 * j if j > 0 else 0
                        mask = mlm[:, 0:512 - q0] if j >= 0 else None
                        gens.append(mla_tile(ckvT[:, 0, kt * 128:(kt + 1) * 128], ckvT[:, 1, kt * 128:(kt + 1) * 128], kpeT[0:32, kt * 128:(kt + 1) * 128],
                                             ckvtok[:, kt, :], 128, q0, 512, mask, kt == 0, kt == nt - 1, hh % 2, tu, hh % 2))
                        tu += 1
                    hook = (lambda hh=hh: head_q(hh + 1, 0, 512, (hh + 1) % 2)) if hh < 7 else None
                    run_pipelined(gens, 1, hook)
                    head_out(hh, 0, 512)
            else:
                for ss in range(int(os.environ.get("KLIM_MLA", SPB))):
                    for qtr in range(4):
                        P.dma(lambda e, ss=ss, qtr=qtr: e.dma_start(out=cstg[:, 0:2048].rearrange("p (a d) -> p a d", a=8), in_=cckv[ss, qtr * 1024:(qtr + 1) * 1024, :].rearrange("(a p) d -> p a d", p=128)),
                              writes=["cstg"], semkey="cstg")
                        P.op("dve", lambda e, qtr=qtr: e.tensor_copy(out=ckvtok_s[:, qtr * 8:(qtr + 1) * 8, :], in_=cstg[:, 0:2048].rearrange("p (a d) -> p a d", a=8)), reads=["cstg"], writes=["ckvtok"])
                    P.dma(lambda e, ss=ss: e.dma_start(out=cstg[:, 0:1024].rearrange("p (a d) -> p a d", a=32), in_=ckpe[ss, :, :].rearrange("(a p) d -> p a d", p=128)),
                          writes=["cstg"], semkey="cstg")
                    P.op("dve", lambda e: e.tensor_copy(out=kpbs[:, :, :], in_=cstg[:, 0:1024].rearrange("p (a d) -> p a d", a=32)), reads=["cstg"], writes=["kpbs"])
                    for kt in range(32):
                        for cc in range(2):
                            if (kt * 2 + cc) % 4 == 0:
                                bank = dbank()
                            i4 = (kt * 2 + cc) % 4
                            P.op("pe", lambda e, kt=kt, cc=cc, bank=bank, i4=i4: e.transpose(psb[bank][:, i4 * 128:(i4 + 1) * 128], ckvtok_s[:, kt, cc * 128:(cc + 1) * 128], identb[:, :]),
                                 reads=["ckvtok", "identb"], writes=[bkey(bank)])
                            P.op("act", lambda e, kt=kt, cc=cc, bank=bank, i4=i4: e.activation(out=ckvT_s[:, cc, kt * 128:(kt + 1) * 128], in_=psb[bank][:, i4 * 128:(i4 + 1) * 128], func=AF.Copy),
                                 reads=[bkey(bank)], writes=["ckvT"])
                    for g in range(8):
                        bank = dbank()
                        P.op("pe", lambda e, g=g, bank=bank: [e.transpose(psb[bank][0:32, i * 128:(i + 1) * 128], kpbs[:, 4 * g + i, :], identb[:, :]) for i in range(4)][-1],
                             reads=["kpbs", "identb"], writes=[bkey(bank)])
                        P.op("dve", lambda e, g=g, bank=bank: e.tensor_copy(out=kpeT_s[0:32, g * 512:(g + 1) * 512], in_=psb[bank][0:32, 0:512]), reads=[bkey(bank)], writes=["kpeT"])
                    P.dma(lambda e, ss=ss: e.dma_start(out=ckvtok_s[0:16, 32, :], in_=ckvn[ss * 16:(ss + 1) * 16, :]), reads=["ckvtok"], writes=["ckvtok"], semkey="ckvnn")
                    P.op("act", lambda e, ss=ss: e.activation(out=ckvT_s[:, 0, 4096:4112], in_=sq[:, 0, ss * 16:(ss + 1) * 16], func=AF.Copy), reads=["sq0"], writes=["ckvT"])
                    P.op("act", lambda e, ss=ss: e.activation(out=ckvT_s[:, 1, 4096:4112], in_=sq[:, 0, 64 + ss * 16:64 + (ss + 1) * 16], func=AF.Copy), reads=["sq0"], writes=["ckvT"])
                    P.op("act", lambda e, ss=ss: e.activation(out=kpeT_s[0:32, 4096:4112], in_=sq[0:32, 0, 128 + ss * 16:128 + (ss + 1) * 16], func=AF.Copy), reads=["sq0"], writes=["kpeT"])
                    q0, q1 = ss * 16, ss * 16 + 16
                    for hh in range(8):
                        head_q(hh, q0, q1, 0, d0=hh * 16)
                    gens = []
                    for kt in range(33):
                        nk = 16 if kt == 32 else 128
                        k0 = kt * 128
                        gens.append(mla_tile(ckvT_s[:, 0, k0:k0 + nk], ckvT_s[:, 1, k0:k0 + nk], kpeT_s[0:32, k0:k0 + nk],
                                             ckvtok_s[0:nk, kt, :], nk, 0, 128, None, kt == 0, kt == 32, 0, tu, 0))
                        tu += 1
                    run_pipelined(gens)
                    for hh in range(8):
                        head_out(hh, q0, q1, s0=hh * 16)
            if float(os.environ.get("KSTOP", 99)) <= 6:
                return
            wout_and_post(SLAB_OOUT, 16 + 8, U)
            ffn(1, U)

            if float(os.environ.get("KSTOP", 99)) <= 7:
                return
            for tt in range(NTT):
                yout = io_b[tt % 2]
                yk = "io%d" % (tt % 2)
                for half in range(2):
                    bank = dbank()
                    P.op("pe", lambda e, half=half, bank=bank, tt=tt: [e.transpose(ps[bank][0:TT, c4 * 128:(c4 + 1) * 128], xT[:, half * 4 + c4, tt * TT:(tt + 1) * TT], identf[:, :]) for c4 in range(4)][-1],
                         reads=["xT%d" % (half * 4 + c4) for c4 in range(4)] + ["identf"], writes=[bkey(bank)])
                    P.op("act", lambda e, half=half, bank=bank, yout=yout: e.activation(out=yout[0:TT, half * 512:(half + 1) * 512], in_=ps[bank][0:TT, :], func=AF.Copy),
                         reads=[bkey(bank)], writes=[yk])
                dst = y_p[s, row0 + tt * 128: row0 + tt * 128 + 128, :] if isp else y_s[0:64, :]
                P.dma(lambda e, dst=dst, yout=yout: e.dma_start(out=dst, in_=yout[0:TT, :]), reads=[yk], writes=[], semkey=yk)
            barrier()

        setup()
        barrier()
        for (kind, s, qb) in units:
            unit(kind, s, qb)
        P.emit(st)
    return nc


def _consts():
    kl = np.arange(128)[:, None]
    x = np.arange(512)[None, :]
    cst = np.zeros((128, 1408), np.float32)
    cst[:, 0:128] = np.eye(128, dtype=np.float32)
    jj = np.arange(128)[:, None]; kk = np.arange(128)[None, :]
    cst[:, 128:256] = -(jj >= kk).astype(np.float32)
    cst[:, 256:768] = (kl < x).astype(np.float32)
    cst[:, 768:1280] = ((kl // 64) <= (x // 64)).astype(np.float32)
    tt = np.arange(128)[:, None]; ss = np.arange(128)[None, :]
    cst[:, 1280:1408] = (ss <= tt).astype(np.float32)
    half = 16
    inv = (10000.0 ** (-(np.arange(half, dtype=np.float32) / np.float32(half)))).astype(np.float32)

    def cs(pos):
        ang = (pos.astype(np.float32)[:, None] * inv[None, :]).astype(np.float32)
        return np.cos(ang.astype(np.float64)).astype(np.float32), np.sin(ang.astype(np.float64)).astype(np.float32)
    cp, sp_ = cs(np.arange(T))
    rope_t = np.zeros((128, 512), np.float32)
    rope_t[:, 0:256] = cp.reshape(16, 128, 16).transpose(1, 0, 2).reshape(128, 256)
    rope_t[:, 256:512] = sp_.reshape(16, 128, 16).transpose(1, 0, 2).reshape(128, 256)
    cs_, ss_ = cs(PAST + np.arange(DS))
    rope_ts = np.zeros((64, 32), np.float32)
    rope_ts[:, 0:16] = np.tile(cs_, (4, 1))
    rope_ts[:, 16:32] = np.tile(ss_, (4, 1))
    rope_f = np.zeros((2, 32, T), np.float32)
    rope_f[0, 0:16] = cp.T; rope_f[0, 16:32] = cp.T
    rope_f[1, 0:16] = -sp_.T; rope_f[1, 16:32] = sp_.T
    rope_fs = np.zeros((2, 32, 64), np.float32)
    rope_fs[0, 0:16] = np.tile(cs_.T, (1, 4)); rope_fs[0, 16:32] = np.tile(cs_.T, (1, 4))
    rope_fs[1, 0:16] = -np.tile(ss_.T, (1, 4)); rope_fs[1, 16:32] = np.tile(ss_.T, (1, 4))
    return cst, rope_t, rope_ts, rope_f, rope_fs


_CACHE = {}


def kernel(x_prompt, x_sample, cache_sb_k, cache_sb_v, state_conv, cache_mla_ckv, cache_mla_kpe,
           mix_pre_g, mix_post_g, ffn_pre_g, ffn_post_g, even_w_in, even_w_conv, even_w_out,
           odd_w_in, sgu_ln_g, sgu_ln_b, sgu_w_s, sgu_b_s, mla_q_norm_g, mla_kv_norm_g,
           mla_w_uq, mla_w_uk, mla_w_uv, odd_w_out, ffn_w_up, ffn_w_down):
    f = lambda a: np.ascontiguousarray(np.asarray(a, dtype=np.float32))
    dbg = os.environ.get("KDEBUG", "")
    if dbg:
        units = []
        for tok in dbg.split(","):
            if tok == "s":
                units.append(("s", 0, 0))
            else:
                a, b = tok.split(":")
                units.append(("p", int(a), int(b)))
    else:
        units = [("p", s, qb) for s in range(SPB) for qb in range(4)] + [("s", 0, 0)]
    key = tuple(units)
    if key not in _CACHE:
        _CACHE[key] = build_program(units)
    nc = _CACHE[key]
    cst, rope_t, rope_ts, rope_f, rope_fs = _consts()
    grow = np.concatenate([f(mix_pre_g).reshape(16, 128), f(mix_post_g).reshape(16, 128), f(ffn_pre_g).reshape(16, 128),
                           f(ffn_post_g).reshape(16, 128), f(even_w_conv).reshape(12, 128), f(mla_q_norm_g).reshape(3, 128)], axis=0)
    shared = dict(
        ein=f(even_w_in)[0], eout=f(even_w_out)[0], oin=f(odd_w_in)[0], oout=f(odd_w_out)[0],
        fup=f(ffn_w_up), fdn=f(ffn_w_down), grow=f(grow),
        lng=f(sgu_ln_g).reshape(1, 512), lnb=f(sgu_ln_b).reshape(1, 512), kvg=f(mla_kv_norm_g).reshape(1, 256),
        bsd=f(sgu_b_s).reshape(1, 512), bss=f(f(sgu_b_s)[0][:, 0:16]).reshape(1, 4, 16),
        wsd=f(sgu_w_s)[0], wuq=f(mla_w_uq)[0], wuk=f(mla_w_uk)[0], wuv=f(mla_w_uv)[0],
        cst=cst, rope_t=rope_t, rope_ts=rope_ts, rope_f=rope_f, rope_fs=rope_fs)
    xp = f(x_prompt); xs = f(x_sample)
    csk = f(cache_sb_k)[0].reshape(32, PAST, 512); csv = f(cache_sb_v)[0].reshape(32, PAST, 512)
    sc = f(state_conv)[0]; cckv = f(cache_mla_ckv)[0]; ckpe = f(cache_mla_kpe)[0]
    in_maps = []
    for c in range(NCORES):
        sl = slice(c * SPB, (c + 1) * SPB)
        m = dict(shared)
        m.update(xp=xp[sl], xs=f(xs[sl].reshape(SPB * DS, D)), csk=csk[sl], csv=csv[sl], sconv=sc[sl], cckv=cckv[sl], ckpe=ckpe[sl])
        in_maps.append(m)
    res = run_bass_kernel_spmd(nc, in_maps, core_ids=list(range(NCORES)))
    R = res.results
    cat = lambda name: np.concatenate([np.asarray(R[c][name], dtype=np.float32) for c in range(NCORES)], axis=0)
    y_p = cat("y_p")
    y_s = cat("y_s").reshape(32, DS, D)
    sbk_p = cat("sbk_p").reshape(1, 32, T, 8, 64)
    sbv_p = cat("sbv_p").reshape(1, 32, T, 8, 64)
    conv_p = cat("conv_p").reshape(1, 32, 2, 512)
    ckv_p = cat("ckv_p").reshape(1, 32, T, 256)
    kpe_p = cat("kpe_p").reshape(1, 32, T, 32)
    sbk_s = cat("sbk_s").reshape(1, 32, DS, 8, 64)
    sbv_s = cat("sbv_s").reshape(1, 32, DS, 8, 64)
    conv_s = cat("conv_s").reshape(1, 32, 2, 512)
    ckv_s = cat("ckv_s").reshape(1, 32, DS, 256)
    kpe_s = cat("kpe_s").reshape(1, 32, DS, 32)
    sguv_s = cat("sguv_s").reshape(1, 32, DS, 512)
    return (y_p, y_s, sbk_p, sbv_p, conv_p, ckv_p, kpe_p, sbk_s, sbv_s, conv_s, ckv_s, kpe_s, sguv_s)
```
